# Optimizing a Trainium2 kernel written in Bass

```python
import math
import jax, jax.numpy as jnp
from jax import lax
import numpy as np

D_MODEL = 1024
BATCH = 4
SEQ = 8192
DEPTH = 2

N_MIXERS = 2
N_META = 16
GDN_HEADS = 8
GDN_HEAD_DIM = 128
GDN_WIDTH = GDN_HEADS * GDN_HEAD_DIM
GDN_CONV = 4
CHUNK = 64
META_PAD = (-N_META) % CHUNK
SC_WIDTH = D_MODEL
SC_CONV = 3
D_FF = 2816
FFN_CONV = 3
N_LAYERS_A = (DEPTH + 1) // 2
N_LAYERS_B = DEPTH // 2
ALPHA = (2.0 * DEPTH) ** 0.25
BETA_INIT = (8.0 * DEPTH) ** -0.25
LN_EPS = 1e-5
RMS_EPS = 1e-6
L2_EPS = 1e-6

kernel_name = "hybrid_gdn_shortconv_convffn_deepnorm"


def causal_dwconv(x, w):
    width, ch = w.shape
    return lax.conv_general_dilated(
        x, w[:, None, :].astype(x.dtype), window_strides=(1,), padding=[(width - 1, 0)],
        dimension_numbers=("NWC", "WIO", "NWC"), feature_group_count=ch)


def layer_norm(x, g, b):
    xf = x.astype(jnp.float32)
    mu = jnp.mean(xf, axis=-1, keepdims=True)
    var = jnp.mean(jnp.square(xf - mu), axis=-1, keepdims=True)
    y = (xf - mu) * lax.rsqrt(var + LN_EPS) * g.astype(jnp.float32) + b.astype(jnp.float32)
    return y.astype(x.dtype)


def l2norm(x):
    return x * lax.rsqrt(jnp.sum(x * x, axis=-1, keepdims=True) + L2_EPS)


def chunk_gated_delta_rule(q, k, v, g, beta):
    bsz, t_len, h, dk = q.shape
    n = t_len // CHUNK

    def to_chunks(t):
        return jnp.transpose(t.reshape(bsz, n, CHUNK, h, -1), (0, 3, 1, 2, 4))

    q, k, v = to_chunks(q), to_chunks(k), to_chunks(v)
    g = jnp.transpose(g.reshape(bsz, n, CHUNK, h), (0, 3, 1, 2))
    beta = jnp.transpose(beta.reshape(bsz, n, CHUNK, h), (0, 3, 1, 2))
    g = jnp.cumsum(g, axis=-1)

    causal = jnp.tril(jnp.ones((CHUNK, CHUNK), dtype=bool))
    strict = jnp.tril(jnp.ones((CHUNK, CHUNK), dtype=bool), -1)
    decay = jnp.exp(jnp.where(causal, g[..., :, None] - g[..., None, :], -jnp.inf))

    k_beta = k * beta[..., None]
    v_beta = v * beta[..., None]
    m = jnp.where(strict, jnp.einsum("bhnid,bhnjd->bhnij", k_beta, k) * decay, 0.0)
    a_mat = jnp.eye(CHUNK, dtype=q.dtype) + m
    u = lax.linalg.triangular_solve(a_mat, v_beta, left_side=True, lower=True)
    w = lax.linalg.triangular_solve(a_mat, k_beta * jnp.exp(g)[..., None], left_side=True, lower=True)

    qk = jnp.einsum("bhnid,bhnjd->bhnij", q, k) * decay
    q_g = q * jnp.exp(g)[..., None]
    g_last = g[..., -1]
    k_dec = k * jnp.exp(g_last[..., None] - g)[..., None]

    def step(state, inp):
        qg_i, kd_i, u_i, w_i, qk_i, gl_i = inp
        v_new = u_i - jnp.einsum("bhcd,bhde->bhce", w_i, state)
        o_i = jnp.einsum("bhcd,bhde->bhce", qg_i, state) + jnp.einsum("bhij,bhje->bhie", qk_i, v_new)
        state = state * jnp.exp(gl_i)[..., None, None] + jnp.einsum("bhcd,bhce->bhde", kd_i, v_new)
        return state, o_i

    xs = tuple(jnp.moveaxis(t, 2, 0) for t in (q_g, k_dec, u, w, qk, g_last))
    state0 = jnp.zeros((bsz, h, dk, v.shape[-1]), dtype=q.dtype)
    _, o = lax.scan(step, state0, xs)
    return jnp.transpose(o, (1, 0, 3, 2, 4)).reshape(bsz, t_len, h, -1)


def gated_deltanet(h, w_in, conv_w, a_log, dt_bias, norm_w, w_out):
    bsz, seq_len, _ = h.shape
    proj = h @ w_in
    qkv, z, b_raw, a_raw = jnp.split(
        proj, [3 * GDN_WIDTH, 4 * GDN_WIDTH, 4 * GDN_WIDTH + GDN_HEADS], axis=-1)
    qkv = jax.nn.silu(causal_dwconv(qkv, conv_w))
    q, k, v = [t.reshape(bsz, seq_len, GDN_HEADS, GDN_HEAD_DIM).astype(jnp.float32)
               for t in jnp.split(qkv, 3, axis=-1)]
    q = l2norm(q) * (GDN_HEAD_DIM ** -0.5)
    k = l2norm(k)
    beta = jax.nn.sigmoid(b_raw.astype(jnp.float32))
    g = -jnp.exp(a_log.astype(jnp.float32)) * jax.nn.softplus(
        a_raw.astype(jnp.float32) + dt_bias.astype(jnp.float32))
    pad4 = ((0, 0), (META_PAD, 0), (0, 0), (0, 0))
    pad3 = ((0, 0), (META_PAD, 0), (0, 0))
    o = chunk_gated_delta_rule(jnp.pad(q, pad4), jnp.pad(k, pad4), jnp.pad(v, pad4),
                               jnp.pad(g, pad3), jnp.pad(beta, pad3))[:, META_PAD:]
    o = o * lax.rsqrt(jnp.mean(o * o, axis=-1, keepdims=True) + RMS_EPS) * norm_w.astype(jnp.float32)
    o = o * jax.nn.silu(z.reshape(bsz, seq_len, GDN_HEADS, GDN_HEAD_DIM).astype(jnp.float32))
    return o.reshape(bsz, seq_len, GDN_WIDTH).astype(h.dtype) @ w_out


def short_conv_mixer(h, w_in, conv_w, w_out):
    b_gate, c_gate, xv = jnp.split(h @ w_in, 3, axis=-1)
    u = causal_dwconv(c_gate * xv, conv_w)
    return (b_gate * u) @ w_out


def conv_ffn(h, w_up, conv_w, w_down):
    u, gate = jnp.split(h @ w_up, 2, axis=-1)
    u = causal_dwconv(u, conv_w)
    return (jax.nn.silu(u) * gate) @ w_down


def setup_inputs(seed: int = 0) -> dict:
    key = jax.random.key(seed)
    ks = iter(jax.random.split(key, 32))
    f32 = jnp.float32

    def nrm(shape, scale):
        return jax.random.normal(next(ks), shape, f32) * scale

    a_in_cols = 4 * GDN_WIDTH + 2 * GDN_HEADS
    dt = jnp.exp(jax.random.uniform(next(ks), (N_LAYERS_A, GDN_HEADS), f32,
                                    math.log(1e-3), math.log(1e-1)))
    return {
        "x": nrm((BATCH, SEQ, D_MODEL), 1.0),
        "meta": nrm((N_META, D_MODEL), 1.0),
        "a_w_in": nrm((N_LAYERS_A, D_MODEL, a_in_cols), D_MODEL ** -0.5),
        "a_conv": nrm((N_LAYERS_A, GDN_CONV, 3 * GDN_WIDTH), GDN_CONV ** -0.5),
        "a_log": jnp.log(jax.random.uniform(next(ks), (N_LAYERS_A, GDN_HEADS), f32, 1.0, 16.0)),
        "a_dt_bias": dt + jnp.log(-jnp.expm1(-dt)),
        "a_norm": 1.0 + nrm((N_LAYERS_A, GDN_HEAD_DIM), 0.02),
        "a_w_out": nrm((N_LAYERS_A, GDN_WIDTH, D_MODEL), BETA_INIT * GDN_WIDTH ** -0.5),
        "b_w_in": nrm((N_LAYERS_B, D_MODEL, 3 * SC_WIDTH), D_MODEL ** -0.5),
        "b_conv": nrm((N_LAYERS_B, SC_CONV, SC_WIDTH), SC_CONV ** -0.5),
        "b_w_out": nrm((N_LAYERS_B, SC_WIDTH, D_MODEL), BETA_INIT * SC_WIDTH ** -0.5),
        "ln_mix_g": 1.0 + nrm((DEPTH, D_MODEL), 0.02),
        "ln_mix_b": nrm((DEPTH, D_MODEL), 0.02),
        "ffn_w_up": nrm((DEPTH, D_MODEL, 2 * D_FF), D_MODEL ** -0.5),
        "ffn_conv": nrm((DEPTH, FFN_CONV, D_FF), FFN_CONV ** -0.5),
        "ffn_w_down": nrm((DEPTH, D_FF, D_MODEL), BETA_INIT * D_FF ** -0.5),
        "ln_ffn_g": 1.0 + nrm((DEPTH, D_MODEL), 0.02),
        "ln_ffn_b": nrm((DEPTH, D_MODEL), 0.02),
    }


def reference(x, meta, a_w_in, a_conv, a_log, a_dt_bias, a_norm, a_w_out,
              b_w_in, b_conv, b_w_out, ln_mix_g, ln_mix_b,
              ffn_w_up, ffn_conv, ffn_w_down, ln_ffn_g, ln_ffn_b):
    bsz = x.shape[0]
    h = jnp.concatenate(
        [jnp.broadcast_to(meta.astype(x.dtype)[None], (bsz, N_META, D_MODEL)), x], axis=1)
    for i in range(DEPTH):
        j = i // N_MIXERS
        if i % N_MIXERS == 0:
            mix = gated_deltanet(h, a_w_in[j], a_conv[j], a_log[j], a_dt_bias[j], a_norm[j], a_w_out[j])
        else:
            mix = short_conv_mixer(h, b_w_in[j], b_conv[j], b_w_out[j])
        h = layer_norm(ALPHA * h + mix, ln_mix_g[i], ln_mix_b[i])
        h = layer_norm(ALPHA * h + conv_ffn(h, ffn_w_up[i], ffn_conv[i], ffn_w_down[i]),
                       ln_ffn_g[i], ln_ffn_b[i])
    return h[:, N_META:]
```

```python
import contextlib
import numpy as np
import concourse.bass as bass
import concourse.mybir as mybir
from concourse.bass_utils import run_bass_kernel_spmd

F32 = mybir.dt.float32
BF16 = mybir.dt.bfloat16
AF = mybir.ActivationFunctionType
ALU = mybir.AluOpType
AX = mybir.AxisListType

NSEM_PER_ENG = 6
SAME_ENG_SYNC = True

D = 1024
KC = 8
TILE = 128
BLK = 384
CH = 64
HALO = 3
DFF = 2816
FC = DFF // 128
ALPHA = float((2.0 * 2) ** 0.25)
NPRE_FULL = 11
NMAIN_FULL = 11


class Buf:
    __slots__ = ("name", "writers", "readers", "open", "pre")

    def __init__(self, name):
        self.name = name
        self.writers = []
        self.readers = []
        self.open = False
        self.pre = []


class Op:
    __slots__ = ("eng", "emit", "deps", "idx", "dma_key", "dma_cnt", "waits")

    def __init__(self, eng, emit):
        self.eng = eng
        self.emit = emit
        self.deps = []
        self.idx = -1
        self.dma_key = None
        self.dma_cnt = 0
        self.waits = []


class Prog:
    ENGS = ("pe", "act", "dve", "pool", "sp")

    def __init__(self, nc):
        self.nc = nc
        self.ops = {e: [] for e in self.ENGS}
        self.ncomp = {e: 0 for e in self.ENGS}
        self.dma_counts = {}
        self.stack = contextlib.ExitStack()
        self.nbuf = 0
        self.nname = 0

    def sb(self, shape, dt, name=None):
        self.nname += 1
        return self.stack.enter_context(self.nc.sbuf_tensor(f"{name or 'sb'}_{self.nname}", list(shape), dt))

    def ps(self, shape, dt=F32, name=None):
        self.nname += 1
        return self.stack.enter_context(self.nc.psum_tensor(name or f"ps{self.nname}", list(shape), dt))

    def buf(self, name=None):
        self.nbuf += 1
        return Buf(name or f"b{self.nbuf}")

    def _track(self, op, reads, writes, pwrites):
        for b in reads:
            op.deps.extend(b.writers)
            b.readers.append(op)
            b.open = False
        for b in writes:
            op.deps.extend(b.readers)
            op.deps.extend(b.writers)
            b.pre = list(b.readers) + list(b.writers)
            b.writers = [op]
            b.readers = []
            b.open = True
        for b in pwrites:
            if b.open:
                op.deps.extend(b.pre)
                b.writers.append(op)
            else:
                op.deps.extend(b.readers)
                op.deps.extend(b.writers)
                b.pre = list(b.readers) + list(b.writers)
                b.writers = [op]
                b.readers = []
                b.open = True

    def op(self, eng, emit, reads=(), writes=(), pwrites=()):
        o = Op(eng, emit)
        self._track(o, reads, writes, pwrites)
        o.idx = self.ncomp[eng]
        self.ncomp[eng] += 1
        self.ops[eng].append(o)
        return o

    def dma(self, eng, emit, key, reads=(), writes=(), pwrites=()):
        o = Op(eng, emit)
        self._track(o, reads, writes, pwrites)
        o.idx = -1
        o.dma_key = key
        self.dma_counts[key] = self.dma_counts.get(key, 0) + 1
        o.dma_cnt = self.dma_counts[key]
        self.ops[eng].append(o)
        return o

    def setup_sems(self):
        nc = self.nc
        st = self.stack
        self.comp_sems = {}
        for e in ("pe", "act", "dve", "pool"):
            self.comp_sems[e] = [st.enter_context(nc.semaphore(f"s_{e}_{i}")) for i in range(NSEM_PER_ENG)]
        self.dma_sems = {}
        self.emitted = {e: 0 for e in self.ENGS}
        self.waited_idx = {e: {x: -1 for x in self.ENGS} for e in self.ENGS}
        self.waited_dma = {e: {} for e in self.ENGS}

    def emit(self):
        nc = self.nc
        comp_sems, dma_sems = self.comp_sems, self.dma_sems
        for k in self.dma_counts:
            if k not in dma_sems:
                dma_sems[k] = self.stack.enter_context(nc.semaphore(f"d_{len(dma_sems)}"))
        new_ops = {e: self.ops[e][self.emitted[e]:] for e in self.ENGS}
        for e in self.ENGS:
            waited_idx = self.waited_idx[e]
            waited_dma = self.waited_dma[e]
            for o in new_ops[e]:
                need_idx = {}
                need_dma = {}
                for d in o.deps:
                    if d is o:
                        continue
                    if d.dma_key is not None:
                        if d.dma_cnt > need_dma.get(d.dma_key, 0):
                            need_dma[d.dma_key] = d.dma_cnt
                    else:
                        if d.eng == e and (e == "pe" or not SAME_ENG_SYNC):
                            continue
                        if d.idx > need_idx.get(d.eng, -1):
                            need_idx[d.eng] = d.idx
                for src, k in need_idx.items():
                    if k > waited_idx[src]:
                        waited_idx[src] = k
                        o.waits.append((comp_sems[src][k % NSEM_PER_ENG], k // NSEM_PER_ENG + 1))
                for key, c in need_dma.items():
                    if c > waited_dma.get(key, 0):
                        waited_dma[key] = c
                        o.waits.append((dma_sems[key], 16 * c))
            self.emitted[e] = len(self.ops[e])
        final = [(dma_sems[k], 16 * c) for k, c in self.dma_counts.items()]

        def replay(ename, eng):
            for o in new_ops[ename]:
                for s, v in o.waits:
                    eng.wait_ge(s, v)
                ins = o.emit(eng)
                if o.dma_key is not None:
                    ins.then_inc(dma_sems[o.dma_key], 16)
                else:
                    ins.then_inc(comp_sems[ename][o.idx % NSEM_PER_ENG], 1)
            if ename == "sp":
                for s, v in final:
                    eng.wait_ge(s, v)

        with nc.Block() as block:
            @block.tensor
            def _(eng):
                replay("pe", eng)

            @block.scalar
            def _(eng):
                replay("act", eng)

            @block.vector
            def _(eng):
                replay("dve", eng)

            @block.gpsimd
            def _(eng):
                replay("pool", eng)

            @block.sync
            def _(eng):
                replay("sp", eng)

    @contextlib.contextmanager
    def phase(self):
        outer = self.stack
        self.stack = contextlib.ExitStack()
        ph = self.stack
        try:
            yield
            self.stack = outer
            self.emit()
        finally:
            self.stack = outer
            ph.close()


class Ring:
    def __init__(self, slots):
        self.slots = slots
        self.i = 0

    def next(self):
        s = self.slots[self.i % len(self.slots)]
        self.i += 1
        return s


def build_program(NPRE, NMAIN, upto=4):
    nc = bass.Bass("TRN2", target_bir_lowering=False)
    NB = NPRE + NMAIN
    NTOK = NB * BLK
    NMT = NMAIN * BLK
    MAIN0 = NPRE * BLK

    def din(name, shape):
        return nc.dram_tensor(name, list(shape), F32, kind="ExternalInput").ap()

    xT = din("xT", [D, HALO + NTOK])
    xtok = din("xtok", [NMT, D])
    maskd = din("mask", [128, 1])
    consts = din("consts", [128, 448])
    a_w_in = din("a_w_in", [D, 4112])
    a_convT = din("a_convT", [128, 24 * 4])
    a_vec = din("a_vec", [128, 16])
    a_normT = din("a_normT", [128, 1])
    a_w_out = din("a_w_out", [D, D])
    b_w_in = din("b_w_in", [D, 3 * D])
    b_convT = din("b_convT", [128, 8 * 3])
    b_w_out = din("b_w_out", [D, D])
    lng = din("lng", [4, 128, D])
    lnb = din("lnb", [4, 128, D])
    f_w_up = din("f_w_up", [2, D, 2 * DFF])
    f_convT = din("f_convT", [2, 128, FC * 3])
    f_w_down = din("f_w_down", [2, DFF, D])
    out = nc.dram_tensor("out", [NMT - TILE, D], F32, kind="ExternalOutput").ap()

    htok_s = [nc.dram_tensor(f"htok_s{i}", [NMT, D], F32, kind="Internal").ap() for i in range(3)]
    hT_s = [nc.dram_tensor(f"hT_s{i}", [D, HALO + NMT], BF16, kind="Internal").ap() for i in range(3)]

    P = Prog(nc)
    P.setup_sems()
    store_ops = []

    cst = P.sb([128, 448], F32, "cst")
    b_cst = P.buf("cst")
    P.dma("sp", lambda e: e.dma_start(out=cst[:], in_=consts[:, :]), "cst", writes=[b_cst])
    ident = cst[:, 0:128]
    Uincl = cst[0:64, 128:192]
    Lstrict = cst[0:64, 192:256]
    Ustrict = cst[0:64, 256:320]
    ones64 = cst[0:64, 320:448]
    ones1 = cst[:, 320:321]
    identb = P.sb([128, 128], BF16, "identb")
    b_identb = P.buf()
    P.op("dve", lambda e: e.tensor_copy(out=identb[:], in_=cst[:, 0:128]), reads=[b_cst], writes=[b_identb])
    onesb = P.sb([128, 2], BF16, "onesb")
    b_onesb = P.buf()
    P.op("dve", lambda e: e.tensor_copy(out=onesb[:], in_=cst[:, 320:322]), reads=[b_cst], writes=[b_onesb])
    maskt = P.sb([128, 1], F32, "maskt")
    b_mask = P.buf()
    P.dma("sp", lambda e: e.dma_start(out=maskt[:], in_=maskd[:, :]), "maskt", writes=[b_mask])
    zeros = P.sb([128, KC * HALO], BF16, "zeros")
    b_zeros = P.buf()
    P.op("pool", lambda e: e.memset(zeros[:], 0.0), writes=[b_zeros])
    b_hTs = [P.buf(f"hTs{i}") for i in range(3)]
    b_hts = [P.buf(f"htoks{i}") for i in range(3)]
    for i in range(3):
        v = hT_s[i].rearrange("(kc p) t -> p kc t", p=128)
        P.dma("sp", lambda e, v=v: e.dma_start(out=v[:, :, 0:HALO], in_=zeros[:].rearrange("p (k t) -> p k t", k=KC)),
              f"zeros{i}", reads=[b_zeros], pwrites=[b_hTs[i]])

    banks = [P.ps([128, 512], F32, f"bank{i}") for i in range(8)]
    bbank = [P.buf(f"bank{i}") for i in range(8)]
    big = Ring([(banks[i], bbank[i]) for i in (0, 1)])
    big6 = Ring([(banks[i], bbank[i]) for i in (0, 1, 4, 5, 6, 7)])
    wide0 = banks[2]
    wide1 = banks[3]
    b_wide = P.buf("wide")
    small = Ring([((banks[i], 0), bbank[i]) for i in (4, 5, 6, 7)])
    sm64 = md128 = su_ring = sc_ring = small

    gam = P.sb([128, D], F32, "gam")
    bet = P.sb([128, D], F32, "bet")
    b_gam = P.buf()
    b_bet = P.buf()
    hres_r = Ring([(P.sb([128, D], F32, f"hres{i}"), P.buf()) for i in range(1)])
    hTt_r = Ring([(P.sb([128, KC, TILE], BF16, f"hTt{i}"), P.buf()) for i in range(1)])
    lnst = P.sb([128, 16], F32, "lnst")
    b_lnst = P.buf()

    def load_ln(i):
        P.dma("sp", lambda e: e.dma_start(out=gam[:], in_=lng[i]), "gam", writes=[b_gam])
        P.dma("sp", lambda e: e.dma_start(out=bet[:], in_=lnb[i]), "bet", writes=[b_bet])

    def ln_epilogue(ti, src_tok, b_src, dst_i, final):
        r0, r1 = ti * TILE, (ti + 1) * TILE
        hres, b_hres = hres_r.next()
        hn, b_hn = hres, b_hres
        P.dma("sp", lambda e: e.dma_start(out=hres[:], in_=src_tok[r0:r1, :]), "hres0",
              reads=[b_src] if b_src is not None else [], writes=[b_hres])
        for hf, bk in ((0, wide0), (1, wide1)):
            P.op("dve", lambda e, hf=hf, bk=bk: e.scalar_tensor_tensor(
                out=hres[:, hf * 512:(hf + 1) * 512], in0=hres[:, hf * 512:(hf + 1) * 512], scalar=ALPHA,
                in1=bk[:, :], op0=ALU.mult, op1=ALU.add), reads=[b_hres, b_wide], writes=[b_hres])
        for hf in range(2):
            P.op("dve", lambda e, hf=hf: e.bn_stats(out=lnst[:, hf * 6:(hf + 1) * 6], in_=hres[:, hf * 512:(hf + 1) * 512]),
                 reads=[b_hres], pwrites=[b_lnst])
        P.op("dve", lambda e: e.bn_aggr(out=lnst[:, 12:14], in_=lnst[:, 0:12]), reads=[b_lnst], writes=[b_lnst])
        P.op("dve", lambda e: e.tensor_scalar(out=lnst[:, 14:15], in0=lnst[:, 13:14], scalar1=1e-5, scalar2=None, op0=ALU.add),
             reads=[b_lnst], writes=[b_lnst])
        P.op("act", lambda e: e.activation(out=lnst[:, 14:15], in_=lnst[:, 14:15], func=AF.Ln), reads=[b_lnst], writes=[b_lnst])
        P.op("act", lambda e: e.activation(out=lnst[:, 15:16], in_=lnst[:, 14:15], func=AF.Exp, scale=-0.5),
             reads=[b_lnst], writes=[b_lnst])
        P.op("dve", lambda e: e.tensor_scalar(out=hres[:], in0=hres[:], scalar1=lnst[:, 12:13], scalar2=lnst[:, 15:16],
                                              op0=ALU.subtract, op1=ALU.mult), reads=[b_hres, b_lnst], writes=[b_hres])
        P.op("pool", lambda e: e.tensor_tensor(out=hn[:], in0=hn[:], in1=gam[:], op=ALU.mult), reads=[b_hn, b_gam], writes=[b_hn])
        P.op("pool", lambda e: e.tensor_tensor(out=hn[:], in0=hn[:], in1=bet[:], op=ALU.add), reads=[b_hn, b_bet], writes=[b_hn])
        if ti == 0:
            P.op("dve", lambda e: e.tensor_scalar(out=hn[:], in0=hn[:], scalar1=maskt[:, 0:1], scalar2=None, op0=ALU.mult),
                 reads=[b_hn, b_mask], writes=[b_hn])
        if final:
            if ti > 0:
                o = P.dma("sp", lambda e: e.dma_start(out=out[r0 - TILE:r1 - TILE, :], in_=hn[:]), "hres0", reads=[b_hn])
                store_ops.append(o)
            return
        P.dma("sp", lambda e: e.dma_start(out=htok_s[dst_i][r0:r1, :], in_=hn[:]), "hres0",
              reads=[b_hn], pwrites=[b_hts[dst_i]])
        hTt, b_hTt = hTt_r.next()
        for half in range(2):
            bk, b_bk = big.next()
            for q in range(4):
                kc = half * 4 + q
                P.op("pe", lambda e, bk=bk, q=q, kc=kc: e.transpose(out=bk[:, q * 128:(q + 1) * 128], in_=hn[:, kc * 128:(kc + 1) * 128],
                                                                   identity=ident), reads=[b_hn, b_cst], pwrites=[b_bk])
            P.op("act", lambda e, bk=bk, half=half: e.activation(
                out=hTt[:, half * 4:(half + 1) * 4, :], in_=bk[:, :].rearrange("p (k t) -> p k t", k=4), func=AF.Copy),
                reads=[b_bk], pwrites=[b_hTt])
        v = hT_s[dst_i].rearrange("(kc p) t -> p kc t", p=128)
        P.dma("sp", lambda e: e.dma_start(out=v[:, :, HALO + r0:HALO + r1], in_=hTt[:]), "hTt0",
              reads=[b_hTt], pwrites=[b_hTs[dst_i]])

    def load_w_cast(dst, b_dst, src_view, ncols, key):
        nmid = src_view.shape[1]
        for m0 in range(0, nmid, 8):
            m1 = min(nmid, m0 + 8)
            c = 0
            while c < ncols:
                w = min(2048, ncols - c)
                P.dma("pool", lambda e, c=c, w=w, m0=m0, m1=m1: e.dma_start(out=dst[:, m0:m1, c:c + w], in_=src_view[:, m0:m1, c:c + w]),
                      key, pwrites=[b_dst])
                c += w

    def make_diag(dg, b_dg, convT_dram, nchunk, ntap, tmp_name):
        ct = P.sb([128, nchunk * ntap], F32, tmp_name)
        b_ct = P.buf()
        P.dma("sp", lambda e: e.dma_start(out=ct[:], in_=convT_dram), tmp_name, writes=[b_ct])
        for m in range(nchunk):
            for j in range(ntap):
                P.op("dve", lambda e, m=m, j=j: e.tensor_scalar(out=dg[:, j, m, :], in0=ident, scalar1=ct[:, m * ntap + j:m * ntap + j + 1],
                                                                scalar2=None, op0=ALU.mult), reads=[b_cst, b_ct], pwrites=[b_dg])

    def phase1():
        Wqkv = P.sb([128, KC, 3072], BF16, "Wqkv")
        Wz = P.sb([128, KC, 1024], BF16, "Wz")
        Wba = P.sb([128, KC, 16], BF16, "Wba")
        Wout = P.sb([128, KC, D], BF16, "Wout")
        dg = P.sb([128, 4, 24, 128], BF16, "dgA")
        b_Wqkv, b_Wz, b_Wba, b_Wout, b_dg = P.buf(), P.buf(), P.buf(), P.buf(), P.buf()
        win_v = a_w_in.rearrange("(kc p) n -> p kc n", p=128)
        load_w_cast(Wqkv, b_Wqkv, win_v[:, :, 0:3072], 3072, "Wqkv")
        load_w_cast(Wz, b_Wz, win_v[:, :, 3072:4096], 1024, "Wz")
        load_w_cast(Wba, b_Wba, win_v[:, :, 4096:4112], 16, "Wba")
        load_w_cast(Wout, b_Wout, a_w_out.rearrange("(kc p) n -> p kc n", p=128), D, "Wout")
        make_diag(dg, b_dg, a_convT[:, :], 24, 4, "ctA")
        load_ln(0)
        avec = P.sb([128, 16], F32, "avec")
        b_avec = P.buf()
        P.dma("sp", lambda e: e.dma_start(out=avec[:], in_=a_vec[:, :]), "avec", writes=[b_avec])
        negA = P.sb([128, 8], F32, "negA")
        b_negA = P.buf()
        P.op("act", lambda e: e.activation(out=negA[:], in_=avec[:, 0:8], func=AF.Exp), reads=[b_avec], writes=[b_negA])
        P.op("dve", lambda e: e.tensor_scalar(out=negA[:], in0=negA[:], scalar1=-1.0, scalar2=None, op0=ALU.mult),
             reads=[b_negA], writes=[b_negA])
        normT = P.sb([128, 1], F32, "normT")
        b_normT = P.buf()
        P.dma("sp", lambda e: e.dma_start(out=normT[:], in_=a_normT[:, :]), "normT", writes=[b_normT])

        S = P.sb([128, 8, 128], F32, "S")
        b_S = [P.buf(f"S{h}") for h in range(8)]
        P.op("pool", lambda e: e.memset(S[:], 0.0), writes=b_S)

        xb_r = Ring([(P.sb([128, KC, BLK + HALO], BF16, f"xb{i}"), P.buf()) for i in range(1)])
        pre_r = Ring([(P.sb([128, BLK + HALO], BF16, f"pre{i}"), P.buf()) for i in range(2)])
        qkvT = P.sb([128, 24, BLK], F32, "qkvT")
        b_qkv = [P.buf(f"qkv{m}") for m in range(24)]
        sq = P.sb([128, 16, BLK], BF16, "sq")
        b_sq = [P.buf(f"sq{m}") for m in range(16)]
        sz = P.sb([64, 2, D], BF16, "sz")
        b_sz = [P.buf(f"sz{c}") for c in range(2)]
        ogT = P.sb([128, 8, TILE], BF16, "ogT")
        b_ogT = P.buf()
        NS = 20
        tsc_r = Ring([(P.sb([128, NS, 8], F32, f"tsc{i}"), P.buf()) for i in range(2)])
        o_all_r = Ring([(P.sb([64, 8, 128], F32, f"oall{i}"), P.buf()) for i in range(1)])
        og_r = Ring([(P.sb([64, 8, 128], F32, f"og{i}"), P.buf()) for i in range(1)])

        def t64(n, name):
            return Ring([(P.sb([64, 64], F32, f"{name}{i}"), P.buf()) for i in range(n)])

        def t64x128(n, name):
            return Ring([(P.sb([64, 128], F32, f"{name}{i}"), P.buf()) for i in range(n)])

        gdU_r, E_r, dS_r, dI_r, QKT_r = t64(2, "gdU"), t64(2, "E"), t64(2, "dS"), t64(2, "dI"), t64(2, "QKT")
        Y_r, Z_r, Pm_r = t64(4, "Y"), t64(4, "Z"), t64(3, "Pm")
        kd_r, vp_r, D1_r, vn_r, o1_r = t64x128(2, "kd"), t64x128(2, "vp"), t64x128(2, "D1"), t64x128(2, "vn"), t64x128(2, "o1")

        def smslot():
            (bk, c0), b = sm64.next()
            return bk[0:64, c0:c0 + 64], b

        def mdslot():
            (bk, c0), b = md128.next()
            return bk[0:64, c0:c0 + 128], b

        for bi in range(NB):
            main = bi >= NPRE
            t0 = bi * BLK
            xb, b_xb = xb_r.next()
            xv = xT.rearrange("(kc p) t -> p kc t", p=128)
            P.dma("pool", lambda e, xb=xb, t0=t0: e.dma_start(out=xb[:], in_=xv[:, :, t0:t0 + BLK + HALO]), "xb0",
                  writes=[b_xb])
            chunks = list(range(24)) if main else list(range(8, 24))
            for m in chunks:
                bk, b_bk = big.next()
                for kc in range(KC):
                    P.op("pe", lambda e, bk=bk, m=m, kc=kc, xb=xb: e.matmul(
                        bk[:, 0:BLK + HALO], lhsT=Wqkv[:, kc, m * 128:(m + 1) * 128], rhs=xb[:, kc, :],
                        start=(kc == 0), stop=(kc == KC - 1)), reads=[b_Wqkv, b_xb], pwrites=[b_bk])
                pre, b_pre = pre_r.next()
                P.op("dve", lambda e, pre=pre, bk=bk: e.tensor_copy(out=pre[:], in_=bk[:, 0:BLK + HALO]), reads=[b_bk], writes=[b_pre])
                bk2, b_bk2 = big.next()
                for j in range(4):
                    P.op("pe", lambda e, bk2=bk2, m=m, j=j, pre=pre: e.matmul(
                        bk2[:, 0:BLK], lhsT=dg[:, j, m, :], rhs=pre[:, j:j + BLK], start=(j == 0), stop=(j == 3)),
                        reads=[b_dg, b_pre], pwrites=[b_bk2])
                P.op("act", lambda e, bk2=bk2, m=m: e.activation(out=qkvT[:, m, :], in_=bk2[:, 0:BLK], func=AF.Silu),
                     reads=[b_bk2], writes=[b_qkv[m]])
                if m < 16:
                    P.op("pool", lambda e, m=m: e.tensor_tensor(out=sq[:, m, :], in0=qkvT[:, m, :], in1=qkvT[:, m, :], op=ALU.mult),
                         reads=[b_qkv[m]], writes=[b_sq[m]])
            for c in range(6):
                cs = slice(c * CH, (c + 1) * CH)
                if main and c % 2 == 0:
                    for cc in (c, c + 1):
                        for kc in range(KC):
                            for hf, bkw in ((0, wide0), (1, wide1)):
                                P.op("pe", lambda e, cc=cc, kc=kc, hf=hf, bkw=bkw, xb=xb: e.matmul(
                                    bkw[0:64, :], lhsT=xb[:, kc, HALO + cc * CH:HALO + (cc + 1) * CH],
                                    rhs=Wz[:, kc, hf * 512:(hf + 1) * 512], start=(kc == 0), stop=(kc == KC - 1)),
                                    reads=[b_xb, b_Wz], pwrites=[b_wide])
                        for hf, bkw in ((0, wide0), (1, wide1)):
                            P.op("act", lambda e, cc=cc, hf=hf, bkw=bkw: e.activation(out=sz[:, cc % 2, hf * 512:(hf + 1) * 512], in_=bkw[0:64, :],
                                                                                   func=AF.Silu), reads=[b_wide], pwrites=[b_sz[cc % 2]])
                tsc, b_tsc = tsc_r.next()

                def T(i, rows=64):
                    return tsc[0:rows, i, :]
                (bk, c0), b_sc = sc_ring.next()
                ba_ps = bk[0:64, c0:c0 + 16]
                for kc in range(KC):
                    P.op("pe", lambda e, kc=kc, ba_ps=ba_ps, xb=xb, c=c: e.matmul(
                        ba_ps, lhsT=xb[:, kc, HALO + c * CH:HALO + (c + 1) * CH], rhs=Wba[:, kc, :],
                        start=(kc == 0), stop=(kc == KC - 1)), reads=[b_xb, b_Wba], pwrites=[b_sc])
                P.op("act", lambda e, ba_ps=ba_ps, T=T: e.activation(out=T(0), in_=ba_ps[:, 0:8], func=AF.Exp, scale=-1.0),
                     reads=[b_sc], writes=[b_tsc])
                P.op("dve", lambda e, T=T: e.tensor_scalar(out=T(0), in0=T(0), scalar1=1.0, scalar2=None, op0=ALU.add),
                     reads=[b_tsc], writes=[b_tsc])
                P.op("dve", lambda e, T=T: e.reciprocal(out=T(0), in_=T(0)), reads=[b_tsc], writes=[b_tsc])
                P.op("dve", lambda e, ba_ps=ba_ps, T=T: e.tensor_tensor(out=T(1), in0=ba_ps[:, 8:16], in1=avec[0:64, 8:16], op=ALU.add),
                     reads=[b_sc, b_avec, b_tsc], writes=[b_tsc])
                P.op("act", lambda e, T=T: e.activation(out=T(1), in_=T(1), func=AF.Exp), reads=[b_tsc], writes=[b_tsc])
                P.op("dve", lambda e, T=T: e.tensor_scalar(out=T(1), in0=T(1), scalar1=1.0, scalar2=None, op0=ALU.add),
                     reads=[b_tsc], writes=[b_tsc])
                P.op("act", lambda e, T=T: e.activation(out=T(1), in_=T(1), func=AF.Ln), reads=[b_tsc], writes=[b_tsc])
                P.op("dve", lambda e, T=T: e.tensor_tensor(out=T(1), in0=T(1), in1=negA[0:64, :], op=ALU.mult),
                     reads=[b_tsc, b_negA], writes=[b_tsc])
                (bk, c0), b_sc2 = sc_ring.next()
                ssq_ps = bk[0:64, c0:c0 + 16]
                for h in range(8):
                    P.op("pe", lambda e, h=h, ssq_ps=ssq_ps, cs=cs: e.matmul(ssq_ps[:, h:h + 1], lhsT=sq[:, 8 + h, cs], rhs=onesb[:, 0:1],
                                                                             start=True, stop=True),
                         reads=[b_sq[8 + h], b_onesb], pwrites=[b_sc2])
                    if main:
                        P.op("pe", lambda e, h=h, ssq_ps=ssq_ps, cs=cs: e.matmul(ssq_ps[:, 8 + h:9 + h], lhsT=sq[:, h, cs], rhs=onesb[:, 0:1],
                                                                                 start=True, stop=True),
                             reads=[b_sq[h], b_onesb], pwrites=[b_sc2])
                P.op("dve", lambda e, ssq_ps=ssq_ps, T=T: e.tensor_scalar(out=T(12), in0=ssq_ps[:, 0:8], scalar1=1e-6, scalar2=None, op0=ALU.add),
                     reads=[b_sc2, b_tsc], writes=[b_tsc])
                P.op("act", lambda e, T=T: e.activation(out=T(12), in_=T(12), func=AF.Ln), reads=[b_tsc], writes=[b_tsc])
                P.op("act", lambda e, T=T: e.activation(out=T(2), in_=T(12), func=AF.Exp, scale=-0.5), reads=[b_tsc], writes=[b_tsc])
                P.op("act", lambda e, T=T: e.activation(out=T(3), in_=T(12), func=AF.Exp, scale=0.5), reads=[b_tsc], writes=[b_tsc])
                P.op("dve", lambda e, T=T: e.tensor_tensor(out=T(9), in0=T(0), in1=T(2), op=ALU.mult), reads=[b_tsc], writes=[b_tsc])
                P.op("dve", lambda e, T=T: e.scalar_tensor_tensor(out=T(4), in0=T(9), scalar=-1.0, in1=T(2), op0=ALU.mult, op1=ALU.mult),
                     reads=[b_tsc], writes=[b_tsc])
                (bk, c0), b_sc3 = sc_ring.next()
                Gc_ps = bk[0:64, c0:c0 + 8]
                Gr_ps = bk[0:64, c0 + 8:c0 + 16]
                P.op("pe", lambda e, Gc_ps=Gc_ps, T=T: e.matmul(Gc_ps, lhsT=Uincl, rhs=T(1), start=True, stop=True),
                     reads=[b_cst, b_tsc], pwrites=[b_sc3])
                P.op("pe", lambda e, Gr_ps=Gr_ps, T=T: e.matmul(Gr_ps, lhsT=Lstrict, rhs=T(1), start=True, stop=True),
                     reads=[b_cst, b_tsc], pwrites=[b_sc3])
                (bk, c0), b_sc4 = sc_ring.next()
                Gl_ps = bk[:, c0:c0 + 8]
                P.op("pe", lambda e, Gl_ps=Gl_ps, T=T: e.matmul(Gl_ps, lhsT=ones64, rhs=T(1), start=True, stop=True),
                     reads=[b_cst, b_tsc], pwrites=[b_sc4])
                P.op("act", lambda e, Gc_ps=Gc_ps, T=T: e.activation(out=T(5), in_=Gc_ps, func=AF.Exp), reads=[b_sc3], writes=[b_tsc])
                P.op("act", lambda e, Gr_ps=Gr_ps, T=T: e.activation(out=T(6), in_=Gr_ps, func=AF.Exp), reads=[b_sc3], writes=[b_tsc])
                P.op("act", lambda e, Gl_ps=Gl_ps, tsc=tsc: e.activation(out=tsc[:, 7, :], in_=Gl_ps, func=AF.Exp), reads=[b_sc4], writes=[b_tsc])
                P.op("dve", lambda e, T=T: e.tensor_tensor(out=T(6), in0=T(6), in1=T(2), op=ALU.mult), reads=[b_tsc], writes=[b_tsc])
                P.op("dve", lambda e, T=T: e.tensor_scalar(out=T(8), in0=T(5), scalar1=-1.0, scalar2=None, op0=ALU.mult),
                     reads=[b_tsc], writes=[b_tsc])
                if main:
                    P.op("dve", lambda e, ssq_ps=ssq_ps, T=T: e.tensor_scalar(out=T(14), in0=ssq_ps[:, 8:16], scalar1=1e-6, scalar2=None, op0=ALU.add),
                         reads=[b_sc2, b_tsc], writes=[b_tsc])
                    P.op("act", lambda e, T=T: e.activation(out=T(14), in_=T(14), func=AF.Ln), reads=[b_tsc], writes=[b_tsc])
                    P.op("act", lambda e, T=T: e.activation(out=T(10), in_=T(14), func=AF.Exp, scale=-0.5), reads=[b_tsc], writes=[b_tsc])
                    P.op("dve", lambda e, T=T: e.tensor_scalar(out=T(10), in0=T(10), scalar1=float(128 ** -0.5), scalar2=None, op0=ALU.mult),
                         reads=[b_tsc], writes=[b_tsc])
                    P.op("dve", lambda e, T=T: e.tensor_tensor(out=T(11), in0=T(10), in1=T(5), op=ALU.mult), reads=[b_tsc], writes=[b_tsc])
                    o_all, b_oall = o_all_r.next()

                for h in range(8):
                    kT = qkvT[:, 8 + h, cs]
                    vT = qkvT[:, 16 + h, cs]
                    qT = qkvT[:, h, cs]
                    b_k, b_v, b_q = b_qkv[8 + h], b_qkv[16 + h], b_qkv[h]

                    def sc(i, h=h, T=T):
                        return T(i)[:, h:h + 1]
                    tk_ps, b_tk = mdslot()
                    P.op("pe", lambda e, tk_ps=tk_ps, kT=kT: e.transpose(out=tk_ps, in_=kT, identity=ident), reads=[b_k, b_cst], writes=[b_tk])
                    kd, b_kd = kd_r.next()
                    P.op("act", lambda e, kd=kd, tk_ps=tk_ps, sc=sc: e.activation(out=kd[:], in_=tk_ps, func=AF.Identity, scale=sc(6)),
                         reads=[b_tk, b_tsc], writes=[b_kd])
                    tv_ps, b_tv = mdslot()
                    P.op("pe", lambda e, tv_ps=tv_ps, vT=vT: e.transpose(out=tv_ps, in_=vT, identity=ident), reads=[b_v, b_cst], writes=[b_tv])
                    vp, b_vp = vp_r.next()
                    P.op("dve", lambda e, vp=vp, tv_ps=tv_ps, sc=sc: e.tensor_scalar(out=vp[:], in0=tv_ps, scalar1=sc(3), scalar2=None, op0=ALU.mult),
                         reads=[b_tv, b_tsc], writes=[b_vp])
                    kk_ps, b_kk = smslot()
                    P.op("pe", lambda e, kk_ps=kk_ps, kT=kT: e.matmul(kk_ps, lhsT=kT, rhs=kT, start=True, stop=True), reads=[b_k], writes=[b_kk])
                    gdU, b_gdU = gdU_r.next()
                    P.op("dve", lambda e, gdU=gdU, sc=sc: e.tensor_scalar(out=gdU[:], in0=Uincl, scalar1=sc(1), scalar2=None, op0=ALU.mult),
                         reads=[b_cst, b_tsc], writes=[b_gdU])
                    gd_ps, b_gd = smslot()
                    P.op("pe", lambda e, gd_ps=gd_ps, gdU=gdU: e.matmul(gd_ps, lhsT=Lstrict, rhs=gdU[:], start=True, stop=True),
                         reads=[b_cst, b_gdU], writes=[b_gd])
                    E, b_E = E_r.next()
                    P.op("act", lambda e, E=E, gd_ps=gd_ps: e.activation(out=E[:], in_=gd_ps, func=AF.Exp), reads=[b_gd], writes=[b_E])
                    dS, b_dS = dS_r.next()
                    P.op("pool", lambda e, dS=dS, E=E: e.tensor_tensor(out=dS[:], in0=E[:], in1=Ustrict, op=ALU.mult), reads=[b_E, b_cst], writes=[b_dS])
                    Y, b_Y = Y_r.next()
                    P.op("dve", lambda e, Y=Y, kk_ps=kk_ps, dS=dS, sc=sc: e.scalar_tensor_tensor(
                        out=Y[:], in0=kk_ps, scalar=sc(4), in1=dS[:], op0=ALU.mult, op1=ALU.mult), reads=[b_kk, b_dS, b_tsc], writes=[b_Y])
                    if main:
                        dI, b_dI = dI_r.next()
                        P.op("pool", lambda e, dI=dI, E=E: e.tensor_tensor(out=dI[:], in0=E[:], in1=Uincl, op=ALU.mult), reads=[b_E, b_cst], writes=[b_dI])
                        kq_ps, b_kq = smslot()
                        P.op("pe", lambda e, kq_ps=kq_ps, kT=kT, qT=qT: e.matmul(kq_ps, lhsT=kT, rhs=qT, start=True, stop=True),
                             reads=[b_k, b_q], writes=[b_kq])
                        QKT, b_QKT = QKT_r.next()
                        P.op("dve", lambda e, QKT=QKT, kq_ps=kq_ps, dI=dI, sc=sc: e.scalar_tensor_tensor(
                            out=QKT[:], in0=kq_ps, scalar=sc(2), in1=dI[:], op0=ALU.mult, op1=ALU.mult), reads=[b_kq, b_dI, b_tsc], writes=[b_QKT])
                    z_ps, b_zps = smslot()
                    P.op("pe", lambda e, z_ps=z_ps, Y=Y: e.transpose(out=z_ps, in_=Y[:], identity=cst[0:64, 0:64]), reads=[b_Y, b_cst], writes=[b_zps])
                    Z, b_Z = Z_r.next()
                    P.op("act", lambda e, Z=Z, z_ps=z_ps: e.activation(out=Z[:], in_=z_ps, func=AF.Copy), reads=[b_zps], writes=[b_Z])
                    Pm, b_Pm = Pm_r.next()
                    P.op("pool", lambda e, Pm=Pm, Y=Y: e.tensor_tensor(out=Pm[:], in0=Y[:], in1=cst[0:64, 0:64], op=ALU.add), reads=[b_Y, b_cst], writes=[b_Pm])
                    for lvl in range(1, 6):
                        zn_ps, b_znps = smslot()
                        P.op("pe", lambda e, zn_ps=zn_ps, Y=Y, Z=Z: e.matmul(zn_ps, lhsT=Y[:], rhs=Z[:], start=True, stop=True),
                             reads=[b_Y, b_Z], writes=[b_znps])
                        if lvl < 5:
                            yn_ps, b_ynps = smslot()
                            P.op("pe", lambda e, yn_ps=yn_ps, Y=Y, Z=Z: e.matmul(yn_ps, lhsT=Z[:], rhs=Y[:], start=True, stop=True),
                                 reads=[b_Y, b_Z], writes=[b_ynps])
                        Zn, b_Zn = Z_r.next()
                        P.op("act", lambda e, Zn=Zn, zn_ps=zn_ps: e.activation(out=Zn[:], in_=zn_ps, func=AF.Copy), reads=[b_znps], writes=[b_Zn])
                        if lvl < 5:
                            Yn, b_Yn = Y_r.next()
                            P.op("dve", lambda e, Yn=Yn, yn_ps=yn_ps: e.tensor_copy(out=Yn[:], in_=yn_ps), reads=[b_ynps], writes=[b_Yn])
                            Y, b_Y = Yn, b_Yn
                        Z, b_Z = Zn, b_Zn
                        pu_ps, b_pups = smslot()
                        P.op("pe", lambda e, pu_ps=pu_ps, Z=Z, Pm=Pm: e.matmul(pu_ps, lhsT=Z[:], rhs=Pm[:], start=True, stop=True),
                             reads=[b_Z, b_Pm], writes=[b_pups])
                        Pn, b_Pn = Pm_r.next()
                        P.op("dve", lambda e, Pn=Pn, pu_ps=pu_ps, Pm=Pm: e.tensor_tensor(out=Pn[:], in0=pu_ps, in1=Pm[:], op=ALU.add),
                             reads=[b_pups, b_Pm], writes=[b_Pn])
                        Pm, b_Pm = Pn, b_Pn
                    Sh = S[:, h, :]
                    r0_ps, b_r0 = mdslot()
                    P.op("pe", lambda e, r0_ps=r0_ps, kT=kT, Sh=Sh: e.matmul(r0_ps, lhsT=kT, rhs=Sh, start=True, stop=True),
                         reads=[b_k, b_S[h]], writes=[b_r0])
                    D1, b_D1 = D1_r.next()
                    P.op("dve", lambda e, D1=D1, r0_ps=r0_ps, vp=vp, sc=sc: e.scalar_tensor_tensor(
                        out=D1[:], in0=r0_ps, scalar=sc(8), in1=vp[:], op0=ALU.mult, op1=ALU.add), reads=[b_r0, b_vp, b_tsc], writes=[b_D1])
                    vn_ps, b_vnps = mdslot()
                    P.op("pe", lambda e, vn_ps=vn_ps, Pm=Pm, D1=D1: e.matmul(vn_ps, lhsT=Pm[:], rhs=D1[:], start=True, stop=True),
                         reads=[b_Pm, b_D1], writes=[b_vnps])
                    vn, b_vn = vn_r.next()
                    P.op("act", lambda e, vn=vn, vn_ps=vn_ps, sc=sc: e.activation(out=vn[:], in_=vn_ps, func=AF.Identity, scale=sc(9)),
                         reads=[b_vnps, b_tsc], writes=[b_vn])
                    if main:
                        p1_ps, b_p1 = mdslot()
                        P.op("pe", lambda e, p1_ps=p1_ps, qT=qT, Sh=Sh: e.matmul(p1_ps, lhsT=qT, rhs=Sh, start=True, stop=True),
                             reads=[b_q, b_S[h]], writes=[b_p1])
                        p2_ps, b_p2 = mdslot()
                        P.op("pe", lambda e, p2_ps=p2_ps, QKT=QKT, vn=vn: e.matmul(p2_ps, lhsT=QKT[:], rhs=vn[:], start=True, stop=True),
                             reads=[b_QKT, b_vn], writes=[b_p2])
                        o1, b_o1 = o1_r.next()
                        P.op("act", lambda e, o1=o1, p1_ps=p1_ps, sc=sc: e.activation(out=o1[:], in_=p1_ps, func=AF.Identity, scale=sc(11)),
                             reads=[b_p1, b_tsc], writes=[b_o1])
                        P.op("dve", lambda e, o_all=o_all, h=h, p2_ps=p2_ps, o1=o1, sc=sc: e.scalar_tensor_tensor(
                            out=o_all[:, h, :], in0=p2_ps, scalar=sc(10), in1=o1[:], op0=ALU.mult, op1=ALU.add),
                            reads=[b_p2, b_o1, b_tsc], pwrites=[b_oall])
                    (bk, c0), b_su = su_ring.next()
                    su_ps = bk[:, c0:c0 + 128]
                    P.op("pe", lambda e, su_ps=su_ps, kd=kd, vn=vn: e.matmul(su_ps, lhsT=kd[:], rhs=vn[:], start=True, stop=True),
                         reads=[b_kd, b_vn], writes=[b_su])
                    P.op("dve", lambda e, Sh=Sh, su_ps=su_ps, tsc=tsc, h=h: e.scalar_tensor_tensor(
                        out=Sh, in0=Sh, scalar=tsc[:, 7, h:h + 1], in1=su_ps, op0=ALU.mult, op1=ALU.add),
                        reads=[b_S[h], b_su, b_tsc], writes=[b_S[h]])
                if not main:
                    continue
                og, b_og = og_r.next()
                P.op("pool", lambda e, o_all=o_all, og=og: e.tensor_tensor(out=og[:], in0=o_all[:], in1=o_all[:], op=ALU.mult),
                     reads=[b_oall], writes=[b_og])
                P.op("dve", lambda e, T=T, og=og: e.tensor_reduce(out=T(13), in_=og[:], axis=AX.X, op=ALU.add), reads=[b_og, b_tsc], writes=[b_tsc])
                P.op("dve", lambda e, T=T: e.tensor_scalar(out=T(13), in0=T(13), scalar1=float(1.0 / 128), scalar2=1e-6, op0=ALU.mult, op1=ALU.add),
                     reads=[b_tsc], writes=[b_tsc])
                P.op("act", lambda e, T=T: e.activation(out=T(13), in_=T(13), func=AF.Ln), reads=[b_tsc], writes=[b_tsc])
                P.op("act", lambda e, T=T: e.activation(out=T(13), in_=T(13), func=AF.Exp, scale=-0.5), reads=[b_tsc], writes=[b_tsc])
                for h in range(8):
                    P.op("dve", lambda e, og=og, o_all=o_all, h=h, T=T, c=c: e.scalar_tensor_tensor(
                        out=og[:, h, :], in0=o_all[:, h, :], scalar=T(13)[:, h:h + 1], in1=sz[:, c % 2, h * 128:(h + 1) * 128],
                        op0=ALU.mult, op1=ALU.mult), reads=[b_oall, b_tsc, b_sz[c % 2]], pwrites=[b_og])
                bk, b_bk = big.next()
                for h in range(8):
                    P.op("pe", lambda e, bk=bk, h=h, og=og: e.transpose(out=bk[:, h * 64:(h + 1) * 64], in_=og[:, h, :], identity=cst[0:64, 0:64]),
                         reads=[b_og, b_cst], pwrites=[b_bk])
                half = c % 2
                P.op("act", lambda e, bk=bk, half=half: e.activation(
                    out=ogT[:, :, half * 64:(half + 1) * 64], in_=bk[:, :].rearrange("p (h t) -> p h t", h=8), func=AF.Identity, scale=normT[:, 0:1]),
                    reads=[b_bk, b_normT], pwrites=[b_ogT])
                if half == 1:
                    ti = (bi - NPRE) * 3 + c // 2
                    for h in range(8):
                        for hf, bkw in ((0, wide0), (1, wide1)):
                            P.op("pe", lambda e, h=h, hf=hf, bkw=bkw: e.matmul(
                                bkw[:, :], lhsT=ogT[:, h, :], rhs=Wout[:, h, hf * 512:(hf + 1) * 512], start=(h == 0), stop=(h == 7)),
                                reads=[b_ogT, b_Wout], pwrites=[b_wide])
                    ln_epilogue(ti, xtok, None, 0, final=(upto == 1))

    def load_xb(xb, b_xb, src_i, t0, key):
        v = hT_s[src_i].rearrange("(kc p) t -> p kc t", p=128)
        P.dma("sp", lambda e: e.dma_start(out=xb[:], in_=v[:, :, HALO + t0 - 2:HALO + t0 + BLK]), key,
              reads=[b_hTs[src_i]], writes=[b_xb])

    def phase_ffn(li, src_i, dst_i, final):
        Wup = P.sb([128, KC, 2 * DFF], BF16, "Wup")
        Wd = P.sb([128, FC, D], BF16, "Wd")
        dgF = P.sb([128, 3, FC, 128], BF16, "dgF")
        b_Wup, b_Wd, b_dgF = P.buf(), P.buf(), P.buf()
        load_w_cast(Wup, b_Wup, f_w_up[li].rearrange("(kc p) n -> p kc n", p=128), 2 * DFF, "Wup")
        load_w_cast(Wd, b_Wd, f_w_down[li].rearrange("(m p) n -> p m n", p=128), D, "Wd")
        make_diag(dgF, b_dgF, f_convT[li], FC, 3, "ctF")
        load_ln(1 + 2 * li)
        xb_r = Ring([(P.sb([128, KC, BLK + 2], BF16, f"fxb{i}"), P.buf()) for i in range(1)])
        pre_r = Ring([(P.sb([128, BLK + 2], BF16, f"fpre{i}"), P.buf()) for i in range(2)])
        su_r = Ring([(P.sb([128, BLK], BF16, f"fsu{i}"), P.buf()) for i in range(2)])
        aT = P.sb([128, FC, BLK], BF16, "aT")
        b_aT = P.buf()
        for bi in range(NMAIN):
            t0 = bi * BLK
            xb, b_xb = xb_r.next()
            load_xb(xb, b_xb, src_i, t0, "fxb0")
            for m in range(FC):
                bk, b_bk = big6.next()
                for kc in range(KC):
                    P.op("pe", lambda e, bk=bk, m=m, kc=kc, xb=xb: e.matmul(
                        bk[:, 0:BLK + 2], lhsT=Wup[:, kc, m * 128:(m + 1) * 128], rhs=xb[:, kc, :],
                        start=(kc == 0), stop=(kc == KC - 1)), reads=[b_Wup, b_xb], pwrites=[b_bk])
                pre, b_pre = pre_r.next()
                P.op("dve", lambda e, pre=pre, bk=bk: e.tensor_copy(out=pre[:], in_=bk[:, 0:BLK + 2]), reads=[b_bk], writes=[b_pre])
                bk2, b_bk2 = big6.next()
                for j in range(3):
                    P.op("pe", lambda e, bk2=bk2, m=m, j=j, pre=pre: e.matmul(
                        bk2[:, 0:BLK], lhsT=dgF[:, j, m, :], rhs=pre[:, j:j + BLK], start=(j == 0), stop=(j == 2)),
                        reads=[b_dgF, b_pre], pwrites=[b_bk2])
                su, b_su = su_r.next()
                P.op("act", lambda e, bk2=bk2, su=su: e.activation(out=su[:], in_=bk2[:, 0:BLK], func=AF.Silu), reads=[b_bk2], writes=[b_su])
                bk3, b_bk3 = big6.next()
                for kc in range(KC):
                    P.op("pe", lambda e, bk3=bk3, m=m, kc=kc, xb=xb: e.matmul(
                        bk3[:, 0:BLK], lhsT=Wup[:, kc, DFF + m * 128:DFF + (m + 1) * 128], rhs=xb[:, kc, 2:BLK + 2],
                        start=(kc == 0), stop=(kc == KC - 1)), reads=[b_Wup, b_xb], pwrites=[b_bk3])
                P.op("dve", lambda e, bk3=bk3, su=su, m=m: e.tensor_tensor(out=aT[:, m, :], in0=bk3[:, 0:BLK], in1=su[:], op=ALU.mult),
                     reads=[b_bk3, b_su], pwrites=[b_aT])
            for tl in range(3):
                for m in range(FC):
                    for hf, bkw in ((0, wide0), (1, wide1)):
                        P.op("pe", lambda e, m=m, hf=hf, bkw=bkw, tl=tl: e.matmul(
                            bkw[:, :], lhsT=aT[:, m, tl * TILE:(tl + 1) * TILE], rhs=Wd[:, m, hf * 512:(hf + 1) * 512],
                            start=(m == 0), stop=(m == FC - 1)), reads=[b_aT, b_Wd], pwrites=[b_wide])
                ln_epilogue(bi * 3 + tl, htok_s[src_i], b_hts[src_i], dst_i, final)

    def phase_sconv(src_i, dst_i, final):
        Win = P.sb([128, KC, 3 * D], BF16, "Win")
        Wo = P.sb([128, KC, D], BF16, "Wo")
        dgB = P.sb([128, 3, 8, 128], BF16, "dgB")
        b_Win, b_Wo, b_dgB = P.buf(), P.buf(), P.buf()
        load_w_cast(Win, b_Win, b_w_in.rearrange("(kc p) n -> p kc n", p=128), 3 * D, "Win")
        load_w_cast(Wo, b_Wo, b_w_out.rearrange("(kc p) n -> p kc n", p=128), D, "Wo")
        make_diag(dgB, b_dgB, b_convT[:, :], 8, 3, "ctB")
        load_ln(2)
        xb_r = Ring([(P.sb([128, KC, BLK + 2], BF16, f"sxb{i}"), P.buf()) for i in range(2)])
        c_r = Ring([(P.sb([128, BLK + 2], BF16, f"scs{i}"), P.buf()) for i in range(2)])
        cx_r = Ring([(P.sb([128, BLK + 2], BF16, f"scx{i}"), P.buf()) for i in range(2)])
        bs_r = Ring([(P.sb([128, BLK], BF16, f"sbs{i}"), P.buf()) for i in range(2)])
        vT = P.sb([128, 8, BLK], BF16, "vTs")
        b_vT = P.buf()
        for bi in range(NMAIN):
            t0 = bi * BLK
            xb, b_xb = xb_r.next()
            load_xb(xb, b_xb, src_i, t0, f"sxb{xb_r.i % 2}")
            for m in range(8):
                bk, b_bk = big6.next()
                for kc in range(KC):
                    P.op("pe", lambda e, bk=bk, m=m, kc=kc, xb=xb: e.matmul(
                        bk[:, 0:BLK + 2], lhsT=Win[:, kc, D + m * 128:D + (m + 1) * 128], rhs=xb[:, kc, :],
                        start=(kc == 0), stop=(kc == KC - 1)), reads=[b_Win, b_xb], pwrites=[b_bk])
                cs_, b_cs = c_r.next()
                P.op("act", lambda e, cs_=cs_, bk=bk: e.activation(out=cs_[:], in_=bk[:, 0:BLK + 2], func=AF.Copy), reads=[b_bk], writes=[b_cs])
                bk2, b_bk2 = big6.next()
                for kc in range(KC):
                    P.op("pe", lambda e, bk2=bk2, m=m, kc=kc, xb=xb: e.matmul(
                        bk2[:, 0:BLK + 2], lhsT=Win[:, kc, 2 * D + m * 128:2 * D + (m + 1) * 128], rhs=xb[:, kc, :],
                        start=(kc == 0), stop=(kc == KC - 1)), reads=[b_Win, b_xb], pwrites=[b_bk2])
                cx, b_cx = cx_r.next()
                P.op("dve", lambda e, cx=cx, bk2=bk2, cs_=cs_: e.tensor_tensor(out=cx[:], in0=bk2[:, 0:BLK + 2], in1=cs_[:], op=ALU.mult),
                     reads=[b_bk2, b_cs], writes=[b_cx])
                bk3, b_bk3 = big6.next()
                for j in range(3):
                    P.op("pe", lambda e, bk3=bk3, m=m, j=j, cx=cx: e.matmul(
                        bk3[:, 0:BLK], lhsT=dgB[:, j, m, :], rhs=cx[:, j:j + BLK], start=(j == 0), stop=(j == 2)),
                        reads=[b_dgB, b_cx], pwrites=[b_bk3])
                bk4, b_bk4 = big6.next()
                for kc in range(KC):
                    P.op("pe", lambda e, bk4=bk4, m=m, kc=kc, xb=xb: e.matmul(
                        bk4[:, 0:BLK], lhsT=Win[:, kc, m * 128:(m + 1) * 128], rhs=xb[:, kc, 2:BLK + 2],
                        start=(kc == 0), stop=(kc == KC - 1)), reads=[b_Win, b_xb], pwrites=[b_bk4])
                bs, b_bs = bs_r.next()
                P.op("act", lambda e, bs=bs, bk4=bk4: e.activation(out=bs[:], in_=bk4[:, 0:BLK], func=AF.Copy), reads=[b_bk4], writes=[b_bs])
                P.op("dve", lambda e, bk3=bk3, bs=bs, m=m: e.tensor_tensor(out=vT[:, m, :], in0=bk3[:, 0:BLK], in1=bs[:], op=ALU.mult),
                     reads=[b_bk3, b_bs], pwrites=[b_vT])
            for tl in range(3):
                for m in range(8):
                    for hf, bkw in ((0, wide0), (1, wide1)):
                        P.op("pe", lambda e, m=m, hf=hf, bkw=bkw, tl=tl: e.matmul(
                            bkw[:, :], lhsT=vT[:, m, tl * TILE:(tl + 1) * TILE], rhs=Wo[:, m, hf * 512:(hf + 1) * 512],
                            start=(m == 0), stop=(m == 7)), reads=[b_vT, b_Wo], pwrites=[b_wide])
                ln_epilogue(bi * 3 + tl, htok_s[src_i], b_hts[src_i], dst_i, final)

    with P.phase():
        phase1()
    if upto >= 2:
        with P.phase():
            phase_ffn(0, 0, 1, final=(upto == 2))
    if upto >= 3:
        with P.phase():
            phase_sconv(1, 2, final=(upto == 3))
    if upto >= 4:
        with P.phase():
            phase_ffn(1, 2, 0, final=True)
    P.stack.close()
    return nc


def make_consts():
    c = np.zeros((128, 448), np.float32)
    c[:, 0:128] = np.eye(128, dtype=np.float32)
    r = np.arange(64)[:, None]
    q = np.arange(64)[None, :]
    c[0:64, 128:192] = (r <= q)
    c[0:64, 192:256] = (r > q)
    c[0:64, 256:320] = (r < q)
    c[:, 320:448] = 1.0
    return c


def rep128(v):
    v = np.asarray(v, np.float32).reshape(1, -1)
    return np.ascontiguousarray(np.repeat(v, 128, axis=0))


def weight_inputs(a_w_in, a_conv, a_log, a_dt_bias, a_norm, a_w_out, b_w_in, b_conv, b_w_out,
                  ln_mix_g, ln_mix_b, ffn_w_up, ffn_conv, ffn_w_down, ln_ffn_g, ln_ffn_b):
    f = np.float32
    d = {}
    d["consts"] = make_consts()
    d["a_w_in"] = np.ascontiguousarray(a_w_in[0], f)
    d["a_convT"] = np.ascontiguousarray(a_conv[0].T.reshape(24, 128, 4).transpose(1, 0, 2).reshape(128, 96), f)
    d["a_vec"] = np.concatenate([rep128(a_log[0]), rep128(a_dt_bias[0])], axis=1)
    d["a_normT"] = np.ascontiguousarray(a_norm[0].reshape(128, 1), f)
    d["a_w_out"] = np.ascontiguousarray(a_w_out[0], f)
    d["b_w_in"] = np.ascontiguousarray(b_w_in[0], f)
    d["b_convT"] = np.ascontiguousarray(b_conv[0].T.reshape(8, 128, 3).transpose(1, 0, 2).reshape(128, 24), f)
    d["b_w_out"] = np.ascontiguousarray(b_w_out[0], f)
    d["lng"] = np.stack([rep128(ln_mix_g[0]), rep128(ln_ffn_g[0]), rep128(ln_mix_g[1]), rep128(ln_ffn_g[1])])
    d["lnb"] = np.stack([rep128(ln_mix_b[0]), rep128(ln_ffn_b[0]), rep128(ln_mix_b[1]), rep128(ln_ffn_b[1])])
    d["f_w_up"] = np.ascontiguousarray(ffn_w_up, f)
    d["f_convT"] = np.stack([np.ascontiguousarray(ffn_conv[i].T.reshape(FC, 128, 3).transpose(1, 0, 2).reshape(128, FC * 3)) for i in range(2)]).astype(f)
    d["f_w_down"] = np.ascontiguousarray(ffn_w_down, f)
    return d


def core_stream_inputs(stream, valid, NPRE, NMAIN):
    ntok = (NPRE + NMAIN) * BLK
    assert stream.shape[0] == ntok
    xT = np.zeros((D, HALO + ntok), np.float32)
    xT[:, HALO:] = stream.T
    m0 = NPRE * BLK
    return {"xT": xT, "xtok": np.ascontiguousarray(stream[m0:]), "mask": valid[m0:m0 + 128].astype(np.float32).reshape(128, 1)}


_CACHE = {}


def kernel(x, meta, a_w_in, a_conv, a_log, a_dt_bias, a_norm, a_w_out, b_w_in, b_conv, b_w_out,
           ln_mix_g, ln_mix_b, ffn_w_up, ffn_conv, ffn_w_down, ln_ffn_g, ln_ffn_b):
    x = np.asarray(x, np.float32)
    meta = np.asarray(meta, np.float32)
    B, SEQ, _ = x.shape
    NPRE, NMAIN = NPRE_FULL, NMAIN_FULL
    ntok = (NPRE + NMAIN) * BLK
    half_tok = NMAIN * BLK - TILE
    w = weight_inputs(*[np.asarray(a, np.float32) for a in (a_w_in, a_conv, a_log, a_dt_bias, a_norm, a_w_out, b_w_in, b_conv, b_w_out,
                                                            ln_mix_g, ln_mix_b, ffn_w_up, ffn_conv, ffn_w_down, ln_ffn_g, ln_ffn_b)])
    in_maps = []
    for core in range(8):
        b, half = core // 2, core % 2
        stream = np.zeros((ntok, D), np.float32)
        valid = np.zeros((ntok,), bool)
        n_x = half_tok * (half + 1)
        seq = np.concatenate([meta, x[b, :n_x]], axis=0)
        stream[ntok - seq.shape[0]:] = seq
        valid[ntok - seq.shape[0]:] = True
        m = dict(w)
        m.update(core_stream_inputs(stream, valid, NPRE, NMAIN))
        in_maps.append(m)
    if "nc" not in _CACHE:
        _CACHE["nc"] = build_program(NPRE, NMAIN)
    res = run_bass_kernel_spmd(_CACHE["nc"], in_maps, core_ids=list(range(8)))
    outp = np.zeros((B, SEQ, D), np.float32)
    for core in range(8):
        b, half = core // 2, core % 2
        outp[b, half * half_tok:(half + 1) * half_tok] = res.results[core]["out"]
    return outp
```

```python
import contextlib
import numpy as np
import concourse.bass as bass
import concourse.mybir as mybir
from concourse.bass_utils import run_bass_kernel_spmd

F32 = mybir.dt.float32
BF16 = mybir.dt.bfloat16
AF = mybir.ActivationFunctionType
ALU = mybir.AluOpType
AX = mybir.AxisListType

NSEM_PER_ENG = 6
SAME_ENG_SYNC = True
NEU_DT = F32

D = 1024
KC = 8
TILE = 128
BLK = 384
CH = 64
HALO = 3
DFF = 2816
FC = DFF // 128
ALPHA = float((2.0 * 2) ** 0.25)
NPRE_FULL = 11
NMAIN_FULL = 11


class Buf:
    __slots__ = ("name", "writers", "readers", "open", "pre")

    def __init__(self, name):
        self.name = name
        self.writers = []
        self.readers = []
        self.open = False
        self.pre = []


class Op:
    __slots__ = ("eng", "emit", "deps", "idx", "dma_key", "dma_cnt", "waits")

    def __init__(self, eng, emit):
        self.eng = eng
        self.emit = emit
        self.deps = []
        self.idx = -1
        self.dma_key = None
        self.dma_cnt = 0
        self.waits = []


class Prog:
    ENGS = ("pe", "act", "dve", "pool", "sp")

    def __init__(self, nc):
        self.nc = nc
        self.ops = {e: [] for e in self.ENGS}
        self.ncomp = {e: 0 for e in self.ENGS}
        self.dma_counts = {}
        self.stack = contextlib.ExitStack()
        self.nbuf = 0
        self.nname = 0

    def sb(self, shape, dt, name=None):
        self.nname += 1
        return self.stack.enter_context(self.nc.sbuf_tensor(f"{name or 'sb'}_{self.nname}", list(shape), dt))

    def ps(self, shape, dt=F32, name=None):
        self.nname += 1
        return self.stack.enter_context(self.nc.psum_tensor(name or f"ps{self.nname}", list(shape), dt))

    def buf(self, name=None):
        self.nbuf += 1
        return Buf(name or f"b{self.nbuf}")

    def _track(self, op, reads, writes, pwrites):
        for b in reads:
            op.deps.extend(b.writers)
            b.readers.append(op)
            b.open = False
        for b in writes:
            op.deps.extend(b.readers)
            op.deps.extend(b.writers)
            b.pre = list(b.readers) + list(b.writers)
            b.writers = [op]
            b.readers = []
            b.open = True
        for b in pwrites:
            if b.open:
                op.deps.extend(b.pre)
                b.writers.append(op)
            else:
                op.deps.extend(b.readers)
                op.deps.extend(b.writers)
                b.pre = list(b.readers) + list(b.writers)
                b.writers = [op]
                b.readers = []
                b.open = True

    def op(self, eng, emit, reads=(), writes=(), pwrites=()):
        o = Op(eng, emit)
        self._track(o, reads, writes, pwrites)
        o.idx = self.ncomp[eng]
        self.ncomp[eng] += 1
        self.ops[eng].append(o)
        return o

    def dma(self, eng, emit, key, reads=(), writes=(), pwrites=()):
        o = Op(eng, emit)
        self._track(o, reads, writes, pwrites)
        o.idx = -1
        o.dma_key = key
        self.dma_counts[key] = self.dma_counts.get(key, 0) + 1
        o.dma_cnt = self.dma_counts[key]
        self.ops[eng].append(o)
        return o

    def setup_sems(self):
        nc = self.nc
        st = self.stack
        self.comp_sems = {}
        for e in ("pe", "act", "dve", "pool"):
            self.comp_sems[e] = [st.enter_context(nc.semaphore(f"s_{e}_{i}")) for i in range(NSEM_PER_ENG)]
        self.dma_sems = {}
        self.emitted = {e: 0 for e in self.ENGS}
        self.waited_idx = {e: {x: -1 for x in self.ENGS} for e in self.ENGS}
        self.waited_dma = {e: {} for e in self.ENGS}

    def emit(self):
        nc = self.nc
        comp_sems, dma_sems = self.comp_sems, self.dma_sems
        for k in self.dma_counts:
            if k not in dma_sems:
                dma_sems[k] = self.stack.enter_context(nc.semaphore(f"d_{len(dma_sems)}"))
        new_ops = {e: self.ops[e][self.emitted[e]:] for e in self.ENGS}
        for e in self.ENGS:
            waited_idx = self.waited_idx[e]
            waited_dma = self.waited_dma[e]
            for o in new_ops[e]:
                need_idx = {}
                need_dma = {}
                for d in o.deps:
                    if d is o:
                        continue
                    if d.dma_key is not None:
                        if d.dma_cnt > need_dma.get(d.dma_key, 0):
                            need_dma[d.dma_key] = d.dma_cnt
                    else:
                        if d.eng == e and (e == "pe" or not SAME_ENG_SYNC):
                            continue
                        if d.idx > need_idx.get(d.eng, -1):
                            need_idx[d.eng] = d.idx
                for src, k in need_idx.items():
                    if k > waited_idx[src]:
                        waited_idx[src] = k
                        o.waits.append((comp_sems[src][k % NSEM_PER_ENG], k // NSEM_PER_ENG + 1))
                for key, c in need_dma.items():
                    if c > waited_dma.get(key, 0):
                        waited_dma[key] = c
                        o.waits.append((dma_sems[key], 16 * c))
            self.emitted[e] = len(self.ops[e])
        final = [(dma_sems[k], 16 * c) for k, c in self.dma_counts.items()]

        def replay(ename, eng):
            for o in new_ops[ename]:
                for s, v in o.waits:
                    eng.wait_ge(s, v)
                ins = o.emit(eng)
                if o.dma_key is not None:
                    ins.then_inc(dma_sems[o.dma_key], 16)
                else:
                    ins.then_inc(comp_sems[ename][o.idx % NSEM_PER_ENG], 1)
            if ename == "sp":
                for s, v in final:
                    eng.wait_ge(s, v)

        with nc.Block() as block:
            @block.tensor
            def _(eng):
                replay("pe", eng)

            @block.scalar
            def _(eng):
                replay("act", eng)

            @block.vector
            def _(eng):
                replay("dve", eng)

            @block.gpsimd
            def _(eng):
                replay("pool", eng)

            @block.sync
            def _(eng):
                replay("sp", eng)

    @contextlib.contextmanager
    def phase(self):
        outer = self.stack
        self.stack = contextlib.ExitStack()
        ph = self.stack
        try:
            yield
            self.stack = outer
            self.emit()
        finally:
            self.stack = outer
            ph.close()


class Ring:
    def __init__(self, slots):
        self.slots = slots
        self.i = 0

    def next(self):
        s = self.slots[self.i % len(self.slots)]
        self.i += 1
        return s


def build_program(NPRE, NMAIN, upto=4):
    nc = bass.Bass("TRN2", target_bir_lowering=False)
    NB = NPRE + NMAIN
    NTOK = NB * BLK
    NMT = NMAIN * BLK
    MAIN0 = NPRE * BLK

    def din(name, shape):
        return nc.dram_tensor(name, list(shape), F32, kind="ExternalInput").ap()

    xT = din("xT", [D, HALO + NTOK])
    xtok = din("xtok", [NMT, D])
    maskd = din("mask", [128, 1])
    consts = din("consts", [128, 448])
    a_w_in = din("a_w_in", [D, 4112])
    a_convT = din("a_convT", [128, 24 * 4])
    a_vec = din("a_vec", [128, 16])
    a_normT = din("a_normT", [128, 1])
    a_w_out = din("a_w_out", [D, D])
    b_w_in = din("b_w_in", [D, 3 * D])
    b_convT = din("b_convT", [128, 8 * 3])
    b_w_out = din("b_w_out", [D, D])
    lng = din("lng", [4, 128, D])
    lnb = din("lnb", [4, 128, D])
    f_w_up = din("f_w_up", [2, D, 2 * DFF])
    f_convT = din("f_convT", [2, 128, FC * 3])
    f_w_down = din("f_w_down", [2, DFF, D])
    out = nc.dram_tensor("out", [NMT - TILE, D], F32, kind="ExternalOutput").ap()

    htok_s = [nc.dram_tensor(f"htok_s{i}", [NMT, D], F32, kind="Internal").ap() for i in range(3)]
    hT_s = [nc.dram_tensor(f"hT_s{i}", [D, HALO + NMT], BF16, kind="Internal").ap() for i in range(3)]

    P = Prog(nc)
    P.setup_sems()
    store_ops = []

    cst = P.sb([128, 448], F32, "cst")
    b_cst = P.buf("cst")
    P.dma("sp", lambda e: e.dma_start(out=cst[:], in_=consts[:, :]), "cst", writes=[b_cst])
    ident = cst[:, 0:128]
    Uincl = cst[0:64, 128:192]
    Lstrict = cst[0:64, 192:256]
    Ustrict = cst[0:64, 256:320]
    ones64 = cst[0:64, 320:448]
    ones1 = cst[:, 320:321]
    identb = P.sb([128, 128], BF16, "identb")
    b_identb = P.buf()
    P.op("dve", lambda e: e.tensor_copy(out=identb[:], in_=cst[:, 0:128]), reads=[b_cst], writes=[b_identb])
    onesb = P.sb([128, 2], BF16, "onesb")
    b_onesb = P.buf()
    P.op("dve", lambda e: e.tensor_copy(out=onesb[:], in_=cst[:, 320:322]), reads=[b_cst], writes=[b_onesb])
    maskt = P.sb([128, 1], F32, "maskt")
    b_mask = P.buf()
    P.dma("sp", lambda e: e.dma_start(out=maskt[:], in_=maskd[:, :]), "maskt", writes=[b_mask])
    zeros = P.sb([128, KC * HALO], BF16, "zeros")
    b_zeros = P.buf()
    P.op("pool", lambda e: e.memset(zeros[:], 0.0), writes=[b_zeros])
    b_hTs = [P.buf(f"hTs{i}") for i in range(3)]
    b_hts = [P.buf(f"htoks{i}") for i in range(3)]
    for i in range(3):
        v = hT_s[i].rearrange("(kc p) t -> p kc t", p=128)
        P.dma("sp", lambda e, v=v: e.dma_start(out=v[:, :, 0:HALO], in_=zeros[:].rearrange("p (k t) -> p k t", k=KC)),
              f"zeros{i}", reads=[b_zeros], pwrites=[b_hTs[i]])

    banks = [P.ps([128, 512], F32, f"bank{i}") for i in range(8)]
    bbank = [P.buf(f"bank{i}") for i in range(8)]
    big = Ring([(banks[i], bbank[i]) for i in (0, 1)])
    big6 = Ring([(banks[i], bbank[i]) for i in (0, 1, 4, 5, 6, 7)])
    wide0 = banks[2]
    wide1 = banks[3]
    b_wide = P.buf("wide")
    small = Ring([((banks[i], 0), bbank[i]) for i in (4, 5, 6, 7)])
    sm64 = md128 = su_ring = sc_ring = small

    gam = P.sb([128, D], F32, "gam")
    bet = P.sb([128, D], F32, "bet")
    b_gam = P.buf()
    b_bet = P.buf()
    hres_r = Ring([(P.sb([128, D], F32, f"hres{i}"), P.buf()) for i in range(1)])
    hTt_r = Ring([(P.sb([128, KC, TILE], BF16, f"hTt{i}"), P.buf()) for i in range(1)])
    lnst = P.sb([128, 16], F32, "lnst")
    b_lnst = P.buf()

    def load_ln(i):
        P.dma("sp", lambda e: e.dma_start(out=gam[:], in_=lng[i]), "gam", writes=[b_gam])
        P.dma("sp", lambda e: e.dma_start(out=bet[:], in_=lnb[i]), "bet", writes=[b_bet])

    def ln_epilogue(ti, src_tok, b_src, dst_i, final):
        r0, r1 = ti * TILE, (ti + 1) * TILE
        hres, b_hres = hres_r.next()
        hn, b_hn = hres, b_hres
        P.dma("sp", lambda e: e.dma_start(out=hres[:], in_=src_tok[r0:r1, :]), "hres0",
              reads=[b_src] if b_src is not None else [], writes=[b_hres])
        for hf, bk in ((0, wide0), (1, wide1)):
            P.op("dve", lambda e, hf=hf, bk=bk: e.scalar_tensor_tensor(
                out=hres[:, hf * 512:(hf + 1) * 512], in0=hres[:, hf * 512:(hf + 1) * 512], scalar=ALPHA,
                in1=bk[:, :], op0=ALU.mult, op1=ALU.add), reads=[b_hres, b_wide], writes=[b_hres])
        for hf in range(2):
            P.op("dve", lambda e, hf=hf: e.bn_stats(out=lnst[:, hf * 6:(hf + 1) * 6], in_=hres[:, hf * 512:(hf + 1) * 512]),
                 reads=[b_hres], pwrites=[b_lnst])
        P.op("dve", lambda e: e.bn_aggr(out=lnst[:, 12:14], in_=lnst[:, 0:12]), reads=[b_lnst], writes=[b_lnst])
        P.op("dve", lambda e: e.tensor_scalar(out=lnst[:, 14:15], in0=lnst[:, 13:14], scalar1=1e-5, scalar2=None, op0=ALU.add),
             reads=[b_lnst], writes=[b_lnst])
        P.op("act", lambda e: e.activation(out=lnst[:, 14:15], in_=lnst[:, 14:15], func=AF.Ln), reads=[b_lnst], writes=[b_lnst])
        P.op("act", lambda e: e.activation(out=lnst[:, 15:16], in_=lnst[:, 14:15], func=AF.Exp, scale=-0.5),
             reads=[b_lnst], writes=[b_lnst])
        P.op("dve", lambda e: e.tensor_scalar(out=hres[:], in0=hres[:], scalar1=lnst[:, 12:13], scalar2=lnst[:, 15:16],
                                              op0=ALU.subtract, op1=ALU.mult), reads=[b_hres, b_lnst], writes=[b_hres])
        P.op("pool", lambda e: e.tensor_tensor(out=hn[:], in0=hn[:], in1=gam[:], op=ALU.mult), reads=[b_hn, b_gam], writes=[b_hn])
        P.op("pool", lambda e: e.tensor_tensor(out=hn[:], in0=hn[:], in1=bet[:], op=ALU.add), reads=[b_hn, b_bet], writes=[b_hn])
        if ti == 0:
            P.op("dve", lambda e: e.tensor_scalar(out=hn[:], in0=hn[:], scalar1=maskt[:, 0:1], scalar2=None, op0=ALU.mult),
                 reads=[b_hn, b_mask], writes=[b_hn])
        if final:
            if ti > 0:
                o = P.dma("sp", lambda e: e.dma_start(out=out[r0 - TILE:r1 - TILE, :], in_=hn[:]), "hres0", reads=[b_hn])
                store_ops.append(o)
            return
        P.dma("sp", lambda e: e.dma_start(out=htok_s[dst_i][r0:r1, :], in_=hn[:]), "hres0",
              reads=[b_hn], pwrites=[b_hts[dst_i]])
        hTt, b_hTt = hTt_r.next()
        for half in range(2):
            bk, b_bk = big.next()
            for q in range(4):
                kc = half * 4 + q
                P.op("pe", lambda e, bk=bk, q=q, kc=kc: e.transpose(out=bk[:, q * 128:(q + 1) * 128], in_=hn[:, kc * 128:(kc + 1) * 128],
                                                                   identity=ident), reads=[b_hn, b_cst], pwrites=[b_bk])
            P.op("act", lambda e, bk=bk, half=half: e.activation(
                out=hTt[:, half * 4:(half + 1) * 4, :], in_=bk[:, :].rearrange("p (k t) -> p k t", k=4), func=AF.Copy),
                reads=[b_bk], pwrites=[b_hTt])
        v = hT_s[dst_i].rearrange("(kc p) t -> p kc t", p=128)
        P.dma("sp", lambda e: e.dma_start(out=v[:, :, HALO + r0:HALO + r1], in_=hTt[:]), "hTt0",
              reads=[b_hTt], pwrites=[b_hTs[dst_i]])

    def load_w_cast(dst, b_dst, src_view, ncols, key):
        nmid = src_view.shape[1]
        for m0 in range(0, nmid, 8):
            m1 = min(nmid, m0 + 8)
            c = 0
            while c < ncols:
                w = min(2048, ncols - c)
                P.dma("pool", lambda e, c=c, w=w, m0=m0, m1=m1: e.dma_start(out=dst[:, m0:m1, c:c + w], in_=src_view[:, m0:m1, c:c + w]),
                      key, pwrites=[b_dst])
                c += w

    def make_diag(dg, b_dg, convT_dram, nchunk, ntap, tmp_name):
        ct = P.sb([128, nchunk * ntap], F32, tmp_name)
        b_ct = P.buf()
        P.dma("sp", lambda e: e.dma_start(out=ct[:], in_=convT_dram), tmp_name, writes=[b_ct])
        for m in range(nchunk):
            for j in range(ntap):
                P.op("dve", lambda e, m=m, j=j: e.tensor_scalar(out=dg[:, j, m, :], in0=ident, scalar1=ct[:, m * ntap + j:m * ntap + j + 1],
                                                                scalar2=None, op0=ALU.mult), reads=[b_cst, b_ct], pwrites=[b_dg])

    def phase1():
        Wqkv = P.sb([128, KC, 3072], BF16, "Wqkv")
        Wz = P.sb([128, KC, 1024], BF16, "Wz")
        Wba = P.sb([128, KC, 16], BF16, "Wba")
        Wout = P.sb([128, KC, D], BF16, "Wout")
        dg = P.sb([128, 4, 24, 128], BF16, "dgA")
        b_Wqkv, b_Wz, b_Wba, b_Wout, b_dg = P.buf(), P.buf(), P.buf(), P.buf(), P.buf()
        win_v = a_w_in.rearrange("(kc p) n -> p kc n", p=128)
        load_w_cast(Wqkv, b_Wqkv, win_v[:, :, 0:3072], 3072, "Wqkv")
        load_w_cast(Wz, b_Wz, win_v[:, :, 3072:4096], 1024, "Wz")
        load_w_cast(Wba, b_Wba, win_v[:, :, 4096:4112], 16, "Wba")
        load_w_cast(Wout, b_Wout, a_w_out.rearrange("(kc p) n -> p kc n", p=128), D, "Wout")
        make_diag(dg, b_dg, a_convT[:, :], 24, 4, "ctA")
        load_ln(0)
        avec = P.sb([128, 16], F32, "avec")
        b_avec = P.buf()
        P.dma("sp", lambda e: e.dma_start(out=avec[:], in_=a_vec[:, :]), "avec", writes=[b_avec])
        negA = P.sb([128, 8], F32, "negA")
        b_negA = P.buf()
        P.op("act", lambda e: e.activation(out=negA[:], in_=avec[:, 0:8], func=AF.Exp), reads=[b_avec], writes=[b_negA])
        P.op("dve", lambda e: e.tensor_scalar(out=negA[:], in0=negA[:], scalar1=-1.0, scalar2=None, op0=ALU.mult),
             reads=[b_negA], writes=[b_negA])
        normT = P.sb([128, 1], F32, "normT")
        b_normT = P.buf()
        P.dma("sp", lambda e: e.dma_start(out=normT[:], in_=a_normT[:, :]), "normT", writes=[b_normT])

        GD = NEU_DT
        S = P.sb([128, 8, 128], F32, "S")
        Sbf = P.sb([128, 8, 128], BF16, "Sbf")
        b_S, b_Sbf = P.buf("S"), P.buf("Sbf")
        P.op("pool", lambda e: e.memset(S[:], 0.0), writes=[b_S])
        P.op("pool", lambda e: e.memset(Sbf[:], 0.0), writes=[b_Sbf])

        xb_r = Ring([(P.sb([128, KC, BLK + HALO], BF16, f"xb{i}"), P.buf()) for i in range(1)])
        pre_r = Ring([(P.sb([128, BLK + HALO], BF16, f"pre{i}"), P.buf()) for i in range(2)])
        qkvT = P.sb([128, 24, BLK], BF16, "qkvT")
        b_qkv = [P.buf(f"qkv{m}") for m in range(24)]
        sq = P.sb([128, 8, BLK], BF16, "sq")
        b_sq = [P.buf(f"sq{m}") for m in range(8)]
        sz = P.sb([64, 2, D], BF16, "sz")
        b_sz = [P.buf(f"sz{c}") for c in range(2)]
        ogT = P.sb([128, 8, TILE], BF16, "ogT")
        b_ogT = P.buf()
        NS = 20
        tsc_r = Ring([(P.sb([128, NS, 8], F32, f"tsc{i}"), P.buf()) for i in range(2)])

        def one(shape, dt, name):
            return P.sb(shape, dt, name), P.buf(name)

        kd_all, b_kd = one([64, 8, 128], BF16, "kd_all")
        vp_all, b_vp = one([64, 8, 128], BF16, "vp_all")
        gdU_all, b_gdU = one([64, 8, 64], F32, "gdU_all")
        E_all, b_E = one([64, 8, 64], F32, "E_all")
        dS_all, b_dS = one([64, 8, 64], F32, "dS_all")
        dI_all, b_dI = one([64, 8, 64], F32, "dI_all")
        QKT_all, b_QKT = one([64, 8, 64], BF16, "QKT_all")
        Y_all, b_Y = one([64, 8, 64], GD, "Y_all")
        Z_all, b_Z = one([64, 8, 64], GD, "Z_all")
        P_all, b_P = one([64, 8, 64], GD, "P_all")
        KK_sb, b_KK = one([64, 8, 64], F32, "KK_sb")
        tR, b_tR = one([64, 8, 128], F32, "tR")
        D1_all, b_D1 = one([64, 8, 128], GD, "D1_all")
        vn_all, b_vn = one([64, 8, 128], BF16, "vn_all")
        o_all, b_oall = one([64, 8, 128], F32, "o_all")
        og, b_og = one([64, 8, 128], F32, "og")
        identG = ident if GD == F32 else identb
        b_identG = b_cst if GD == F32 else b_identb

        pairs = Ring([((banks[4], banks[5]), (bbank[4], bbank[5])), ((banks[6], banks[7]), (bbank[6], bbank[7]))])

        def bc_in(ap2, k, n, rows=64):
            return ap2.unsqueeze(2).to_broadcast([rows, k, n])

        def bc_mid(ap2, k, n):
            return ap2.unsqueeze(1).to_broadcast([64, k, n])

        def single():
            (bk, _c0), b = small.next()
            return bk, b

        def mm8(dst_fn, lhs_fn, rhs_fn, reads, bbufs):
            for h in range(8):
                d_, l_, r_ = dst_fn(h), lhs_fn(h), rhs_fn(h)
                P.op("pe", lambda e, d_=d_, l_=l_, r_=r_: e.matmul(d_, lhsT=l_, rhs=r_, start=True, stop=True),
                     reads=reads(h), pwrites=bbufs(h))

        def do_chunk(bi, main, xb, b_xb, c):
            cs = slice(c * CH, (c + 1) * CH)
            if main and c % 2 == 0:
                for cc in (c, c + 1):
                    for kc in range(KC):
                        for hf, bkw in ((0, wide0), (1, wide1)):
                            P.op("pe", lambda e, cc=cc, kc=kc, hf=hf, bkw=bkw, xb=xb: e.matmul(
                                bkw[0:64, :], lhsT=xb[:, kc, HALO + cc * CH:HALO + (cc + 1) * CH],
                                rhs=Wz[:, kc, hf * 512:(hf + 1) * 512], start=(kc == 0), stop=(kc == KC - 1)),
                                reads=[b_xb, b_Wz], pwrites=[b_wide])
                    for hf, bkw in ((0, wide0), (1, wide1)):
                        P.op("act", lambda e, cc=cc, hf=hf, bkw=bkw: e.activation(out=sz[:, cc % 2, hf * 512:(hf + 1) * 512], in_=bkw[0:64, :],
                                                                               func=AF.Silu), reads=[b_wide], pwrites=[b_sz[cc % 2]])
            tsc, b_tsc = tsc_r.next()

            def T(i, rows=64, tsc=tsc):
                return tsc[0:rows, i, :]

            def tiny(eng, fn):
                P.op(eng, fn, reads=[b_tsc], writes=[b_tsc])
            kTh = lambda h: qkvT[:, 8 + h, cs]
            vTh = lambda h: qkvT[:, 16 + h, cs]
            qTh = lambda h: qkvT[:, h, cs]
            kk_bk, b_kk = single()
            mm8(lambda h: kk_bk[0:64, h * 64:(h + 1) * 64], kTh, kTh, lambda h: [b_qkv[8 + h]], lambda h: [b_kk])
            bk_sc, b_sc = single()
            ba_ps = bk_sc[0:64, 0:16]
            for kc in range(KC):
                P.op("pe", lambda e, kc=kc, ba_ps=ba_ps, xb=xb, c=c: e.matmul(
                    ba_ps, lhsT=xb[:, kc, HALO + c * CH:HALO + (c + 1) * CH], rhs=Wba[:, kc, :],
                    start=(kc == 0), stop=(kc == KC - 1)), reads=[b_xb, b_Wba], pwrites=[b_sc])
            if main:
                ssq_ps = bk_sc[0:64, 16:24]
                for h in range(8):
                    P.op("pe", lambda e, h=h, ssq_ps=ssq_ps: e.matmul(ssq_ps[:, h:h + 1], lhsT=sq[:, h, cs], rhs=onesb[:, 0:1],
                                                                    start=True, stop=True), reads=[b_sq[h], b_onesb], pwrites=[b_sc])
            P.op("act", lambda e, T=T: e.activation(out=T(0), in_=ba_ps[:, 0:8], func=AF.Exp, scale=-1.0), reads=[b_sc], writes=[b_tsc])
            tiny("dve", lambda e, T=T: e.tensor_scalar(out=T(0), in0=T(0), scalar1=1.0, scalar2=None, op0=ALU.add))
            tiny("dve", lambda e, T=T: e.reciprocal(out=T(0), in_=T(0)))
            P.op("dve", lambda e, T=T: e.tensor_tensor(out=T(1), in0=ba_ps[:, 8:16], in1=avec[0:64, 8:16], op=ALU.add),
                 reads=[b_sc, b_avec, b_tsc], writes=[b_tsc])
            tiny("act", lambda e, T=T: e.activation(out=T(1), in_=T(1), func=AF.Exp))
            tiny("dve", lambda e, T=T: e.tensor_scalar(out=T(1), in0=T(1), scalar1=1.0, scalar2=None, op0=ALU.add))
            tiny("act", lambda e, T=T: e.activation(out=T(1), in_=T(1), func=AF.Ln))
            P.op("dve", lambda e, T=T: e.tensor_tensor(out=T(1), in0=T(1), in1=negA[0:64, :], op=ALU.mult),
                 reads=[b_tsc, b_negA], writes=[b_tsc])
            if main:
                P.op("dve", lambda e, T=T, ssq_ps=ssq_ps: e.tensor_scalar(out=T(14), in0=ssq_ps, scalar1=1e-6, scalar2=None, op0=ALU.add),
                     reads=[b_sc, b_tsc], writes=[b_tsc])
            P.op("act", lambda e: e.activation(out=KK_sb[:], in_=kk_bk[0:64, :].rearrange("p (h c) -> p h c", h=8), func=AF.Copy),
                 reads=[b_kk], writes=[b_KK])
            P.op("pool", lambda e: e.tensor_tensor(out=dS_all[:], in0=KK_sb[:], in1=bc_mid(cst[0:64, 0:64], 8, 64), op=ALU.mult),
                 reads=[b_KK, b_cst], writes=[b_dS])
            P.op("dve", lambda e, T=T: e.tensor_reduce(out=T(12), in_=dS_all[:], axis=AX.X, op=ALU.add), reads=[b_dS, b_tsc], writes=[b_tsc])
            tiny("dve", lambda e, T=T: e.tensor_scalar(out=T(12), in0=T(12), scalar1=1e-6, scalar2=None, op0=ALU.add))
            tiny("act", lambda e, T=T: e.activation(out=T(12), in_=T(12), func=AF.Ln))
            tiny("act", lambda e, T=T: e.activation(out=T(2), in_=T(12), func=AF.Exp, scale=-0.5))
            tiny("act", lambda e, T=T: e.activation(out=T(3), in_=T(12), func=AF.Exp, scale=0.5))
            tiny("dve", lambda e, T=T: e.tensor_tensor(out=T(9), in0=T(0), in1=T(2), op=ALU.mult))
            tiny("dve", lambda e, T=T: e.scalar_tensor_tensor(out=T(4), in0=T(9), scalar=-1.0, in1=T(2), op0=ALU.mult, op1=ALU.mult))
            bk_g, b_g = single()
            Gc_ps, Gr_ps, Gl_ps = bk_g[0:64, 0:8], bk_g[0:64, 8:16], bk_g[:, 16:24]
            P.op("pe", lambda e, T=T: e.matmul(Gc_ps, lhsT=Uincl, rhs=T(1), start=True, stop=True), reads=[b_cst, b_tsc], pwrites=[b_g])
            P.op("pe", lambda e, T=T: e.matmul(Gr_ps, lhsT=Lstrict, rhs=T(1), start=True, stop=True), reads=[b_cst, b_tsc], pwrites=[b_g])
            P.op("pe", lambda e, T=T: e.matmul(Gl_ps, lhsT=ones64, rhs=T(1), start=True, stop=True), reads=[b_cst, b_tsc], pwrites=[b_g])
            P.op("act", lambda e, T=T: e.activation(out=T(5), in_=Gc_ps, func=AF.Exp), reads=[b_g], writes=[b_tsc])
            P.op("act", lambda e, T=T: e.activation(out=T(6), in_=Gr_ps, func=AF.Exp), reads=[b_g], writes=[b_tsc])
            P.op("act", lambda e, tsc=tsc: e.activation(out=tsc[:, 7, :], in_=Gl_ps, func=AF.Exp), reads=[b_g], writes=[b_tsc])
            tiny("dve", lambda e, T=T: e.tensor_tensor(out=T(6), in0=T(6), in1=T(2), op=ALU.mult))
            tiny("dve", lambda e, T=T: e.tensor_scalar(out=T(8), in0=T(5), scalar1=-1.0, scalar2=None, op0=ALU.mult))
            if main:
                tiny("act", lambda e, T=T: e.activation(out=T(14), in_=T(14), func=AF.Ln))
                tiny("act", lambda e, T=T: e.activation(out=T(10), in_=T(14), func=AF.Exp, scale=-0.5))
                tiny("dve", lambda e, T=T: e.tensor_scalar(out=T(10), in0=T(10), scalar1=float(128 ** -0.5), scalar2=None, op0=ALU.mult))
            (pa, pb), (b_pa, b_pb) = pairs.next()
            pk = (pa, pb)
            bpk = (b_pa, b_pb)
            mm8(lambda h: pk[h // 4][0:64, (h % 4) * 128:(h % 4 + 1) * 128], kTh, lambda h: identb[:],
                lambda h: [b_qkv[8 + h], b_identb], lambda h: [bpk[h // 4]])
            for i in range(2):
                P.op("dve", lambda e, i=i, T=T: e.tensor_tensor(
                    out=kd_all[:, 4 * i:4 * i + 4, :], in0=pk[i][0:64, :].rearrange("p (h d) -> p h d", h=4),
                    in1=bc_in(T(6)[:, 4 * i:4 * i + 4], 4, 128), op=ALU.mult), reads=[bpk[i], b_tsc], pwrites=[b_kd])
            (pa2, pb2), (b_pa2, b_pb2) = pairs.next()
            pv = (pa2, pb2)
            bpv = (b_pa2, b_pb2)
            mm8(lambda h: pv[h // 4][0:64, (h % 4) * 128:(h % 4 + 1) * 128], vTh, lambda h: identb[:],
                lambda h: [b_qkv[16 + h], b_identb], lambda h: [bpv[h // 4]])
            for i in range(2):
                P.op("dve", lambda e, i=i, T=T: e.tensor_tensor(
                    out=vp_all[:, 4 * i:4 * i + 4, :], in0=pv[i][0:64, :].rearrange("p (h d) -> p h d", h=4),
                    in1=bc_in(T(3)[:, 4 * i:4 * i + 4], 4, 128), op=ALU.mult), reads=[bpv[i], b_tsc], pwrites=[b_vp])
            P.op("pool", lambda e, T=T: e.tensor_tensor(out=gdU_all[:], in0=bc_mid(Uincl, 8, 64), in1=bc_in(T(1), 8, 64), op=ALU.mult),
                 reads=[b_cst, b_tsc], writes=[b_gdU])
            gd_bk, b_gd = single()
            mm8(lambda h: gd_bk[0:64, h * 64:(h + 1) * 64], lambda h: Lstrict, lambda h: gdU_all[:, h, :],
                lambda h: [b_cst, b_gdU], lambda h: [b_gd])
            P.op("act", lambda e: e.activation(out=E_all[:], in_=gd_bk[0:64, :].rearrange("p (h c) -> p h c", h=8), func=AF.Exp),
                 reads=[b_gd], writes=[b_E])
            P.op("pool", lambda e: e.tensor_tensor(out=dS_all[:], in0=E_all[:], in1=bc_mid(Ustrict, 8, 64), op=ALU.mult),
                 reads=[b_E, b_cst], writes=[b_dS])
            P.op("pool", lambda e, T=T: e.tensor_tensor(out=dS_all[:], in0=dS_all[:], in1=bc_in(T(4), 8, 64), op=ALU.mult),
                 reads=[b_dS, b_tsc], writes=[b_dS])
            P.op("pool", lambda e: e.tensor_tensor(out=Y_all[:], in0=KK_sb[:], in1=dS_all[:], op=ALU.mult),
                 reads=[b_KK, b_dS], writes=[b_Y])
            if main:
                P.op("pool", lambda e: e.tensor_tensor(out=dI_all[:], in0=E_all[:], in1=bc_mid(Uincl, 8, 64), op=ALU.mult),
                     reads=[b_E, b_cst], writes=[b_dI])
                P.op("pool", lambda e, T=T: e.tensor_tensor(out=dI_all[:], in0=dI_all[:], in1=bc_in(T(2), 8, 64), op=ALU.mult),
                     reads=[b_dI, b_tsc], writes=[b_dI])
                kq_bk, b_kq = single()
                mm8(lambda h: kq_bk[0:64, h * 64:(h + 1) * 64], kTh, qTh, lambda h: [b_qkv[8 + h], b_qkv[h]], lambda h: [b_kq])
                P.op("dve", lambda e: e.tensor_tensor(out=QKT_all[:], in0=kq_bk[0:64, :].rearrange("p (h c) -> p h c", h=8), in1=dI_all[:], op=ALU.mult),
                     reads=[b_kq, b_dI], writes=[b_QKT])
            z_bk, b_zb = single()
            mm8(lambda h: z_bk[0:64, h * 64:(h + 1) * 64], lambda h: Y_all[:, h, :], lambda h: identG[0:64, 0:64],
                lambda h: [b_Y, b_identG], lambda h: [b_zb])
            P.op("act", lambda e: e.activation(out=Z_all[:], in_=z_bk[0:64, :].rearrange("p (h c) -> p h c", h=8), func=AF.Copy),
                 reads=[b_zb], writes=[b_Z])
            P.op("pool", lambda e: e.tensor_tensor(out=P_all[:], in0=Y_all[:], in1=bc_mid(cst[0:64, 0:64], 8, 64), op=ALU.add),
                 reads=[b_Y, b_cst], writes=[b_P])
            for lvl in range(1, 6):
                zn_bk, b_zn = single()
                mm8(lambda h: zn_bk[0:64, h * 64:(h + 1) * 64], lambda h: Y_all[:, h, :], lambda h: Z_all[:, h, :],
                    lambda h: [b_Y, b_Z], lambda h: [b_zn])
                if lvl < 5:
                    yn_bk, b_yn = single()
                    mm8(lambda h: yn_bk[0:64, h * 64:(h + 1) * 64], lambda h: Z_all[:, h, :], lambda h: Y_all[:, h, :],
                        lambda h: [b_Y, b_Z], lambda h: [b_yn])
                P.op("act", lambda e, zn_bk=zn_bk: e.activation(out=Z_all[:], in_=zn_bk[0:64, :].rearrange("p (h c) -> p h c", h=8), func=AF.Copy),
                     reads=[b_zn], writes=[b_Z])
                if lvl < 5:
                    P.op("act", lambda e, yn_bk=yn_bk: e.activation(out=Y_all[:], in_=yn_bk[0:64, :].rearrange("p (h c) -> p h c", h=8), func=AF.Copy),
                         reads=[b_yn], writes=[b_Y])
                pu_bk, b_pu = single()
                mm8(lambda h: pu_bk[0:64, h * 64:(h + 1) * 64], lambda h: Z_all[:, h, :], lambda h: P_all[:, h, :],
                    lambda h: [b_Z, b_P], lambda h: [b_pu])
                P.op("dve", lambda e, pu_bk=pu_bk: e.tensor_tensor(out=P_all[:], in0=pu_bk[0:64, :].rearrange("p (h c) -> p h c", h=8), in1=P_all[:], op=ALU.add),
                     reads=[b_pu, b_P], writes=[b_P])
            (ra, rb), (b_ra, b_rb) = pairs.next()
            pr = (ra, rb)
            bpr = (b_ra, b_rb)
            mm8(lambda h: pr[h // 4][0:64, (h % 4) * 128:(h % 4 + 1) * 128], kTh, lambda h: Sbf[:, h, :],
                lambda h: [b_qkv[8 + h], b_Sbf], lambda h: [bpr[h // 4]])
            for i in range(2):
                P.op("dve", lambda e, i=i, T=T: e.tensor_tensor(
                    out=tR[:, 4 * i:4 * i + 4, :], in0=pr[i][0:64, :].rearrange("p (h d) -> p h d", h=4),
                    in1=bc_in(T(8)[:, 4 * i:4 * i + 4], 4, 128), op=ALU.mult), reads=[bpr[i], b_tsc], pwrites=[b_tR])
            P.op("pool", lambda e: e.tensor_tensor(out=D1_all[:], in0=tR[:], in1=vp_all[:], op=ALU.add), reads=[b_tR, b_vp], writes=[b_D1])
            (va, vb), (b_va, b_vb) = pairs.next()
            pvn = (va, vb)
            bpvn = (b_va, b_vb)
            mm8(lambda h: pvn[h // 4][0:64, (h % 4) * 128:(h % 4 + 1) * 128], lambda h: P_all[:, h, :], lambda h: D1_all[:, h, :],
                lambda h: [b_P, b_D1], lambda h: [bpvn[h // 4]])
            for i in range(2):
                P.op("dve", lambda e, i=i, T=T: e.tensor_tensor(
                    out=vn_all[:, 4 * i:4 * i + 4, :], in0=pvn[i][0:64, :].rearrange("p (h d) -> p h d", h=4),
                    in1=bc_in(T(9)[:, 4 * i:4 * i + 4], 4, 128), op=ALU.mult), reads=[bpvn[i], b_tsc], pwrites=[b_vn])
            if main:
                (qa, qb), (b_qa, b_qb) = pairs.next()
                p1 = (qa, qb)
                bp1 = (b_qa, b_qb)
                mm8(lambda h: p1[h // 4][0:64, (h % 4) * 128:(h % 4 + 1) * 128], qTh, lambda h: Sbf[:, h, :],
                    lambda h: [b_qkv[h], b_Sbf], lambda h: [bp1[h // 4]])
                for i in range(2):
                    P.op("dve", lambda e, i=i, T=T: e.tensor_tensor(
                        out=o_all[:, 4 * i:4 * i + 4, :], in0=p1[i][0:64, :].rearrange("p (h d) -> p h d", h=4),
                        in1=bc_in(T(5)[:, 4 * i:4 * i + 4], 4, 128), op=ALU.mult), reads=[bp1[i], b_tsc], pwrites=[b_oall])
                (wa, wb), (b_wa, b_wb) = pairs.next()
                p2 = (wa, wb)
                bp2 = (b_wa, b_wb)
                mm8(lambda h: p2[h // 4][0:64, (h % 4) * 128:(h % 4 + 1) * 128], lambda h: QKT_all[:, h, :], lambda h: vn_all[:, h, :],
                    lambda h: [b_QKT, b_vn], lambda h: [bp2[h // 4]])
                for i in range(2):
                    P.op("dve", lambda e, i=i: e.tensor_tensor(
                        out=o_all[:, 4 * i:4 * i + 4, :], in0=p2[i][0:64, :].rearrange("p (h d) -> p h d", h=4),
                        in1=o_all[:, 4 * i:4 * i + 4, :], op=ALU.add), reads=[bp2[i], b_oall], writes=[b_oall])
            (sa, sb_), (b_sa, b_sb) = pairs.next()
            psu = (sa, sb_)
            bpsu = (b_sa, b_sb)
            mm8(lambda h: psu[h // 4][:, (h % 4) * 128:(h % 4 + 1) * 128], lambda h: kd_all[:, h, :], lambda h: vn_all[:, h, :],
                lambda h: [b_kd, b_vn], lambda h: [bpsu[h // 4]])
            P.op("pool", lambda e, tsc=tsc: e.tensor_tensor(out=S[:], in0=S[:], in1=bc_in(tsc[:, 7, :], 8, 128, rows=128), op=ALU.mult),
                 reads=[b_S, b_tsc], writes=[b_S])
            for i in range(2):
                P.op("dve", lambda e, i=i: e.tensor_tensor(
                    out=S[:, 4 * i:4 * i + 4, :], in0=psu[i][:, :].rearrange("p (h d) -> p h d", h=4),
                    in1=S[:, 4 * i:4 * i + 4, :], op=ALU.add), reads=[bpsu[i], b_S], writes=[b_S])
            P.op("act", lambda e: e.activation(out=Sbf[:], in_=S[:], func=AF.Copy), reads=[b_S], writes=[b_Sbf])
            if not main:
                return
            P.op("pool", lambda e: e.tensor_tensor(out=og[:], in0=o_all[:], in1=o_all[:], op=ALU.mult), reads=[b_oall], writes=[b_og])
            P.op("dve", lambda e, T=T: e.tensor_reduce(out=T(13), in_=og[:], axis=AX.X, op=ALU.add), reads=[b_og, b_tsc], writes=[b_tsc])
            tiny("dve", lambda e, T=T: e.tensor_tensor(out=T(13), in0=T(13), in1=T(10), op=ALU.mult))
            tiny("dve", lambda e, T=T: e.tensor_tensor(out=T(13), in0=T(13), in1=T(10), op=ALU.mult))
            tiny("dve", lambda e, T=T: e.tensor_scalar(out=T(13), in0=T(13), scalar1=float(1.0 / 128), scalar2=1e-6, op0=ALU.mult, op1=ALU.add))
            tiny("act", lambda e, T=T: e.activation(out=T(13), in_=T(13), func=AF.Ln))
            tiny("act", lambda e, T=T: e.activation(out=T(13), in_=T(13), func=AF.Exp, scale=-0.5))
            tiny("dve", lambda e, T=T: e.tensor_tensor(out=T(11), in0=T(13), in1=T(10), op=ALU.mult))
            P.op("pool", lambda e, T=T: e.tensor_tensor(out=og[:], in0=o_all[:], in1=bc_in(T(11), 8, 128), op=ALU.mult),
                 reads=[b_oall, b_tsc, b_og], writes=[b_og])
            P.op("pool", lambda e, c=c: e.tensor_tensor(out=og[:], in0=og[:], in1=sz[:, c % 2, :].rearrange("p (h d) -> p h d", h=8), op=ALU.mult),
                 reads=[b_og, b_sz[c % 2]], writes=[b_og])
            bk, b_bk = big.next()
            for h in range(8):
                P.op("pe", lambda e, bk=bk, h=h: e.transpose(out=bk[:, h * 64:(h + 1) * 64], in_=og[:, h, :], identity=cst[0:64, 0:64]),
                     reads=[b_og, b_cst], pwrites=[b_bk])
            half = c % 2
            P.op("act", lambda e, bk=bk, half=half: e.activation(
                out=ogT[:, :, half * 64:(half + 1) * 64], in_=bk[:, :].rearrange("p (h t) -> p h t", h=8), func=AF.Identity, scale=normT[:, 0:1]),
                reads=[b_bk, b_normT], pwrites=[b_ogT])
            if half == 1:
                ti = (bi - NPRE) * 3 + c // 2
                for h in range(8):
                    for hf, bkw in ((0, wide0), (1, wide1)):
                        P.op("pe", lambda e, h=h, hf=hf, bkw=bkw: e.matmul(
                            bkw[:, :], lhsT=ogT[:, h, :], rhs=Wout[:, h, hf * 512:(hf + 1) * 512], start=(h == 0), stop=(h == 7)),
                            reads=[b_ogT, b_Wout], pwrites=[b_wide])
                ln_epilogue(ti, xtok, None, 0, final=(upto == 1))

        for bi in range(NB):
            main = bi >= NPRE
            t0 = bi * BLK
            xb, b_xb = xb_r.next()
            xv = xT.rearrange("(kc p) t -> p kc t", p=128)
            P.dma("pool", lambda e, xb=xb, t0=t0: e.dma_start(out=xb[:], in_=xv[:, :, t0:t0 + BLK + HALO]), "xb0",
                  writes=[b_xb])
            chunks = list(range(24)) if main else list(range(8, 24))
            for m in chunks:
                bk, b_bk = big.next()
                for kc in range(KC):
                    P.op("pe", lambda e, bk=bk, m=m, kc=kc, xb=xb: e.matmul(
                        bk[:, 0:BLK + HALO], lhsT=Wqkv[:, kc, m * 128:(m + 1) * 128], rhs=xb[:, kc, :],
                        start=(kc == 0), stop=(kc == KC - 1)), reads=[b_Wqkv, b_xb], pwrites=[b_bk])
                pre, b_pre = pre_r.next()
                P.op("dve", lambda e, pre=pre, bk=bk: e.tensor_copy(out=pre[:], in_=bk[:, 0:BLK + HALO]), reads=[b_bk], writes=[b_pre])
                bk2, b_bk2 = big.next()
                for j in range(4):
                    P.op("pe", lambda e, bk2=bk2, m=m, j=j, pre=pre: e.matmul(
                        bk2[:, 0:BLK], lhsT=dg[:, j, m, :], rhs=pre[:, j:j + BLK], start=(j == 0), stop=(j == 3)),
                        reads=[b_dg, b_pre], pwrites=[b_bk2])
                P.op("act", lambda e, bk2=bk2, m=m: e.activation(out=qkvT[:, m, :], in_=bk2[:, 0:BLK], func=AF.Silu),
                     reads=[b_bk2], writes=[b_qkv[m]])
                if m < 8:
                    P.op("pool", lambda e, m=m: e.tensor_tensor(out=sq[:, m, :], in0=qkvT[:, m, :], in1=qkvT[:, m, :], op=ALU.mult),
                         reads=[b_qkv[m]], writes=[b_sq[m]])
            for c in range(6):
                do_chunk(bi, main, xb, b_xb, c)

    def load_xb(xb, b_xb, src_i, t0, key):
        v = hT_s[src_i].rearrange("(kc p) t -> p kc t", p=128)
        P.dma("sp", lambda e: e.dma_start(out=xb[:], in_=v[:, :, HALO + t0 - 2:HALO + t0 + BLK]), key,
              reads=[b_hTs[src_i]], writes=[b_xb])

    def phase_ffn(li, src_i, dst_i, final):
        Wup = P.sb([128, KC, 2 * DFF], BF16, "Wup")
        Wd = P.sb([128, FC, D], BF16, "Wd")
        dgF = P.sb([128, 3, FC, 128], BF16, "dgF")
        b_Wup, b_Wd, b_dgF = P.buf(), P.buf(), P.buf()
        load_w_cast(Wup, b_Wup, f_w_up[li].rearrange("(kc p) n -> p kc n", p=128), 2 * DFF, "Wup")
        load_w_cast(Wd, b_Wd, f_w_down[li].rearrange("(m p) n -> p m n", p=128), D, "Wd")
        make_diag(dgF, b_dgF, f_convT[li], FC, 3, "ctF")
        load_ln(1 + 2 * li)
        xb_r = Ring([(P.sb([128, KC, BLK + 2], BF16, f"fxb{i}"), P.buf()) for i in range(1)])
        pre_r = Ring([(P.sb([128, BLK + 2], BF16, f"fpre{i}"), P.buf()) for i in range(2)])
        su_r = Ring([(P.sb([128, BLK], BF16, f"fsu{i}"), P.buf()) for i in range(2)])
        aT = P.sb([128, FC, BLK], BF16, "aT")
        b_aT = P.buf()
        for bi in range(NMAIN):
            t0 = bi * BLK
            xb, b_xb = xb_r.next()
            load_xb(xb, b_xb, src_i, t0, "fxb0")
            for m in range(FC):
                bk, b_bk = big6.next()
                for kc in range(KC):
                    P.op("pe", lambda e, bk=bk, m=m, kc=kc, xb=xb: e.matmul(
                        bk[:, 0:BLK + 2], lhsT=Wup[:, kc, m * 128:(m + 1) * 128], rhs=xb[:, kc, :],
                        start=(kc == 0), stop=(kc == KC - 1)), reads=[b_Wup, b_xb], pwrites=[b_bk])
                pre, b_pre = pre_r.next()
                P.op("dve", lambda e, pre=pre, bk=bk: e.tensor_copy(out=pre[:], in_=bk[:, 0:BLK + 2]), reads=[b_bk], writes=[b_pre])
                bk2, b_bk2 = big6.next()
                for j in range(3):
                    P.op("pe", lambda e, bk2=bk2, m=m, j=j, pre=pre: e.matmul(
                        bk2[:, 0:BLK], lhsT=dgF[:, j, m, :], rhs=pre[:, j:j + BLK], start=(j == 0), stop=(j == 2)),
                        reads=[b_dgF, b_pre], pwrites=[b_bk2])
                su, b_su = su_r.next()
                P.op("act", lambda e, bk2=bk2, su=su: e.activation(out=su[:], in_=bk2[:, 0:BLK], func=AF.Silu), reads=[b_bk2], writes=[b_su])
                bk3, b_bk3 = big6.next()
                for kc in range(KC):
                    P.op("pe", lambda e, bk3=bk3, m=m, kc=kc, xb=xb: e.matmul(
                        bk3[:, 0:BLK], lhsT=Wup[:, kc, DFF + m * 128:DFF + (m + 1) * 128], rhs=xb[:, kc, 2:BLK + 2],
                        start=(kc == 0), stop=(kc == KC - 1)), reads=[b_Wup, b_xb], pwrites=[b_bk3])
                P.op("dve", lambda e, bk3=bk3, su=su, m=m: e.tensor_tensor(out=aT[:, m, :], in0=bk3[:, 0:BLK], in1=su[:], op=ALU.mult),
                     reads=[b_bk3, b_su], pwrites=[b_aT])
            for tl in range(3):
                for m in range(FC):
                    for hf, bkw in ((0, wide0), (1, wide1)):
                        P.op("pe", lambda e, m=m, hf=hf, bkw=bkw, tl=tl: e.matmul(
                            bkw[:, :], lhsT=aT[:, m, tl * TILE:(tl + 1) * TILE], rhs=Wd[:, m, hf * 512:(hf + 1) * 512],
                            start=(m == 0), stop=(m == FC - 1)), reads=[b_aT, b_Wd], pwrites=[b_wide])
                ln_epilogue(bi * 3 + tl, htok_s[src_i], b_hts[src_i], dst_i, final)

    def phase_sconv(src_i, dst_i, final):
        Win = P.sb([128, KC, 3 * D], BF16, "Win")
        Wo = P.sb([128, KC, D], BF16, "Wo")
        dgB = P.sb([128, 3, 8, 128], BF16, "dgB")
        b_Win, b_Wo, b_dgB = P.buf(), P.buf(), P.buf()
        load_w_cast(Win, b_Win, b_w_in.rearrange("(kc p) n -> p kc n", p=128), 3 * D, "Win")
        load_w_cast(Wo, b_Wo, b_w_out.rearrange("(kc p) n -> p kc n", p=128), D, "Wo")
        make_diag(dgB, b_dgB, b_convT[:, :], 8, 3, "ctB")
        load_ln(2)
        xb_r = Ring([(P.sb([128, KC, BLK + 2], BF16, f"sxb{i}"), P.buf()) for i in range(2)])
        c_r = Ring([(P.sb([128, BLK + 2], BF16, f"scs{i}"), P.buf()) for i in range(2)])
        cx_r = Ring([(P.sb([128, BLK + 2], BF16, f"scx{i}"), P.buf()) for i in range(2)])
        bs_r = Ring([(P.sb([128, BLK], BF16, f"sbs{i}"), P.buf()) for i in range(2)])
        vT = P.sb([128, 8, BLK], BF16, "vTs")
        b_vT = P.buf()
        for bi in range(NMAIN):
            t0 = bi * BLK
            xb, b_xb = xb_r.next()
            load_xb(xb, b_xb, src_i, t0, f"sxb{xb_r.i % 2}")
            for m in range(8):
                bk, b_bk = big6.next()
                for kc in range(KC):
                    P.op("pe", lambda e, bk=bk, m=m, kc=kc, xb=xb: e.matmul(
                        bk[:, 0:BLK + 2], lhsT=Win[:, kc, D + m * 128:D + (m + 1) * 128], rhs=xb[:, kc, :],
                        start=(kc == 0), stop=(kc == KC - 1)), reads=[b_Win, b_xb], pwrites=[b_bk])
                cs_, b_cs = c_r.next()
                P.op("act", lambda e, cs_=cs_, bk=bk: e.activation(out=cs_[:], in_=bk[:, 0:BLK + 2], func=AF.Copy), reads=[b_bk], writes=[b_cs])
                bk2, b_bk2 = big6.next()
                for kc in range(KC):
                    P.op("pe", lambda e, bk2=bk2, m=m, kc=kc, xb=xb: e.matmul(
                        bk2[:, 0:BLK + 2], lhsT=Win[:, kc, 2 * D + m * 128:2 * D + (m + 1) * 128], rhs=xb[:, kc, :],
                        start=(kc == 0), stop=(kc == KC - 1)), reads=[b_Win, b_xb], pwrites=[b_bk2])
                cx, b_cx = cx_r.next()
                P.op("dve", lambda e, cx=cx, bk2=bk2, cs_=cs_: e.tensor_tensor(out=cx[:], in0=bk2[:, 0:BLK + 2], in1=cs_[:], op=ALU.mult),
                     reads=[b_bk2, b_cs], writes=[b_cx])
                bk3, b_bk3 = big6.next()
                for j in range(3):
                    P.op("pe", lambda e, bk3=bk3, m=m, j=j, cx=cx: e.matmul(
                        bk3[:, 0:BLK], lhsT=dgB[:, j, m, :], rhs=cx[:, j:j + BLK], start=(j == 0), stop=(j == 2)),
                        reads=[b_dgB, b_cx], pwrites=[b_bk3])
                bk4, b_bk4 = big6.next()
                for kc in range(KC):
                    P.op("pe", lambda e, bk4=bk4, m=m, kc=kc, xb=xb: e.matmul(
                        bk4[:, 0:BLK], lhsT=Win[:, kc, m * 128:(m + 1) * 128], rhs=xb[:, kc, 2:BLK + 2],
                        start=(kc == 0), stop=(kc == KC - 1)), reads=[b_Win, b_xb], pwrites=[b_bk4])
                bs, b_bs = bs_r.next()
                P.op("act", lambda e, bs=bs, bk4=bk4: e.activation(out=bs[:], in_=bk4[:, 0:BLK], func=AF.Copy), reads=[b_bk4], writes=[b_bs])
                P.op("dve", lambda e, bk3=bk3, bs=bs, m=m: e.tensor_tensor(out=vT[:, m, :], in0=bk3[:, 0:BLK], in1=bs[:], op=ALU.mult),
                     reads=[b_bk3, b_bs], pwrites=[b_vT])
            for tl in range(3):
                for m in range(8):
                    for hf, bkw in ((0, wide0), (1, wide1)):
                        P.op("pe", lambda e, m=m, hf=hf, bkw=bkw, tl=tl: e.matmul(
                            bkw[:, :], lhsT=vT[:, m, tl * TILE:(tl + 1) * TILE], rhs=Wo[:, m, hf * 512:(hf + 1) * 512],
                            start=(m == 0), stop=(m == 7)), reads=[b_vT, b_Wo], pwrites=[b_wide])
                ln_epilogue(bi * 3 + tl, htok_s[src_i], b_hts[src_i], dst_i, final)

    with P.phase():
        phase1()
    if upto >= 2:
        with P.phase():
            phase_ffn(0, 0, 1, final=(upto == 2))
    if upto >= 3:
        with P.phase():
            phase_sconv(1, 2, final=(upto == 3))
    if upto >= 4:
        with P.phase():
            phase_ffn(1, 2, 0, final=True)
    P.stack.close()
    return nc


def make_consts():
    c = np.zeros((128, 448), np.float32)
    c[:, 0:128] = np.eye(128, dtype=np.float32)
    r = np.arange(64)[:, None]
    q = np.arange(64)[None, :]
    c[0:64, 128:192] = (r <= q)
    c[0:64, 192:256] = (r > q)
    c[0:64, 256:320] = (r < q)
    c[:, 320:448] = 1.0
    return c


def rep128(v):
    v = np.asarray(v, np.float32).reshape(1, -1)
    return np.ascontiguousarray(np.repeat(v, 128, axis=0))


def weight_inputs(a_w_in, a_conv, a_log, a_dt_bias, a_norm, a_w_out, b_w_in, b_conv, b_w_out,
                  ln_mix_g, ln_mix_b, ffn_w_up, ffn_conv, ffn_w_down, ln_ffn_g, ln_ffn_b):
    f = np.float32
    d = {}
    d["consts"] = make_consts()
    d["a_w_in"] = np.ascontiguousarray(a_w_in[0], f)
    d["a_convT"] = np.ascontiguousarray(a_conv[0].T.reshape(24, 128, 4).transpose(1, 0, 2).reshape(128, 96), f)
    d["a_vec"] = np.concatenate([rep128(a_log[0]), rep128(a_dt_bias[0])], axis=1)
    d["a_normT"] = np.ascontiguousarray(a_norm[0].reshape(128, 1), f)
    d["a_w_out"] = np.ascontiguousarray(a_w_out[0], f)
    d["b_w_in"] = np.ascontiguousarray(b_w_in[0], f)
    d["b_convT"] = np.ascontiguousarray(b_conv[0].T.reshape(8, 128, 3).transpose(1, 0, 2).reshape(128, 24), f)
    d["b_w_out"] = np.ascontiguousarray(b_w_out[0], f)
    d["lng"] = np.stack([rep128(ln_mix_g[0]), rep128(ln_ffn_g[0]), rep128(ln_mix_g[1]), rep128(ln_ffn_g[1])])
    d["lnb"] = np.stack([rep128(ln_mix_b[0]), rep128(ln_ffn_b[0]), rep128(ln_mix_b[1]), rep128(ln_ffn_b[1])])
    d["f_w_up"] = np.ascontiguousarray(ffn_w_up, f)
    d["f_convT"] = np.stack([np.ascontiguousarray(ffn_conv[i].T.reshape(FC, 128, 3).transpose(1, 0, 2).reshape(128, FC * 3)) for i in range(2)]).astype(f)
    d["f_w_down"] = np.ascontiguousarray(ffn_w_down, f)
    return d


def core_stream_inputs(stream, valid, NPRE, NMAIN):
    ntok = (NPRE + NMAIN) * BLK
    assert stream.shape[0] == ntok
    xT = np.zeros((D, HALO + ntok), np.float32)
    xT[:, HALO:] = stream.T
    m0 = NPRE * BLK
    return {"xT": xT, "xtok": np.ascontiguousarray(stream[m0:]), "mask": valid[m0:m0 + 128].astype(np.float32).reshape(128, 1)}


_CACHE = {}


def kernel(x, meta, a_w_in, a_conv, a_log, a_dt_bias, a_norm, a_w_out, b_w_in, b_conv, b_w_out,
           ln_mix_g, ln_mix_b, ffn_w_up, ffn_conv, ffn_w_down, ln_ffn_g, ln_ffn_b):
    x = np.asarray(x, np.float32)
    meta = np.asarray(meta, np.float32)
    B, SEQ, _ = x.shape
    NPRE, NMAIN = NPRE_FULL, NMAIN_FULL
    ntok = (NPRE + NMAIN) * BLK
    half_tok = NMAIN * BLK - TILE
    w = weight_inputs(*[np.asarray(a, np.float32) for a in (a_w_in, a_conv, a_log, a_dt_bias, a_norm, a_w_out, b_w_in, b_conv, b_w_out,
                                                            ln_mix_g, ln_mix_b, ffn_w_up, ffn_conv, ffn_w_down, ln_ffn_g, ln_ffn_b)])
    in_maps = []
    for core in range(8):
        b, half = core // 2, core % 2
        stream = np.zeros((ntok, D), np.float32)
        valid = np.zeros((ntok,), bool)
        n_x = half_tok * (half + 1)
        seq = np.concatenate([meta, x[b, :n_x]], axis=0)
        stream[ntok - seq.shape[0]:] = seq
        valid[ntok - seq.shape[0]:] = True
        m = dict(w)
        m.update(core_stream_inputs(stream, valid, NPRE, NMAIN))
        in_maps.append(m)
    if "nc" not in _CACHE:
        _CACHE["nc"] = build_program(NPRE, NMAIN)
    res = run_bass_kernel_spmd(_CACHE["nc"], in_maps, core_ids=list(range(8)))
    outp = np.zeros((B, SEQ, D), np.float32)
    for core in range(8):
        b, half = core // 2, core % 2
        outp[b, half * half_tok:(half + 1) * half_tok] = res.results[core]["out"]
    return outp
```

```python
import contextlib
import numpy as np
import concourse.bass as bass
import concourse.mybir as mybir
from concourse.bass_utils import run_bass_kernel_spmd

F32 = mybir.dt.float32
BF16 = mybir.dt.bfloat16
AF = mybir.ActivationFunctionType
ALU = mybir.AluOpType
AX = mybir.AxisListType

NSEM_PER_ENG = 6
SAME_ENG_SYNC = True
NEU_DT = BF16

D = 1024
KC = 8
TILE = 128
BLK = 384
CH = 64
HALO = 3
DFF = 2816
FC = DFF // 128
ALPHA = float((2.0 * 2) ** 0.25)
NPRE_FULL = 11
NMAIN_FULL = 11


class Buf:
    __slots__ = ("name", "writers", "readers", "open", "pre")

    def __init__(self, name):
        self.name = name
        self.writers = []
        self.readers = []
        self.open = False
        self.pre = []


class Op:
    __slots__ = ("eng", "emit", "deps", "idx", "dma_key", "dma_cnt", "waits")

    def __init__(self, eng, emit):
        self.eng = eng
        self.emit = emit
        self.deps = []
        self.idx = -1
        self.dma_key = None
        self.dma_cnt = 0
        self.waits = []


class Prog:
    ENGS = ("pe", "act", "dve", "pool", "sp")

    def __init__(self, nc):
        self.nc = nc
        self.ops = {e: [] for e in self.ENGS}
        self.ncomp = {e: 0 for e in self.ENGS}
        self.dma_counts = {}
        self.stack = contextlib.ExitStack()
        self.nbuf = 0
        self.nname = 0

    def sb(self, shape, dt, name=None):
        self.nname += 1
        return self.stack.enter_context(self.nc.sbuf_tensor(f"{name or 'sb'}_{self.nname}", list(shape), dt))

    def ps(self, shape, dt=F32, name=None):
        self.nname += 1
        return self.stack.enter_context(self.nc.psum_tensor(name or f"ps{self.nname}", list(shape), dt))

    def buf(self, name=None):
        self.nbuf += 1
        return Buf(name or f"b{self.nbuf}")

    def _track(self, op, reads, writes, pwrites):
        for b in reads:
            op.deps.extend(b.writers)
            b.readers.append(op)
            b.open = False
        for b in writes:
            op.deps.extend(b.readers)
            op.deps.extend(b.writers)
            b.pre = list(b.readers) + list(b.writers)
            b.writers = [op]
            b.readers = []
            b.open = True
        for b in pwrites:
            if b.open:
                op.deps.extend(b.pre)
                b.writers.append(op)
            else:
                op.deps.extend(b.readers)
                op.deps.extend(b.writers)
                b.pre = list(b.readers) + list(b.writers)
                b.writers = [op]
                b.readers = []
                b.open = True

    def op(self, eng, emit, reads=(), writes=(), pwrites=()):
        o = Op(eng, emit)
        self._track(o, reads, writes, pwrites)
        o.idx = self.ncomp[eng]
        self.ncomp[eng] += 1
        self.ops[eng].append(o)
        return o

    def dma(self, eng, emit, key, reads=(), writes=(), pwrites=()):
        o = Op(eng, emit)
        self._track(o, reads, writes, pwrites)
        o.idx = -1
        o.dma_key = key
        self.dma_counts[key] = self.dma_counts.get(key, 0) + 1
        o.dma_cnt = self.dma_counts[key]
        self.ops[eng].append(o)
        return o

    def setup_sems(self):
        nc = self.nc
        st = self.stack
        self.comp_sems = {}
        for e in ("pe", "act", "dve", "pool"):
            self.comp_sems[e] = [st.enter_context(nc.semaphore(f"s_{e}_{i}")) for i in range(NSEM_PER_ENG)]
        self.dma_sems = {}
        self.emitted = {e: 0 for e in self.ENGS}
        self.waited_idx = {e: {x: -1 for x in self.ENGS} for e in self.ENGS}
        self.waited_dma = {e: {} for e in self.ENGS}

    def emit(self):
        nc = self.nc
        comp_sems, dma_sems = self.comp_sems, self.dma_sems
        for k in self.dma_counts:
            if k not in dma_sems:
                dma_sems[k] = self.stack.enter_context(nc.semaphore(f"d_{len(dma_sems)}"))
        new_ops = {e: self.ops[e][self.emitted[e]:] for e in self.ENGS}
        for e in self.ENGS:
            waited_idx = self.waited_idx[e]
            waited_dma = self.waited_dma[e]
            for o in new_ops[e]:
                need_idx = {}
                need_dma = {}
                for d in o.deps:
                    if d is o:
                        continue
                    if d.dma_key is not None:
                        if d.dma_cnt > need_dma.get(d.dma_key, 0):
                            need_dma[d.dma_key] = d.dma_cnt
                    else:
                        if d.eng == e and (e == "pe" or not SAME_ENG_SYNC):
                            continue
                        if d.idx > need_idx.get(d.eng, -1):
                            need_idx[d.eng] = d.idx
                for src, k in need_idx.items():
                    if k > waited_idx[src]:
                        waited_idx[src] = k
                        o.waits.append((comp_sems[src][k % NSEM_PER_ENG], k // NSEM_PER_ENG + 1))
                for key, c in need_dma.items():
                    if c > waited_dma.get(key, 0):
                        waited_dma[key] = c
                        o.waits.append((dma_sems[key], 16 * c))
            self.emitted[e] = len(self.ops[e])
        final = [(dma_sems[k], 16 * c) for k, c in self.dma_counts.items()]

        def replay(ename, eng):
            for o in new_ops[ename]:
                for s, v in o.waits:
                    eng.wait_ge(s, v)
                ins = o.emit(eng)
                if o.dma_key is not None:
                    ins.then_inc(dma_sems[o.dma_key], 16)
                else:
                    ins.then_inc(comp_sems[ename][o.idx % NSEM_PER_ENG], 1)
            if ename == "sp":
                for s, v in final:
                    eng.wait_ge(s, v)

        with nc.Block() as block:
            @block.tensor
            def _(eng):
                replay("pe", eng)

            @block.scalar
            def _(eng):
                replay("act", eng)

            @block.vector
            def _(eng):
                replay("dve", eng)

            @block.gpsimd
            def _(eng):
                replay("pool", eng)

            @block.sync
            def _(eng):
                replay("sp", eng)

    @contextlib.contextmanager
    def phase(self):
        outer = self.stack
        self.stack = contextlib.ExitStack()
        ph = self.stack
        try:
            yield
            self.stack = outer
            self.emit()
        finally:
            self.stack = outer
            ph.close()


class Ring:
    def __init__(self, slots):
        self.slots = slots
        self.i = 0

    def next(self):
        s = self.slots[self.i % len(self.slots)]
        self.i += 1
        return s


def build_program(NPRE, NMAIN, upto=4):
    nc = bass.Bass("TRN2", target_bir_lowering=False)
    NB = NPRE + NMAIN
    NTOK = NB * BLK
    NMT = NMAIN * BLK
    MAIN0 = NPRE * BLK

    def din(name, shape):
        return nc.dram_tensor(name, list(shape), F32, kind="ExternalInput").ap()

    xT = din("xT", [D, HALO + NTOK])
    xtok = din("xtok", [NMT, D])
    maskd = din("mask", [128, 1])
    consts = din("consts", [128, 448])
    a_w_in = din("a_w_in", [D, 4112])
    a_convT = din("a_convT", [128, 24 * 4])
    a_vec = din("a_vec", [128, 16])
    a_normT = din("a_normT", [128, 1])
    a_w_out = din("a_w_out", [D, D])
    b_w_in = din("b_w_in", [D, 3 * D])
    b_convT = din("b_convT", [128, 8 * 3])
    b_w_out = din("b_w_out", [D, D])
    lng = din("lng", [4, 128, D])
    lnb = din("lnb", [4, 128, D])
    f_w_up = din("f_w_up", [2, D, 2 * DFF])
    f_convT = din("f_convT", [2, 128, FC * 3])
    f_w_down = din("f_w_down", [2, DFF, D])
    out = nc.dram_tensor("out", [NMT - TILE, D], F32, kind="ExternalOutput").ap()

    htok_s = [nc.dram_tensor(f"htok_s{i}", [NMT, D], F32, kind="Internal").ap() for i in range(3)]
    hT_s = [nc.dram_tensor(f"hT_s{i}", [D, HALO + NMT], BF16, kind="Internal").ap() for i in range(3)]

    P = Prog(nc)
    P.setup_sems()
    store_ops = []

    cst = P.sb([128, 448], F32, "cst")
    b_cst = P.buf("cst")
    P.dma("sp", lambda e: e.dma_start(out=cst[:], in_=consts[:, :]), "cst", writes=[b_cst])
    ident = cst[:, 0:128]
    Uincl = cst[0:64, 128:192]
    Lstrict = cst[0:64, 192:256]
    Ustrict = cst[0:64, 256:320]
    ones64 = cst[0:64, 320:448]
    ones1 = cst[:, 320:321]
    identb = P.sb([128, 128], BF16, "identb")
    b_identb = P.buf()
    P.op("dve", lambda e: e.tensor_copy(out=identb[:], in_=cst[:, 0:128]), reads=[b_cst], writes=[b_identb])
    onesb = P.sb([128, 2], BF16, "onesb")
    b_onesb = P.buf()
    P.op("dve", lambda e: e.tensor_copy(out=onesb[:], in_=cst[:, 320:322]), reads=[b_cst], writes=[b_onesb])
    maskt = P.sb([128, 1], F32, "maskt")
    b_mask = P.buf()
    P.dma("sp", lambda e: e.dma_start(out=maskt[:], in_=maskd[:, :]), "maskt", writes=[b_mask])
    zeros = P.sb([128, KC * HALO], BF16, "zeros")
    b_zeros = P.buf()
    P.op("pool", lambda e: e.memset(zeros[:], 0.0), writes=[b_zeros])
    b_hTs = [P.buf(f"hTs{i}") for i in range(3)]
    b_hts = [P.buf(f"htoks{i}") for i in range(3)]
    for i in range(3):
        v = hT_s[i].rearrange("(kc p) t -> p kc t", p=128)
        P.dma("sp", lambda e, v=v: e.dma_start(out=v[:, :, 0:HALO], in_=zeros[:].rearrange("p (k t) -> p k t", k=KC)),
              f"zeros{i}", reads=[b_zeros], pwrites=[b_hTs[i]])

    banks = [P.ps([128, 512], F32, f"bank{i}") for i in range(8)]
    bbank = [P.buf(f"bank{i}") for i in range(8)]
    big = Ring([(banks[i], bbank[i]) for i in (0, 1)])
    big6 = Ring([(banks[i], bbank[i]) for i in (0, 1, 4, 5, 6, 7)])
    wide0 = banks[2]
    wide1 = banks[3]
    b_wide = P.buf("wide")
    small = Ring([((banks[i], 0), bbank[i]) for i in (4, 5, 6, 7)])
    sm64 = md128 = su_ring = sc_ring = small

    gam = P.sb([128, D], F32, "gam")
    bet = P.sb([128, D], F32, "bet")
    b_gam = P.buf()
    b_bet = P.buf()
    hres_r = Ring([(P.sb([128, D], F32, f"hres{i}"), P.buf()) for i in range(1)])
    hTt_r = Ring([(P.sb([128, KC, TILE], BF16, f"hTt{i}"), P.buf()) for i in range(1)])
    lnst = P.sb([128, 16], F32, "lnst")
    b_lnst = P.buf()

    def load_ln(i):
        P.dma("sp", lambda e: e.dma_start(out=gam[:], in_=lng[i]), "gam", writes=[b_gam])
        P.dma("sp", lambda e: e.dma_start(out=bet[:], in_=lnb[i]), "bet", writes=[b_bet])

    def ln_epilogue(ti, src_tok, b_src, dst_i, final):
        r0, r1 = ti * TILE, (ti + 1) * TILE
        hres, b_hres = hres_r.next()
        hn, b_hn = hres, b_hres
        P.dma("sp", lambda e: e.dma_start(out=hres[:], in_=src_tok[r0:r1, :]), "hres0",
              reads=[b_src] if b_src is not None else [], writes=[b_hres])
        for hf, bk in ((0, wide0), (1, wide1)):
            P.op("dve", lambda e, hf=hf, bk=bk: e.scalar_tensor_tensor(
                out=hres[:, hf * 512:(hf + 1) * 512], in0=hres[:, hf * 512:(hf + 1) * 512], scalar=ALPHA,
                in1=bk[:, :], op0=ALU.mult, op1=ALU.add), reads=[b_hres, b_wide], writes=[b_hres])
        for hf in range(2):
            P.op("dve", lambda e, hf=hf: e.bn_stats(out=lnst[:, hf * 6:(hf + 1) * 6], in_=hres[:, hf * 512:(hf + 1) * 512]),
                 reads=[b_hres], pwrites=[b_lnst])
        P.op("dve", lambda e: e.bn_aggr(out=lnst[:, 12:14], in_=lnst[:, 0:12]), reads=[b_lnst], writes=[b_lnst])
        P.op("dve", lambda e: e.tensor_scalar(out=lnst[:, 14:15], in0=lnst[:, 13:14], scalar1=1e-5, scalar2=None, op0=ALU.add),
             reads=[b_lnst], writes=[b_lnst])
        P.op("act", lambda e: e.activation(out=lnst[:, 14:15], in_=lnst[:, 14:15], func=AF.Ln), reads=[b_lnst], writes=[b_lnst])
        P.op("act", lambda e: e.activation(out=lnst[:, 15:16], in_=lnst[:, 14:15], func=AF.Exp, scale=-0.5),
             reads=[b_lnst], writes=[b_lnst])
        P.op("dve", lambda e: e.tensor_scalar(out=hres[:], in0=hres[:], scalar1=lnst[:, 12:13], scalar2=lnst[:, 15:16],
                                              op0=ALU.subtract, op1=ALU.mult), reads=[b_hres, b_lnst], writes=[b_hres])
        P.op("pool", lambda e: e.tensor_tensor(out=hn[:], in0=hn[:], in1=gam[:], op=ALU.mult), reads=[b_hn, b_gam], writes=[b_hn])
        P.op("pool", lambda e: e.tensor_tensor(out=hn[:], in0=hn[:], in1=bet[:], op=ALU.add), reads=[b_hn, b_bet], writes=[b_hn])
        if ti == 0:
            P.op("dve", lambda e: e.tensor_scalar(out=hn[:], in0=hn[:], scalar1=maskt[:, 0:1], scalar2=None, op0=ALU.mult),
                 reads=[b_hn, b_mask], writes=[b_hn])
        if final:
            if ti > 0:
                o = P.dma("sp", lambda e: e.dma_start(out=out[r0 - TILE:r1 - TILE, :], in_=hn[:]), "hres0", reads=[b_hn])
                store_ops.append(o)
            return
        P.dma("sp", lambda e: e.dma_start(out=htok_s[dst_i][r0:r1, :], in_=hn[:]), "hres0",
              reads=[b_hn], pwrites=[b_hts[dst_i]])
        hTt, b_hTt = hTt_r.next()
        for half in range(2):
            bk, b_bk = big.next()
            for q in range(4):
                kc = half * 4 + q
                P.op("pe", lambda e, bk=bk, q=q, kc=kc: e.transpose(out=bk[:, q * 128:(q + 1) * 128], in_=hn[:, kc * 128:(kc + 1) * 128],
                                                                   identity=ident), reads=[b_hn, b_cst], pwrites=[b_bk])
            P.op("act", lambda e, bk=bk, half=half: e.activation(
                out=hTt[:, half * 4:(half + 1) * 4, :], in_=bk[:, :].rearrange("p (k t) -> p k t", k=4), func=AF.Copy),
                reads=[b_bk], pwrites=[b_hTt])
        v = hT_s[dst_i].rearrange("(kc p) t -> p kc t", p=128)
        P.dma("sp", lambda e: e.dma_start(out=v[:, :, HALO + r0:HALO + r1], in_=hTt[:]), "hTt0",
              reads=[b_hTt], pwrites=[b_hTs[dst_i]])

    def load_w_cast(dst, b_dst, src_view, ncols, key):
        nmid = src_view.shape[1]
        for m0 in range(0, nmid, 8):
            m1 = min(nmid, m0 + 8)
            c = 0
            while c < ncols:
                w = min(2048, ncols - c)
                P.dma("pool", lambda e, c=c, w=w, m0=m0, m1=m1: e.dma_start(out=dst[:, m0:m1, c:c + w], in_=src_view[:, m0:m1, c:c + w]),
                      key, pwrites=[b_dst])
                c += w

    def make_diag(dg, b_dg, convT_dram, nchunk, ntap, tmp_name):
        ct = P.sb([128, nchunk * ntap], F32, tmp_name)
        b_ct = P.buf()
        P.dma("sp", lambda e: e.dma_start(out=ct[:], in_=convT_dram), tmp_name, writes=[b_ct])
        for m in range(nchunk):
            for j in range(ntap):
                P.op("dve", lambda e, m=m, j=j: e.tensor_scalar(out=dg[:, j, m, :], in0=ident, scalar1=ct[:, m * ntap + j:m * ntap + j + 1],
                                                                scalar2=None, op0=ALU.mult), reads=[b_cst, b_ct], pwrites=[b_dg])

    def phase1():
        Wqkv = P.sb([128, KC, 3072], BF16, "Wqkv")
        Wz = P.sb([128, KC, 1024], BF16, "Wz")
        Wba = P.sb([128, KC, 16], BF16, "Wba")
        Wout = P.sb([128, KC, D], BF16, "Wout")
        dg = P.sb([128, 4, 24, 128], BF16, "dgA")
        b_Wqkv, b_Wz, b_Wba, b_Wout, b_dg = P.buf(), P.buf(), P.buf(), P.buf(), P.buf()
        win_v = a_w_in.rearrange("(kc p) n -> p kc n", p=128)
        load_w_cast(Wqkv, b_Wqkv, win_v[:, :, 0:3072], 3072, "Wqkv")
        load_w_cast(Wz, b_Wz, win_v[:, :, 3072:4096], 1024, "Wz")
        load_w_cast(Wba, b_Wba, win_v[:, :, 4096:4112], 16, "Wba")
        load_w_cast(Wout, b_Wout, a_w_out.rearrange("(kc p) n -> p kc n", p=128), D, "Wout")
        make_diag(dg, b_dg, a_convT[:, :], 24, 4, "ctA")
        load_ln(0)
        avec = P.sb([128, 16], F32, "avec")
        b_avec = P.buf()
        P.dma("sp", lambda e: e.dma_start(out=avec[:], in_=a_vec[:, :]), "avec", writes=[b_avec])
        negA = P.sb([128, 8], F32, "negA")
        b_negA = P.buf()
        P.op("act", lambda e: e.activation(out=negA[:], in_=avec[:, 0:8], func=AF.Exp), reads=[b_avec], writes=[b_negA])
        P.op("dve", lambda e: e.tensor_scalar(out=negA[:], in0=negA[:], scalar1=-1.0, scalar2=None, op0=ALU.mult),
             reads=[b_negA], writes=[b_negA])
        normT = P.sb([128, 1], F32, "normT")
        b_normT = P.buf()
        P.dma("sp", lambda e: e.dma_start(out=normT[:], in_=a_normT[:, :]), "normT", writes=[b_normT])

        GD = NEU_DT
        S = P.sb([128, 8, 128], F32, "S")
        Sbf = P.sb([128, 8, 128], BF16, "Sbf")
        b_S, b_Sbf = P.buf("S"), P.buf("Sbf")
        P.op("pool", lambda e: e.memset(S[:], 0.0), writes=[b_S])
        P.op("pool", lambda e: e.memset(Sbf[:], 0.0), writes=[b_Sbf])

        xb_r = Ring([(P.sb([128, KC, BLK + HALO], BF16, f"xb{i}"), P.buf()) for i in range(1)])
        pre_r = Ring([(P.sb([128, BLK + HALO], BF16, f"pre{i}"), P.buf()) for i in range(2)])
        qkvT = P.sb([128, 24, BLK], BF16, "qkvT")
        b_qkv = [P.buf(f"qkv{m}") for m in range(24)]
        sq = P.sb([128, 8, BLK], BF16, "sq")
        b_sq = [P.buf(f"sq{m}") for m in range(8)]
        sz = P.sb([64, 4, D], BF16, "sz")
        b_sz = [P.buf(f"sz{c}") for c in range(4)]
        ogT = P.sb([128, 8, TILE], BF16, "ogT")
        b_ogT = P.buf()
        NS = 20
        tsc_r = Ring([(P.sb([128, NS, 8], F32, f"tsc{i}"), P.buf()) for i in range(2)])

        def one(shape, dt, name):
            return P.sb(shape, dt, name), P.buf(name)

        kd_2 = [one([64, 8, 128], BF16, f"kd_all{i}") for i in range(2)]
        vp_2 = [one([64, 8, 128], BF16, f"vp_all{i}") for i in range(2)]
        gdU_all, b_gdU = one([64, 8, 64], F32, "gdU_all")
        E_all, b_E = one([64, 8, 64], F32, "E_all")
        dS_all, b_dS = one([64, 8, 64], F32, "dS_all")
        dI_all, b_dI = E_all, b_E
        QKT_2 = [one([64, 8, 64], BF16, f"QKT_all{i}") for i in range(2)]
        Y_all, b_Y = one([64, 8, 64], GD, "Y_all")
        Z_all, b_Z = one([64, 8, 64], GD, "Z_all")
        P_2 = [one([64, 8, 64], GD, f"P_all{i}") for i in range(2)]
        KK_sb, b_KK = one([64, 8, 64], F32, "KK_sb")
        tR, b_tR = one([64, 8, 128], F32, "tR")
        D1_all, b_D1 = (tR, b_tR) if GD == F32 else one([64, 8, 128], GD, "D1_all")
        vn_all, b_vn = one([64, 8, 128], BF16, "vn_all")
        o_all, b_oall = one([64, 8, 128], F32, "o_all")
        og, b_og = one([64, 8, 128], F32, "og")
        identG = ident if GD == F32 else identb
        b_identG = b_cst if GD == F32 else b_identb

        pairs = Ring([((banks[4], banks[5]), (bbank[4], bbank[5])), ((banks[6], banks[7]), (bbank[6], bbank[7]))])

        def bc_in(ap2, k, n, rows=64):
            return ap2.unsqueeze(2).to_broadcast([rows, k, n])

        def bc_mid(ap2, k, n):
            return ap2.unsqueeze(1).to_broadcast([64, k, n])

        def single():
            (bk, _c0), b = small.next()
            return bk, b

        def mm8(dst_fn, lhs_fn, rhs_fn, reads, bbufs):
            for h in range(8):
                d_, l_, r_ = dst_fn(h), lhs_fn(h), rhs_fn(h)
                P.op("pe", lambda e, d_=d_, l_=l_, r_=r_: e.matmul(d_, lhsT=l_, rhs=r_, start=True, stop=True),
                     reads=reads(h), pwrites=bbufs(h))

        def unpack(ctx):
            return (ctx["bi"], ctx["main"], ctx["xb"], ctx["b_xb"], ctx["c"], ctx["tsc"], ctx["b_tsc"], ctx["par"], ctx["szi"])

        def prep(ctx):
            bi, main, xb, b_xb, c, tsc, b_tsc, par, szi = unpack(ctx)
            kd_all, b_kd = kd_2[par]
            vp_all, b_vp = vp_2[par]
            QKT_all, b_QKT = QKT_2[par]
            P_all, b_P = P_2[par]
            cs = slice(c * CH, (c + 1) * CH)
            if main and c % 2 == 0:
                for cc in (c, c + 1):
                    for kc in range(KC):
                        for hf, bkw in ((0, wide0), (1, wide1)):
                            P.op("pe", lambda e, cc=cc, kc=kc, hf=hf, bkw=bkw, xb=xb: e.matmul(
                                bkw[0:64, :], lhsT=xb[:, kc, HALO + cc * CH:HALO + (cc + 1) * CH],
                                rhs=Wz[:, kc, hf * 512:(hf + 1) * 512], start=(kc == 0), stop=(kc == KC - 1)),
                                reads=[b_xb, b_Wz], pwrites=[b_wide])
                    for hf, bkw in ((0, wide0), (1, wide1)):
                        P.op("act", lambda e, cc=cc, hf=hf, bkw=bkw: e.activation(out=sz[:, szi + cc % 2, hf * 512:(hf + 1) * 512], in_=bkw[0:64, :],
                                                                               func=AF.Silu), reads=[b_wide], pwrites=[b_sz[szi + cc % 2]])
            def T(i, rows=64, tsc=tsc):
                return tsc[0:rows, i, :]

            def tiny(eng, fn):
                P.op(eng, fn, reads=[b_tsc], writes=[b_tsc])
            yield
            kTh = lambda h: qkvT[:, 8 + h, cs]
            vTh = lambda h: qkvT[:, 16 + h, cs]
            qTh = lambda h: qkvT[:, h, cs]
            kk_bk, b_kk = single()
            mm8(lambda h: kk_bk[0:64, h * 64:(h + 1) * 64], kTh, kTh, lambda h: [b_qkv[8 + h]], lambda h: [b_kk])
            bk_sc, b_sc = single()
            ba_ps = bk_sc[0:64, 0:16]
            for kc in range(KC):
                P.op("pe", lambda e, kc=kc, ba_ps=ba_ps, xb=xb, c=c: e.matmul(
                    ba_ps, lhsT=xb[:, kc, HALO + c * CH:HALO + (c + 1) * CH], rhs=Wba[:, kc, :],
                    start=(kc == 0), stop=(kc == KC - 1)), reads=[b_xb, b_Wba], pwrites=[b_sc])
            if main:
                ssq_ps = bk_sc[0:64, 16:24]
                for h in range(8):
                    P.op("pe", lambda e, h=h, ssq_ps=ssq_ps: e.matmul(ssq_ps[:, h:h + 1], lhsT=sq[:, h, cs], rhs=onesb[:, 0:1],
                                                                    start=True, stop=True), reads=[b_sq[h], b_onesb], pwrites=[b_sc])
            P.op("act", lambda e, T=T: e.activation(out=T(0), in_=ba_ps[:, 0:8], func=AF.Exp, scale=-1.0), reads=[b_sc], writes=[b_tsc])
            tiny("dve", lambda e, T=T: e.tensor_scalar(out=T(0), in0=T(0), scalar1=1.0, scalar2=None, op0=ALU.add))
            tiny("dve", lambda e, T=T: e.reciprocal(out=T(0), in_=T(0)))
            P.op("dve", lambda e, T=T: e.tensor_tensor(out=T(1), in0=ba_ps[:, 8:16], in1=avec[0:64, 8:16], op=ALU.add),
                 reads=[b_sc, b_avec, b_tsc], writes=[b_tsc])
            tiny("act", lambda e, T=T: e.activation(out=T(1), in_=T(1), func=AF.Exp))
            tiny("dve", lambda e, T=T: e.tensor_scalar(out=T(1), in0=T(1), scalar1=1.0, scalar2=None, op0=ALU.add))
            tiny("act", lambda e, T=T: e.activation(out=T(1), in_=T(1), func=AF.Ln))
            P.op("dve", lambda e, T=T: e.tensor_tensor(out=T(1), in0=T(1), in1=negA[0:64, :], op=ALU.mult),
                 reads=[b_tsc, b_negA], writes=[b_tsc])
            if main:
                P.op("dve", lambda e, T=T, ssq_ps=ssq_ps: e.tensor_scalar(out=T(14), in0=ssq_ps, scalar1=1e-6, scalar2=None, op0=ALU.add),
                     reads=[b_sc, b_tsc], writes=[b_tsc])
            yield
            P.op("act", lambda e: e.activation(out=KK_sb[:], in_=kk_bk[0:64, :].rearrange("p (h c) -> p h c", h=8), func=AF.Copy),
                 reads=[b_kk], writes=[b_KK])
            P.op("pool", lambda e: e.tensor_tensor(out=dS_all[:], in0=KK_sb[:], in1=bc_mid(cst[0:64, 0:64], 8, 64), op=ALU.mult),
                 reads=[b_KK, b_cst], writes=[b_dS])
            P.op("dve", lambda e, T=T: e.tensor_reduce(out=T(12), in_=dS_all[:], axis=AX.X, op=ALU.add), reads=[b_dS, b_tsc], writes=[b_tsc])
            tiny("dve", lambda e, T=T: e.tensor_scalar(out=T(12), in0=T(12), scalar1=1e-6, scalar2=None, op0=ALU.add))
            tiny("act", lambda e, T=T: e.activation(out=T(12), in_=T(12), func=AF.Ln))
            tiny("act", lambda e, T=T: e.activation(out=T(2), in_=T(12), func=AF.Exp, scale=-0.5))
            tiny("act", lambda e, T=T: e.activation(out=T(3), in_=T(12), func=AF.Exp, scale=0.5))
            tiny("dve", lambda e, T=T: e.tensor_tensor(out=T(9), in0=T(0), in1=T(2), op=ALU.mult))
            tiny("dve", lambda e, T=T: e.scalar_tensor_tensor(out=T(4), in0=T(9), scalar=-1.0, in1=T(2), op0=ALU.mult, op1=ALU.mult))
            yield
            bk_g, b_g = single()
            Gc_ps, Gr_ps, Gl_ps = bk_g[0:64, 0:8], bk_g[0:64, 8:16], bk_g[:, 16:24]
            P.op("pe", lambda e, T=T: e.matmul(Gc_ps, lhsT=Uincl, rhs=T(1), start=True, stop=True), reads=[b_cst, b_tsc], pwrites=[b_g])
            P.op("pe", lambda e, T=T: e.matmul(Gr_ps, lhsT=Lstrict, rhs=T(1), start=True, stop=True), reads=[b_cst, b_tsc], pwrites=[b_g])
            P.op("pe", lambda e, T=T: e.matmul(Gl_ps, lhsT=ones64, rhs=T(1), start=True, stop=True), reads=[b_cst, b_tsc], pwrites=[b_g])
            P.op("act", lambda e, T=T: e.activation(out=T(5), in_=Gc_ps, func=AF.Exp), reads=[b_g], writes=[b_tsc])
            P.op("act", lambda e, T=T: e.activation(out=T(6), in_=Gr_ps, func=AF.Exp), reads=[b_g], writes=[b_tsc])
            P.op("act", lambda e, tsc=tsc: e.activation(out=tsc[:, 7, :], in_=Gl_ps, func=AF.Exp), reads=[b_g], writes=[b_tsc])
            tiny("dve", lambda e, T=T: e.tensor_tensor(out=T(6), in0=T(6), in1=T(2), op=ALU.mult))
            tiny("dve", lambda e, T=T: e.tensor_scalar(out=T(8), in0=T(5), scalar1=-1.0, scalar2=None, op0=ALU.mult))
            if main:
                tiny("act", lambda e, T=T: e.activation(out=T(14), in_=T(14), func=AF.Ln))
                tiny("act", lambda e, T=T: e.activation(out=T(10), in_=T(14), func=AF.Exp, scale=-0.5))
                tiny("dve", lambda e, T=T: e.tensor_scalar(out=T(10), in0=T(10), scalar1=float(128 ** -0.5), scalar2=None, op0=ALU.mult))
            yield
            (pa, pb), (b_pa, b_pb) = pairs.next()
            pk = (pa, pb)
            bpk = (b_pa, b_pb)
            mm8(lambda h: pk[h // 4][0:64, (h % 4) * 128:(h % 4 + 1) * 128], kTh, lambda h: identb[:],
                lambda h: [b_qkv[8 + h], b_identb], lambda h: [bpk[h // 4]])
            for i in range(2):
                P.op("dve", lambda e, i=i, T=T: e.tensor_tensor(
                    out=kd_all[:, 4 * i:4 * i + 4, :], in0=pk[i][0:64, :].rearrange("p (h d) -> p h d", h=4),
                    in1=bc_in(T(6)[:, 4 * i:4 * i + 4], 4, 128), op=ALU.mult), reads=[bpk[i], b_tsc], pwrites=[b_kd])
            yield
            (pa2, pb2), (b_pa2, b_pb2) = pairs.next()
            pv = (pa2, pb2)
            bpv = (b_pa2, b_pb2)
            mm8(lambda h: pv[h // 4][0:64, (h % 4) * 128:(h % 4 + 1) * 128], vTh, lambda h: identb[:],
                lambda h: [b_qkv[16 + h], b_identb], lambda h: [bpv[h // 4]])
            for i in range(2):
                P.op("dve", lambda e, i=i, T=T: e.tensor_tensor(
                    out=vp_all[:, 4 * i:4 * i + 4, :], in0=pv[i][0:64, :].rearrange("p (h d) -> p h d", h=4),
                    in1=bc_in(T(3)[:, 4 * i:4 * i + 4], 4, 128), op=ALU.mult), reads=[bpv[i], b_tsc], pwrites=[b_vp])
            yield
            P.op("pool", lambda e, T=T: e.tensor_tensor(out=gdU_all[:], in0=bc_mid(Uincl, 8, 64), in1=bc_in(T(1), 8, 64), op=ALU.mult),
                 reads=[b_cst, b_tsc], writes=[b_gdU])
            yield
            gd_bk, b_gd = single()
            mm8(lambda h: gd_bk[0:64, h * 64:(h + 1) * 64], lambda h: Lstrict, lambda h: gdU_all[:, h, :],
                lambda h: [b_cst, b_gdU], lambda h: [b_gd])
            P.op("act", lambda e: e.activation(out=E_all[:], in_=gd_bk[0:64, :].rearrange("p (h c) -> p h c", h=8), func=AF.Exp),
                 reads=[b_gd], writes=[b_E])
            yield
            P.op("pool", lambda e: e.tensor_tensor(out=dS_all[:], in0=E_all[:], in1=bc_mid(Ustrict, 8, 64), op=ALU.mult),
                 reads=[b_E, b_cst], writes=[b_dS])
            P.op("pool", lambda e, T=T: e.tensor_tensor(out=dS_all[:], in0=dS_all[:], in1=bc_in(T(4), 8, 64), op=ALU.mult),
                 reads=[b_dS, b_tsc], writes=[b_dS])
            P.op("pool", lambda e: e.tensor_tensor(out=Y_all[:], in0=KK_sb[:], in1=dS_all[:], op=ALU.mult),
                 reads=[b_KK, b_dS], writes=[b_Y])
            if main:
                P.op("pool", lambda e: e.tensor_tensor(out=dI_all[:], in0=E_all[:], in1=bc_mid(Uincl, 8, 64), op=ALU.mult),
                     reads=[b_E, b_cst], writes=[b_dI])
                P.op("pool", lambda e, T=T: e.tensor_tensor(out=dI_all[:], in0=dI_all[:], in1=bc_in(T(2), 8, 64), op=ALU.mult),
                     reads=[b_dI, b_tsc], writes=[b_dI])
                kq_bk, b_kq = single()
                mm8(lambda h: kq_bk[0:64, h * 64:(h + 1) * 64], kTh, qTh, lambda h: [b_qkv[8 + h], b_qkv[h]], lambda h: [b_kq])
                P.op("dve", lambda e: e.tensor_tensor(out=QKT_all[:], in0=kq_bk[0:64, :].rearrange("p (h c) -> p h c", h=8), in1=dI_all[:], op=ALU.mult),
                     reads=[b_kq, b_dI], writes=[b_QKT])
            yield
            z_bk, b_zb = single()
            mm8(lambda h: z_bk[0:64, h * 64:(h + 1) * 64], lambda h: Y_all[:, h, :], lambda h: identG[0:64, 0:64],
                lambda h: [b_Y, b_identG], lambda h: [b_zb])
            P.op("act", lambda e: e.activation(out=Z_all[:], in_=z_bk[0:64, :].rearrange("p (h c) -> p h c", h=8), func=AF.Copy),
                 reads=[b_zb], writes=[b_Z])
            P.op("pool", lambda e: e.tensor_tensor(out=P_all[:], in0=Y_all[:], in1=bc_mid(cst[0:64, 0:64], 8, 64), op=ALU.add),
                 reads=[b_Y, b_cst], writes=[b_P])
            def squares(lvl):
                zn_bk, b_zn = single()
                mm8(lambda h: zn_bk[0:64, h * 64:(h + 1) * 64], lambda h: Y_all[:, h, :], lambda h: Z_all[:, h, :],
                    lambda h: [b_Y, b_Z], lambda h: [b_zn])
                yn = None
                if lvl < 5:
                    yn_bk, b_yn = single()
                    mm8(lambda h: yn_bk[0:64, h * 64:(h + 1) * 64], lambda h: Z_all[:, h, :], lambda h: Y_all[:, h, :],
                        lambda h: [b_Y, b_Z], lambda h: [b_yn])
                    yn = (yn_bk, b_yn)
                return (zn_bk, b_zn), yn

            def evacs(zn, yn):
                zn_bk, b_zn = zn
                P.op("act", lambda e: e.activation(out=Z_all[:], in_=zn_bk[0:64, :].rearrange("p (h c) -> p h c", h=8), func=AF.Copy),
                     reads=[b_zn], writes=[b_Z])
                if yn is not None:
                    yn_bk, b_yn = yn
                    P.op("act", lambda e: e.activation(out=Y_all[:], in_=yn_bk[0:64, :].rearrange("p (h c) -> p h c", h=8), func=AF.Copy),
                         reads=[b_yn], writes=[b_Y])

            def pupdate():
                pu_bk, b_pu = single()
                mm8(lambda h: pu_bk[0:64, h * 64:(h + 1) * 64], lambda h: Z_all[:, h, :], lambda h: P_all[:, h, :],
                    lambda h: [b_Z, b_P], lambda h: [b_pu])
                P.op("dve", lambda e: e.tensor_tensor(out=P_all[:], in0=pu_bk[0:64, :].rearrange("p (h c) -> p h c", h=8), in1=P_all[:], op=ALU.add),
                     reads=[b_pu, b_P], writes=[b_P])

            zn, yn = squares(1)
            yield
            evacs(zn, yn)
            yield
            for lvl in range(2, 6):
                zn, yn = squares(lvl)
                yield
                pupdate()
                yield
                evacs(zn, yn)
                yield
            pupdate()
            yield

        def scan(ctx):
            bi, main, xb, b_xb, c, tsc, b_tsc, par, szi = unpack(ctx)
            kd_all, b_kd = kd_2[par]
            vp_all, b_vp = vp_2[par]
            QKT_all, b_QKT = QKT_2[par]
            P_all, b_P = P_2[par]
            cs = slice(c * CH, (c + 1) * CH)
            kTh = lambda h: qkvT[:, 8 + h, cs]
            qTh = lambda h: qkvT[:, h, cs]

            def T(i, rows=64, tsc=tsc):
                return tsc[0:rows, i, :]

            def tiny(eng, fn):
                P.op(eng, fn, reads=[b_tsc], writes=[b_tsc])
            (ra, rb), (b_ra, b_rb) = pairs.next()
            pr = (ra, rb)
            bpr = (b_ra, b_rb)
            mm8(lambda h: pr[h // 4][0:64, (h % 4) * 128:(h % 4 + 1) * 128], kTh, lambda h: Sbf[:, h, :],
                lambda h: [b_qkv[8 + h], b_Sbf], lambda h: [bpr[h // 4]])
            for i in range(2):
                P.op("dve", lambda e, i=i, T=T: e.tensor_tensor(
                    out=tR[:, 4 * i:4 * i + 4, :], in0=pr[i][0:64, :].rearrange("p (h d) -> p h d", h=4),
                    in1=bc_in(T(8)[:, 4 * i:4 * i + 4], 4, 128), op=ALU.mult), reads=[bpr[i], b_tsc], pwrites=[b_tR])
            yield
            P.op("pool", lambda e: e.tensor_tensor(out=D1_all[:], in0=tR[:], in1=vp_all[:], op=ALU.add), reads=[b_tR, b_vp], writes=[b_D1])
            yield
            (va, vb), (b_va, b_vb) = pairs.next()
            pvn = (va, vb)
            bpvn = (b_va, b_vb)
            mm8(lambda h: pvn[h // 4][0:64, (h % 4) * 128:(h % 4 + 1) * 128], lambda h: P_all[:, h, :], lambda h: D1_all[:, h, :],
                lambda h: [b_P, b_D1], lambda h: [bpvn[h // 4]])
            for i in range(2):
                P.op("dve", lambda e, i=i, T=T: e.tensor_tensor(
                    out=vn_all[:, 4 * i:4 * i + 4, :], in0=pvn[i][0:64, :].rearrange("p (h d) -> p h d", h=4),
                    in1=bc_in(T(9)[:, 4 * i:4 * i + 4], 4, 128), op=ALU.mult), reads=[bpvn[i], b_tsc], pwrites=[b_vn])
            yield
            if main:
                (qa, qb), (b_qa, b_qb) = pairs.next()
                p1 = (qa, qb)
                bp1 = (b_qa, b_qb)
                mm8(lambda h: p1[h // 4][0:64, (h % 4) * 128:(h % 4 + 1) * 128], qTh, lambda h: Sbf[:, h, :],
                    lambda h: [b_qkv[h], b_Sbf], lambda h: [bp1[h // 4]])
                for i in range(2):
                    P.op("dve", lambda e, i=i, T=T: e.tensor_tensor(
                        out=o_all[:, 4 * i:4 * i + 4, :], in0=p1[i][0:64, :].rearrange("p (h d) -> p h d", h=4),
                        in1=bc_in(T(5)[:, 4 * i:4 * i + 4], 4, 128), op=ALU.mult), reads=[bp1[i], b_tsc], pwrites=[b_oall])
                yield
                (wa, wb), (b_wa, b_wb) = pairs.next()
                p2 = (wa, wb)
                bp2 = (b_wa, b_wb)
                mm8(lambda h: p2[h // 4][0:64, (h % 4) * 128:(h % 4 + 1) * 128], lambda h: QKT_all[:, h, :], lambda h: vn_all[:, h, :],
                    lambda h: [b_QKT, b_vn], lambda h: [bp2[h // 4]])
                for i in range(2):
                    P.op("dve", lambda e, i=i: e.tensor_tensor(
                        out=o_all[:, 4 * i:4 * i + 4, :], in0=p2[i][0:64, :].rearrange("p (h d) -> p h d", h=4),
                        in1=o_all[:, 4 * i:4 * i + 4, :], op=ALU.add), reads=[bp2[i], b_oall], writes=[b_oall])
            yield
            (sa, sb_), (b_sa, b_sb) = pairs.next()
            psu = (sa, sb_)
            bpsu = (b_sa, b_sb)
            mm8(lambda h: psu[h // 4][:, (h % 4) * 128:(h % 4 + 1) * 128], lambda h: kd_all[:, h, :], lambda h: vn_all[:, h, :],
                lambda h: [b_kd, b_vn], lambda h: [bpsu[h // 4]])
            yield
            P.op("pool", lambda e, tsc=tsc: e.tensor_tensor(out=S[:], in0=S[:], in1=bc_in(tsc[:, 7, :], 8, 128, rows=128), op=ALU.mult),
                 reads=[b_S, b_tsc], writes=[b_S])
            for i in range(2):
                P.op("dve", lambda e, i=i: e.tensor_tensor(
                    out=S[:, 4 * i:4 * i + 4, :], in0=psu[i][:, :].rearrange("p (h d) -> p h d", h=4),
                    in1=S[:, 4 * i:4 * i + 4, :], op=ALU.add), reads=[bpsu[i], b_S], writes=[b_S])
            P.op("act", lambda e: e.activation(out=Sbf[:], in_=S[:], func=AF.Copy), reads=[b_S], writes=[b_Sbf])
            if not main:
                return
            yield
            P.op("pool", lambda e: e.tensor_tensor(out=og[:], in0=o_all[:], in1=o_all[:], op=ALU.mult), reads=[b_oall], writes=[b_og])
            P.op("dve", lambda e, T=T: e.tensor_reduce(out=T(13), in_=og[:], axis=AX.X, op=ALU.add), reads=[b_og, b_tsc], writes=[b_tsc])
            tiny("dve", lambda e, T=T: e.tensor_tensor(out=T(13), in0=T(13), in1=T(10), op=ALU.mult))
            tiny("dve", lambda e, T=T: e.tensor_tensor(out=T(13), in0=T(13), in1=T(10), op=ALU.mult))
            tiny("dve", lambda e, T=T: e.tensor_scalar(out=T(13), in0=T(13), scalar1=float(1.0 / 128), scalar2=1e-6, op0=ALU.mult, op1=ALU.add))
            tiny("act", lambda e, T=T: e.activation(out=T(13), in_=T(13), func=AF.Ln))
            tiny("act", lambda e, T=T: e.activation(out=T(13), in_=T(13), func=AF.Exp, scale=-0.5))
            tiny("dve", lambda e, T=T: e.tensor_tensor(out=T(11), in0=T(13), in1=T(10), op=ALU.mult))
            P.op("pool", lambda e, T=T: e.tensor_tensor(out=og[:], in0=o_all[:], in1=bc_in(T(11), 8, 128), op=ALU.mult),
                 reads=[b_oall, b_tsc, b_og], writes=[b_og])
            P.op("pool", lambda e, c=c: e.tensor_tensor(out=og[:], in0=og[:], in1=sz[:, szi + c % 2, :].rearrange("p (h d) -> p h d", h=8), op=ALU.mult),
                 reads=[b_og, b_sz[szi + c % 2]], writes=[b_og])
            yield
            bk, b_bk = big.next()
            for h in range(8):
                P.op("pe", lambda e, bk=bk, h=h: e.transpose(out=bk[:, h * 64:(h + 1) * 64], in_=og[:, h, :], identity=cst[0:64, 0:64]),
                     reads=[b_og, b_cst], pwrites=[b_bk])
            half = c % 2
            P.op("act", lambda e, bk=bk, half=half: e.activation(
                out=ogT[:, :, half * 64:(half + 1) * 64], in_=bk[:, :].rearrange("p (h t) -> p h t", h=8), func=AF.Identity, scale=normT[:, 0:1]),
                reads=[b_bk, b_normT], pwrites=[b_ogT])
            yield
            if half == 1:
                ti = (bi - NPRE) * 3 + c // 2
                for h in range(8):
                    for hf, bkw in ((0, wide0), (1, wide1)):
                        P.op("pe", lambda e, h=h, hf=hf, bkw=bkw: e.matmul(
                            bkw[:, :], lhsT=ogT[:, h, :], rhs=Wout[:, h, hf * 512:(hf + 1) * 512], start=(h == 0), stop=(h == 7)),
                            reads=[b_ogT, b_Wout], pwrites=[b_wide])
                ln_epilogue(ti, xtok, None, 0, final=(upto == 1))

        nchunk = [0]

        def run_interleaved(gens):
            gens = list(gens)
            while gens:
                for g in list(gens):
                    try:
                        next(g)
                    except StopIteration:
                        gens.remove(g)

        for bi in range(NB):
            main = bi >= NPRE
            t0 = bi * BLK
            xb, b_xb = xb_r.next()
            xv = xT.rearrange("(kc p) t -> p kc t", p=128)
            P.dma("pool", lambda e, xb=xb, t0=t0: e.dma_start(out=xb[:], in_=xv[:, :, t0:t0 + BLK + HALO]), "xb0",
                  writes=[b_xb])
            chunks = list(range(24)) if main else list(range(8, 24))
            for m in chunks:
                bk, b_bk = big.next()
                for kc in range(KC):
                    P.op("pe", lambda e, bk=bk, m=m, kc=kc, xb=xb: e.matmul(
                        bk[:, 0:BLK + HALO], lhsT=Wqkv[:, kc, m * 128:(m + 1) * 128], rhs=xb[:, kc, :],
                        start=(kc == 0), stop=(kc == KC - 1)), reads=[b_Wqkv, b_xb], pwrites=[b_bk])
                pre, b_pre = pre_r.next()
                P.op("dve", lambda e, pre=pre, bk=bk: e.tensor_copy(out=pre[:], in_=bk[:, 0:BLK + HALO]), reads=[b_bk], writes=[b_pre])
                bk2, b_bk2 = big.next()
                for j in range(4):
                    P.op("pe", lambda e, bk2=bk2, m=m, j=j, pre=pre: e.matmul(
                        bk2[:, 0:BLK], lhsT=dg[:, j, m, :], rhs=pre[:, j:j + BLK], start=(j == 0), stop=(j == 3)),
                        reads=[b_dg, b_pre], pwrites=[b_bk2])
                P.op("act", lambda e, bk2=bk2, m=m: e.activation(out=qkvT[:, m, :], in_=bk2[:, 0:BLK], func=AF.Silu),
                     reads=[b_bk2], writes=[b_qkv[m]])
                if m < 8:
                    P.op("pool", lambda e, m=m: e.tensor_tensor(out=sq[:, m, :], in0=qkvT[:, m, :], in1=qkvT[:, m, :], op=ALU.mult),
                         reads=[b_qkv[m]], writes=[b_sq[m]])
            pending = None
            for c in range(6):
                tsc, b_tsc = tsc_r.next()
                nchunk[0] += 1
                ctx = dict(bi=bi, main=main, xb=xb, b_xb=b_xb, c=c, tsc=tsc, b_tsc=b_tsc, par=nchunk[0] % 2,
                           szi=((c // 2) % 2) * 2)
                gens = [prep(ctx)] + ([pending] if pending is not None else [])
                run_interleaved(gens)
                pending = scan(ctx)
            run_interleaved([pending])

    def load_xb(xb, b_xb, src_i, t0, key):
        v = hT_s[src_i].rearrange("(kc p) t -> p kc t", p=128)
        P.dma("sp", lambda e: e.dma_start(out=xb[:], in_=v[:, :, HALO + t0 - 2:HALO + t0 + BLK]), key,
              reads=[b_hTs[src_i]], writes=[b_xb])

    def phase_ffn(li, src_i, dst_i, final):
        Wup = P.sb([128, KC, 2 * DFF], BF16, "Wup")
        Wd = P.sb([128, FC, D], BF16, "Wd")
        dgF = P.sb([128, 3, FC, 128], BF16, "dgF")
        b_Wup, b_Wd, b_dgF = P.buf(), P.buf(), P.buf()
        load_w_cast(Wup, b_Wup, f_w_up[li].rearrange("(kc p) n -> p kc n", p=128), 2 * DFF, "Wup")
        load_w_cast(Wd, b_Wd, f_w_down[li].rearrange("(m p) n -> p m n", p=128), D, "Wd")
        make_diag(dgF, b_dgF, f_convT[li], FC, 3, "ctF")
        load_ln(1 + 2 * li)
        xb_r = Ring([(P.sb([128, KC, BLK + 2], BF16, f"fxb{i}"), P.buf()) for i in range(1)])
        pre_r = Ring([(P.sb([128, BLK + 2], BF16, f"fpre{i}"), P.buf()) for i in range(2)])
        su_r = Ring([(P.sb([128, BLK], BF16, f"fsu{i}"), P.buf()) for i in range(2)])
        aT = P.sb([128, FC, BLK], BF16, "aT")
        b_aT = P.buf()
        for bi in range(NMAIN):
            t0 = bi * BLK
            xb, b_xb = xb_r.next()
            load_xb(xb, b_xb, src_i, t0, "fxb0")
            for m in range(FC):
                bk, b_bk = big6.next()
                for kc in range(KC):
                    P.op("pe", lambda e, bk=bk, m=m, kc=kc, xb=xb: e.matmul(
                        bk[:, 0:BLK + 2], lhsT=Wup[:, kc, m * 128:(m + 1) * 128], rhs=xb[:, kc, :],
                        start=(kc == 0), stop=(kc == KC - 1)), reads=[b_Wup, b_xb], pwrites=[b_bk])
                pre, b_pre = pre_r.next()
                P.op("dve", lambda e, pre=pre, bk=bk: e.tensor_copy(out=pre[:], in_=bk[:, 0:BLK + 2]), reads=[b_bk], writes=[b_pre])
                bk2, b_bk2 = big6.next()
                for j in range(3):
                    P.op("pe", lambda e, bk2=bk2, m=m, j=j, pre=pre: e.matmul(
                        bk2[:, 0:BLK], lhsT=dgF[:, j, m, :], rhs=pre[:, j:j + BLK], start=(j == 0), stop=(j == 2)),
                        reads=[b_dgF, b_pre], pwrites=[b_bk2])
                su, b_su = su_r.next()
                P.op("act", lambda e, bk2=bk2, su=su: e.activation(out=su[:], in_=bk2[:, 0:BLK], func=AF.Silu), reads=[b_bk2], writes=[b_su])
                bk3, b_bk3 = big6.next()
                for kc in range(KC):
                    P.op("pe", lambda e, bk3=bk3, m=m, kc=kc, xb=xb: e.matmul(
                        bk3[:, 0:BLK], lhsT=Wup[:, kc, DFF + m * 128:DFF + (m + 1) * 128], rhs=xb[:, kc, 2:BLK + 2],
                        start=(kc == 0), stop=(kc == KC - 1)), reads=[b_Wup, b_xb], pwrites=[b_bk3])
                P.op("dve", lambda e, bk3=bk3, su=su, m=m: e.tensor_tensor(out=aT[:, m, :], in0=bk3[:, 0:BLK], in1=su[:], op=ALU.mult),
                     reads=[b_bk3, b_su], pwrites=[b_aT])
            for tl in range(3):
                for m in range(FC):
                    for hf, bkw in ((0, wide0), (1, wide1)):
                        P.op("pe", lambda e, m=m, hf=hf, bkw=bkw, tl=tl: e.matmul(
                            bkw[:, :], lhsT=aT[:, m, tl * TILE:(tl + 1) * TILE], rhs=Wd[:, m, hf * 512:(hf + 1) * 512],
                            start=(m == 0), stop=(m == FC - 1)), reads=[b_aT, b_Wd], pwrites=[b_wide])
                ln_epilogue(bi * 3 + tl, htok_s[src_i], b_hts[src_i], dst_i, final)

    def phase_sconv(src_i, dst_i, final):
        Win = P.sb([128, KC, 3 * D], BF16, "Win")
        Wo = P.sb([128, KC, D], BF16, "Wo")
        dgB = P.sb([128, 3, 8, 128], BF16, "dgB")
        b_Win, b_Wo, b_dgB = P.buf(), P.buf(), P.buf()
        load_w_cast(Win, b_Win, b_w_in.rearrange("(kc p) n -> p kc n", p=128), 3 * D, "Win")
        load_w_cast(Wo, b_Wo, b_w_out.rearrange("(kc p) n -> p kc n", p=128), D, "Wo")
        make_diag(dgB, b_dgB, b_convT[:, :], 8, 3, "ctB")
        load_ln(2)
        xb_r = Ring([(P.sb([128, KC, BLK + 2], BF16, f"sxb{i}"), P.buf()) for i in range(2)])
        c_r = Ring([(P.sb([128, BLK + 2], BF16, f"scs{i}"), P.buf()) for i in range(2)])
        cx_r = Ring([(P.sb([128, BLK + 2], BF16, f"scx{i}"), P.buf()) for i in range(2)])
        bs_r = Ring([(P.sb([128, BLK], BF16, f"sbs{i}"), P.buf()) for i in range(2)])
        vT = P.sb([128, 8, BLK], BF16, "vTs")
        b_vT = P.buf()
        for bi in range(NMAIN):
            t0 = bi * BLK
            xb, b_xb = xb_r.next()
            load_xb(xb, b_xb, src_i, t0, f"sxb{xb_r.i % 2}")
            for m in range(8):
                bk, b_bk = big6.next()
                for kc in range(KC):
                    P.op("pe", lambda e, bk=bk, m=m, kc=kc, xb=xb: e.matmul(
                        bk[:, 0:BLK + 2], lhsT=Win[:, kc, D + m * 128:D + (m + 1) * 128], rhs=xb[:, kc, :],
                        start=(kc == 0), stop=(kc == KC - 1)), reads=[b_Win, b_xb], pwrites=[b_bk])
                cs_, b_cs = c_r.next()
                P.op("act", lambda e, cs_=cs_, bk=bk: e.activation(out=cs_[:], in_=bk[:, 0:BLK + 2], func=AF.Copy), reads=[b_bk], writes=[b_cs])
                bk2, b_bk2 = big6.next()
                for kc in range(KC):
                    P.op("pe", lambda e, bk2=bk2, m=m, kc=kc, xb=xb: e.matmul(
                        bk2[:, 0:BLK + 2], lhsT=Win[:, kc, 2 * D + m * 128:2 * D + (m + 1) * 128], rhs=xb[:, kc, :],
                        start=(kc == 0), stop=(kc == KC - 1)), reads=[b_Win, b_xb], pwrites=[b_bk2])
                cx, b_cx = cx_r.next()
                P.op("dve", lambda e, cx=cx, bk2=bk2, cs_=cs_: e.tensor_tensor(out=cx[:], in0=bk2[:, 0:BLK + 2], in1=cs_[:], op=ALU.mult),
                     reads=[b_bk2, b_cs], writes=[b_cx])
                bk3, b_bk3 = big6.next()
                for j in range(3):
                    P.op("pe", lambda e, bk3=bk3, m=m, j=j, cx=cx: e.matmul(
                        bk3[:, 0:BLK], lhsT=dgB[:, j, m, :], rhs=cx[:, j:j + BLK], start=(j == 0), stop=(j == 2)),
                        reads=[b_dgB, b_cx], pwrites=[b_bk3])
                bk4, b_bk4 = big6.next()
                for kc in range(KC):
                    P.op("pe", lambda e, bk4=bk4, m=m, kc=kc, xb=xb: e.matmul(
                        bk4[:, 0:BLK], lhsT=Win[:, kc, m * 128:(m + 1) * 128], rhs=xb[:, kc, 2:BLK + 2],
                        start=(kc == 0), stop=(kc == KC - 1)), reads=[b_Win, b_xb], pwrites=[b_bk4])
                bs, b_bs = bs_r.next()
                P.op("act", lambda e, bs=bs, bk4=bk4: e.activation(out=bs[:], in_=bk4[:, 0:BLK], func=AF.Copy), reads=[b_bk4], writes=[b_bs])
                P.op("dve", lambda e, bk3=bk3, bs=bs, m=m: e.tensor_tensor(out=vT[:, m, :], in0=bk3[:, 0:BLK], in1=bs[:], op=ALU.mult),
                     reads=[b_bk3, b_bs], pwrites=[b_vT])
            for tl in range(3):
                for m in range(8):
                    for hf, bkw in ((0, wide0), (1, wide1)):
                        P.op("pe", lambda e, m=m, hf=hf, bkw=bkw, tl=tl: e.matmul(
                            bkw[:, :], lhsT=vT[:, m, tl * TILE:(tl + 1) * TILE], rhs=Wo[:, m, hf * 512:(hf + 1) * 512],
                            start=(m == 0), stop=(m == 7)), reads=[b_vT, b_Wo], pwrites=[b_wide])
                ln_epilogue(bi * 3 + tl, htok_s[src_i], b_hts[src_i], dst_i, final)

    with P.phase():
        phase1()
    if upto >= 2:
        with P.phase():
            phase_ffn(0, 0, 1, final=(upto == 2))
    if upto >= 3:
        with P.phase():
            phase_sconv(1, 2, final=(upto == 3))
    if upto >= 4:
        with P.phase():
            phase_ffn(1, 2, 0, final=True)
    P.stack.close()
    return nc


def make_consts():
    c = np.zeros((128, 448), np.float32)
    c[:, 0:128] = np.eye(128, dtype=np.float32)
    r = np.arange(64)[:, None]
    q = np.arange(64)[None, :]
    c[0:64, 128:192] = (r <= q)
    c[0:64, 192:256] = (r > q)
    c[0:64, 256:320] = (r < q)
    c[:, 320:448] = 1.0
    return c


def rep128(v):
    v = np.asarray(v, np.float32).reshape(1, -1)
    return np.ascontiguousarray(np.repeat(v, 128, axis=0))


def weight_inputs(a_w_in, a_conv, a_log, a_dt_bias, a_norm, a_w_out, b_w_in, b_conv, b_w_out,
                  ln_mix_g, ln_mix_b, ffn_w_up, ffn_conv, ffn_w_down, ln_ffn_g, ln_ffn_b):
    f = np.float32
    d = {}
    d["consts"] = make_consts()
    d["a_w_in"] = np.ascontiguousarray(a_w_in[0], f)
    d["a_convT"] = np.ascontiguousarray(a_conv[0].T.reshape(24, 128, 4).transpose(1, 0, 2).reshape(128, 96), f)
    d["a_vec"] = np.concatenate([rep128(a_log[0]), rep128(a_dt_bias[0])], axis=1)
    d["a_normT"] = np.ascontiguousarray(a_norm[0].reshape(128, 1), f)
    d["a_w_out"] = np.ascontiguousarray(a_w_out[0], f)
    d["b_w_in"] = np.ascontiguousarray(b_w_in[0], f)
    d["b_convT"] = np.ascontiguousarray(b_conv[0].T.reshape(8, 128, 3).transpose(1, 0, 2).reshape(128, 24), f)
    d["b_w_out"] = np.ascontiguousarray(b_w_out[0], f)
    d["lng"] = np.stack([rep128(ln_mix_g[0]), rep128(ln_ffn_g[0]), rep128(ln_mix_g[1]), rep128(ln_ffn_g[1])])
    d["lnb"] = np.stack([rep128(ln_mix_b[0]), rep128(ln_ffn_b[0]), rep128(ln_mix_b[1]), rep128(ln_ffn_b[1])])
    d["f_w_up"] = np.ascontiguousarray(ffn_w_up, f)
    d["f_convT"] = np.stack([np.ascontiguousarray(ffn_conv[i].T.reshape(FC, 128, 3).transpose(1, 0, 2).reshape(128, FC * 3)) for i in range(2)]).astype(f)
    d["f_w_down"] = np.ascontiguousarray(ffn_w_down, f)
    return d


def core_stream_inputs(stream, valid, NPRE, NMAIN):
    ntok = (NPRE + NMAIN) * BLK
    assert stream.shape[0] == ntok
    xT = np.zeros((D, HALO + ntok), np.float32)
    xT[:, HALO:] = stream.T
    m0 = NPRE * BLK
    return {"xT": xT, "xtok": np.ascontiguousarray(stream[m0:]), "mask": valid[m0:m0 + 128].astype(np.float32).reshape(128, 1)}


_CACHE = {}


def kernel(x, meta, a_w_in, a_conv, a_log, a_dt_bias, a_norm, a_w_out, b_w_in, b_conv, b_w_out,
           ln_mix_g, ln_mix_b, ffn_w_up, ffn_conv, ffn_w_down, ln_ffn_g, ln_ffn_b):
    x = np.asarray(x, np.float32)
    meta = np.asarray(meta, np.float32)
    B, SEQ, _ = x.shape
    NPRE, NMAIN = NPRE_FULL, NMAIN_FULL
    ntok = (NPRE + NMAIN) * BLK
    half_tok = NMAIN * BLK - TILE
    w = weight_inputs(*[np.asarray(a, np.float32) for a in (a_w_in, a_conv, a_log, a_dt_bias, a_norm, a_w_out, b_w_in, b_conv, b_w_out,
                                                            ln_mix_g, ln_mix_b, ffn_w_up, ffn_conv, ffn_w_down, ln_ffn_g, ln_ffn_b)])
    in_maps = []
    for core in range(8):
        b, half = core // 2, core % 2
        stream = np.zeros((ntok, D), np.float32)
        valid = np.zeros((ntok,), bool)
        n_x = half_tok * (half + 1)
        seq = np.concatenate([meta, x[b, :n_x]], axis=0)
        stream[ntok - seq.shape[0]:] = seq
        valid[ntok - seq.shape[0]:] = True
        m = dict(w)
        m.update(core_stream_inputs(stream, valid, NPRE, NMAIN))
        in_maps.append(m)
    if "nc" not in _CACHE:
        _CACHE["nc"] = build_program(NPRE, NMAIN)
    res = run_bass_kernel_spmd(_CACHE["nc"], in_maps, core_ids=list(range(8)))
    outp = np.zeros((B, SEQ, D), np.float32)
    for core in range(8):
        b, half = core // 2, core % 2
        outp[b, half * half_tok:(half + 1) * half_tok] = res.results[core]["out"]
    return outp
```

```python
import contextlib
import numpy as np
import concourse.bass as bass
import concourse.mybir as mybir
from concourse.bass_utils import run_bass_kernel_spmd

F32 = mybir.dt.float32
BF16 = mybir.dt.bfloat16
AF = mybir.ActivationFunctionType
ALU = mybir.AluOpType
AX = mybir.AxisListType

NSEM_PER_ENG = 6
SAME_ENG_SYNC = True
NEU_DT = BF16

D = 1024
KC = 8
TILE = 128
BLK = 384
CH = 64
HALO = 3
DFF = 2816
FC = DFF // 128
ALPHA = float((2.0 * 2) ** 0.25)
NPRE_FULL = 11
NMAIN_FULL = 11


class Buf:
    __slots__ = ("name", "writers", "readers", "open", "pre")

    def __init__(self, name):
        self.name = name
        self.writers = []
        self.readers = []
        self.open = False
        self.pre = []


class Op:
    __slots__ = ("eng", "emit", "deps", "idx", "dma_key", "dma_cnt", "waits")

    def __init__(self, eng, emit):
        self.eng = eng
        self.emit = emit
        self.deps = []
        self.idx = -1
        self.dma_key = None
        self.dma_cnt = 0
        self.waits = []


class Prog:
    ENGS = ("pe", "act", "dve", "pool", "sp")

    def __init__(self, nc):
        self.nc = nc
        self.ops = {e: [] for e in self.ENGS}
        self.ncomp = {e: 0 for e in self.ENGS}
        self.dma_counts = {}
        self.stack = contextlib.ExitStack()
        self.nbuf = 0
        self.nname = 0

    def sb(self, shape, dt, name=None):
        self.nname += 1
        return self.stack.enter_context(self.nc.sbuf_tensor(f"{name or 'sb'}_{self.nname}", list(shape), dt))

    def ps(self, shape, dt=F32, name=None):
        self.nname += 1
        return self.stack.enter_context(self.nc.psum_tensor(name or f"ps{self.nname}", list(shape), dt))

    def buf(self, name=None):
        self.nbuf += 1
        return Buf(name or f"b{self.nbuf}")

    def _track(self, op, reads, writes, pwrites):
        for b in reads:
            op.deps.extend(b.writers)
            b.readers.append(op)
            b.open = False
        for b in writes:
            op.deps.extend(b.readers)
            op.deps.extend(b.writers)
            b.pre = list(b.readers) + list(b.writers)
            b.writers = [op]
            b.readers = []
            b.open = True
        for b in pwrites:
            if b.open:
                op.deps.extend(b.pre)
                b.writers.append(op)
            else:
                op.deps.extend(b.readers)
                op.deps.extend(b.writers)
                b.pre = list(b.readers) + list(b.writers)
                b.writers = [op]
                b.readers = []
                b.open = True

    def op(self, eng, emit, reads=(), writes=(), pwrites=()):
        o = Op(eng, emit)
        self._track(o, reads, writes, pwrites)
        o.idx = self.ncomp[eng]
        self.ncomp[eng] += 1
        self.ops[eng].append(o)
        return o

    def dma(self, eng, emit, key, reads=(), writes=(), pwrites=()):
        o = Op(eng, emit)
        self._track(o, reads, writes, pwrites)
        o.idx = -1
        o.dma_key = key
        self.dma_counts[key] = self.dma_counts.get(key, 0) + 1
        o.dma_cnt = self.dma_counts[key]
        self.ops[eng].append(o)
        return o

    def setup_sems(self):
        nc = self.nc
        st = self.stack
        self.comp_sems = {}
        for e in ("pe", "act", "dve", "pool"):
            self.comp_sems[e] = [st.enter_context(nc.semaphore(f"s_{e}_{i}")) for i in range(NSEM_PER_ENG)]
        self.dma_sems = {}
        self.emitted = {e: 0 for e in self.ENGS}
        self.waited_idx = {e: {x: -1 for x in self.ENGS} for e in self.ENGS}
        self.waited_dma = {e: {} for e in self.ENGS}

    def emit(self):
        nc = self.nc
        comp_sems, dma_sems = self.comp_sems, self.dma_sems
        for k in self.dma_counts:
            if k not in dma_sems:
                dma_sems[k] = self.stack.enter_context(nc.semaphore(f"d_{len(dma_sems)}"))
        new_ops = {e: self.ops[e][self.emitted[e]:] for e in self.ENGS}
        for e in self.ENGS:
            waited_idx = self.waited_idx[e]
            waited_dma = self.waited_dma[e]
            for o in new_ops[e]:
                need_idx = {}
                need_dma = {}
                for d in o.deps:
                    if d is o:
                        continue
                    if d.dma_key is not None:
                        if d.dma_cnt > need_dma.get(d.dma_key, 0):
                            need_dma[d.dma_key] = d.dma_cnt
                    else:
                        if d.eng == e and (e == "pe" or not SAME_ENG_SYNC):
                            continue
                        if d.idx > need_idx.get(d.eng, -1):
                            need_idx[d.eng] = d.idx
                for src, k in need_idx.items():
                    if k > waited_idx[src]:
                        waited_idx[src] = k
                        o.waits.append((comp_sems[src][k % NSEM_PER_ENG], k // NSEM_PER_ENG + 1))
                for key, c in need_dma.items():
                    if c > waited_dma.get(key, 0):
                        waited_dma[key] = c
                        o.waits.append((dma_sems[key], 16 * c))
            self.emitted[e] = len(self.ops[e])
        final = [(dma_sems[k], 16 * c) for k, c in self.dma_counts.items()]

        def replay(ename, eng):
            for o in new_ops[ename]:
                for s, v in o.waits:
                    eng.wait_ge(s, v)
                ins = o.emit(eng)
                if o.dma_key is not None:
                    ins.then_inc(dma_sems[o.dma_key], 16)
                else:
                    ins.then_inc(comp_sems[ename][o.idx % NSEM_PER_ENG], 1)
            if ename == "sp":
                for s, v in final:
                    eng.wait_ge(s, v)

        with nc.Block() as block:
            @block.tensor
            def _(eng):
                replay("pe", eng)

            @block.scalar
            def _(eng):
                replay("act", eng)

            @block.vector
            def _(eng):
                replay("dve", eng)

            @block.gpsimd
            def _(eng):
                replay("pool", eng)

            @block.sync
            def _(eng):
                replay("sp", eng)

    @contextlib.contextmanager
    def phase(self):
        outer = self.stack
        self.stack = contextlib.ExitStack()
        ph = self.stack
        try:
            yield
            self.stack = outer
            self.emit()
        finally:
            self.stack = outer
            ph.close()


class Ring:
    def __init__(self, slots):
        self.slots = slots
        self.i = 0

    def next(self):
        s = self.slots[self.i % len(self.slots)]
        self.i += 1
        return s


def build_program(NPRE, NMAIN, upto=4):
    nc = bass.Bass("TRN2", target_bir_lowering=False)
    NB = NPRE + NMAIN
    NTOK = NB * BLK
    NMT = NMAIN * BLK
    MAIN0 = NPRE * BLK

    def din(name, shape):
        return nc.dram_tensor(name, list(shape), F32, kind="ExternalInput").ap()

    xT = din("xT", [D, HALO + NTOK])
    xtok = din("xtok", [NMT, D])
    maskd = din("mask", [128, 1])
    consts = din("consts", [128, 448])
    a_w_in = din("a_w_in", [D, 4112])
    a_convT = din("a_convT", [128, 24 * 4])
    a_vec = din("a_vec", [128, 16])
    a_normT = din("a_normT", [128, 1])
    a_w_out = din("a_w_out", [D, D])
    b_w_in = din("b_w_in", [D, 3 * D])
    b_convT = din("b_convT", [128, 8 * 3])
    b_w_out = din("b_w_out", [D, D])
    lng = din("lng", [4, 128, D])
    lnb = din("lnb", [4, 128, D])
    f_w_up = din("f_w_up", [2, D, 2 * DFF])
    f_convT = din("f_convT", [2, 128, FC * 3])
    f_w_down = din("f_w_down", [2, DFF, D])
    out = nc.dram_tensor("out", [NMT - TILE, D], F32, kind="ExternalOutput").ap()

    htok_s = [nc.dram_tensor(f"htok_s{i}", [NMT, D], F32, kind="Internal").ap() for i in range(3)]
    hT_s = [nc.dram_tensor(f"hT_s{i}", [D, HALO + NMT], BF16, kind="Internal").ap() for i in range(3)]

    P = Prog(nc)
    P.setup_sems()
    store_ops = []

    cst = P.sb([128, 448], F32, "cst")
    b_cst = P.buf("cst")
    P.dma("sp", lambda e: e.dma_start(out=cst[:], in_=consts[:, :]), "cst", writes=[b_cst])
    ident = cst[:, 0:128]
    Uincl = cst[0:64, 128:192]
    Lstrict = cst[0:64, 192:256]
    Ustrict = cst[0:64, 256:320]
    ones64 = cst[0:64, 320:448]
    ones1 = cst[:, 320:321]
    identb = P.sb([128, 128], BF16, "identb")
    b_identb = P.buf()
    P.op("dve", lambda e: e.tensor_copy(out=identb[:], in_=cst[:, 0:128]), reads=[b_cst], writes=[b_identb])
    onesb = P.sb([128, 2], BF16, "onesb")
    b_onesb = P.buf()
    P.op("dve", lambda e: e.tensor_copy(out=onesb[:], in_=cst[:, 320:322]), reads=[b_cst], writes=[b_onesb])
    maskt = P.sb([128, 1], F32, "maskt")
    b_mask = P.buf()
    P.dma("sp", lambda e: e.dma_start(out=maskt[:], in_=maskd[:, :]), "maskt", writes=[b_mask])
    zeros = P.sb([128, KC * HALO], BF16, "zeros")
    b_zeros = P.buf()
    P.op("pool", lambda e: e.memset(zeros[:], 0.0), writes=[b_zeros])
    b_hTs = [P.buf(f"hTs{i}") for i in range(3)]
    b_hts = [P.buf(f"htoks{i}") for i in range(3)]
    for i in range(3):
        v = hT_s[i].rearrange("(kc p) t -> p kc t", p=128)
        P.dma("sp", lambda e, v=v: e.dma_start(out=v[:, :, 0:HALO], in_=zeros[:].rearrange("p (k t) -> p k t", k=KC)),
              f"zeros{i}", reads=[b_zeros], pwrites=[b_hTs[i]])

    banks = [P.ps([128, 512], F32, f"bank{i}") for i in range(8)]
    bbank = [P.buf(f"bank{i}") for i in range(8)]
    big = Ring([(banks[i], bbank[i]) for i in (0, 1)])
    big6 = Ring([(banks[i], bbank[i]) for i in (0, 1, 4, 5, 6, 7)])
    wide0 = banks[2]
    wide1 = banks[3]
    b_wide = P.buf("wide")
    small = Ring([((banks[i], 0), bbank[i]) for i in (4, 5, 6, 7)])
    sm64 = md128 = su_ring = sc_ring = small

    gam = P.sb([128, D], F32, "gam")
    bet = P.sb([128, D], F32, "bet")
    b_gam = P.buf()
    b_bet = P.buf()
    hres_r = Ring([(P.sb([128, D], F32, f"hres{i}"), P.buf()) for i in range(1)])
    hTt_r = Ring([(P.sb([128, KC, TILE], BF16, f"hTt{i}"), P.buf()) for i in range(1)])
    lnst = P.sb([128, 16], F32, "lnst")
    b_lnst = P.buf()

    def load_ln(i):
        P.dma("sp", lambda e: e.dma_start(out=gam[:], in_=lng[i]), "gam", writes=[b_gam])
        P.dma("sp", lambda e: e.dma_start(out=bet[:], in_=lnb[i]), "bet", writes=[b_bet])

    def ln_epilogue(ti, src_tok, b_src, dst_i, final):
        r0, r1 = ti * TILE, (ti + 1) * TILE
        hres, b_hres = hres_r.next()
        hn, b_hn = hres, b_hres
        P.dma("sp", lambda e: e.dma_start(out=hres[:], in_=src_tok[r0:r1, :]), "hres0",
              reads=[b_src] if b_src is not None else [], writes=[b_hres])
        for hf, bk in ((0, wide0), (1, wide1)):
            P.op("dve", lambda e, hf=hf, bk=bk: e.scalar_tensor_tensor(
                out=hres[:, hf * 512:(hf + 1) * 512], in0=hres[:, hf * 512:(hf + 1) * 512], scalar=ALPHA,
                in1=bk[:, :], op0=ALU.mult, op1=ALU.add), reads=[b_hres, b_wide], writes=[b_hres])
        for hf in range(2):
            P.op("dve", lambda e, hf=hf: e.bn_stats(out=lnst[:, hf * 6:(hf + 1) * 6], in_=hres[:, hf * 512:(hf + 1) * 512]),
                 reads=[b_hres], pwrites=[b_lnst])
        P.op("dve", lambda e: e.bn_aggr(out=lnst[:, 12:14], in_=lnst[:, 0:12]), reads=[b_lnst], writes=[b_lnst])
        P.op("dve", lambda e: e.tensor_scalar(out=lnst[:, 14:15], in0=lnst[:, 13:14], scalar1=1e-5, scalar2=None, op0=ALU.add),
             reads=[b_lnst], writes=[b_lnst])
        P.op("act", lambda e: e.activation(out=lnst[:, 14:15], in_=lnst[:, 14:15], func=AF.Ln), reads=[b_lnst], writes=[b_lnst])
        P.op("act", lambda e: e.activation(out=lnst[:, 15:16], in_=lnst[:, 14:15], func=AF.Exp, scale=-0.5),
             reads=[b_lnst], writes=[b_lnst])
        P.op("dve", lambda e: e.tensor_scalar(out=hres[:], in0=hres[:], scalar1=lnst[:, 12:13], scalar2=lnst[:, 15:16],
                                              op0=ALU.subtract, op1=ALU.mult), reads=[b_hres, b_lnst], writes=[b_hres])
        P.op("pool", lambda e: e.tensor_tensor(out=hn[:], in0=hn[:], in1=gam[:], op=ALU.mult), reads=[b_hn, b_gam], writes=[b_hn])
        P.op("pool", lambda e: e.tensor_tensor(out=hn[:], in0=hn[:], in1=bet[:], op=ALU.add), reads=[b_hn, b_bet], writes=[b_hn])
        if ti == 0:
            P.op("dve", lambda e: e.tensor_scalar(out=hn[:], in0=hn[:], scalar1=maskt[:, 0:1], scalar2=None, op0=ALU.mult),
                 reads=[b_hn, b_mask], writes=[b_hn])
        if final:
            if ti > 0:
                o = P.dma("sp", lambda e: e.dma_start(out=out[r0 - TILE:r1 - TILE, :], in_=hn[:]), "hres0", reads=[b_hn])
                store_ops.append(o)
            return
        P.dma("sp", lambda e: e.dma_start(out=htok_s[dst_i][r0:r1, :], in_=hn[:]), "hres0",
              reads=[b_hn], pwrites=[b_hts[dst_i]])
        hTt, b_hTt = hTt_r.next()
        for half in range(2):
            bk, b_bk = big.next()
            for q in range(4):
                kc = half * 4 + q
                P.op("pe", lambda e, bk=bk, q=q, kc=kc: e.transpose(out=bk[:, q * 128:(q + 1) * 128], in_=hn[:, kc * 128:(kc + 1) * 128],
                                                                   identity=ident), reads=[b_hn, b_cst], pwrites=[b_bk])
            P.op("act", lambda e, bk=bk, half=half: e.activation(
                out=hTt[:, half * 4:(half + 1) * 4, :], in_=bk[:, :].rearrange("p (k t) -> p k t", k=4), func=AF.Copy),
                reads=[b_bk], pwrites=[b_hTt])
        v = hT_s[dst_i].rearrange("(kc p) t -> p kc t", p=128)
        P.dma("sp", lambda e: e.dma_start(out=v[:, :, HALO + r0:HALO + r1], in_=hTt[:]), "hTt0",
              reads=[b_hTt], pwrites=[b_hTs[dst_i]])

    def load_w_cast(dst, b_dst, src_view, ncols, key):
        nmid = src_view.shape[1]
        table = {}
        for m0 in range(0, nmid, 8):
            m1 = min(nmid, m0 + 8)
            c = 0
            while c < ncols:
                w = min(2048, ncols - c)
                pb = P.buf(f"{key}_{m0}_{c}")
                table[(m0 // 8, c // 2048)] = pb
                P.dma("pool", lambda e, c=c, w=w, m0=m0, m1=m1: e.dma_start(out=dst[:, m0:m1, c:c + w], in_=src_view[:, m0:m1, c:c + w]),
                      f"{key}_{m0}_{c}", writes=[pb])
                c += w
        return lambda mid, col: table[(mid // 8, col // 2048)]

    def make_diag(dg, b_dg, convT_dram, nchunk, ntap, tmp_name):
        ct = P.sb([128, nchunk * ntap], F32, tmp_name)
        b_ct = P.buf()
        P.dma("sp", lambda e: e.dma_start(out=ct[:], in_=convT_dram), tmp_name, writes=[b_ct])
        for m in range(nchunk):
            for j in range(ntap):
                P.op("dve", lambda e, m=m, j=j: e.tensor_scalar(out=dg[:, j, m, :], in0=ident, scalar1=ct[:, m * ntap + j:m * ntap + j + 1],
                                                                scalar2=None, op0=ALU.mult), reads=[b_cst, b_ct], pwrites=[b_dg])

    def phase1():
        Wqkv = P.sb([128, KC, 3072], BF16, "Wqkv")
        Wz = P.sb([128, KC, 1024], BF16, "Wz")
        Wba = P.sb([128, KC, 16], BF16, "Wba")
        Wout = P.sb([128, KC, D], BF16, "Wout")
        dg = P.sb([128, 4, 24, 128], BF16, "dgA")
        b_Wqkv, b_Wz, b_Wba, b_Wout, b_dg = P.buf(), P.buf(), P.buf(), P.buf(), P.buf()
        win_v = a_w_in.rearrange("(kc p) n -> p kc n", p=128)
        wqkv_b = load_w_cast(Wqkv, b_Wqkv, win_v[:, :, 0:3072], 3072, "Wqkv")
        b_Wz = load_w_cast(Wz, b_Wz, win_v[:, :, 3072:4096], 1024, "Wz")(0, 0)
        b_Wba = load_w_cast(Wba, b_Wba, win_v[:, :, 4096:4112], 16, "Wba")(0, 0)
        b_Wout = load_w_cast(Wout, b_Wout, a_w_out.rearrange("(kc p) n -> p kc n", p=128), D, "Wout")(0, 0)
        make_diag(dg, b_dg, a_convT[:, :], 24, 4, "ctA")
        load_ln(0)
        avec = P.sb([128, 16], F32, "avec")
        b_avec = P.buf()
        P.dma("sp", lambda e: e.dma_start(out=avec[:], in_=a_vec[:, :]), "avec", writes=[b_avec])
        negA = P.sb([128, 8], F32, "negA")
        b_negA = P.buf()
        P.op("act", lambda e: e.activation(out=negA[:], in_=avec[:, 0:8], func=AF.Exp), reads=[b_avec], writes=[b_negA])
        P.op("dve", lambda e: e.tensor_scalar(out=negA[:], in0=negA[:], scalar1=-1.0, scalar2=None, op0=ALU.mult),
             reads=[b_negA], writes=[b_negA])
        normT = P.sb([128, 1], F32, "normT")
        b_normT = P.buf()
        P.dma("sp", lambda e: e.dma_start(out=normT[:], in_=a_normT[:, :]), "normT", writes=[b_normT])

        GD = NEU_DT
        S = P.sb([128, 8, 128], F32, "S")
        Sbf = P.sb([128, 8, 128], BF16, "Sbf")
        b_S, b_Sbf = P.buf("S"), P.buf("Sbf")
        P.op("pool", lambda e: e.memset(S[:], 0.0), writes=[b_S])
        P.op("pool", lambda e: e.memset(Sbf[:], 0.0), writes=[b_Sbf])

        xb_r = Ring([(P.sb([128, KC, BLK + HALO], BF16, f"xb{i}"), P.buf()) for i in range(1)])
        pre_r = Ring([(P.sb([128, BLK + HALO], BF16, f"pre{i}"), P.buf()) for i in range(2)])
        qkvT = P.sb([128, 24, BLK], BF16, "qkvT")
        b_qkv = [P.buf(f"qkv{m}") for m in range(24)]
        sq = P.sb([128, 8, BLK], BF16, "sq")
        b_sq = [P.buf(f"sq{m}") for m in range(8)]
        sz = P.sb([64, 4, D], BF16, "sz")
        b_sz = [P.buf(f"sz{c}") for c in range(4)]
        ogT = P.sb([128, 8, TILE], BF16, "ogT")
        b_ogT = P.buf()
        NS = 20
        tsc_r = Ring([(P.sb([128, NS, 8], F32, f"tsc{i}"), P.buf()) for i in range(2)])

        def one(shape, dt, name):
            return P.sb(shape, dt, name), P.buf(name)

        kd_2 = [one([64, 8, 128], BF16, f"kd_all{i}") for i in range(2)]
        vp_2 = [one([64, 8, 128], BF16, f"vp_all{i}") for i in range(2)]
        gdU_all, b_gdU = one([64, 8, 64], F32, "gdU_all")
        E_all, b_E = one([64, 8, 64], F32, "E_all")
        dS_all, b_dS = one([64, 8, 64], F32, "dS_all")
        dI_all, b_dI = E_all, b_E
        QKT_2 = [one([64, 8, 64], BF16, f"QKT_all{i}") for i in range(2)]
        Y_all, b_Y = one([64, 8, 64], GD, "Y_all")
        Z_all, b_Z = one([64, 8, 64], GD, "Z_all")
        P_2 = [one([64, 8, 64], GD, f"P_all{i}") for i in range(2)]
        KK_sb, b_KK = one([64, 8, 64], F32, "KK_sb")
        tR, b_tR = one([64, 8, 128], F32, "tR")
        D1_all, b_D1 = (tR, b_tR) if GD == F32 else one([64, 8, 128], GD, "D1_all")
        vn_all, b_vn = one([64, 8, 128], BF16, "vn_all")
        o_all, b_oall = one([64, 8, 128], F32, "o_all")
        og, b_og = one([64, 8, 128], F32, "og")
        identG = ident if GD == F32 else identb
        b_identG = b_cst if GD == F32 else b_identb

        pairs = Ring([((banks[4], banks[5]), (bbank[4], bbank[5])), ((banks[6], banks[7]), (bbank[6], bbank[7]))])

        def bc_in(ap2, k, n, rows=64):
            return ap2.unsqueeze(2).to_broadcast([rows, k, n])

        def bc_mid(ap2, k, n):
            return ap2.unsqueeze(1).to_broadcast([64, k, n])

        def single():
            (bk, _c0), b = small.next()
            return bk, b

        def mm8(dst_fn, lhs_fn, rhs_fn, reads, bbufs):
            for h in range(8):
                d_, l_, r_ = dst_fn(h), lhs_fn(h), rhs_fn(h)
                P.op("pe", lambda e, d_=d_, l_=l_, r_=r_: e.matmul(d_, lhsT=l_, rhs=r_, start=True, stop=True),
                     reads=reads(h), pwrites=bbufs(h))

        def unpack(ctx):
            return (ctx["bi"], ctx["main"], ctx["xb"], ctx["b_xb"], ctx["c"], ctx["tsc"], ctx["b_tsc"], ctx["par"], ctx["szi"])

        def prep(ctx):
            bi, main, xb, b_xb, c, tsc, b_tsc, par, szi = unpack(ctx)
            kd_all, b_kd = kd_2[par]
            vp_all, b_vp = vp_2[par]
            QKT_all, b_QKT = QKT_2[par]
            P_all, b_P = P_2[par]
            cs = slice(c * CH, (c + 1) * CH)
            if main and c % 2 == 0:
                for cc in (c, c + 1):
                    for kc in range(KC):
                        for hf, bkw in ((0, wide0), (1, wide1)):
                            P.op("pe", lambda e, cc=cc, kc=kc, hf=hf, bkw=bkw, xb=xb: e.matmul(
                                bkw[0:64, :], lhsT=xb[:, kc, HALO + cc * CH:HALO + (cc + 1) * CH],
                                rhs=Wz[:, kc, hf * 512:(hf + 1) * 512], start=(kc == 0), stop=(kc == KC - 1)),
                                reads=[b_xb, b_Wz], pwrites=[b_wide])
                    for hf, bkw in ((0, wide0), (1, wide1)):
                        P.op("act", lambda e, cc=cc, hf=hf, bkw=bkw: e.activation(out=sz[:, szi + cc % 2, hf * 512:(hf + 1) * 512], in_=bkw[0:64, :],
                                                                               func=AF.Silu), reads=[b_wide], pwrites=[b_sz[szi + cc % 2]])
            def T(i, rows=64, tsc=tsc):
                return tsc[0:rows, i, :]

            def tiny(eng, fn):
                P.op(eng, fn, reads=[b_tsc], writes=[b_tsc])
            yield
            kTh = lambda h: qkvT[:, 8 + h, cs]
            vTh = lambda h: qkvT[:, 16 + h, cs]
            qTh = lambda h: qkvT[:, h, cs]
            kk_bk, b_kk = single()
            mm8(lambda h: kk_bk[0:64, h * 64:(h + 1) * 64], kTh, kTh, lambda h: [b_qkv[8 + h]], lambda h: [b_kk])
            bk_sc, b_sc = single()
            ba_ps = bk_sc[0:64, 0:16]
            for kc in range(KC):
                P.op("pe", lambda e, kc=kc, ba_ps=ba_ps, xb=xb, c=c: e.matmul(
                    ba_ps, lhsT=xb[:, kc, HALO + c * CH:HALO + (c + 1) * CH], rhs=Wba[:, kc, :],
                    start=(kc == 0), stop=(kc == KC - 1)), reads=[b_xb, b_Wba], pwrites=[b_sc])
            if main:
                ssq_ps = bk_sc[0:64, 16:24]
                for h in range(8):
                    P.op("pe", lambda e, h=h, ssq_ps=ssq_ps: e.matmul(ssq_ps[:, h:h + 1], lhsT=sq[:, h, cs], rhs=onesb[:, 0:1],
                                                                    start=True, stop=True), reads=[b_sq[h], b_onesb], pwrites=[b_sc])
            P.op("act", lambda e, T=T: e.activation(out=T(0), in_=ba_ps[:, 0:8], func=AF.Exp, scale=-1.0), reads=[b_sc], writes=[b_tsc])
            tiny("dve", lambda e, T=T: e.tensor_scalar(out=T(0), in0=T(0), scalar1=1.0, scalar2=None, op0=ALU.add))
            tiny("dve", lambda e, T=T: e.reciprocal(out=T(0), in_=T(0)))
            P.op("dve", lambda e, T=T: e.tensor_tensor(out=T(1), in0=ba_ps[:, 8:16], in1=avec[0:64, 8:16], op=ALU.add),
                 reads=[b_sc, b_avec, b_tsc], writes=[b_tsc])
            tiny("act", lambda e, T=T: e.activation(out=T(1), in_=T(1), func=AF.Exp))
            tiny("dve", lambda e, T=T: e.tensor_scalar(out=T(1), in0=T(1), scalar1=1.0, scalar2=None, op0=ALU.add))
            tiny("act", lambda e, T=T: e.activation(out=T(1), in_=T(1), func=AF.Ln))
            P.op("dve", lambda e, T=T: e.tensor_tensor(out=T(1), in0=T(1), in1=negA[0:64, :], op=ALU.mult),
                 reads=[b_tsc, b_negA], writes=[b_tsc])
            if main:
                P.op("dve", lambda e, T=T, ssq_ps=ssq_ps: e.tensor_scalar(out=T(14), in0=ssq_ps, scalar1=1e-6, scalar2=None, op0=ALU.add),
                     reads=[b_sc, b_tsc], writes=[b_tsc])
            yield
            P.op("act", lambda e: e.activation(out=KK_sb[:], in_=kk_bk[0:64, :].rearrange("p (h c) -> p h c", h=8), func=AF.Copy),
                 reads=[b_kk], writes=[b_KK])
            P.op("pool", lambda e: e.tensor_tensor(out=dS_all[:], in0=KK_sb[:], in1=bc_mid(cst[0:64, 0:64], 8, 64), op=ALU.mult),
                 reads=[b_KK, b_cst], writes=[b_dS])
            P.op("dve", lambda e, T=T: e.tensor_reduce(out=T(12), in_=dS_all[:], axis=AX.X, op=ALU.add), reads=[b_dS, b_tsc], writes=[b_tsc])
            tiny("dve", lambda e, T=T: e.tensor_scalar(out=T(12), in0=T(12), scalar1=1e-6, scalar2=None, op0=ALU.add))
            tiny("act", lambda e, T=T: e.activation(out=T(12), in_=T(12), func=AF.Ln))
            tiny("act", lambda e, T=T: e.activation(out=T(2), in_=T(12), func=AF.Exp, scale=-0.5))
            tiny("act", lambda e, T=T: e.activation(out=T(3), in_=T(12), func=AF.Exp, scale=0.5))
            tiny("dve", lambda e, T=T: e.tensor_tensor(out=T(9), in0=T(0), in1=T(2), op=ALU.mult))
            tiny("dve", lambda e, T=T: e.scalar_tensor_tensor(out=T(4), in0=T(9), scalar=-1.0, in1=T(2), op0=ALU.mult, op1=ALU.mult))
            yield
            bk_g, b_g = single()
            Gc_ps, Gr_ps, Gl_ps = bk_g[0:64, 0:8], bk_g[0:64, 8:16], bk_g[:, 16:24]
            P.op("pe", lambda e, T=T: e.matmul(Gc_ps, lhsT=Uincl, rhs=T(1), start=True, stop=True), reads=[b_cst, b_tsc], pwrites=[b_g])
            P.op("pe", lambda e, T=T: e.matmul(Gr_ps, lhsT=Lstrict, rhs=T(1), start=True, stop=True), reads=[b_cst, b_tsc], pwrites=[b_g])
            P.op("pe", lambda e, T=T: e.matmul(Gl_ps, lhsT=ones64, rhs=T(1), start=True, stop=True), reads=[b_cst, b_tsc], pwrites=[b_g])
            P.op("act", lambda e, T=T: e.activation(out=T(5), in_=Gc_ps, func=AF.Exp), reads=[b_g], writes=[b_tsc])
            P.op("act", lambda e, T=T: e.activation(out=T(6), in_=Gr_ps, func=AF.Exp), reads=[b_g], writes=[b_tsc])
            P.op("act", lambda e, tsc=tsc: e.activation(out=tsc[:, 7, :], in_=Gl_ps, func=AF.Exp), reads=[b_g], writes=[b_tsc])
            tiny("dve", lambda e, T=T: e.tensor_tensor(out=T(6), in0=T(6), in1=T(2), op=ALU.mult))
            tiny("dve", lambda e, T=T: e.tensor_scalar(out=T(8), in0=T(5), scalar1=-1.0, scalar2=None, op0=ALU.mult))
            if main:
                tiny("act", lambda e, T=T: e.activation(out=T(14), in_=T(14), func=AF.Ln))
                tiny("act", lambda e, T=T: e.activation(out=T(10), in_=T(14), func=AF.Exp, scale=-0.5))
                tiny("dve", lambda e, T=T: e.tensor_scalar(out=T(10), in0=T(10), scalar1=float(128 ** -0.5), scalar2=None, op0=ALU.mult))
            yield
            (pa, pb), (b_pa, b_pb) = pairs.next()
            pk = (pa, pb)
            bpk = (b_pa, b_pb)
            mm8(lambda h: pk[h // 4][0:64, (h % 4) * 128:(h % 4 + 1) * 128], kTh, lambda h: identb[:],
                lambda h: [b_qkv[8 + h], b_identb], lambda h: [bpk[h // 4]])
            for i in range(2):
                P.op("dve", lambda e, i=i, T=T: e.tensor_tensor(
                    out=kd_all[:, 4 * i:4 * i + 4, :], in0=pk[i][0:64, :].rearrange("p (h d) -> p h d", h=4),
                    in1=bc_in(T(6)[:, 4 * i:4 * i + 4], 4, 128), op=ALU.mult), reads=[bpk[i], b_tsc], pwrites=[b_kd])
            yield
            (pa2, pb2), (b_pa2, b_pb2) = pairs.next()
            pv = (pa2, pb2)
            bpv = (b_pa2, b_pb2)
            mm8(lambda h: pv[h // 4][0:64, (h % 4) * 128:(h % 4 + 1) * 128], vTh, lambda h: identb[:],
                lambda h: [b_qkv[16 + h], b_identb], lambda h: [bpv[h // 4]])
            for i in range(2):
                P.op("dve", lambda e, i=i, T=T: e.tensor_tensor(
                    out=vp_all[:, 4 * i:4 * i + 4, :], in0=pv[i][0:64, :].rearrange("p (h d) -> p h d", h=4),
                    in1=bc_in(T(3)[:, 4 * i:4 * i + 4], 4, 128), op=ALU.mult), reads=[bpv[i], b_tsc], pwrites=[b_vp])
            yield
            P.op("pool", lambda e, T=T: e.tensor_tensor(out=gdU_all[:], in0=bc_mid(Uincl, 8, 64), in1=bc_in(T(1), 8, 64), op=ALU.mult),
                 reads=[b_cst, b_tsc], writes=[b_gdU])
            yield
            gd_bk, b_gd = single()
            P.op("pe", lambda e: e.matmul(gd_bk[0:64, :], lhsT=Lstrict, rhs=gdU_all[:].rearrange("p h c -> p (h c)"), start=True, stop=True),
                 reads=[b_cst, b_gdU], pwrites=[b_gd])
            P.op("act", lambda e: e.activation(out=E_all[:], in_=gd_bk[0:64, :].rearrange("p (h c) -> p h c", h=8), func=AF.Exp),
                 reads=[b_gd], writes=[b_E])
            yield
            P.op("pool", lambda e: e.tensor_tensor(out=dS_all[:], in0=E_all[:], in1=bc_mid(Ustrict, 8, 64), op=ALU.mult),
                 reads=[b_E, b_cst], writes=[b_dS])
            P.op("pool", lambda e, T=T: e.tensor_tensor(out=dS_all[:], in0=dS_all[:], in1=bc_in(T(4), 8, 64), op=ALU.mult),
                 reads=[b_dS, b_tsc], writes=[b_dS])
            P.op("pool", lambda e: e.tensor_tensor(out=Y_all[:], in0=KK_sb[:], in1=dS_all[:], op=ALU.mult),
                 reads=[b_KK, b_dS], writes=[b_Y])
            if main:
                P.op("pool", lambda e: e.tensor_tensor(out=dI_all[:], in0=E_all[:], in1=bc_mid(Uincl, 8, 64), op=ALU.mult),
                     reads=[b_E, b_cst], writes=[b_dI])
                P.op("pool", lambda e, T=T: e.tensor_tensor(out=dI_all[:], in0=dI_all[:], in1=bc_in(T(2), 8, 64), op=ALU.mult),
                     reads=[b_dI, b_tsc], writes=[b_dI])
                kq_bk, b_kq = single()
                mm8(lambda h: kq_bk[0:64, h * 64:(h + 1) * 64], kTh, qTh, lambda h: [b_qkv[8 + h], b_qkv[h]], lambda h: [b_kq])
                P.op("dve", lambda e: e.tensor_tensor(out=QKT_all[:], in0=kq_bk[0:64, :].rearrange("p (h c) -> p h c", h=8), in1=dI_all[:], op=ALU.mult),
                     reads=[b_kq, b_dI], writes=[b_QKT])
            yield
            z_bk, b_zb = single()
            mm8(lambda h: z_bk[0:64, h * 64:(h + 1) * 64], lambda h: Y_all[:, h, :], lambda h: identG[0:64, 0:64],
                lambda h: [b_Y, b_identG], lambda h: [b_zb])
            P.op("act", lambda e: e.activation(out=Z_all[:], in_=z_bk[0:64, :].rearrange("p (h c) -> p h c", h=8), func=AF.Copy),
                 reads=[b_zb], writes=[b_Z])
            P.op("pool", lambda e: e.tensor_tensor(out=P_all[:], in0=Y_all[:], in1=bc_mid(cst[0:64, 0:64], 8, 64), op=ALU.add),
                 reads=[b_Y, b_cst], writes=[b_P])
            def squares(lvl):
                zn_bk, b_zn = single()
                mm8(lambda h: zn_bk[0:64, h * 64:(h + 1) * 64], lambda h: Y_all[:, h, :], lambda h: Z_all[:, h, :],
                    lambda h: [b_Y, b_Z], lambda h: [b_zn])
                yn = None
                if lvl < 5:
                    yn_bk, b_yn = single()
                    mm8(lambda h: yn_bk[0:64, h * 64:(h + 1) * 64], lambda h: Z_all[:, h, :], lambda h: Y_all[:, h, :],
                        lambda h: [b_Y, b_Z], lambda h: [b_yn])
                    yn = (yn_bk, b_yn)
                return (zn_bk, b_zn), yn

            def evacs(zn, yn):
                zn_bk, b_zn = zn
                P.op("act", lambda e: e.activation(out=Z_all[:], in_=zn_bk[0:64, :].rearrange("p (h c) -> p h c", h=8), func=AF.Copy),
                     reads=[b_zn], writes=[b_Z])
                if yn is not None:
                    yn_bk, b_yn = yn
                    P.op("act", lambda e: e.activation(out=Y_all[:], in_=yn_bk[0:64, :].rearrange("p (h c) -> p h c", h=8), func=AF.Copy),
                         reads=[b_yn], writes=[b_Y])

            def pupdate():
                pu_bk, b_pu = single()
                mm8(lambda h: pu_bk[0:64, h * 64:(h + 1) * 64], lambda h: Z_all[:, h, :], lambda h: P_all[:, h, :],
                    lambda h: [b_Z, b_P], lambda h: [b_pu])
                P.op("dve", lambda e: e.tensor_tensor(out=P_all[:], in0=pu_bk[0:64, :].rearrange("p (h c) -> p h c", h=8), in1=P_all[:], op=ALU.add),
                     reads=[b_pu, b_P], writes=[b_P])

            zn, yn = squares(1)
            yield
            evacs(zn, yn)
            yield
            for lvl in range(2, 6):
                zn, yn = squares(lvl)
                yield
                pupdate()
                yield
                evacs(zn, yn)
                yield
            pupdate()
            yield

        def scan(ctx):
            bi, main, xb, b_xb, c, tsc, b_tsc, par, szi = unpack(ctx)
            kd_all, b_kd = kd_2[par]
            vp_all, b_vp = vp_2[par]
            QKT_all, b_QKT = QKT_2[par]
            P_all, b_P = P_2[par]
            cs = slice(c * CH, (c + 1) * CH)
            kTh = lambda h: qkvT[:, 8 + h, cs]
            qTh = lambda h: qkvT[:, h, cs]

            def T(i, rows=64, tsc=tsc):
                return tsc[0:rows, i, :]

            def tiny(eng, fn):
                P.op(eng, fn, reads=[b_tsc], writes=[b_tsc])
            (ra, rb), (b_ra, b_rb) = pairs.next()
            pr = (ra, rb)
            bpr = (b_ra, b_rb)
            mm8(lambda h: pr[h // 4][0:64, (h % 4) * 128:(h % 4 + 1) * 128], kTh, lambda h: Sbf[:, h, :],
                lambda h: [b_qkv[8 + h], b_Sbf], lambda h: [bpr[h // 4]])
            for i in range(2):
                P.op("dve", lambda e, i=i, T=T: e.tensor_tensor(
                    out=tR[:, 4 * i:4 * i + 4, :], in0=pr[i][0:64, :].rearrange("p (h d) -> p h d", h=4),
                    in1=bc_in(T(8)[:, 4 * i:4 * i + 4], 4, 128), op=ALU.mult), reads=[bpr[i], b_tsc], pwrites=[b_tR])
            yield
            P.op("pool", lambda e: e.tensor_tensor(out=D1_all[:], in0=tR[:], in1=vp_all[:], op=ALU.add), reads=[b_tR, b_vp], writes=[b_D1])
            yield
            (va, vb), (b_va, b_vb) = pairs.next()
            pvn = (va, vb)
            bpvn = (b_va, b_vb)
            mm8(lambda h: pvn[h // 4][0:64, (h % 4) * 128:(h % 4 + 1) * 128], lambda h: P_all[:, h, :], lambda h: D1_all[:, h, :],
                lambda h: [b_P, b_D1], lambda h: [bpvn[h // 4]])
            for i in range(2):
                P.op("dve", lambda e, i=i, T=T: e.tensor_tensor(
                    out=vn_all[:, 4 * i:4 * i + 4, :], in0=pvn[i][0:64, :].rearrange("p (h d) -> p h d", h=4),
                    in1=bc_in(T(9)[:, 4 * i:4 * i + 4], 4, 128), op=ALU.mult), reads=[bpvn[i], b_tsc], pwrites=[b_vn])
            yield
            if main:
                (qa, qb), (b_qa, b_qb) = pairs.next()
                p1 = (qa, qb)
                bp1 = (b_qa, b_qb)
                mm8(lambda h: p1[h // 4][0:64, (h % 4) * 128:(h % 4 + 1) * 128], qTh, lambda h: Sbf[:, h, :],
                    lambda h: [b_qkv[h], b_Sbf], lambda h: [bp1[h // 4]])
                for i in range(2):
                    P.op("dve", lambda e, i=i, T=T: e.tensor_tensor(
                        out=o_all[:, 4 * i:4 * i + 4, :], in0=p1[i][0:64, :].rearrange("p (h d) -> p h d", h=4),
                        in1=bc_in(T(5)[:, 4 * i:4 * i + 4], 4, 128), op=ALU.mult), reads=[bp1[i], b_tsc], pwrites=[b_oall])
                yield
                (wa, wb), (b_wa, b_wb) = pairs.next()
                p2 = (wa, wb)
                bp2 = (b_wa, b_wb)
                mm8(lambda h: p2[h // 4][0:64, (h % 4) * 128:(h % 4 + 1) * 128], lambda h: QKT_all[:, h, :], lambda h: vn_all[:, h, :],
                    lambda h: [b_QKT, b_vn], lambda h: [bp2[h // 4]])
                for i in range(2):
                    P.op("dve", lambda e, i=i: e.tensor_tensor(
                        out=o_all[:, 4 * i:4 * i + 4, :], in0=p2[i][0:64, :].rearrange("p (h d) -> p h d", h=4),
                        in1=o_all[:, 4 * i:4 * i + 4, :], op=ALU.add), reads=[bp2[i], b_oall], writes=[b_oall])
            yield
            (sa, sb_), (b_sa, b_sb) = pairs.next()
            psu = (sa, sb_)
            bpsu = (b_sa, b_sb)
            mm8(lambda h: psu[h // 4][:, (h % 4) * 128:(h % 4 + 1) * 128], lambda h: kd_all[:, h, :], lambda h: vn_all[:, h, :],
                lambda h: [b_kd, b_vn], lambda h: [bpsu[h // 4]])
            yield
            P.op("pool", lambda e, tsc=tsc: e.tensor_tensor(out=S[:], in0=S[:], in1=bc_in(tsc[:, 7, :], 8, 128, rows=128), op=ALU.mult),
                 reads=[b_S, b_tsc], writes=[b_S])
            for i in range(2):
                P.op("dve", lambda e, i=i: e.tensor_tensor(
                    out=S[:, 4 * i:4 * i + 4, :], in0=psu[i][:, :].rearrange("p (h d) -> p h d", h=4),
                    in1=S[:, 4 * i:4 * i + 4, :], op=ALU.add), reads=[bpsu[i], b_S], writes=[b_S])
            P.op("act", lambda e: e.activation(out=Sbf[:], in_=S[:], func=AF.Copy), reads=[b_S], writes=[b_Sbf])
            if not main:
                return
            yield
            P.op("pool", lambda e: e.tensor_tensor(out=og[:], in0=o_all[:], in1=o_all[:], op=ALU.mult), reads=[b_oall], writes=[b_og])
            P.op("dve", lambda e, T=T: e.tensor_reduce(out=T(13), in_=og[:], axis=AX.X, op=ALU.add), reads=[b_og, b_tsc], writes=[b_tsc])
            tiny("dve", lambda e, T=T: e.tensor_tensor(out=T(13), in0=T(13), in1=T(10), op=ALU.mult))
            tiny("dve", lambda e, T=T: e.tensor_tensor(out=T(13), in0=T(13), in1=T(10), op=ALU.mult))
            tiny("dve", lambda e, T=T: e.tensor_scalar(out=T(13), in0=T(13), scalar1=float(1.0 / 128), scalar2=1e-6, op0=ALU.mult, op1=ALU.add))
            tiny("act", lambda e, T=T: e.activation(out=T(13), in_=T(13), func=AF.Ln))
            tiny("act", lambda e, T=T: e.activation(out=T(13), in_=T(13), func=AF.Exp, scale=-0.5))
            tiny("dve", lambda e, T=T: e.tensor_tensor(out=T(11), in0=T(13), in1=T(10), op=ALU.mult))
            P.op("pool", lambda e, T=T: e.tensor_tensor(out=og[:], in0=o_all[:], in1=bc_in(T(11), 8, 128), op=ALU.mult),
                 reads=[b_oall, b_tsc, b_og], writes=[b_og])
            P.op("pool", lambda e, c=c: e.tensor_tensor(out=og[:], in0=og[:], in1=sz[:, szi + c % 2, :].rearrange("p (h d) -> p h d", h=8), op=ALU.mult),
                 reads=[b_og, b_sz[szi + c % 2]], writes=[b_og])
            yield
            bk, b_bk = big.next()
            for h in range(8):
                P.op("pe", lambda e, bk=bk, h=h: e.transpose(out=bk[:, h * 64:(h + 1) * 64], in_=og[:, h, :], identity=cst[0:64, 0:64]),
                     reads=[b_og, b_cst], pwrites=[b_bk])
            half = c % 2
            P.op("act", lambda e, bk=bk, half=half: e.activation(
                out=ogT[:, :, half * 64:(half + 1) * 64], in_=bk[:, :].rearrange("p (h t) -> p h t", h=8), func=AF.Identity, scale=normT[:, 0:1]),
                reads=[b_bk, b_normT], pwrites=[b_ogT])
            yield
            if half == 1:
                ti = (bi - NPRE) * 3 + c // 2
                for h in range(8):
                    for hf, bkw in ((0, wide0), (1, wide1)):
                        P.op("pe", lambda e, h=h, hf=hf, bkw=bkw: e.matmul(
                            bkw[:, :], lhsT=ogT[:, h, :], rhs=Wout[:, h, hf * 512:(hf + 1) * 512], start=(h == 0), stop=(h == 7)),
                            reads=[b_ogT, b_Wout], pwrites=[b_wide])
                ln_epilogue(ti, xtok, None, 0, final=(upto == 1))

        nchunk = [0]

        def run_interleaved(gens):
            gens = list(gens)
            while gens:
                for g in list(gens):
                    try:
                        next(g)
                    except StopIteration:
                        gens.remove(g)

        for bi in range(NB):
            main = bi >= NPRE
            t0 = bi * BLK
            xb, b_xb = xb_r.next()
            xv = xT.rearrange("(kc p) t -> p kc t", p=128)
            P.dma("pool", lambda e, xb=xb, t0=t0: e.dma_start(out=xb[:], in_=xv[:, :, t0:t0 + BLK + HALO]), "xb0",
                  writes=[b_xb])
            chunks = list(range(24)) if main else list(range(8, 24))
            for m in chunks:
                bk, b_bk = big.next()
                for kc in range(KC):
                    P.op("pe", lambda e, bk=bk, m=m, kc=kc, xb=xb: e.matmul(
                        bk[:, 0:BLK + HALO], lhsT=Wqkv[:, kc, m * 128:(m + 1) * 128], rhs=xb[:, kc, :],
                        start=(kc == 0), stop=(kc == KC - 1)), reads=[wqkv_b(kc, m * 128), b_xb], pwrites=[b_bk])
                pre, b_pre = pre_r.next()
                P.op("dve", lambda e, pre=pre, bk=bk: e.tensor_copy(out=pre[:], in_=bk[:, 0:BLK + HALO]), reads=[b_bk], writes=[b_pre])
                bk2, b_bk2 = big.next()
                for j in range(4):
                    P.op("pe", lambda e, bk2=bk2, m=m, j=j, pre=pre: e.matmul(
                        bk2[:, 0:BLK], lhsT=dg[:, j, m, :], rhs=pre[:, j:j + BLK], start=(j == 0), stop=(j == 3)),
                        reads=[b_dg, b_pre], pwrites=[b_bk2])
                P.op("act", lambda e, bk2=bk2, m=m: e.activation(out=qkvT[:, m, :], in_=bk2[:, 0:BLK], func=AF.Silu),
                     reads=[b_bk2], writes=[b_qkv[m]])
                if m < 8:
                    P.op("pool", lambda e, m=m: e.tensor_tensor(out=sq[:, m, :], in0=qkvT[:, m, :], in1=qkvT[:, m, :], op=ALU.mult),
                         reads=[b_qkv[m]], writes=[b_sq[m]])
            pending = None
            for c in range(6):
                tsc, b_tsc = tsc_r.next()
                nchunk[0] += 1
                ctx = dict(bi=bi, main=main, xb=xb, b_xb=b_xb, c=c, tsc=tsc, b_tsc=b_tsc, par=nchunk[0] % 2,
                           szi=((c // 2) % 2) * 2)
                gens = [prep(ctx)] + ([pending] if pending is not None else [])
                run_interleaved(gens)
                pending = scan(ctx)
            run_interleaved([pending])

    def load_xb(xb, b_xb, src_i, t0, key):
        v = hT_s[src_i].rearrange("(kc p) t -> p kc t", p=128)
        P.dma("sp", lambda e: e.dma_start(out=xb[:], in_=v[:, :, HALO + t0 - 2:HALO + t0 + BLK]), key,
              reads=[b_hTs[src_i]], writes=[b_xb])

    def phase_ffn(li, src_i, dst_i, final):
        Wup = P.sb([128, KC, 2 * DFF], BF16, "Wup")
        Wd = P.sb([128, FC, D], BF16, "Wd")
        dgF = P.sb([128, 3, FC, 128], BF16, "dgF")
        b_Wup, b_Wd, b_dgF = P.buf(), P.buf(), P.buf()
        wup_b = load_w_cast(Wup, b_Wup, f_w_up[li].rearrange("(kc p) n -> p kc n", p=128), 2 * DFF, "Wup")
        wd_b = load_w_cast(Wd, b_Wd, f_w_down[li].rearrange("(m p) n -> p m n", p=128), D, "Wd")
        make_diag(dgF, b_dgF, f_convT[li], FC, 3, "ctF")
        load_ln(1 + 2 * li)
        xb_r = Ring([(P.sb([128, KC, BLK + 2], BF16, f"fxb{i}"), P.buf()) for i in range(1)])
        pre_r = Ring([(P.sb([128, BLK + 2], BF16, f"fpre{i}"), P.buf()) for i in range(2)])
        su_r = Ring([(P.sb([128, BLK], BF16, f"fsu{i}"), P.buf()) for i in range(2)])
        aT = P.sb([128, FC, BLK], BF16, "aT")
        b_aT = P.buf()
        for bi in range(NMAIN):
            t0 = bi * BLK
            xb, b_xb = xb_r.next()
            load_xb(xb, b_xb, src_i, t0, "fxb0")
            for m in range(FC):
                bk, b_bk = big6.next()
                for kc in range(KC):
                    P.op("pe", lambda e, bk=bk, m=m, kc=kc, xb=xb: e.matmul(
                        bk[:, 0:BLK + 2], lhsT=Wup[:, kc, m * 128:(m + 1) * 128], rhs=xb[:, kc, :],
                        start=(kc == 0), stop=(kc == KC - 1)), reads=[wup_b(kc, m * 128), b_xb], pwrites=[b_bk])
                pre, b_pre = pre_r.next()
                P.op("dve", lambda e, pre=pre, bk=bk: e.tensor_copy(out=pre[:], in_=bk[:, 0:BLK + 2]), reads=[b_bk], writes=[b_pre])
                bk2, b_bk2 = big6.next()
                for j in range(3):
                    P.op("pe", lambda e, bk2=bk2, m=m, j=j, pre=pre: e.matmul(
                        bk2[:, 0:BLK], lhsT=dgF[:, j, m, :], rhs=pre[:, j:j + BLK], start=(j == 0), stop=(j == 2)),
                        reads=[b_dgF, b_pre], pwrites=[b_bk2])
                su, b_su = su_r.next()
                P.op("act", lambda e, bk2=bk2, su=su: e.activation(out=su[:], in_=bk2[:, 0:BLK], func=AF.Silu), reads=[b_bk2], writes=[b_su])
                bk3, b_bk3 = big6.next()
                for kc in range(KC):
                    P.op("pe", lambda e, bk3=bk3, m=m, kc=kc, xb=xb: e.matmul(
                        bk3[:, 0:BLK], lhsT=Wup[:, kc, DFF + m * 128:DFF + (m + 1) * 128], rhs=xb[:, kc, 2:BLK + 2],
                        start=(kc == 0), stop=(kc == KC - 1)), reads=[wup_b(kc, DFF + m * 128), b_xb], pwrites=[b_bk3])
                P.op("dve", lambda e, bk3=bk3, su=su, m=m: e.tensor_tensor(out=aT[:, m, :], in0=bk3[:, 0:BLK], in1=su[:], op=ALU.mult),
                     reads=[b_bk3, b_su], pwrites=[b_aT])
            for tl in range(3):
                for m in range(FC):
                    for hf, bkw in ((0, wide0), (1, wide1)):
                        P.op("pe", lambda e, m=m, hf=hf, bkw=bkw, tl=tl: e.matmul(
                            bkw[:, :], lhsT=aT[:, m, tl * TILE:(tl + 1) * TILE], rhs=Wd[:, m, hf * 512:(hf + 1) * 512],
                            start=(m == 0), stop=(m == FC - 1)), reads=[b_aT, wd_b(m, 0)], pwrites=[b_wide])
                ln_epilogue(bi * 3 + tl, htok_s[src_i], b_hts[src_i], dst_i, final)

    def phase_sconv(src_i, dst_i, final):
        Win = P.sb([128, KC, 3 * D], BF16, "Win")
        Wo = P.sb([128, KC, D], BF16, "Wo")
        dgB = P.sb([128, 3, 8, 128], BF16, "dgB")
        b_Win, b_Wo, b_dgB = P.buf(), P.buf(), P.buf()
        win_b = load_w_cast(Win, b_Win, b_w_in.rearrange("(kc p) n -> p kc n", p=128), 3 * D, "Win")
        b_Wo = load_w_cast(Wo, b_Wo, b_w_out.rearrange("(kc p) n -> p kc n", p=128), D, "Wo")(0, 0)
        make_diag(dgB, b_dgB, b_convT[:, :], 8, 3, "ctB")
        load_ln(2)
        xb_r = Ring([(P.sb([128, KC, BLK + 2], BF16, f"sxb{i}"), P.buf()) for i in range(2)])
        c_r = Ring([(P.sb([128, BLK + 2], BF16, f"scs{i}"), P.buf()) for i in range(2)])
        cx_r = Ring([(P.sb([128, BLK + 2], BF16, f"scx{i}"), P.buf()) for i in range(2)])
        bs_r = Ring([(P.sb([128, BLK], BF16, f"sbs{i}"), P.buf()) for i in range(2)])
        vT = P.sb([128, 8, BLK], BF16, "vTs")
        b_vT = P.buf()
        for bi in range(NMAIN):
            t0 = bi * BLK
            xb, b_xb = xb_r.next()
            load_xb(xb, b_xb, src_i, t0, f"sxb{xb_r.i % 2}")
            for m in range(8):
                bk, b_bk = big6.next()
                for kc in range(KC):
                    P.op("pe", lambda e, bk=bk, m=m, kc=kc, xb=xb: e.matmul(
                        bk[:, 0:BLK + 2], lhsT=Win[:, kc, D + m * 128:D + (m + 1) * 128], rhs=xb[:, kc, :],
                        start=(kc == 0), stop=(kc == KC - 1)), reads=[win_b(kc, D + m * 128), b_xb], pwrites=[b_bk])
                cs_, b_cs = c_r.next()
                P.op("act", lambda e, cs_=cs_, bk=bk: e.activation(out=cs_[:], in_=bk[:, 0:BLK + 2], func=AF.Copy), reads=[b_bk], writes=[b_cs])
                bk2, b_bk2 = big6.next()
                for kc in range(KC):
                    P.op("pe", lambda e, bk2=bk2, m=m, kc=kc, xb=xb: e.matmul(
                        bk2[:, 0:BLK + 2], lhsT=Win[:, kc, 2 * D + m * 128:2 * D + (m + 1) * 128], rhs=xb[:, kc, :],
                        start=(kc == 0), stop=(kc == KC - 1)), reads=[win_b(kc, 2 * D + m * 128), b_xb], pwrites=[b_bk2])
                cx, b_cx = cx_r.next()
                P.op("dve", lambda e, cx=cx, bk2=bk2, cs_=cs_: e.tensor_tensor(out=cx[:], in0=bk2[:, 0:BLK + 2], in1=cs_[:], op=ALU.mult),
                     reads=[b_bk2, b_cs], writes=[b_cx])
                bk3, b_bk3 = big6.next()
                for j in range(3):
                    P.op("pe", lambda e, bk3=bk3, m=m, j=j, cx=cx: e.matmul(
                        bk3[:, 0:BLK], lhsT=dgB[:, j, m, :], rhs=cx[:, j:j + BLK], start=(j == 0), stop=(j == 2)),
                        reads=[b_dgB, b_cx], pwrites=[b_bk3])
                bk4, b_bk4 = big6.next()
                for kc in range(KC):
                    P.op("pe", lambda e, bk4=bk4, m=m, kc=kc, xb=xb: e.matmul(
                        bk4[:, 0:BLK], lhsT=Win[:, kc, m * 128:(m + 1) * 128], rhs=xb[:, kc, 2:BLK + 2],
                        start=(kc == 0), stop=(kc == KC - 1)), reads=[win_b(kc, m * 128), b_xb], pwrites=[b_bk4])
                bs, b_bs = bs_r.next()
                P.op("act", lambda e, bs=bs, bk4=bk4: e.activation(out=bs[:], in_=bk4[:, 0:BLK], func=AF.Copy), reads=[b_bk4], writes=[b_bs])
                P.op("dve", lambda e, bk3=bk3, bs=bs, m=m: e.tensor_tensor(out=vT[:, m, :], in0=bk3[:, 0:BLK], in1=bs[:], op=ALU.mult),
                     reads=[b_bk3, b_bs], pwrites=[b_vT])
            for tl in range(3):
                for m in range(8):
                    for hf, bkw in ((0, wide0), (1, wide1)):
                        P.op("pe", lambda e, m=m, hf=hf, bkw=bkw, tl=tl: e.matmul(
                            bkw[:, :], lhsT=vT[:, m, tl * TILE:(tl + 1) * TILE], rhs=Wo[:, m, hf * 512:(hf + 1) * 512],
                            start=(m == 0), stop=(m == 7)), reads=[b_vT, b_Wo], pwrites=[b_wide])
                ln_epilogue(bi * 3 + tl, htok_s[src_i], b_hts[src_i], dst_i, final)

    with P.phase():
        phase1()
    if upto >= 2:
        with P.phase():
            phase_ffn(0, 0, 1, final=(upto == 2))
    if upto >= 3:
        with P.phase():
            phase_sconv(1, 2, final=(upto == 3))
    if upto >= 4:
        with P.phase():
            phase_ffn(1, 2, 0, final=True)
    P.stack.close()
    return nc


def make_consts():
    c = np.zeros((128, 448), np.float32)
    c[:, 0:128] = np.eye(128, dtype=np.float32)
    r = np.arange(64)[:, None]
    q = np.arange(64)[None, :]
    c[0:64, 128:192] = (r <= q)
    c[0:64, 192:256] = (r > q)
    c[0:64, 256:320] = (r < q)
    c[:, 320:448] = 1.0
    return c


def rep128(v):
    v = np.asarray(v, np.float32).reshape(1, -1)
    return np.ascontiguousarray(np.repeat(v, 128, axis=0))


def weight_inputs(a_w_in, a_conv, a_log, a_dt_bias, a_norm, a_w_out, b_w_in, b_conv, b_w_out,
                  ln_mix_g, ln_mix_b, ffn_w_up, ffn_conv, ffn_w_down, ln_ffn_g, ln_ffn_b):
    f = np.float32
    d = {}
    d["consts"] = make_consts()
    d["a_w_in"] = np.ascontiguousarray(a_w_in[0], f)
    d["a_convT"] = np.ascontiguousarray(a_conv[0].T.reshape(24, 128, 4).transpose(1, 0, 2).reshape(128, 96), f)
    d["a_vec"] = np.concatenate([rep128(a_log[0]), rep128(a_dt_bias[0])], axis=1)
    d["a_normT"] = np.ascontiguousarray(a_norm[0].reshape(128, 1), f)
    d["a_w_out"] = np.ascontiguousarray(a_w_out[0], f)
    d["b_w_in"] = np.ascontiguousarray(b_w_in[0], f)
    d["b_convT"] = np.ascontiguousarray(b_conv[0].T.reshape(8, 128, 3).transpose(1, 0, 2).reshape(128, 24), f)
    d["b_w_out"] = np.ascontiguousarray(b_w_out[0], f)
    d["lng"] = np.stack([rep128(ln_mix_g[0]), rep128(ln_ffn_g[0]), rep128(ln_mix_g[1]), rep128(ln_ffn_g[1])])
    d["lnb"] = np.stack([rep128(ln_mix_b[0]), rep128(ln_ffn_b[0]), rep128(ln_mix_b[1]), rep128(ln_ffn_b[1])])
    d["f_w_up"] = np.ascontiguousarray(ffn_w_up, f)
    d["f_convT"] = np.stack([np.ascontiguousarray(ffn_conv[i].T.reshape(FC, 128, 3).transpose(1, 0, 2).reshape(128, FC * 3)) for i in range(2)]).astype(f)
    d["f_w_down"] = np.ascontiguousarray(ffn_w_down, f)
    return d


def core_stream_inputs(stream, valid, NPRE, NMAIN):
    ntok = (NPRE + NMAIN) * BLK
    assert stream.shape[0] == ntok
    xT = np.zeros((D, HALO + ntok), np.float32)
    xT[:, HALO:] = stream.T
    m0 = NPRE * BLK
    return {"xT": xT, "xtok": np.ascontiguousarray(stream[m0:]), "mask": valid[m0:m0 + 128].astype(np.float32).reshape(128, 1)}


_CACHE = {}


def kernel(x, meta, a_w_in, a_conv, a_log, a_dt_bias, a_norm, a_w_out, b_w_in, b_conv, b_w_out,
           ln_mix_g, ln_mix_b, ffn_w_up, ffn_conv, ffn_w_down, ln_ffn_g, ln_ffn_b):
    x = np.asarray(x, np.float32)
    meta = np.asarray(meta, np.float32)
    B, SEQ, _ = x.shape
    NPRE, NMAIN = NPRE_FULL, NMAIN_FULL
    ntok = (NPRE + NMAIN) * BLK
    half_tok = NMAIN * BLK - TILE
    w = weight_inputs(*[np.asarray(a, np.float32) for a in (a_w_in, a_conv, a_log, a_dt_bias, a_norm, a_w_out, b_w_in, b_conv, b_w_out,
                                                            ln_mix_g, ln_mix_b, ffn_w_up, ffn_conv, ffn_w_down, ln_ffn_g, ln_ffn_b)])
    in_maps = []
    for core in range(8):
        b, half = core // 2, core % 2
        stream = np.zeros((ntok, D), np.float32)
        valid = np.zeros((ntok,), bool)
        n_x = half_tok * (half + 1)
        seq = np.concatenate([meta, x[b, :n_x]], axis=0)
        stream[ntok - seq.shape[0]:] = seq
        valid[ntok - seq.shape[0]:] = True
        m = dict(w)
        m.update(core_stream_inputs(stream, valid, NPRE, NMAIN))
        in_maps.append(m)
    if "nc" not in _CACHE:
        _CACHE["nc"] = build_program(NPRE, NMAIN)
    res = run_bass_kernel_spmd(_CACHE["nc"], in_maps, core_ids=list(range(8)))
    outp = np.zeros((B, SEQ, D), np.float32)
    for core in range(8):
        b, half = core // 2, core % 2
        outp[b, half * half_tok:(half + 1) * half_tok] = res.results[core]["out"]
    return outp
```

```python
import contextlib
import numpy as np
import concourse.bass as bass
import concourse.mybir as mybir
from concourse.bass_utils import run_bass_kernel_spmd

F32 = mybir.dt.float32
BF16 = mybir.dt.bfloat16
AF = mybir.ActivationFunctionType
ALU = mybir.AluOpType
AX = mybir.AxisListType

NSEM_PER_ENG = 6
SAME_ENG_SYNC = True
NEU_DT = BF16

D = 1024
KC = 8
TILE = 128
BLK = 384
CH = 64
HALO = 3
DFF = 2816
FC = DFF // 128
ALPHA = float((2.0 * 2) ** 0.25)
NPRE_FULL = 11
NMAIN_FULL = 11


class Buf:
    __slots__ = ("name", "writers", "readers", "open", "pre")

    def __init__(self, name):
        self.name = name
        self.writers = []
        self.readers = []
        self.open = False
        self.pre = []


class Op:
    __slots__ = ("eng", "emit", "deps", "idx", "dma_key", "dma_cnt", "waits")

    def __init__(self, eng, emit):
        self.eng = eng
        self.emit = emit
        self.deps = []
        self.idx = -1
        self.dma_key = None
        self.dma_cnt = 0
        self.waits = []


class Prog:
    ENGS = ("pe", "act", "dve", "pool", "sp")

    def __init__(self, nc):
        self.nc = nc
        self.ops = {e: [] for e in self.ENGS}
        self.ncomp = {e: 0 for e in self.ENGS}
        self.dma_counts = {}
        self.stack = contextlib.ExitStack()
        self.nbuf = 0
        self.nname = 0

    def sb(self, shape, dt, name=None):
        self.nname += 1
        return self.stack.enter_context(self.nc.sbuf_tensor(f"{name or 'sb'}_{self.nname}", list(shape), dt))

    def ps(self, shape, dt=F32, name=None):
        self.nname += 1
        return self.stack.enter_context(self.nc.psum_tensor(name or f"ps{self.nname}", list(shape), dt))

    def buf(self, name=None):
        self.nbuf += 1
        return Buf(name or f"b{self.nbuf}")

    def _track(self, op, reads, writes, pwrites):
        for b in reads:
            op.deps.extend(b.writers)
            b.readers.append(op)
            b.open = False
        for b in writes:
            op.deps.extend(b.readers)
            op.deps.extend(b.writers)
            b.pre = list(b.readers) + list(b.writers)
            b.writers = [op]
            b.readers = []
            b.open = True
        for b in pwrites:
            if b.open:
                op.deps.extend(b.pre)
                b.writers.append(op)
            else:
                op.deps.extend(b.readers)
                op.deps.extend(b.writers)
                b.pre = list(b.readers) + list(b.writers)
                b.writers = [op]
                b.readers = []
                b.open = True

    def op(self, eng, emit, reads=(), writes=(), pwrites=()):
        o = Op(eng, emit)
        self._track(o, reads, writes, pwrites)
        o.idx = self.ncomp[eng]
        self.ncomp[eng] += 1
        self.ops[eng].append(o)
        return o

    def dma(self, eng, emit, key, reads=(), writes=(), pwrites=()):
        o = Op(eng, emit)
        self._track(o, reads, writes, pwrites)
        o.idx = -1
        o.dma_key = key
        self.dma_counts[key] = self.dma_counts.get(key, 0) + 1
        o.dma_cnt = self.dma_counts[key]
        self.ops[eng].append(o)
        return o

    def setup_sems(self):
        nc = self.nc
        st = self.stack
        self.comp_sems = {}
        for e in ("pe", "act", "dve", "pool"):
            self.comp_sems[e] = [st.enter_context(nc.semaphore(f"s_{e}_{i}")) for i in range(NSEM_PER_ENG)]
        self.dma_sems = {}
        self.emitted = {e: 0 for e in self.ENGS}
        self.waited_idx = {e: {x: -1 for x in self.ENGS} for e in self.ENGS}
        self.waited_dma = {e: {} for e in self.ENGS}

    def emit(self):
        nc = self.nc
        comp_sems, dma_sems = self.comp_sems, self.dma_sems
        for k in self.dma_counts:
            if k not in dma_sems:
                dma_sems[k] = self.stack.enter_context(nc.semaphore(f"d_{len(dma_sems)}"))
        new_ops = {e: self.ops[e][self.emitted[e]:] for e in self.ENGS}
        for e in self.ENGS:
            waited_idx = self.waited_idx[e]
            waited_dma = self.waited_dma[e]
            for o in new_ops[e]:
                need_idx = {}
                need_dma = {}
                for d in o.deps:
                    if d is o:
                        continue
                    if d.dma_key is not None:
                        if d.dma_cnt > need_dma.get(d.dma_key, 0):
                            need_dma[d.dma_key] = d.dma_cnt
                    else:
                        if d.eng == e and (e == "pe" or not SAME_ENG_SYNC):
                            continue
                        if d.idx > need_idx.get(d.eng, -1):
                            need_idx[d.eng] = d.idx
                for src, k in need_idx.items():
                    if k > waited_idx[src]:
                        waited_idx[src] = k
                        o.waits.append((comp_sems[src][k % NSEM_PER_ENG], k // NSEM_PER_ENG + 1))
                for key, c in need_dma.items():
                    if c > waited_dma.get(key, 0):
                        waited_dma[key] = c
                        o.waits.append((dma_sems[key], 16 * c))
            self.emitted[e] = len(self.ops[e])
        final = [(dma_sems[k], 16 * c) for k, c in self.dma_counts.items()]

        def replay(ename, eng):
            for o in new_ops[ename]:
                for s, v in o.waits:
                    eng.wait_ge(s, v)
                ins = o.emit(eng)
                if o.dma_key is not None:
                    ins.then_inc(dma_sems[o.dma_key], 16)
                else:
                    ins.then_inc(comp_sems[ename][o.idx % NSEM_PER_ENG], 1)
            if ename == "sp":
                for s, v in final:
                    eng.wait_ge(s, v)

        with nc.Block() as block:
            @block.tensor
            def _(eng):
                replay("pe", eng)

            @block.scalar
            def _(eng):
                replay("act", eng)

            @block.vector
            def _(eng):
                replay("dve", eng)

            @block.gpsimd
            def _(eng):
                replay("pool", eng)

            @block.sync
            def _(eng):
                replay("sp", eng)

    @contextlib.contextmanager
    def phase(self):
        outer = self.stack
        self.stack = contextlib.ExitStack()
        ph = self.stack
        try:
            yield
            self.stack = outer
            self.emit()
        finally:
            self.stack = outer
            ph.close()


class Ring:
    def __init__(self, slots):
        self.slots = slots
        self.i = 0

    def next(self):
        s = self.slots[self.i % len(self.slots)]
        self.i += 1
        return s


def build_program(NPRE, NMAIN, upto=4):
    nc = bass.Bass("TRN2", target_bir_lowering=False)
    NB = NPRE + NMAIN
    NTOK = NB * BLK
    NMT = NMAIN * BLK
    MAIN0 = NPRE * BLK

    def din(name, shape):
        return nc.dram_tensor(name, list(shape), F32, kind="ExternalInput").ap()

    xT = din("xT", [D, HALO + NTOK])
    xtok = din("xtok", [NMT, D])
    maskd = din("mask", [128, 1])
    consts = din("consts", [128, 448])
    a_w_in = din("a_w_in", [D, 4112])
    a_convT = din("a_convT", [128, 24 * 4])
    a_vec = din("a_vec", [128, 16])
    a_normT = din("a_normT", [128, 1])
    a_w_out = din("a_w_out", [D, D])
    b_w_in = din("b_w_in", [D, 3 * D])
    b_convT = din("b_convT", [128, 8 * 3])
    b_w_out = din("b_w_out", [D, D])
    lng = din("lng", [4, 128, D])
    lnb = din("lnb", [4, 128, D])
    f_w_up = din("f_w_up", [2, D, 2 * DFF])
    f_convT = din("f_convT", [2, 128, FC * 3])
    f_w_down = din("f_w_down", [2, DFF, D])
    out = nc.dram_tensor("out", [NMT - TILE, D], F32, kind="ExternalOutput").ap()

    htok_s = [nc.dram_tensor(f"htok_s{i}", [NMT, D], F32, kind="Internal").ap() for i in range(3)]
    hT_s = [nc.dram_tensor(f"hT_s{i}", [D, HALO + NMT], BF16, kind="Internal").ap() for i in range(3)]

    P = Prog(nc)
    P.setup_sems()
    store_ops = []

    cst = P.sb([128, 448], F32, "cst")
    b_cst = P.buf("cst")
    P.dma("sp", lambda e: e.dma_start(out=cst[:], in_=consts[:, :]), "cst", writes=[b_cst])
    ident = cst[:, 0:128]
    Uincl = cst[0:64, 128:192]
    Lstrict = cst[0:64, 192:256]
    Ustrict = cst[0:64, 256:320]
    ones64 = cst[0:64, 320:448]
    ones1 = cst[:, 320:321]
    identb = P.sb([128, 128], BF16, "identb")
    b_identb = P.buf()
    P.op("dve", lambda e: e.tensor_copy(out=identb[:], in_=cst[:, 0:128]), reads=[b_cst], writes=[b_identb])
    onesb = P.sb([128, 2], BF16, "onesb")
    b_onesb = P.buf()
    P.op("dve", lambda e: e.tensor_copy(out=onesb[:], in_=cst[:, 320:322]), reads=[b_cst], writes=[b_onesb])
    maskt = P.sb([128, 1], F32, "maskt")
    b_mask = P.buf()
    P.dma("sp", lambda e: e.dma_start(out=maskt[:], in_=maskd[:, :]), "maskt", writes=[b_mask])
    zeros = P.sb([128, KC * HALO], BF16, "zeros")
    b_zeros = P.buf()
    P.op("pool", lambda e: e.memset(zeros[:], 0.0), writes=[b_zeros])
    b_hTs = [P.buf(f"hTs{i}") for i in range(3)]
    b_hts = [P.buf(f"htoks{i}") for i in range(3)]
    for i in range(3):
        v = hT_s[i].rearrange("(kc p) t -> p kc t", p=128)
        P.dma("sp", lambda e, v=v: e.dma_start(out=v[:, :, 0:HALO], in_=zeros[:].rearrange("p (k t) -> p k t", k=KC)),
              f"zeros{i}", reads=[b_zeros], pwrites=[b_hTs[i]])

    banks = [P.ps([128, 512], F32, f"bank{i}") for i in range(8)]
    bbank = [P.buf(f"bank{i}") for i in range(8)]
    big = Ring([(banks[i], bbank[i]) for i in (0, 1)])
    big6 = Ring([(banks[i], bbank[i]) for i in (0, 1, 4, 5, 6, 7)])
    wide0 = banks[2]
    wide1 = banks[3]
    b_wide = P.buf("wide")
    small = Ring([((banks[i], 0), bbank[i]) for i in (4, 5, 6, 7)])
    sm64 = md128 = su_ring = sc_ring = small

    gam = P.sb([128, D], F32, "gam")
    bet = P.sb([128, D], F32, "bet")
    b_gam = P.buf()
    b_bet = P.buf()
    hres_r = Ring([(P.sb([128, D], F32, f"hres{i}"), P.buf()) for i in range(1)])
    hTt_r = Ring([(P.sb([128, KC, TILE], BF16, f"hTt{i}"), P.buf()) for i in range(1)])
    lnst = P.sb([128, 16], F32, "lnst")
    b_lnst = P.buf()

    def load_ln(i):
        P.dma("sp", lambda e: e.dma_start(out=gam[:], in_=lng[i]), "gam", writes=[b_gam])
        P.dma("sp", lambda e: e.dma_start(out=bet[:], in_=lnb[i]), "bet", writes=[b_bet])

    def ln_epilogue(ti, src_tok, b_src, dst_i, final):
        r0, r1 = ti * TILE, (ti + 1) * TILE
        hres, b_hres = hres_r.next()
        hn, b_hn = hres, b_hres
        P.dma("sp", lambda e: e.dma_start(out=hres[:], in_=src_tok[r0:r1, :]), "hres0",
              reads=[b_src] if b_src is not None else [], writes=[b_hres])
        for hf, bk in ((0, wide0), (1, wide1)):
            P.op("dve", lambda e, hf=hf, bk=bk: e.scalar_tensor_tensor(
                out=hres[:, hf * 512:(hf + 1) * 512], in0=hres[:, hf * 512:(hf + 1) * 512], scalar=ALPHA,
                in1=bk[:, :], op0=ALU.mult, op1=ALU.add), reads=[b_hres, b_wide], writes=[b_hres])
        for hf in range(2):
            P.op("dve", lambda e, hf=hf: e.bn_stats(out=lnst[:, hf * 6:(hf + 1) * 6], in_=hres[:, hf * 512:(hf + 1) * 512]),
                 reads=[b_hres], pwrites=[b_lnst])
        P.op("dve", lambda e: e.bn_aggr(out=lnst[:, 12:14], in_=lnst[:, 0:12]), reads=[b_lnst], writes=[b_lnst])
        P.op("dve", lambda e: e.tensor_scalar(out=lnst[:, 14:15], in0=lnst[:, 13:14], scalar1=1e-5, scalar2=None, op0=ALU.add),
             reads=[b_lnst], writes=[b_lnst])
        P.op("act", lambda e: e.activation(out=lnst[:, 14:15], in_=lnst[:, 14:15], func=AF.Ln), reads=[b_lnst], writes=[b_lnst])
        P.op("act", lambda e: e.activation(out=lnst[:, 15:16], in_=lnst[:, 14:15], func=AF.Exp, scale=-0.5),
             reads=[b_lnst], writes=[b_lnst])
        P.op("dve", lambda e: e.tensor_scalar(out=hres[:], in0=hres[:], scalar1=lnst[:, 12:13], scalar2=lnst[:, 15:16],
                                              op0=ALU.subtract, op1=ALU.mult), reads=[b_hres, b_lnst], writes=[b_hres])
        P.op("pool", lambda e: e.tensor_tensor(out=hn[:], in0=hn[:], in1=gam[:], op=ALU.mult), reads=[b_hn, b_gam], writes=[b_hn])
        P.op("pool", lambda e: e.tensor_tensor(out=hn[:], in0=hn[:], in1=bet[:], op=ALU.add), reads=[b_hn, b_bet], writes=[b_hn])
        if ti == 0:
            P.op("dve", lambda e: e.tensor_scalar(out=hn[:], in0=hn[:], scalar1=maskt[:, 0:1], scalar2=None, op0=ALU.mult),
                 reads=[b_hn, b_mask], writes=[b_hn])
        if final:
            if ti > 0:
                o = P.dma("sp", lambda e: e.dma_start(out=out[r0 - TILE:r1 - TILE, :], in_=hn[:]), "hres0", reads=[b_hn])
                store_ops.append(o)
            return
        P.dma("sp", lambda e: e.dma_start(out=htok_s[dst_i][r0:r1, :], in_=hn[:]), "hres0",
              reads=[b_hn], pwrites=[b_hts[dst_i]])
        hTt, b_hTt = hTt_r.next()
        for half in range(2):
            bk, b_bk = big.next()
            for q in range(4):
                kc = half * 4 + q
                P.op("pe", lambda e, bk=bk, q=q, kc=kc: e.transpose(out=bk[:, q * 128:(q + 1) * 128], in_=hn[:, kc * 128:(kc + 1) * 128],
                                                                   identity=ident), reads=[b_hn, b_cst], pwrites=[b_bk])
            P.op("act", lambda e, bk=bk, half=half: e.activation(
                out=hTt[:, half * 4:(half + 1) * 4, :], in_=bk[:, :].rearrange("p (k t) -> p k t", k=4), func=AF.Copy),
                reads=[b_bk], pwrites=[b_hTt])
        v = hT_s[dst_i].rearrange("(kc p) t -> p kc t", p=128)
        P.dma("sp", lambda e: e.dma_start(out=v[:, :, HALO + r0:HALO + r1], in_=hTt[:]), "hTt0",
              reads=[b_hTt], pwrites=[b_hTs[dst_i]])

    def load_w_cast(dst, b_dst, src_view, ncols, key):
        nmid = src_view.shape[1]
        table = {}
        for m0 in range(0, nmid, 8):
            m1 = min(nmid, m0 + 8)
            c = 0
            while c < ncols:
                w = min(2048, ncols - c)
                pb = P.buf(f"{key}_{m0}_{c}")
                table[(m0 // 8, c // 2048)] = pb
                P.dma("pool", lambda e, c=c, w=w, m0=m0, m1=m1: e.dma_start(out=dst[:, m0:m1, c:c + w], in_=src_view[:, m0:m1, c:c + w]),
                      f"{key}_{m0}_{c}", writes=[pb])
                c += w
        return lambda mid, col: table[(mid // 8, col // 2048)]

    def make_diag(dg, b_dg, convT_dram, nchunk, ntap, tmp_name):
        ct = P.sb([128, nchunk * ntap], F32, tmp_name)
        b_ct = P.buf()
        P.dma("sp", lambda e: e.dma_start(out=ct[:], in_=convT_dram), tmp_name, writes=[b_ct])
        for m in range(nchunk):
            for j in range(ntap):
                P.op("dve", lambda e, m=m, j=j: e.tensor_scalar(out=dg[:, j, m, :], in0=ident, scalar1=ct[:, m * ntap + j:m * ntap + j + 1],
                                                                scalar2=None, op0=ALU.mult), reads=[b_cst, b_ct], pwrites=[b_dg])

    def phase1():
        Wqkv = P.sb([128, KC, 3072], BF16, "Wqkv")
        Wz = P.sb([128, KC, 1024], BF16, "Wz")
        Wba = P.sb([128, KC, 16], BF16, "Wba")
        Wout = P.sb([128, KC, D], BF16, "Wout")
        dg = P.sb([128, 4, 24, 128], BF16, "dgA")
        b_Wqkv, b_Wz, b_Wba, b_Wout, b_dg = P.buf(), P.buf(), P.buf(), P.buf(), P.buf()
        win_v = a_w_in.rearrange("(kc p) n -> p kc n", p=128)
        wqkv_b = load_w_cast(Wqkv, b_Wqkv, win_v[:, :, 0:3072], 3072, "Wqkv")
        b_Wz = load_w_cast(Wz, b_Wz, win_v[:, :, 3072:4096], 1024, "Wz")(0, 0)
        b_Wba = load_w_cast(Wba, b_Wba, win_v[:, :, 4096:4112], 16, "Wba")(0, 0)
        b_Wout = load_w_cast(Wout, b_Wout, a_w_out.rearrange("(kc p) n -> p kc n", p=128), D, "Wout")(0, 0)
        make_diag(dg, b_dg, a_convT[:, :], 24, 4, "ctA")
        load_ln(0)
        avec = P.sb([128, 16], F32, "avec")
        b_avec = P.buf()
        P.dma("sp", lambda e: e.dma_start(out=avec[:], in_=a_vec[:, :]), "avec", writes=[b_avec])
        negA = P.sb([128, 8], F32, "negA")
        b_negA = P.buf()
        P.op("act", lambda e: e.activation(out=negA[:], in_=avec[:, 0:8], func=AF.Exp), reads=[b_avec], writes=[b_negA])
        P.op("dve", lambda e: e.tensor_scalar(out=negA[:], in0=negA[:], scalar1=-1.0, scalar2=None, op0=ALU.mult),
             reads=[b_negA], writes=[b_negA])
        normT = P.sb([128, 1], F32, "normT")
        b_normT = P.buf()
        P.dma("sp", lambda e: e.dma_start(out=normT[:], in_=a_normT[:, :]), "normT", writes=[b_normT])

        GD = NEU_DT
        S = P.sb([128, 8, 128], F32, "S")
        Sbf = P.sb([128, 8, 128], BF16, "Sbf")
        b_S, b_Sbf = P.buf("S"), P.buf("Sbf")
        P.op("pool", lambda e: e.memset(S[:], 0.0), writes=[b_S])
        P.op("pool", lambda e: e.memset(Sbf[:], 0.0), writes=[b_Sbf])

        xb_r = Ring([(P.sb([128, KC, BLK + HALO], BF16, f"xb{i}"), P.buf()) for i in range(1)])
        pre_r = Ring([(P.sb([128, BLK + HALO], BF16, f"pre{i}"), P.buf()) for i in range(2)])
        qkvT = P.sb([128, 24, BLK], BF16, "qkvT")
        b_qkv = [P.buf(f"qkv{m}") for m in range(24)]
        sq = P.sb([128, 8, BLK], BF16, "sq")
        b_sq = [P.buf(f"sq{m}") for m in range(8)]
        sz = P.sb([64, 4, D], BF16, "sz")
        b_sz = [P.buf(f"sz{c}") for c in range(4)]
        ogT = P.sb([128, 8, TILE], BF16, "ogT")
        b_ogT = P.buf()
        NS = 20
        tsc_r = Ring([(P.sb([128, NS, 8], F32, f"tsc{i}"), P.buf()) for i in range(2)])

        def one(shape, dt, name):
            return P.sb(shape, dt, name), P.buf(name)

        kd_2 = [one([64, 8, 128], BF16, f"kd_all{i}") for i in range(2)]
        vp_2 = [one([64, 8, 128], BF16, f"vp_all{i}") for i in range(2)]
        gdU_all, b_gdU = one([64, 8, 64], F32, "gdU_all")
        E_all, b_E = one([64, 8, 64], F32, "E_all")
        dS_all, b_dS = one([64, 8, 64], F32, "dS_all")
        dI_all, b_dI = E_all, b_E
        QKT_2 = [one([64, 8, 64], BF16, f"QKT_all{i}") for i in range(2)]
        Y_all, b_Y = one([64, 8, 64], GD, "Y_all")
        Z_all, b_Z = one([64, 8, 64], GD, "Z_all")
        P_2 = [one([64, 8, 64], GD, f"P_all{i}") for i in range(2)]
        KK_sb, b_KK = one([64, 8, 64], F32, "KK_sb")
        tR, b_tR = one([64, 8, 128], F32, "tR")
        D1_all, b_D1 = (tR, b_tR) if GD == F32 else one([64, 8, 128], GD, "D1_all")
        vn_all, b_vn = one([64, 8, 128], BF16, "vn_all")
        o_all, b_oall = one([64, 8, 128], F32, "o_all")
        og, b_og = one([64, 8, 128], F32, "og")
        identG = ident if GD == F32 else identb
        b_identG = b_cst if GD == F32 else b_identb

        pairs = Ring([((banks[4], banks[5]), (bbank[4], bbank[5])), ((banks[6], banks[7]), (bbank[6], bbank[7]))])

        def bc_in(ap2, k, n, rows=64):
            return ap2.unsqueeze(2).to_broadcast([rows, k, n])

        def bc_mid(ap2, k, n):
            return ap2.unsqueeze(1).to_broadcast([64, k, n])

        def single():
            (bk, _c0), b = small.next()
            return bk, b

        def mm8(dst_fn, lhs_fn, rhs_fn, reads, bbufs):
            for h in range(8):
                d_, l_, r_ = dst_fn(h), lhs_fn(h), rhs_fn(h)
                P.op("pe", lambda e, d_=d_, l_=l_, r_=r_: e.matmul(d_, lhsT=l_, rhs=r_, start=True, stop=True),
                     reads=reads(h), pwrites=bbufs(h))

        def unpack(ctx):
            return (ctx["bi"], ctx["main"], ctx["xb"], ctx["b_xb"], ctx["c"], ctx["tsc"], ctx["b_tsc"], ctx["par"], ctx["szi"])

        def prep(ctx):
            bi, main, xb, b_xb, c, tsc, b_tsc, par, szi = unpack(ctx)
            kd_all, b_kd = kd_2[par]
            vp_all, b_vp = vp_2[par]
            QKT_all, b_QKT = QKT_2[par]
            P_all, b_P = P_2[par]
            cs = slice(c * CH, (c + 1) * CH)
            if main and c % 2 == 0:
                for cc in (c, c + 1):
                    for kc in range(KC):
                        for hf, bkw in ((0, wide0), (1, wide1)):
                            P.op("pe", lambda e, cc=cc, kc=kc, hf=hf, bkw=bkw, xb=xb: e.matmul(
                                bkw[0:64, :], lhsT=xb[:, kc, HALO + cc * CH:HALO + (cc + 1) * CH],
                                rhs=Wz[:, kc, hf * 512:(hf + 1) * 512], start=(kc == 0), stop=(kc == KC - 1)),
                                reads=[b_xb, b_Wz], pwrites=[b_wide])
                    for hf, bkw in ((0, wide0), (1, wide1)):
                        P.op("act", lambda e, cc=cc, hf=hf, bkw=bkw: e.activation(out=sz[:, szi + cc % 2, hf * 512:(hf + 1) * 512], in_=bkw[0:64, :],
                                                                               func=AF.Silu), reads=[b_wide], pwrites=[b_sz[szi + cc % 2]])
            def T(i, rows=64, tsc=tsc):
                return tsc[0:rows, i, :]

            def tiny(eng, fn):
                P.op(eng, fn, reads=[b_tsc], writes=[b_tsc])
            yield
            kTh = lambda h: qkvT[:, 8 + h, cs]
            vTh = lambda h: qkvT[:, 16 + h, cs]
            qTh = lambda h: qkvT[:, h, cs]
            kk_bk, b_kk = single()
            mm8(lambda h: kk_bk[0:64, h * 64:(h + 1) * 64], kTh, kTh, lambda h: [b_qkv[8 + h]], lambda h: [b_kk])
            bk_sc, b_sc = single()
            ba_ps = bk_sc[0:64, 0:16]
            for kc in range(KC):
                P.op("pe", lambda e, kc=kc, ba_ps=ba_ps, xb=xb, c=c: e.matmul(
                    ba_ps, lhsT=xb[:, kc, HALO + c * CH:HALO + (c + 1) * CH], rhs=Wba[:, kc, :],
                    start=(kc == 0), stop=(kc == KC - 1)), reads=[b_xb, b_Wba], pwrites=[b_sc])
            if main:
                ssq_ps = bk_sc[0:64, 16:24]
                for h in range(8):
                    P.op("pe", lambda e, h=h, ssq_ps=ssq_ps: e.matmul(ssq_ps[:, h:h + 1], lhsT=sq[:, h, cs], rhs=onesb[:, 0:1],
                                                                    start=True, stop=True), reads=[b_sq[h], b_onesb], pwrites=[b_sc])
            P.op("act", lambda e, T=T: e.activation(out=T(0), in_=ba_ps[:, 0:8], func=AF.Exp, scale=-1.0), reads=[b_sc], writes=[b_tsc])
            tiny("dve", lambda e, T=T: e.tensor_scalar(out=T(0), in0=T(0), scalar1=1.0, scalar2=None, op0=ALU.add))
            tiny("dve", lambda e, T=T: e.reciprocal(out=T(0), in_=T(0)))
            P.op("dve", lambda e, T=T: e.tensor_tensor(out=T(1), in0=ba_ps[:, 8:16], in1=avec[0:64, 8:16], op=ALU.add),
                 reads=[b_sc, b_avec, b_tsc], writes=[b_tsc])
            tiny("act", lambda e, T=T: e.activation(out=T(1), in_=T(1), func=AF.Exp))
            tiny("dve", lambda e, T=T: e.tensor_scalar(out=T(1), in0=T(1), scalar1=1.0, scalar2=None, op0=ALU.add))
            tiny("act", lambda e, T=T: e.activation(out=T(1), in_=T(1), func=AF.Ln))
            P.op("dve", lambda e, T=T: e.tensor_tensor(out=T(1), in0=T(1), in1=negA[0:64, :], op=ALU.mult),
                 reads=[b_tsc, b_negA], writes=[b_tsc])
            if main:
                P.op("dve", lambda e, T=T, ssq_ps=ssq_ps: e.tensor_scalar(out=T(14), in0=ssq_ps, scalar1=1e-6, scalar2=None, op0=ALU.add),
                     reads=[b_sc, b_tsc], writes=[b_tsc])
            yield
            P.op("act", lambda e: e.activation(out=KK_sb[:], in_=kk_bk[0:64, :].rearrange("p (h c) -> p h c", h=8), func=AF.Copy),
                 reads=[b_kk], writes=[b_KK])
            P.op("pool", lambda e: e.tensor_tensor(out=dS_all[:], in0=KK_sb[:], in1=bc_mid(cst[0:64, 0:64], 8, 64), op=ALU.mult),
                 reads=[b_KK, b_cst], writes=[b_dS])
            P.op("dve", lambda e, T=T: e.tensor_reduce(out=T(12), in_=dS_all[:], axis=AX.X, op=ALU.add), reads=[b_dS, b_tsc], writes=[b_tsc])
            tiny("dve", lambda e, T=T: e.tensor_scalar(out=T(12), in0=T(12), scalar1=1e-6, scalar2=None, op0=ALU.add))
            tiny("act", lambda e, T=T: e.activation(out=T(12), in_=T(12), func=AF.Ln))
            tiny("act", lambda e, T=T: e.activation(out=T(2), in_=T(12), func=AF.Exp, scale=-0.5))
            tiny("act", lambda e, T=T: e.activation(out=T(3), in_=T(12), func=AF.Exp, scale=0.5))
            tiny("dve", lambda e, T=T: e.tensor_tensor(out=T(9), in0=T(0), in1=T(2), op=ALU.mult))
            tiny("dve", lambda e, T=T: e.scalar_tensor_tensor(out=T(4), in0=T(9), scalar=-1.0, in1=T(2), op0=ALU.mult, op1=ALU.mult))
            yield
            bk_g, b_g = single()
            Gc_ps, Gr_ps, Gl_ps = bk_g[0:64, 0:8], bk_g[0:64, 8:16], bk_g[:, 16:24]
            P.op("pe", lambda e, T=T: e.matmul(Gc_ps, lhsT=Uincl, rhs=T(1), start=True, stop=True), reads=[b_cst, b_tsc], pwrites=[b_g])
            P.op("pe", lambda e, T=T: e.matmul(Gr_ps, lhsT=Lstrict, rhs=T(1), start=True, stop=True), reads=[b_cst, b_tsc], pwrites=[b_g])
            P.op("pe", lambda e, T=T: e.matmul(Gl_ps, lhsT=ones64, rhs=T(1), start=True, stop=True), reads=[b_cst, b_tsc], pwrites=[b_g])
            P.op("act", lambda e, T=T: e.activation(out=T(5), in_=Gc_ps, func=AF.Exp), reads=[b_g], writes=[b_tsc])
            P.op("act", lambda e, T=T: e.activation(out=T(6), in_=Gr_ps, func=AF.Exp), reads=[b_g], writes=[b_tsc])
            P.op("act", lambda e, tsc=tsc: e.activation(out=tsc[:, 7, :], in_=Gl_ps, func=AF.Exp), reads=[b_g], writes=[b_tsc])
            tiny("dve", lambda e, T=T: e.tensor_tensor(out=T(6), in0=T(6), in1=T(2), op=ALU.mult))
            tiny("dve", lambda e, T=T: e.tensor_scalar(out=T(8), in0=T(5), scalar1=-1.0, scalar2=None, op0=ALU.mult))
            if main:
                tiny("act", lambda e, T=T: e.activation(out=T(14), in_=T(14), func=AF.Ln))
                tiny("act", lambda e, T=T: e.activation(out=T(10), in_=T(14), func=AF.Exp, scale=-0.5))
                tiny("dve", lambda e, T=T: e.tensor_scalar(out=T(10), in0=T(10), scalar1=float(128 ** -0.5), scalar2=None, op0=ALU.mult))
            yield
            (pa, pb), (b_pa, b_pb) = pairs.next()
            pk = (pa, pb)
            bpk = (b_pa, b_pb)
            mm8(lambda h: pk[h // 4][0:64, (h % 4) * 128:(h % 4 + 1) * 128], kTh, lambda h: identb[:],
                lambda h: [b_qkv[8 + h], b_identb], lambda h: [bpk[h // 4]])
            for i in range(2):
                P.op("dve", lambda e, i=i, T=T: e.tensor_tensor(
                    out=kd_all[:, 4 * i:4 * i + 4, :], in0=pk[i][0:64, :].rearrange("p (h d) -> p h d", h=4),
                    in1=bc_in(T(6)[:, 4 * i:4 * i + 4], 4, 128), op=ALU.mult), reads=[bpk[i], b_tsc], pwrites=[b_kd])
            yield
            (pa2, pb2), (b_pa2, b_pb2) = pairs.next()
            pv = (pa2, pb2)
            bpv = (b_pa2, b_pb2)
            mm8(lambda h: pv[h // 4][0:64, (h % 4) * 128:(h % 4 + 1) * 128], vTh, lambda h: identb[:],
                lambda h: [b_qkv[16 + h], b_identb], lambda h: [bpv[h // 4]])
            for i in range(2):
                P.op("dve", lambda e, i=i, T=T: e.tensor_tensor(
                    out=vp_all[:, 4 * i:4 * i + 4, :], in0=pv[i][0:64, :].rearrange("p (h d) -> p h d", h=4),
                    in1=bc_in(T(3)[:, 4 * i:4 * i + 4], 4, 128), op=ALU.mult), reads=[bpv[i], b_tsc], pwrites=[b_vp])
            yield
            P.op("pool", lambda e, T=T: e.tensor_tensor(out=gdU_all[:], in0=bc_mid(Uincl, 8, 64), in1=bc_in(T(1), 8, 64), op=ALU.mult),
                 reads=[b_cst, b_tsc], writes=[b_gdU])
            yield
            gd_bk, b_gd = single()
            P.op("pe", lambda e: e.matmul(gd_bk[0:64, :], lhsT=Lstrict, rhs=gdU_all[:].rearrange("p h c -> p (h c)"), start=True, stop=True),
                 reads=[b_cst, b_gdU], pwrites=[b_gd])
            P.op("act", lambda e: e.activation(out=E_all[:], in_=gd_bk[0:64, :].rearrange("p (h c) -> p h c", h=8), func=AF.Exp),
                 reads=[b_gd], writes=[b_E])
            yield
            P.op("pool", lambda e: e.tensor_tensor(out=dS_all[:], in0=E_all[:], in1=bc_mid(Ustrict, 8, 64), op=ALU.mult),
                 reads=[b_E, b_cst], writes=[b_dS])
            P.op("pool", lambda e, T=T: e.tensor_tensor(out=dS_all[:], in0=dS_all[:], in1=bc_in(T(4), 8, 64), op=ALU.mult),
                 reads=[b_dS, b_tsc], writes=[b_dS])
            P.op("pool", lambda e: e.tensor_tensor(out=Y_all[:], in0=KK_sb[:], in1=dS_all[:], op=ALU.mult),
                 reads=[b_KK, b_dS], writes=[b_Y])
            if main:
                P.op("pool", lambda e: e.tensor_tensor(out=dI_all[:], in0=E_all[:], in1=bc_mid(Uincl, 8, 64), op=ALU.mult),
                     reads=[b_E, b_cst], writes=[b_dI])
                P.op("pool", lambda e, T=T: e.tensor_tensor(out=dI_all[:], in0=dI_all[:], in1=bc_in(T(2), 8, 64), op=ALU.mult),
                     reads=[b_dI, b_tsc], writes=[b_dI])
                kq_bk, b_kq = single()
                mm8(lambda h: kq_bk[0:64, h * 64:(h + 1) * 64], kTh, qTh, lambda h: [b_qkv[8 + h], b_qkv[h]], lambda h: [b_kq])
                P.op("dve", lambda e: e.tensor_tensor(out=QKT_all[:], in0=kq_bk[0:64, :].rearrange("p (h c) -> p h c", h=8), in1=dI_all[:], op=ALU.mult),
                     reads=[b_kq, b_dI], writes=[b_QKT])
            yield
            z_bk, b_zb = single()
            mm8(lambda h: z_bk[0:64, h * 64:(h + 1) * 64], lambda h: Y_all[:, h, :], lambda h: identG[0:64, 0:64],
                lambda h: [b_Y, b_identG], lambda h: [b_zb])
            P.op("act", lambda e: e.activation(out=Z_all[:], in_=z_bk[0:64, :].rearrange("p (h c) -> p h c", h=8), func=AF.Copy),
                 reads=[b_zb], writes=[b_Z])
            P.op("pool", lambda e: e.tensor_tensor(out=P_all[:], in0=Y_all[:], in1=bc_mid(cst[0:64, 0:64], 8, 64), op=ALU.add),
                 reads=[b_Y, b_cst], writes=[b_P])
            def squares(lvl):
                zn_bk, b_zn = single()
                mm8(lambda h: zn_bk[0:64, h * 64:(h + 1) * 64], lambda h: Y_all[:, h, :], lambda h: Z_all[:, h, :],
                    lambda h: [b_Y, b_Z], lambda h: [b_zn])
                yn = None
                if lvl < 5:
                    yn_bk, b_yn = single()
                    mm8(lambda h: yn_bk[0:64, h * 64:(h + 1) * 64], lambda h: Z_all[:, h, :], lambda h: Y_all[:, h, :],
                        lambda h: [b_Y, b_Z], lambda h: [b_yn])
                    yn = (yn_bk, b_yn)
                return (zn_bk, b_zn), yn

            def evacs(zn, yn):
                zn_bk, b_zn = zn
                P.op("act", lambda e: e.activation(out=Z_all[:], in_=zn_bk[0:64, :].rearrange("p (h c) -> p h c", h=8), func=AF.Copy),
                     reads=[b_zn], writes=[b_Z])
                if yn is not None:
                    yn_bk, b_yn = yn
                    P.op("act", lambda e: e.activation(out=Y_all[:], in_=yn_bk[0:64, :].rearrange("p (h c) -> p h c", h=8), func=AF.Copy),
                         reads=[b_yn], writes=[b_Y])

            def pupdate():
                pu_bk, b_pu = single()
                mm8(lambda h: pu_bk[0:64, h * 64:(h + 1) * 64], lambda h: Z_all[:, h, :], lambda h: P_all[:, h, :],
                    lambda h: [b_Z, b_P], lambda h: [b_pu])
                P.op("dve", lambda e: e.tensor_tensor(out=P_all[:], in0=pu_bk[0:64, :].rearrange("p (h c) -> p h c", h=8), in1=P_all[:], op=ALU.add),
                     reads=[b_pu, b_P], writes=[b_P])

            zn, yn = squares(1)
            yield
            evacs(zn, yn)
            yield
            for lvl in range(2, 6):
                zn, yn = squares(lvl)
                yield
                pupdate()
                yield
                evacs(zn, yn)
                yield
            pupdate()
            yield

        def scan(ctx):
            bi, main, xb, b_xb, c, tsc, b_tsc, par, szi = unpack(ctx)
            kd_all, b_kd = kd_2[par]
            vp_all, b_vp = vp_2[par]
            QKT_all, b_QKT = QKT_2[par]
            P_all, b_P = P_2[par]
            cs = slice(c * CH, (c + 1) * CH)
            kTh = lambda h: qkvT[:, 8 + h, cs]
            qTh = lambda h: qkvT[:, h, cs]

            def T(i, rows=64, tsc=tsc):
                return tsc[0:rows, i, :]

            def tiny(eng, fn):
                P.op(eng, fn, reads=[b_tsc], writes=[b_tsc])
            (ra, rb), (b_ra, b_rb) = pairs.next()
            pr = (ra, rb)
            bpr = (b_ra, b_rb)
            mm8(lambda h: pr[h // 4][0:64, (h % 4) * 128:(h % 4 + 1) * 128], kTh, lambda h: Sbf[:, h, :],
                lambda h: [b_qkv[8 + h], b_Sbf], lambda h: [bpr[h // 4]])
            for i in range(2):
                P.op("dve", lambda e, i=i, T=T: e.tensor_tensor(
                    out=tR[:, 4 * i:4 * i + 4, :], in0=pr[i][0:64, :].rearrange("p (h d) -> p h d", h=4),
                    in1=bc_in(T(8)[:, 4 * i:4 * i + 4], 4, 128), op=ALU.mult), reads=[bpr[i], b_tsc], pwrites=[b_tR])
            yield
            P.op("pool", lambda e: e.tensor_tensor(out=D1_all[:], in0=tR[:], in1=vp_all[:], op=ALU.add), reads=[b_tR, b_vp], writes=[b_D1])
            yield
            (va, vb), (b_va, b_vb) = pairs.next()
            pvn = (va, vb)
            bpvn = (b_va, b_vb)
            mm8(lambda h: pvn[h // 4][0:64, (h % 4) * 128:(h % 4 + 1) * 128], lambda h: P_all[:, h, :], lambda h: D1_all[:, h, :],
                lambda h: [b_P, b_D1], lambda h: [bpvn[h // 4]])
            for i in range(2):
                P.op("dve", lambda e, i=i, T=T: e.tensor_tensor(
                    out=vn_all[:, 4 * i:4 * i + 4, :], in0=pvn[i][0:64, :].rearrange("p (h d) -> p h d", h=4),
                    in1=bc_in(T(9)[:, 4 * i:4 * i + 4], 4, 128), op=ALU.mult), reads=[bpvn[i], b_tsc], pwrites=[b_vn])
            yield
            if main:
                (qa, qb), (b_qa, b_qb) = pairs.next()
                p1 = (qa, qb)
                bp1 = (b_qa, b_qb)
                mm8(lambda h: p1[h // 4][0:64, (h % 4) * 128:(h % 4 + 1) * 128], qTh, lambda h: Sbf[:, h, :],
                    lambda h: [b_qkv[h], b_Sbf], lambda h: [bp1[h // 4]])
                for i in range(2):
                    P.op("dve", lambda e, i=i, T=T: e.tensor_tensor(
                        out=o_all[:, 4 * i:4 * i + 4, :], in0=p1[i][0:64, :].rearrange("p (h d) -> p h d", h=4),
                        in1=bc_in(T(5)[:, 4 * i:4 * i + 4], 4, 128), op=ALU.mult), reads=[bp1[i], b_tsc], pwrites=[b_oall])
                yield
                (wa, wb), (b_wa, b_wb) = pairs.next()
                p2 = (wa, wb)
                bp2 = (b_wa, b_wb)
                mm8(lambda h: p2[h // 4][0:64, (h % 4) * 128:(h % 4 + 1) * 128], lambda h: QKT_all[:, h, :], lambda h: vn_all[:, h, :],
                    lambda h: [b_QKT, b_vn], lambda h: [bp2[h // 4]])
                for i in range(2):
                    P.op("dve", lambda e, i=i: e.tensor_tensor(
                        out=o_all[:, 4 * i:4 * i + 4, :], in0=p2[i][0:64, :].rearrange("p (h d) -> p h d", h=4),
                        in1=o_all[:, 4 * i:4 * i + 4, :], op=ALU.add), reads=[bp2[i], b_oall], writes=[b_oall])
            yield
            (sa, sb_), (b_sa, b_sb) = pairs.next()
            psu = (sa, sb_)
            bpsu = (b_sa, b_sb)
            mm8(lambda h: psu[h // 4][:, (h % 4) * 128:(h % 4 + 1) * 128], lambda h: kd_all[:, h, :], lambda h: vn_all[:, h, :],
                lambda h: [b_kd, b_vn], lambda h: [bpsu[h // 4]])
            yield
            P.op("pool", lambda e, tsc=tsc: e.tensor_tensor(out=S[:], in0=S[:], in1=bc_in(tsc[:, 7, :], 8, 128, rows=128), op=ALU.mult),
                 reads=[b_S, b_tsc], writes=[b_S])
            for i in range(2):
                P.op("dve", lambda e, i=i: e.tensor_tensor(
                    out=S[:, 4 * i:4 * i + 4, :], in0=psu[i][:, :].rearrange("p (h d) -> p h d", h=4),
                    in1=S[:, 4 * i:4 * i + 4, :], op=ALU.add), reads=[bpsu[i], b_S], writes=[b_S])
            P.op("act", lambda e: e.activation(out=Sbf[:], in_=S[:], func=AF.Copy), reads=[b_S], writes=[b_Sbf])
            if not main:
                return
            yield
            P.op("pool", lambda e: e.tensor_tensor(out=og[:], in0=o_all[:], in1=o_all[:], op=ALU.mult), reads=[b_oall], writes=[b_og])
            P.op("dve", lambda e, T=T: e.tensor_reduce(out=T(13), in_=og[:], axis=AX.X, op=ALU.add), reads=[b_og, b_tsc], writes=[b_tsc])
            tiny("dve", lambda e, T=T: e.tensor_tensor(out=T(13), in0=T(13), in1=T(10), op=ALU.mult))
            tiny("dve", lambda e, T=T: e.tensor_tensor(out=T(13), in0=T(13), in1=T(10), op=ALU.mult))
            tiny("dve", lambda e, T=T: e.tensor_scalar(out=T(13), in0=T(13), scalar1=float(1.0 / 128), scalar2=1e-6, op0=ALU.mult, op1=ALU.add))
            tiny("act", lambda e, T=T: e.activation(out=T(13), in_=T(13), func=AF.Ln))
            tiny("act", lambda e, T=T: e.activation(out=T(13), in_=T(13), func=AF.Exp, scale=-0.5))
            tiny("dve", lambda e, T=T: e.tensor_tensor(out=T(11), in0=T(13), in1=T(10), op=ALU.mult))
            P.op("pool", lambda e, T=T: e.tensor_tensor(out=og[:], in0=o_all[:], in1=bc_in(T(11), 8, 128), op=ALU.mult),
                 reads=[b_oall, b_tsc, b_og], writes=[b_og])
            P.op("pool", lambda e, c=c: e.tensor_tensor(out=og[:], in0=og[:], in1=sz[:, szi + c % 2, :].rearrange("p (h d) -> p h d", h=8), op=ALU.mult),
                 reads=[b_og, b_sz[szi + c % 2]], writes=[b_og])
            yield
            bk, b_bk = big.next()
            for h in range(8):
                P.op("pe", lambda e, bk=bk, h=h: e.transpose(out=bk[:, h * 64:(h + 1) * 64], in_=og[:, h, :], identity=cst[0:64, 0:64]),
                     reads=[b_og, b_cst], pwrites=[b_bk])
            half = c % 2
            P.op("act", lambda e, bk=bk, half=half: e.activation(
                out=ogT[:, :, half * 64:(half + 1) * 64], in_=bk[:, :].rearrange("p (h t) -> p h t", h=8), func=AF.Identity, scale=normT[:, 0:1]),
                reads=[b_bk, b_normT], pwrites=[b_ogT])
            yield
            if half == 1:
                ti = (bi - NPRE) * 3 + c // 2
                for h in range(8):
                    for hf, bkw in ((0, wide0), (1, wide1)):
                        P.op("pe", lambda e, h=h, hf=hf, bkw=bkw: e.matmul(
                            bkw[:, :], lhsT=ogT[:, h, :], rhs=Wout[:, h, hf * 512:(hf + 1) * 512], start=(h == 0), stop=(h == 7)),
                            reads=[b_ogT, b_Wout], pwrites=[b_wide])
                ln_epilogue(ti, xtok, None, 0, final=(upto == 1))

        nchunk = [0]

        def run_interleaved(gens):
            gens = list(gens)
            while gens:
                for g in list(gens):
                    try:
                        next(g)
                    except StopIteration:
                        gens.remove(g)

        for bi in range(NB):
            main = bi >= NPRE
            t0 = bi * BLK
            xb, b_xb = xb_r.next()
            xv = xT.rearrange("(kc p) t -> p kc t", p=128)
            P.dma("pool", lambda e, xb=xb, t0=t0: e.dma_start(out=xb[:], in_=xv[:, :, t0:t0 + BLK + HALO]), "xb0",
                  writes=[b_xb])
            chunks = list(range(24)) if main else list(range(8, 24))

            def conv_stage(m, pre, b_pre):
                bk2, b_bk2 = big.next()
                for j in range(4):
                    P.op("pe", lambda e, bk2=bk2, m=m, j=j, pre=pre: e.matmul(
                        bk2[:, 0:BLK], lhsT=dg[:, j, m, :], rhs=pre[:, j:j + BLK], start=(j == 0), stop=(j == 3)),
                        reads=[b_dg, b_pre], pwrites=[b_bk2])
                P.op("act", lambda e, bk2=bk2, m=m: e.activation(out=qkvT[:, m, :], in_=bk2[:, 0:BLK], func=AF.Silu),
                     reads=[b_bk2], writes=[b_qkv[m]])
                if m < 8:
                    P.op("pool", lambda e, m=m: e.tensor_tensor(out=sq[:, m, :], in0=qkvT[:, m, :], in1=qkvT[:, m, :], op=ALU.mult),
                         reads=[b_qkv[m]], writes=[b_sq[m]])

            prev = None
            for m in chunks:
                bk, b_bk = big.next()
                for kc in range(KC):
                    P.op("pe", lambda e, bk=bk, m=m, kc=kc, xb=xb: e.matmul(
                        bk[:, 0:BLK + HALO], lhsT=Wqkv[:, kc, m * 128:(m + 1) * 128], rhs=xb[:, kc, :],
                        start=(kc == 0), stop=(kc == KC - 1)), reads=[wqkv_b(kc, m * 128), b_xb], pwrites=[b_bk])
                pre, b_pre = pre_r.next()
                P.op("dve", lambda e, pre=pre, bk=bk: e.tensor_copy(out=pre[:], in_=bk[:, 0:BLK + HALO]), reads=[b_bk], writes=[b_pre])
                if prev is not None:
                    conv_stage(*prev)
                prev = (m, pre, b_pre)
            conv_stage(*prev)
            pending = None
            for c in range(6):
                tsc, b_tsc = tsc_r.next()
                nchunk[0] += 1
                ctx = dict(bi=bi, main=main, xb=xb, b_xb=b_xb, c=c, tsc=tsc, b_tsc=b_tsc, par=nchunk[0] % 2,
                           szi=((c // 2) % 2) * 2)
                gens = [prep(ctx)] + ([pending] if pending is not None else [])
                run_interleaved(gens)
                pending = scan(ctx)
            run_interleaved([pending])

    def load_xb(xb, b_xb, src_i, t0, key):
        v = hT_s[src_i].rearrange("(kc p) t -> p kc t", p=128)
        P.dma("sp", lambda e: e.dma_start(out=xb[:], in_=v[:, :, HALO + t0 - 2:HALO + t0 + BLK]), key,
              reads=[b_hTs[src_i]], writes=[b_xb])

    def phase_ffn(li, src_i, dst_i, final):
        Wup = P.sb([128, KC, 2 * DFF], BF16, "Wup")
        Wd = P.sb([128, FC, D], BF16, "Wd")
        dgF = P.sb([128, 3, FC, 128], BF16, "dgF")
        b_Wup, b_Wd, b_dgF = P.buf(), P.buf(), P.buf()
        wup_b = load_w_cast(Wup, b_Wup, f_w_up[li].rearrange("(kc p) n -> p kc n", p=128), 2 * DFF, "Wup")
        wd_b = load_w_cast(Wd, b_Wd, f_w_down[li].rearrange("(m p) n -> p m n", p=128), D, "Wd")
        make_diag(dgF, b_dgF, f_convT[li], FC, 3, "ctF")
        load_ln(1 + 2 * li)
        xb_r = Ring([(P.sb([128, KC, BLK + 2], BF16, f"fxb{i}"), P.buf()) for i in range(1)])
        pre_r = Ring([(P.sb([128, BLK + 2], BF16, f"fpre{i}"), P.buf()) for i in range(2)])
        su_r = Ring([(P.sb([128, BLK], BF16, f"fsu{i}"), P.buf()) for i in range(2)])
        aT = P.sb([128, FC, BLK], BF16, "aT")
        b_aT = P.buf()
        for bi in range(NMAIN):
            t0 = bi * BLK
            xb, b_xb = xb_r.next()
            load_xb(xb, b_xb, src_i, t0, "fxb0")
            for m in range(FC):
                bk, b_bk = big6.next()
                for kc in range(KC):
                    P.op("pe", lambda e, bk=bk, m=m, kc=kc, xb=xb: e.matmul(
                        bk[:, 0:BLK + 2], lhsT=Wup[:, kc, m * 128:(m + 1) * 128], rhs=xb[:, kc, :],
                        start=(kc == 0), stop=(kc == KC - 1)), reads=[wup_b(kc, m * 128), b_xb], pwrites=[b_bk])
                pre, b_pre = pre_r.next()
                P.op("dve", lambda e, pre=pre, bk=bk: e.tensor_copy(out=pre[:], in_=bk[:, 0:BLK + 2]), reads=[b_bk], writes=[b_pre])
                bk3, b_bk3 = big6.next()
                for kc in range(KC):
                    P.op("pe", lambda e, bk3=bk3, m=m, kc=kc, xb=xb: e.matmul(
                        bk3[:, 0:BLK], lhsT=Wup[:, kc, DFF + m * 128:DFF + (m + 1) * 128], rhs=xb[:, kc, 2:BLK + 2],
                        start=(kc == 0), stop=(kc == KC - 1)), reads=[wup_b(kc, DFF + m * 128), b_xb], pwrites=[b_bk3])
                bk2, b_bk2 = big6.next()
                for j in range(3):
                    P.op("pe", lambda e, bk2=bk2, m=m, j=j, pre=pre: e.matmul(
                        bk2[:, 0:BLK], lhsT=dgF[:, j, m, :], rhs=pre[:, j:j + BLK], start=(j == 0), stop=(j == 2)),
                        reads=[b_dgF, b_pre], pwrites=[b_bk2])
                su, b_su = su_r.next()
                P.op("act", lambda e, bk2=bk2, su=su: e.activation(out=su[:], in_=bk2[:, 0:BLK], func=AF.Silu), reads=[b_bk2], writes=[b_su])
                P.op("dve", lambda e, bk3=bk3, su=su, m=m: e.tensor_tensor(out=aT[:, m, :], in0=bk3[:, 0:BLK], in1=su[:], op=ALU.mult),
                     reads=[b_bk3, b_su], pwrites=[b_aT])
            for tl in range(3):
                for m in range(FC):
                    for hf, bkw in ((0, wide0), (1, wide1)):
                        P.op("pe", lambda e, m=m, hf=hf, bkw=bkw, tl=tl: e.matmul(
                            bkw[:, :], lhsT=aT[:, m, tl * TILE:(tl + 1) * TILE], rhs=Wd[:, m, hf * 512:(hf + 1) * 512],
                            start=(m == 0), stop=(m == FC - 1)), reads=[b_aT, wd_b(m, 0)], pwrites=[b_wide])
                ln_epilogue(bi * 3 + tl, htok_s[src_i], b_hts[src_i], dst_i, final)

    def phase_sconv(src_i, dst_i, final):
        Win = P.sb([128, KC, 3 * D], BF16, "Win")
        Wo = P.sb([128, KC, D], BF16, "Wo")
        dgB = P.sb([128, 3, 8, 128], BF16, "dgB")
        b_Win, b_Wo, b_dgB = P.buf(), P.buf(), P.buf()
        win_b = load_w_cast(Win, b_Win, b_w_in.rearrange("(kc p) n -> p kc n", p=128), 3 * D, "Win")
        b_Wo = load_w_cast(Wo, b_Wo, b_w_out.rearrange("(kc p) n -> p kc n", p=128), D, "Wo")(0, 0)
        make_diag(dgB, b_dgB, b_convT[:, :], 8, 3, "ctB")
        load_ln(2)
        xb_r = Ring([(P.sb([128, KC, BLK + 2], BF16, f"sxb{i}"), P.buf()) for i in range(2)])
        c_r = Ring([(P.sb([128, BLK + 2], BF16, f"scs{i}"), P.buf()) for i in range(2)])
        cx_r = Ring([(P.sb([128, BLK + 2], BF16, f"scx{i}"), P.buf()) for i in range(2)])
        bs_r = Ring([(P.sb([128, BLK], BF16, f"sbs{i}"), P.buf()) for i in range(2)])
        vT = P.sb([128, 8, BLK], BF16, "vTs")
        b_vT = P.buf()
        for bi in range(NMAIN):
            t0 = bi * BLK
            xb, b_xb = xb_r.next()
            load_xb(xb, b_xb, src_i, t0, f"sxb{xb_r.i % 2}")
            for m in range(8):
                bk, b_bk = big6.next()
                for kc in range(KC):
                    P.op("pe", lambda e, bk=bk, m=m, kc=kc, xb=xb: e.matmul(
                        bk[:, 0:BLK + 2], lhsT=Win[:, kc, D + m * 128:D + (m + 1) * 128], rhs=xb[:, kc, :],
                        start=(kc == 0), stop=(kc == KC - 1)), reads=[win_b(kc, D + m * 128), b_xb], pwrites=[b_bk])
                cs_, b_cs = c_r.next()
                P.op("act", lambda e, cs_=cs_, bk=bk: e.activation(out=cs_[:], in_=bk[:, 0:BLK + 2], func=AF.Copy), reads=[b_bk], writes=[b_cs])
                bk2, b_bk2 = big6.next()
                for kc in range(KC):
                    P.op("pe", lambda e, bk2=bk2, m=m, kc=kc, xb=xb: e.matmul(
                        bk2[:, 0:BLK + 2], lhsT=Win[:, kc, 2 * D + m * 128:2 * D + (m + 1) * 128], rhs=xb[:, kc, :],
                        start=(kc == 0), stop=(kc == KC - 1)), reads=[win_b(kc, 2 * D + m * 128), b_xb], pwrites=[b_bk2])
                cx, b_cx = cx_r.next()
                P.op("dve", lambda e, cx=cx, bk2=bk2, cs_=cs_: e.tensor_tensor(out=cx[:], in0=bk2[:, 0:BLK + 2], in1=cs_[:], op=ALU.mult),
                     reads=[b_bk2, b_cs], writes=[b_cx])
                bk4, b_bk4 = big6.next()
                for kc in range(KC):
                    P.op("pe", lambda e, bk4=bk4, m=m, kc=kc, xb=xb: e.matmul(
                        bk4[:, 0:BLK], lhsT=Win[:, kc, m * 128:(m + 1) * 128], rhs=xb[:, kc, 2:BLK + 2],
                        start=(kc == 0), stop=(kc == KC - 1)), reads=[win_b(kc, m * 128), b_xb], pwrites=[b_bk4])
                bk3, b_bk3 = big6.next()
                for j in range(3):
                    P.op("pe", lambda e, bk3=bk3, m=m, j=j, cx=cx: e.matmul(
                        bk3[:, 0:BLK], lhsT=dgB[:, j, m, :], rhs=cx[:, j:j + BLK], start=(j == 0), stop=(j == 2)),
                        reads=[b_dgB, b_cx], pwrites=[b_bk3])
                bs, b_bs = bs_r.next()
                P.op("act", lambda e, bs=bs, bk4=bk4: e.activation(out=bs[:], in_=bk4[:, 0:BLK], func=AF.Copy), reads=[b_bk4], writes=[b_bs])
                P.op("dve", lambda e, bk3=bk3, bs=bs, m=m: e.tensor_tensor(out=vT[:, m, :], in0=bk3[:, 0:BLK], in1=bs[:], op=ALU.mult),
                     reads=[b_bk3, b_bs], pwrites=[b_vT])
            for tl in range(3):
                for m in range(8):
                    for hf, bkw in ((0, wide0), (1, wide1)):
                        P.op("pe", lambda e, m=m, hf=hf, bkw=bkw, tl=tl: e.matmul(
                            bkw[:, :], lhsT=vT[:, m, tl * TILE:(tl + 1) * TILE], rhs=Wo[:, m, hf * 512:(hf + 1) * 512],
                            start=(m == 0), stop=(m == 7)), reads=[b_vT, b_Wo], pwrites=[b_wide])
                ln_epilogue(bi * 3 + tl, htok_s[src_i], b_hts[src_i], dst_i, final)

    with P.phase():
        phase1()
    if upto >= 2:
        with P.phase():
            phase_ffn(0, 0, 1, final=(upto == 2))
    if upto >= 3:
        with P.phase():
            phase_sconv(1, 2, final=(upto == 3))
    if upto >= 4:
        with P.phase():
            phase_ffn(1, 2, 0, final=True)
    P.stack.close()
    return nc


def make_consts():
    c = np.zeros((128, 448), np.float32)
    c[:, 0:128] = np.eye(128, dtype=np.float32)
    r = np.arange(64)[:, None]
    q = np.arange(64)[None, :]
    c[0:64, 128:192] = (r <= q)
    c[0:64, 192:256] = (r > q)
    c[0:64, 256:320] = (r < q)
    c[:, 320:448] = 1.0
    return c


def rep128(v):
    v = np.asarray(v, np.float32).reshape(1, -1)
    return np.ascontiguousarray(np.repeat(v, 128, axis=0))


def weight_inputs(a_w_in, a_conv, a_log, a_dt_bias, a_norm, a_w_out, b_w_in, b_conv, b_w_out,
                  ln_mix_g, ln_mix_b, ffn_w_up, ffn_conv, ffn_w_down, ln_ffn_g, ln_ffn_b):
    f = np.float32
    d = {}
    d["consts"] = make_consts()
    d["a_w_in"] = np.ascontiguousarray(a_w_in[0], f)
    d["a_convT"] = np.ascontiguousarray(a_conv[0].T.reshape(24, 128, 4).transpose(1, 0, 2).reshape(128, 96), f)
    d["a_vec"] = np.concatenate([rep128(a_log[0]), rep128(a_dt_bias[0])], axis=1)
    d["a_normT"] = np.ascontiguousarray(a_norm[0].reshape(128, 1), f)
    d["a_w_out"] = np.ascontiguousarray(a_w_out[0], f)
    d["b_w_in"] = np.ascontiguousarray(b_w_in[0], f)
    d["b_convT"] = np.ascontiguousarray(b_conv[0].T.reshape(8, 128, 3).transpose(1, 0, 2).reshape(128, 24), f)
    d["b_w_out"] = np.ascontiguousarray(b_w_out[0], f)
    d["lng"] = np.stack([rep128(ln_mix_g[0]), rep128(ln_ffn_g[0]), rep128(ln_mix_g[1]), rep128(ln_ffn_g[1])])
    d["lnb"] = np.stack([rep128(ln_mix_b[0]), rep128(ln_ffn_b[0]), rep128(ln_mix_b[1]), rep128(ln_ffn_b[1])])
    d["f_w_up"] = np.ascontiguousarray(ffn_w_up, f)
    d["f_convT"] = np.stack([np.ascontiguousarray(ffn_conv[i].T.reshape(FC, 128, 3).transpose(1, 0, 2).reshape(128, FC * 3)) for i in range(2)]).astype(f)
    d["f_w_down"] = np.ascontiguousarray(ffn_w_down, f)
    return d


def core_stream_inputs(stream, valid, NPRE, NMAIN):
    ntok = (NPRE + NMAIN) * BLK
    assert stream.shape[0] == ntok
    xT = np.zeros((D, HALO + ntok), np.float32)
    xT[:, HALO:] = stream.T
    m0 = NPRE * BLK
    return {"xT": xT, "xtok": np.ascontiguousarray(stream[m0:]), "mask": valid[m0:m0 + 128].astype(np.float32).reshape(128, 1)}


_CACHE = {}


def kernel(x, meta, a_w_in, a_conv, a_log, a_dt_bias, a_norm, a_w_out, b_w_in, b_conv, b_w_out,
           ln_mix_g, ln_mix_b, ffn_w_up, ffn_conv, ffn_w_down, ln_ffn_g, ln_ffn_b):
    x = np.asarray(x, np.float32)
    meta = np.asarray(meta, np.float32)
    B, SEQ, _ = x.shape
    NPRE, NMAIN = NPRE_FULL, NMAIN_FULL
    ntok = (NPRE + NMAIN) * BLK
    half_tok = NMAIN * BLK - TILE
    w = weight_inputs(*[np.asarray(a, np.float32) for a in (a_w_in, a_conv, a_log, a_dt_bias, a_norm, a_w_out, b_w_in, b_conv, b_w_out,
                                                            ln_mix_g, ln_mix_b, ffn_w_up, ffn_conv, ffn_w_down, ln_ffn_g, ln_ffn_b)])
    in_maps = []
    for core in range(8):
        b, half = core // 2, core % 2
        stream = np.zeros((ntok, D), np.float32)
        valid = np.zeros((ntok,), bool)
        n_x = half_tok * (half + 1)
        seq = np.concatenate([meta, x[b, :n_x]], axis=0)
        stream[ntok - seq.shape[0]:] = seq
        valid[ntok - seq.shape[0]:] = True
        m = dict(w)
        m.update(core_stream_inputs(stream, valid, NPRE, NMAIN))
        in_maps.append(m)
    if "nc" not in _CACHE:
        _CACHE["nc"] = build_program(NPRE, NMAIN)
    res = run_bass_kernel_spmd(_CACHE["nc"], in_maps, core_ids=list(range(8)))
    outp = np.zeros((B, SEQ, D), np.float32)
    for core in range(8):
        b, half = core // 2, core % 2
        outp[b, half * half_tok:(half + 1) * half_tok] = res.results[core]["out"]
    return outp
```

```python
import contextlib
import numpy as np
import concourse.bass as bass
import concourse.mybir as mybir
from concourse.bass_utils import run_bass_kernel_spmd

F32 = mybir.dt.float32
BF16 = mybir.dt.bfloat16
AF = mybir.ActivationFunctionType
ALU = mybir.AluOpType
AX = mybir.AxisListType

NSEM_PER_ENG = 6
SAME_ENG_SYNC = True
NEU_DT = BF16

D = 1024
KC = 8
TILE = 128
BLK = 384
CH = 64
HALO = 3
DFF = 2816
FC = DFF // 128
ALPHA = float((2.0 * 2) ** 0.25)
NPRE_FULL = 11
NMAIN_FULL = 11


class Buf:
    __slots__ = ("name", "writers", "readers", "open", "pre")

    def __init__(self, name):
        self.name = name
        self.writers = []
        self.readers = []
        self.open = False
        self.pre = []


class Op:
    __slots__ = ("eng", "emit", "deps", "idx", "dma_key", "dma_cnt", "waits")

    def __init__(self, eng, emit):
        self.eng = eng
        self.emit = emit
        self.deps = []
        self.idx = -1
        self.dma_key = None
        self.dma_cnt = 0
        self.waits = []


class Prog:
    ENGS = ("pe", "act", "dve", "pool", "sp")

    def __init__(self, nc):
        self.nc = nc
        self.ops = {e: [] for e in self.ENGS}
        self.ncomp = {e: 0 for e in self.ENGS}
        self.dma_counts = {}
        self.stack = contextlib.ExitStack()
        self.nbuf = 0
        self.nname = 0

    def sb(self, shape, dt, name=None):
        self.nname += 1
        return self.stack.enter_context(self.nc.sbuf_tensor(f"{name or 'sb'}_{self.nname}", list(shape), dt))

    def ps(self, shape, dt=F32, name=None):
        self.nname += 1
        return self.stack.enter_context(self.nc.psum_tensor(name or f"ps{self.nname}", list(shape), dt))

    def buf(self, name=None):
        self.nbuf += 1
        return Buf(name or f"b{self.nbuf}")

    def _track(self, op, reads, writes, pwrites):
        for b in reads:
            op.deps.extend(b.writers)
            b.readers.append(op)
            b.open = False
        for b in writes:
            op.deps.extend(b.readers)
            op.deps.extend(b.writers)
            b.pre = list(b.readers) + list(b.writers)
            b.writers = [op]
            b.readers = []
            b.open = True
        for b in pwrites:
            if b.open:
                op.deps.extend(b.pre)
                b.writers.append(op)
            else:
                op.deps.extend(b.readers)
                op.deps.extend(b.writers)
                b.pre = list(b.readers) + list(b.writers)
                b.writers = [op]
                b.readers = []
                b.open = True

    def op(self, eng, emit, reads=(), writes=(), pwrites=()):
        o = Op(eng, emit)
        self._track(o, reads, writes, pwrites)
        o.idx = self.ncomp[eng]
        self.ncomp[eng] += 1
        self.ops[eng].append(o)
        return o

    def dma(self, eng, emit, key, reads=(), writes=(), pwrites=()):
        o = Op(eng, emit)
        self._track(o, reads, writes, pwrites)
        o.idx = -1
        o.dma_key = key
        self.dma_counts[key] = self.dma_counts.get(key, 0) + 1
        o.dma_cnt = self.dma_counts[key]
        self.ops[eng].append(o)
        return o

    def setup_sems(self):
        nc = self.nc
        st = self.stack
        self.comp_sems = {}
        for e in ("pe", "act", "dve", "pool"):
            self.comp_sems[e] = [st.enter_context(nc.semaphore(f"s_{e}_{i}")) for i in range(NSEM_PER_ENG)]
        self.dma_sems = {}
        self.emitted = {e: 0 for e in self.ENGS}
        self.waited_idx = {e: {x: -1 for x in self.ENGS} for e in self.ENGS}
        self.waited_dma = {e: {} for e in self.ENGS}

    def emit(self):
        nc = self.nc
        comp_sems, dma_sems = self.comp_sems, self.dma_sems
        for k in self.dma_counts:
            if k not in dma_sems:
                dma_sems[k] = self.stack.enter_context(nc.semaphore(f"d_{len(dma_sems)}"))
        new_ops = {e: self.ops[e][self.emitted[e]:] for e in self.ENGS}
        for e in self.ENGS:
            waited_idx = self.waited_idx[e]
            waited_dma = self.waited_dma[e]
            for o in new_ops[e]:
                need_idx = {}
                need_dma = {}
                for d in o.deps:
                    if d is o:
                        continue
                    if d.dma_key is not None:
                        if d.dma_cnt > need_dma.get(d.dma_key, 0):
                            need_dma[d.dma_key] = d.dma_cnt
                    else:
                        if d.eng == e and (e == "pe" or not SAME_ENG_SYNC):
                            continue
                        if d.idx > need_idx.get(d.eng, -1):
                            need_idx[d.eng] = d.idx
                for src, k in need_idx.items():
                    if k > waited_idx[src]:
                        waited_idx[src] = k
                        o.waits.append((comp_sems[src][k % NSEM_PER_ENG], k // NSEM_PER_ENG + 1))
                for key, c in need_dma.items():
                    if c > waited_dma.get(key, 0):
                        waited_dma[key] = c
                        o.waits.append((dma_sems[key], 16 * c))
            self.emitted[e] = len(self.ops[e])
        final = [(dma_sems[k], 16 * c) for k, c in self.dma_counts.items()]

        def replay(ename, eng):
            for o in new_ops[ename]:
                for s, v in o.waits:
                    eng.wait_ge(s, v)
                ins = o.emit(eng)
                if o.dma_key is not None:
                    ins.then_inc(dma_sems[o.dma_key], 16)
                else:
                    ins.then_inc(comp_sems[ename][o.idx % NSEM_PER_ENG], 1)
            if ename == "sp":
                for s, v in final:
                    eng.wait_ge(s, v)

        with nc.Block() as block:
            @block.tensor
            def _(eng):
                replay("pe", eng)

            @block.scalar
            def _(eng):
                replay("act", eng)

            @block.vector
            def _(eng):
                replay("dve", eng)

            @block.gpsimd
            def _(eng):
                replay("pool", eng)

            @block.sync
            def _(eng):
                replay("sp", eng)

    @contextlib.contextmanager
    def phase(self):
        outer = self.stack
        self.stack = contextlib.ExitStack()
        ph = self.stack
        try:
            yield
            self.stack = outer
            self.emit()
        finally:
            self.stack = outer
            ph.close()


class Ring:
    def __init__(self, slots):
        self.slots = slots
        self.i = 0

    def next(self):
        s = self.slots[self.i % len(self.slots)]
        self.i += 1
        return s


def build_program(NPRE, NMAIN, upto=4):
    nc = bass.Bass("TRN2", target_bir_lowering=False)
    NB = NPRE + NMAIN
    NTOK = NB * BLK
    NMT = NMAIN * BLK
    MAIN0 = NPRE * BLK

    def din(name, shape):
        return nc.dram_tensor(name, list(shape), F32, kind="ExternalInput").ap()

    xT = din("xT", [D, HALO + NTOK])
    xtok = din("xtok", [NMT, D])
    maskd = din("mask", [128, 1])
    consts = din("consts", [128, 448])
    a_w_in = din("a_w_in", [D, 4112])
    a_convT = din("a_convT", [128, 24 * 4])
    a_vec = din("a_vec", [128, 16])
    a_normT = din("a_normT", [128, 1])
    a_w_out = din("a_w_out", [D, D])
    b_w_in = din("b_w_in", [D, 3 * D])
    b_convT = din("b_convT", [128, 8 * 3])
    b_w_out = din("b_w_out", [D, D])
    lng = din("lng", [4, 128, D])
    lnb = din("lnb", [4, 128, D])
    f_w_up = din("f_w_up", [2, D, 2 * DFF])
    f_convT = din("f_convT", [2, 128, FC * 3])
    f_w_down = din("f_w_down", [2, DFF, D])
    out = nc.dram_tensor("out", [NMT - TILE, D], F32, kind="ExternalOutput").ap()

    htok_s = [nc.dram_tensor(f"htok_s{i}", [NMT, D], F32, kind="Internal").ap() for i in range(3)]
    hT_s = [nc.dram_tensor(f"hT_s{i}", [D, HALO + NMT], BF16, kind="Internal").ap() for i in range(3)]

    P = Prog(nc)
    P.setup_sems()
    store_ops = []

    cst = P.sb([128, 448], F32, "cst")
    b_cst = P.buf("cst")
    P.dma("sp", lambda e: e.dma_start(out=cst[:], in_=consts[:, :]), "cst", writes=[b_cst])
    ident = cst[:, 0:128]
    Uincl = cst[0:64, 128:192]
    Lstrict = cst[0:64, 192:256]
    Ustrict = cst[0:64, 256:320]
    ones64 = cst[0:64, 320:448]
    ones1 = cst[:, 320:321]
    identb = P.sb([128, 128], BF16, "identb")
    b_identb = P.buf()
    P.op("dve", lambda e: e.tensor_copy(out=identb[:], in_=cst[:, 0:128]), reads=[b_cst], writes=[b_identb])
    onesb = P.sb([128, 2], BF16, "onesb")
    b_onesb = P.buf()
    P.op("dve", lambda e: e.tensor_copy(out=onesb[:], in_=cst[:, 320:322]), reads=[b_cst], writes=[b_onesb])
    maskt = P.sb([128, 1], F32, "maskt")
    b_mask = P.buf()
    P.dma("sp", lambda e: e.dma_start(out=maskt[:], in_=maskd[:, :]), "maskt", writes=[b_mask])
    zeros = P.sb([128, KC * HALO], BF16, "zeros")
    b_zeros = P.buf()
    P.op("pool", lambda e: e.memset(zeros[:], 0.0), writes=[b_zeros])
    b_hTs = [P.buf(f"hTs{i}") for i in range(3)]
    b_hts = [P.buf(f"htoks{i}") for i in range(3)]
    for i in range(3):
        v = hT_s[i].rearrange("(kc p) t -> p kc t", p=128)
        P.dma("sp", lambda e, v=v: e.dma_start(out=v[:, :, 0:HALO], in_=zeros[:].rearrange("p (k t) -> p k t", k=KC)),
              f"zeros{i}", reads=[b_zeros], pwrites=[b_hTs[i]])

    banks = [P.ps([128, 512], F32, f"bank{i}") for i in range(8)]
    bbank = [P.buf(f"bank{i}") for i in range(8)]
    big = Ring([(banks[i], bbank[i]) for i in (0, 1)])
    big6 = Ring([(banks[i], bbank[i]) for i in (0, 1, 4, 5)])
    b_wideB = P.buf("wideB")
    wide0 = banks[2]
    wide1 = banks[3]
    b_wide = P.buf("wide")
    small = Ring([((banks[i], 0), bbank[i]) for i in (4, 5, 6, 7)])
    sm64 = md128 = su_ring = sc_ring = small

    gam = P.sb([128, D], F32, "gam")
    bet = P.sb([128, D], F32, "bet")
    b_gam = P.buf()
    b_bet = P.buf()
    hres_r = Ring([(P.sb([128, D], F32, f"hres{i}"), P.buf()) for i in range(1)])
    hTt_r = Ring([(P.sb([128, KC, TILE], BF16, f"hTt{i}"), P.buf()) for i in range(1)])
    lnst = P.sb([128, 16], F32, "lnst")
    b_lnst = P.buf()

    def load_ln(i):
        P.dma("sp", lambda e: e.dma_start(out=gam[:], in_=lng[i]), "gam", writes=[b_gam])
        P.dma("sp", lambda e: e.dma_start(out=bet[:], in_=lnb[i]), "bet", writes=[b_bet])

    def ln_epilogue(ti, src_tok, b_src, dst_i, final, wide=None, hres_ring=None):
        r0, r1 = ti * TILE, (ti + 1) * TILE
        w0_, w1_, bw_ = wide if wide is not None else (wide0, wide1, b_wide)
        if hres_ring is None:
            hres, b_hres = hres_r.next()
            hkey = "hres0"
        else:
            hres, b_hres, hkey = hres_ring.next()
        hn, b_hn = hres, b_hres
        P.dma("sp", lambda e: e.dma_start(out=hres[:], in_=src_tok[r0:r1, :]), hkey,
              reads=[b_src] if b_src is not None else [], writes=[b_hres])
        for hf, bk in ((0, w0_), (1, w1_)):
            P.op("dve", lambda e, hf=hf, bk=bk: e.scalar_tensor_tensor(
                out=hres[:, hf * 512:(hf + 1) * 512], in0=hres[:, hf * 512:(hf + 1) * 512], scalar=ALPHA,
                in1=bk[:, :], op0=ALU.mult, op1=ALU.add), reads=[b_hres, bw_], writes=[b_hres])
        for hf in range(2):
            P.op("dve", lambda e, hf=hf: e.bn_stats(out=lnst[:, hf * 6:(hf + 1) * 6], in_=hres[:, hf * 512:(hf + 1) * 512]),
                 reads=[b_hres], pwrites=[b_lnst])
        P.op("dve", lambda e: e.bn_aggr(out=lnst[:, 12:14], in_=lnst[:, 0:12]), reads=[b_lnst], writes=[b_lnst])
        P.op("dve", lambda e: e.tensor_scalar(out=lnst[:, 14:15], in0=lnst[:, 13:14], scalar1=1e-5, scalar2=None, op0=ALU.add),
             reads=[b_lnst], writes=[b_lnst])
        P.op("act", lambda e: e.activation(out=lnst[:, 14:15], in_=lnst[:, 14:15], func=AF.Ln), reads=[b_lnst], writes=[b_lnst])
        P.op("act", lambda e: e.activation(out=lnst[:, 15:16], in_=lnst[:, 14:15], func=AF.Exp, scale=-0.5),
             reads=[b_lnst], writes=[b_lnst])
        P.op("dve", lambda e: e.tensor_scalar(out=hres[:], in0=hres[:], scalar1=lnst[:, 12:13], scalar2=lnst[:, 15:16],
                                              op0=ALU.subtract, op1=ALU.mult), reads=[b_hres, b_lnst], writes=[b_hres])
        P.op("pool", lambda e: e.tensor_tensor(out=hn[:], in0=hn[:], in1=gam[:], op=ALU.mult), reads=[b_hn, b_gam], writes=[b_hn])
        P.op("pool", lambda e: e.tensor_tensor(out=hn[:], in0=hn[:], in1=bet[:], op=ALU.add), reads=[b_hn, b_bet], writes=[b_hn])
        if ti == 0:
            P.op("dve", lambda e: e.tensor_scalar(out=hn[:], in0=hn[:], scalar1=maskt[:, 0:1], scalar2=None, op0=ALU.mult),
                 reads=[b_hn, b_mask], writes=[b_hn])
        if final:
            if ti > 0:
                o = P.dma("sp", lambda e: e.dma_start(out=out[r0 - TILE:r1 - TILE, :], in_=hn[:]), hkey, reads=[b_hn])
                store_ops.append(o)
            return
        P.dma("sp", lambda e: e.dma_start(out=htok_s[dst_i][r0:r1, :], in_=hn[:]), hkey,
              reads=[b_hn], pwrites=[b_hts[dst_i]])
        hTt, b_hTt = hTt_r.next()
        for half in range(2):
            bk, b_bk = big.next()
            for q in range(4):
                kc = half * 4 + q
                P.op("pe", lambda e, bk=bk, q=q, kc=kc: e.transpose(out=bk[:, q * 128:(q + 1) * 128], in_=hn[:, kc * 128:(kc + 1) * 128],
                                                                   identity=ident), reads=[b_hn, b_cst], pwrites=[b_bk])
            P.op("act", lambda e, bk=bk, half=half: e.activation(
                out=hTt[:, half * 4:(half + 1) * 4, :], in_=bk[:, :].rearrange("p (k t) -> p k t", k=4), func=AF.Copy),
                reads=[b_bk], pwrites=[b_hTt])
        v = hT_s[dst_i].rearrange("(kc p) t -> p kc t", p=128)
        P.dma("sp", lambda e: e.dma_start(out=v[:, :, HALO + r0:HALO + r1], in_=hTt[:]), "hTt0",
              reads=[b_hTt], pwrites=[b_hTs[dst_i]])

    def load_w_cast(dst, b_dst, src_view, ncols, key):
        nmid = src_view.shape[1]
        table = {}
        for m0 in range(0, nmid, 8):
            m1 = min(nmid, m0 + 8)
            c = 0
            while c < ncols:
                w = min(2048, ncols - c)
                pb = P.buf(f"{key}_{m0}_{c}")
                table[(m0 // 8, c // 2048)] = pb
                P.dma("pool", lambda e, c=c, w=w, m0=m0, m1=m1: e.dma_start(out=dst[:, m0:m1, c:c + w], in_=src_view[:, m0:m1, c:c + w]),
                      f"{key}_{m0}_{c}", writes=[pb])
                c += w
        return lambda mid, col: table[(mid // 8, col // 2048)]

    def make_diag(dg, b_dg, convT_dram, nchunk, ntap, tmp_name):
        ct = P.sb([128, nchunk * ntap], F32, tmp_name)
        b_ct = P.buf()
        P.dma("sp", lambda e: e.dma_start(out=ct[:], in_=convT_dram), tmp_name, writes=[b_ct])
        for m in range(nchunk):
            for j in range(ntap):
                P.op("dve", lambda e, m=m, j=j: e.tensor_scalar(out=dg[:, j, m, :], in0=ident, scalar1=ct[:, m * ntap + j:m * ntap + j + 1],
                                                                scalar2=None, op0=ALU.mult), reads=[b_cst, b_ct], pwrites=[b_dg])

    def phase1():
        Wqkv = P.sb([128, KC, 3072], BF16, "Wqkv")
        Wz = P.sb([128, KC, 1024], BF16, "Wz")
        Wba = P.sb([128, KC, 16], BF16, "Wba")
        Wout = P.sb([128, KC, D], BF16, "Wout")
        dg = P.sb([128, 4, 24, 128], BF16, "dgA")
        b_Wqkv, b_Wz, b_Wba, b_Wout, b_dg = P.buf(), P.buf(), P.buf(), P.buf(), P.buf()
        win_v = a_w_in.rearrange("(kc p) n -> p kc n", p=128)
        wqkv_b = load_w_cast(Wqkv, b_Wqkv, win_v[:, :, 0:3072], 3072, "Wqkv")
        b_Wz = load_w_cast(Wz, b_Wz, win_v[:, :, 3072:4096], 1024, "Wz")(0, 0)
        b_Wba = load_w_cast(Wba, b_Wba, win_v[:, :, 4096:4112], 16, "Wba")(0, 0)
        b_Wout = load_w_cast(Wout, b_Wout, a_w_out.rearrange("(kc p) n -> p kc n", p=128), D, "Wout")(0, 0)
        make_diag(dg, b_dg, a_convT[:, :], 24, 4, "ctA")
        load_ln(0)
        avec = P.sb([128, 16], F32, "avec")
        b_avec = P.buf()
        P.dma("sp", lambda e: e.dma_start(out=avec[:], in_=a_vec[:, :]), "avec", writes=[b_avec])
        negA = P.sb([128, 8], F32, "negA")
        b_negA = P.buf()
        P.op("act", lambda e: e.activation(out=negA[:], in_=avec[:, 0:8], func=AF.Exp), reads=[b_avec], writes=[b_negA])
        P.op("dve", lambda e: e.tensor_scalar(out=negA[:], in0=negA[:], scalar1=-1.0, scalar2=None, op0=ALU.mult),
             reads=[b_negA], writes=[b_negA])
        normT = P.sb([128, 1], F32, "normT")
        b_normT = P.buf()
        P.dma("sp", lambda e: e.dma_start(out=normT[:], in_=a_normT[:, :]), "normT", writes=[b_normT])

        GD = NEU_DT
        S = P.sb([128, 8, 128], F32, "S")
        Sbf = P.sb([128, 8, 128], BF16, "Sbf")
        b_S, b_Sbf = P.buf("S"), P.buf("Sbf")
        P.op("pool", lambda e: e.memset(S[:], 0.0), writes=[b_S])
        P.op("pool", lambda e: e.memset(Sbf[:], 0.0), writes=[b_Sbf])

        xb_r = Ring([(P.sb([128, KC, BLK + HALO], BF16, f"xb{i}"), P.buf()) for i in range(1)])
        pre_r = Ring([(P.sb([128, BLK + HALO], BF16, f"pre{i}"), P.buf()) for i in range(2)])
        qkvT = P.sb([128, 24, BLK], BF16, "qkvT")
        b_qkv = [P.buf(f"qkv{m}") for m in range(24)]
        sq = P.sb([128, 8, BLK], BF16, "sq")
        b_sq = [P.buf(f"sq{m}") for m in range(8)]
        sz = P.sb([64, 4, D], BF16, "sz")
        b_sz = [P.buf(f"sz{c}") for c in range(4)]
        ogT = P.sb([128, 8, TILE], BF16, "ogT")
        b_ogT = P.buf()
        NS = 20
        tsc_r = Ring([(P.sb([128, NS, 8], F32, f"tsc{i}"), P.buf()) for i in range(2)])

        def one(shape, dt, name):
            return P.sb(shape, dt, name), P.buf(name)

        kd_2 = [one([64, 8, 128], BF16, f"kd_all{i}") for i in range(2)]
        vp_2 = [one([64, 8, 128], BF16, f"vp_all{i}") for i in range(2)]
        gdU_all, b_gdU = one([64, 8, 64], F32, "gdU_all")
        E_all, b_E = one([64, 8, 64], F32, "E_all")
        dS_all, b_dS = one([64, 8, 64], F32, "dS_all")
        dI_all, b_dI = E_all, b_E
        QKT_2 = [one([64, 8, 64], BF16, f"QKT_all{i}") for i in range(2)]
        Y_all, b_Y = one([64, 8, 64], GD, "Y_all")
        Z_all, b_Z = one([64, 8, 64], GD, "Z_all")
        P_2 = [one([64, 8, 64], GD, f"P_all{i}") for i in range(2)]
        KK_sb, b_KK = one([64, 8, 64], F32, "KK_sb")
        tR, b_tR = one([64, 8, 128], F32, "tR")
        D1_all, b_D1 = (tR, b_tR) if GD == F32 else one([64, 8, 128], GD, "D1_all")
        vn_all, b_vn = one([64, 8, 128], BF16, "vn_all")
        o_all, b_oall = one([64, 8, 128], F32, "o_all")
        og, b_og = one([64, 8, 128], F32, "og")
        identG = ident if GD == F32 else identb
        b_identG = b_cst if GD == F32 else b_identb

        pairs = Ring([((banks[4], banks[5]), (bbank[4], bbank[5])), ((banks[6], banks[7]), (bbank[6], bbank[7]))])

        def bc_in(ap2, k, n, rows=64):
            return ap2.unsqueeze(2).to_broadcast([rows, k, n])

        def bc_mid(ap2, k, n):
            return ap2.unsqueeze(1).to_broadcast([64, k, n])

        def single():
            (bk, _c0), b = small.next()
            return bk, b

        def mm8(dst_fn, lhs_fn, rhs_fn, reads, bbufs):
            for h in range(8):
                d_, l_, r_ = dst_fn(h), lhs_fn(h), rhs_fn(h)
                P.op("pe", lambda e, d_=d_, l_=l_, r_=r_: e.matmul(d_, lhsT=l_, rhs=r_, start=True, stop=True),
                     reads=reads(h), pwrites=bbufs(h))

        def unpack(ctx):
            return (ctx["bi"], ctx["main"], ctx["xb"], ctx["b_xb"], ctx["c"], ctx["tsc"], ctx["b_tsc"], ctx["par"], ctx["szi"])

        def prep(ctx):
            bi, main, xb, b_xb, c, tsc, b_tsc, par, szi = unpack(ctx)
            kd_all, b_kd = kd_2[par]
            vp_all, b_vp = vp_2[par]
            QKT_all, b_QKT = QKT_2[par]
            P_all, b_P = P_2[par]
            cs = slice(c * CH, (c + 1) * CH)
            if main and c % 2 == 0:
                for cc in (c, c + 1):
                    for kc in range(KC):
                        for hf, bkw in ((0, wide0), (1, wide1)):
                            P.op("pe", lambda e, cc=cc, kc=kc, hf=hf, bkw=bkw, xb=xb: e.matmul(
                                bkw[0:64, :], lhsT=xb[:, kc, HALO + cc * CH:HALO + (cc + 1) * CH],
                                rhs=Wz[:, kc, hf * 512:(hf + 1) * 512], start=(kc == 0), stop=(kc == KC - 1)),
                                reads=[b_xb, b_Wz], pwrites=[b_wide])
                    for hf, bkw in ((0, wide0), (1, wide1)):
                        P.op("act", lambda e, cc=cc, hf=hf, bkw=bkw: e.activation(out=sz[:, szi + cc % 2, hf * 512:(hf + 1) * 512], in_=bkw[0:64, :],
                                                                               func=AF.Silu), reads=[b_wide], pwrites=[b_sz[szi + cc % 2]])
            def T(i, rows=64, tsc=tsc):
                return tsc[0:rows, i, :]

            def tiny(eng, fn):
                P.op(eng, fn, reads=[b_tsc], writes=[b_tsc])
            yield
            kTh = lambda h: qkvT[:, 8 + h, cs]
            vTh = lambda h: qkvT[:, 16 + h, cs]
            qTh = lambda h: qkvT[:, h, cs]
            kk_bk, b_kk = single()
            mm8(lambda h: kk_bk[0:64, h * 64:(h + 1) * 64], kTh, kTh, lambda h: [b_qkv[8 + h]], lambda h: [b_kk])
            bk_sc, b_sc = single()
            ba_ps = bk_sc[0:64, 0:16]
            for kc in range(KC):
                P.op("pe", lambda e, kc=kc, ba_ps=ba_ps, xb=xb, c=c: e.matmul(
                    ba_ps, lhsT=xb[:, kc, HALO + c * CH:HALO + (c + 1) * CH], rhs=Wba[:, kc, :],
                    start=(kc == 0), stop=(kc == KC - 1)), reads=[b_xb, b_Wba], pwrites=[b_sc])
            if main:
                ssq_ps = bk_sc[0:64, 16:24]
                for h in range(8):
                    P.op("pe", lambda e, h=h, ssq_ps=ssq_ps: e.matmul(ssq_ps[:, h:h + 1], lhsT=sq[:, h, cs], rhs=onesb[:, 0:1],
                                                                    start=True, stop=True), reads=[b_sq[h], b_onesb], pwrites=[b_sc])
            P.op("act", lambda e, T=T: e.activation(out=T(0), in_=ba_ps[:, 0:8], func=AF.Exp, scale=-1.0), reads=[b_sc], writes=[b_tsc])
            tiny("dve", lambda e, T=T: e.tensor_scalar(out=T(0), in0=T(0), scalar1=1.0, scalar2=None, op0=ALU.add))
            tiny("dve", lambda e, T=T: e.reciprocal(out=T(0), in_=T(0)))
            P.op("dve", lambda e, T=T: e.tensor_tensor(out=T(1), in0=ba_ps[:, 8:16], in1=avec[0:64, 8:16], op=ALU.add),
                 reads=[b_sc, b_avec, b_tsc], writes=[b_tsc])
            tiny("act", lambda e, T=T: e.activation(out=T(1), in_=T(1), func=AF.Exp))
            tiny("dve", lambda e, T=T: e.tensor_scalar(out=T(1), in0=T(1), scalar1=1.0, scalar2=None, op0=ALU.add))
            tiny("act", lambda e, T=T: e.activation(out=T(1), in_=T(1), func=AF.Ln))
            P.op("dve", lambda e, T=T: e.tensor_tensor(out=T(1), in0=T(1), in1=negA[0:64, :], op=ALU.mult),
                 reads=[b_tsc, b_negA], writes=[b_tsc])
            if main:
                P.op("dve", lambda e, T=T, ssq_ps=ssq_ps: e.tensor_scalar(out=T(14), in0=ssq_ps, scalar1=1e-6, scalar2=None, op0=ALU.add),
                     reads=[b_sc, b_tsc], writes=[b_tsc])
            yield
            P.op("act", lambda e: e.activation(out=KK_sb[:], in_=kk_bk[0:64, :].rearrange("p (h c) -> p h c", h=8), func=AF.Copy),
                 reads=[b_kk], writes=[b_KK])
            P.op("pool", lambda e: e.tensor_tensor(out=dS_all[:], in0=KK_sb[:], in1=bc_mid(cst[0:64, 0:64], 8, 64), op=ALU.mult),
                 reads=[b_KK, b_cst], writes=[b_dS])
            P.op("dve", lambda e, T=T: e.tensor_reduce(out=T(12), in_=dS_all[:], axis=AX.X, op=ALU.add), reads=[b_dS, b_tsc], writes=[b_tsc])
            tiny("dve", lambda e, T=T: e.tensor_scalar(out=T(12), in0=T(12), scalar1=1e-6, scalar2=None, op0=ALU.add))
            tiny("act", lambda e, T=T: e.activation(out=T(12), in_=T(12), func=AF.Ln))
            tiny("act", lambda e, T=T: e.activation(out=T(2), in_=T(12), func=AF.Exp, scale=-0.5))
            tiny("act", lambda e, T=T: e.activation(out=T(3), in_=T(12), func=AF.Exp, scale=0.5))
            tiny("dve", lambda e, T=T: e.tensor_tensor(out=T(9), in0=T(0), in1=T(2), op=ALU.mult))
            tiny("dve", lambda e, T=T: e.scalar_tensor_tensor(out=T(4), in0=T(9), scalar=-1.0, in1=T(2), op0=ALU.mult, op1=ALU.mult))
            yield
            bk_g, b_g = single()
            Gc_ps, Gr_ps, Gl_ps = bk_g[0:64, 0:8], bk_g[0:64, 8:16], bk_g[:, 16:24]
            P.op("pe", lambda e, T=T: e.matmul(Gc_ps, lhsT=Uincl, rhs=T(1), start=True, stop=True), reads=[b_cst, b_tsc], pwrites=[b_g])
            P.op("pe", lambda e, T=T: e.matmul(Gr_ps, lhsT=Lstrict, rhs=T(1), start=True, stop=True), reads=[b_cst, b_tsc], pwrites=[b_g])
            P.op("pe", lambda e, T=T: e.matmul(Gl_ps, lhsT=ones64, rhs=T(1), start=True, stop=True), reads=[b_cst, b_tsc], pwrites=[b_g])
            P.op("act", lambda e, T=T: e.activation(out=T(5), in_=Gc_ps, func=AF.Exp), reads=[b_g], writes=[b_tsc])
            P.op("act", lambda e, T=T: e.activation(out=T(6), in_=Gr_ps, func=AF.Exp), reads=[b_g], writes=[b_tsc])
            P.op("act", lambda e, tsc=tsc: e.activation(out=tsc[:, 7, :], in_=Gl_ps, func=AF.Exp), reads=[b_g], writes=[b_tsc])
            tiny("dve", lambda e, T=T: e.tensor_tensor(out=T(6), in0=T(6), in1=T(2), op=ALU.mult))
            tiny("dve", lambda e, T=T: e.tensor_scalar(out=T(8), in0=T(5), scalar1=-1.0, scalar2=None, op0=ALU.mult))
            if main:
                tiny("act", lambda e, T=T: e.activation(out=T(14), in_=T(14), func=AF.Ln))
                tiny("act", lambda e, T=T: e.activation(out=T(10), in_=T(14), func=AF.Exp, scale=-0.5))
                tiny("dve", lambda e, T=T: e.tensor_scalar(out=T(10), in0=T(10), scalar1=float(128 ** -0.5), scalar2=None, op0=ALU.mult))
            yield
            (pa, pb), (b_pa, b_pb) = pairs.next()
            pk = (pa, pb)
            bpk = (b_pa, b_pb)
            mm8(lambda h: pk[h // 4][0:64, (h % 4) * 128:(h % 4 + 1) * 128], kTh, lambda h: identb[:],
                lambda h: [b_qkv[8 + h], b_identb], lambda h: [bpk[h // 4]])
            for i in range(2):
                P.op("dve", lambda e, i=i, T=T: e.tensor_tensor(
                    out=kd_all[:, 4 * i:4 * i + 4, :], in0=pk[i][0:64, :].rearrange("p (h d) -> p h d", h=4),
                    in1=bc_in(T(6)[:, 4 * i:4 * i + 4], 4, 128), op=ALU.mult), reads=[bpk[i], b_tsc], pwrites=[b_kd])
            yield
            (pa2, pb2), (b_pa2, b_pb2) = pairs.next()
            pv = (pa2, pb2)
            bpv = (b_pa2, b_pb2)
            mm8(lambda h: pv[h // 4][0:64, (h % 4) * 128:(h % 4 + 1) * 128], vTh, lambda h: identb[:],
                lambda h: [b_qkv[16 + h], b_identb], lambda h: [bpv[h // 4]])
            for i in range(2):
                P.op("dve", lambda e, i=i, T=T: e.tensor_tensor(
                    out=vp_all[:, 4 * i:4 * i + 4, :], in0=pv[i][0:64, :].rearrange("p (h d) -> p h d", h=4),
                    in1=bc_in(T(3)[:, 4 * i:4 * i + 4], 4, 128), op=ALU.mult), reads=[bpv[i], b_tsc], pwrites=[b_vp])
            yield
            P.op("pool", lambda e, T=T: e.tensor_tensor(out=gdU_all[:], in0=bc_mid(Uincl, 8, 64), in1=bc_in(T(1), 8, 64), op=ALU.mult),
                 reads=[b_cst, b_tsc], writes=[b_gdU])
            yield
            gd_bk, b_gd = single()
            P.op("pe", lambda e: e.matmul(gd_bk[0:64, :], lhsT=Lstrict, rhs=gdU_all[:].rearrange("p h c -> p (h c)"), start=True, stop=True),
                 reads=[b_cst, b_gdU], pwrites=[b_gd])
            P.op("act", lambda e: e.activation(out=E_all[:], in_=gd_bk[0:64, :].rearrange("p (h c) -> p h c", h=8), func=AF.Exp),
                 reads=[b_gd], writes=[b_E])
            yield
            P.op("pool", lambda e: e.tensor_tensor(out=dS_all[:], in0=E_all[:], in1=bc_mid(Ustrict, 8, 64), op=ALU.mult),
                 reads=[b_E, b_cst], writes=[b_dS])
            P.op("pool", lambda e, T=T: e.tensor_tensor(out=dS_all[:], in0=dS_all[:], in1=bc_in(T(4), 8, 64), op=ALU.mult),
                 reads=[b_dS, b_tsc], writes=[b_dS])
            P.op("pool", lambda e: e.tensor_tensor(out=Y_all[:], in0=KK_sb[:], in1=dS_all[:], op=ALU.mult),
                 reads=[b_KK, b_dS], writes=[b_Y])
            if main:
                P.op("pool", lambda e: e.tensor_tensor(out=dI_all[:], in0=E_all[:], in1=bc_mid(Uincl, 8, 64), op=ALU.mult),
                     reads=[b_E, b_cst], writes=[b_dI])
                P.op("pool", lambda e, T=T: e.tensor_tensor(out=dI_all[:], in0=dI_all[:], in1=bc_in(T(2), 8, 64), op=ALU.mult),
                     reads=[b_dI, b_tsc], writes=[b_dI])
                kq_bk, b_kq = single()
                mm8(lambda h: kq_bk[0:64, h * 64:(h + 1) * 64], kTh, qTh, lambda h: [b_qkv[8 + h], b_qkv[h]], lambda h: [b_kq])
                P.op("dve", lambda e: e.tensor_tensor(out=QKT_all[:], in0=kq_bk[0:64, :].rearrange("p (h c) -> p h c", h=8), in1=dI_all[:], op=ALU.mult),
                     reads=[b_kq, b_dI], writes=[b_QKT])
            yield
            z_bk, b_zb = single()
            mm8(lambda h: z_bk[0:64, h * 64:(h + 1) * 64], lambda h: Y_all[:, h, :], lambda h: identG[0:64, 0:64],
                lambda h: [b_Y, b_identG], lambda h: [b_zb])
            P.op("act", lambda e: e.activation(out=Z_all[:], in_=z_bk[0:64, :].rearrange("p (h c) -> p h c", h=8), func=AF.Copy),
                 reads=[b_zb], writes=[b_Z])
            P.op("pool", lambda e: e.tensor_tensor(out=P_all[:], in0=Y_all[:], in1=bc_mid(cst[0:64, 0:64], 8, 64), op=ALU.add),
                 reads=[b_Y, b_cst], writes=[b_P])
            def squares(lvl):
                zn_bk, b_zn = single()
                mm8(lambda h: zn_bk[0:64, h * 64:(h + 1) * 64], lambda h: Y_all[:, h, :], lambda h: Z_all[:, h, :],
                    lambda h: [b_Y, b_Z], lambda h: [b_zn])
                yn = None
                if lvl < 5:
                    yn_bk, b_yn = single()
                    mm8(lambda h: yn_bk[0:64, h * 64:(h + 1) * 64], lambda h: Z_all[:, h, :], lambda h: Y_all[:, h, :],
                        lambda h: [b_Y, b_Z], lambda h: [b_yn])
                    yn = (yn_bk, b_yn)
                return (zn_bk, b_zn), yn

            def evacs(zn, yn):
                zn_bk, b_zn = zn
                P.op("act", lambda e: e.activation(out=Z_all[:], in_=zn_bk[0:64, :].rearrange("p (h c) -> p h c", h=8), func=AF.Copy),
                     reads=[b_zn], writes=[b_Z])
                if yn is not None:
                    yn_bk, b_yn = yn
                    P.op("act", lambda e: e.activation(out=Y_all[:], in_=yn_bk[0:64, :].rearrange("p (h c) -> p h c", h=8), func=AF.Copy),
                         reads=[b_yn], writes=[b_Y])

            def pupdate():
                pu_bk, b_pu = single()
                mm8(lambda h: pu_bk[0:64, h * 64:(h + 1) * 64], lambda h: Z_all[:, h, :], lambda h: P_all[:, h, :],
                    lambda h: [b_Z, b_P], lambda h: [b_pu])
                P.op("dve", lambda e: e.tensor_tensor(out=P_all[:], in0=pu_bk[0:64, :].rearrange("p (h c) -> p h c", h=8), in1=P_all[:], op=ALU.add),
                     reads=[b_pu, b_P], writes=[b_P])

            zn, yn = squares(1)
            yield
            evacs(zn, yn)
            yield
            for lvl in range(2, 6):
                zn, yn = squares(lvl)
                yield
                pupdate()
                yield
                evacs(zn, yn)
                yield
            pupdate()
            yield

        def scan(ctx):
            bi, main, xb, b_xb, c, tsc, b_tsc, par, szi = unpack(ctx)
            kd_all, b_kd = kd_2[par]
            vp_all, b_vp = vp_2[par]
            QKT_all, b_QKT = QKT_2[par]
            P_all, b_P = P_2[par]
            cs = slice(c * CH, (c + 1) * CH)
            kTh = lambda h: qkvT[:, 8 + h, cs]
            qTh = lambda h: qkvT[:, h, cs]

            def T(i, rows=64, tsc=tsc):
                return tsc[0:rows, i, :]

            def tiny(eng, fn):
                P.op(eng, fn, reads=[b_tsc], writes=[b_tsc])
            (ra, rb), (b_ra, b_rb) = pairs.next()
            pr = (ra, rb)
            bpr = (b_ra, b_rb)
            mm8(lambda h: pr[h // 4][0:64, (h % 4) * 128:(h % 4 + 1) * 128], kTh, lambda h: Sbf[:, h, :],
                lambda h: [b_qkv[8 + h], b_Sbf], lambda h: [bpr[h // 4]])
            for i in range(2):
                P.op("dve", lambda e, i=i, T=T: e.tensor_tensor(
                    out=tR[:, 4 * i:4 * i + 4, :], in0=pr[i][0:64, :].rearrange("p (h d) -> p h d", h=4),
                    in1=bc_in(T(8)[:, 4 * i:4 * i + 4], 4, 128), op=ALU.mult), reads=[bpr[i], b_tsc], pwrites=[b_tR])
            yield
            P.op("pool", lambda e: e.tensor_tensor(out=D1_all[:], in0=tR[:], in1=vp_all[:], op=ALU.add), reads=[b_tR, b_vp], writes=[b_D1])
            yield
            (va, vb), (b_va, b_vb) = pairs.next()
            pvn = (va, vb)
            bpvn = (b_va, b_vb)
            mm8(lambda h: pvn[h // 4][0:64, (h % 4) * 128:(h % 4 + 1) * 128], lambda h: P_all[:, h, :], lambda h: D1_all[:, h, :],
                lambda h: [b_P, b_D1], lambda h: [bpvn[h // 4]])
            for i in range(2):
                P.op("dve", lambda e, i=i, T=T: e.tensor_tensor(
                    out=vn_all[:, 4 * i:4 * i + 4, :], in0=pvn[i][0:64, :].rearrange("p (h d) -> p h d", h=4),
                    in1=bc_in(T(9)[:, 4 * i:4 * i + 4], 4, 128), op=ALU.mult), reads=[bpvn[i], b_tsc], pwrites=[b_vn])
            yield
            if main:
                (qa, qb), (b_qa, b_qb) = pairs.next()
                p1 = (qa, qb)
                bp1 = (b_qa, b_qb)
                mm8(lambda h: p1[h // 4][0:64, (h % 4) * 128:(h % 4 + 1) * 128], qTh, lambda h: Sbf[:, h, :],
                    lambda h: [b_qkv[h], b_Sbf], lambda h: [bp1[h // 4]])
                for i in range(2):
                    P.op("dve", lambda e, i=i, T=T: e.tensor_tensor(
                        out=o_all[:, 4 * i:4 * i + 4, :], in0=p1[i][0:64, :].rearrange("p (h d) -> p h d", h=4),
                        in1=bc_in(T(5)[:, 4 * i:4 * i + 4], 4, 128), op=ALU.mult), reads=[bp1[i], b_tsc], pwrites=[b_oall])
                yield
                (wa, wb), (b_wa, b_wb) = pairs.next()
                p2 = (wa, wb)
                bp2 = (b_wa, b_wb)
                mm8(lambda h: p2[h // 4][0:64, (h % 4) * 128:(h % 4 + 1) * 128], lambda h: QKT_all[:, h, :], lambda h: vn_all[:, h, :],
                    lambda h: [b_QKT, b_vn], lambda h: [bp2[h // 4]])
                for i in range(2):
                    P.op("dve", lambda e, i=i: e.tensor_tensor(
                        out=o_all[:, 4 * i:4 * i + 4, :], in0=p2[i][0:64, :].rearrange("p (h d) -> p h d", h=4),
                        in1=o_all[:, 4 * i:4 * i + 4, :], op=ALU.add), reads=[bp2[i], b_oall], writes=[b_oall])
            yield
            (sa, sb_), (b_sa, b_sb) = pairs.next()
            psu = (sa, sb_)
            bpsu = (b_sa, b_sb)
            mm8(lambda h: psu[h // 4][:, (h % 4) * 128:(h % 4 + 1) * 128], lambda h: kd_all[:, h, :], lambda h: vn_all[:, h, :],
                lambda h: [b_kd, b_vn], lambda h: [bpsu[h // 4]])
            yield
            P.op("pool", lambda e, tsc=tsc: e.tensor_tensor(out=S[:], in0=S[:], in1=bc_in(tsc[:, 7, :], 8, 128, rows=128), op=ALU.mult),
                 reads=[b_S, b_tsc], writes=[b_S])
            for i in range(2):
                P.op("dve", lambda e, i=i: e.tensor_tensor(
                    out=S[:, 4 * i:4 * i + 4, :], in0=psu[i][:, :].rearrange("p (h d) -> p h d", h=4),
                    in1=S[:, 4 * i:4 * i + 4, :], op=ALU.add), reads=[bpsu[i], b_S], writes=[b_S])
            P.op("act", lambda e: e.activation(out=Sbf[:], in_=S[:], func=AF.Copy), reads=[b_S], writes=[b_Sbf])
            if not main:
                return
            yield
            P.op("pool", lambda e: e.tensor_tensor(out=og[:], in0=o_all[:], in1=o_all[:], op=ALU.mult), reads=[b_oall], writes=[b_og])
            P.op("dve", lambda e, T=T: e.tensor_reduce(out=T(13), in_=og[:], axis=AX.X, op=ALU.add), reads=[b_og, b_tsc], writes=[b_tsc])
            tiny("dve", lambda e, T=T: e.tensor_tensor(out=T(13), in0=T(13), in1=T(10), op=ALU.mult))
            tiny("dve", lambda e, T=T: e.tensor_tensor(out=T(13), in0=T(13), in1=T(10), op=ALU.mult))
            tiny("dve", lambda e, T=T: e.tensor_scalar(out=T(13), in0=T(13), scalar1=float(1.0 / 128), scalar2=1e-6, op0=ALU.mult, op1=ALU.add))
            tiny("act", lambda e, T=T: e.activation(out=T(13), in_=T(13), func=AF.Ln))
            tiny("act", lambda e, T=T: e.activation(out=T(13), in_=T(13), func=AF.Exp, scale=-0.5))
            tiny("dve", lambda e, T=T: e.tensor_tensor(out=T(11), in0=T(13), in1=T(10), op=ALU.mult))
            P.op("pool", lambda e, T=T: e.tensor_tensor(out=og[:], in0=o_all[:], in1=bc_in(T(11), 8, 128), op=ALU.mult),
                 reads=[b_oall, b_tsc, b_og], writes=[b_og])
            P.op("pool", lambda e, c=c: e.tensor_tensor(out=og[:], in0=og[:], in1=sz[:, szi + c % 2, :].rearrange("p (h d) -> p h d", h=8), op=ALU.mult),
                 reads=[b_og, b_sz[szi + c % 2]], writes=[b_og])
            yield
            bk, b_bk = big.next()
            for h in range(8):
                P.op("pe", lambda e, bk=bk, h=h: e.transpose(out=bk[:, h * 64:(h + 1) * 64], in_=og[:, h, :], identity=cst[0:64, 0:64]),
                     reads=[b_og, b_cst], pwrites=[b_bk])
            half = c % 2
            P.op("act", lambda e, bk=bk, half=half: e.activation(
                out=ogT[:, :, half * 64:(half + 1) * 64], in_=bk[:, :].rearrange("p (h t) -> p h t", h=8), func=AF.Identity, scale=normT[:, 0:1]),
                reads=[b_bk, b_normT], pwrites=[b_ogT])
            yield
            if half == 1:
                ti = (bi - NPRE) * 3 + c // 2
                for h in range(8):
                    for hf, bkw in ((0, wide0), (1, wide1)):
                        P.op("pe", lambda e, h=h, hf=hf, bkw=bkw: e.matmul(
                            bkw[:, :], lhsT=ogT[:, h, :], rhs=Wout[:, h, hf * 512:(hf + 1) * 512], start=(h == 0), stop=(h == 7)),
                            reads=[b_ogT, b_Wout], pwrites=[b_wide])
                ln_epilogue(ti, xtok, None, 0, final=(upto == 1))

        nchunk = [0]

        def run_interleaved(gens):
            gens = list(gens)
            while gens:
                for g in list(gens):
                    try:
                        next(g)
                    except StopIteration:
                        gens.remove(g)

        for bi in range(NB):
            main = bi >= NPRE
            t0 = bi * BLK
            xb, b_xb = xb_r.next()
            xv = xT.rearrange("(kc p) t -> p kc t", p=128)
            P.dma("pool", lambda e, xb=xb, t0=t0: e.dma_start(out=xb[:], in_=xv[:, :, t0:t0 + BLK + HALO]), "xb0",
                  writes=[b_xb])
            chunks = list(range(24)) if main else list(range(8, 24))

            def conv_stage(m, pre, b_pre):
                bk2, b_bk2 = big.next()
                for j in range(4):
                    P.op("pe", lambda e, bk2=bk2, m=m, j=j, pre=pre: e.matmul(
                        bk2[:, 0:BLK], lhsT=dg[:, j, m, :], rhs=pre[:, j:j + BLK], start=(j == 0), stop=(j == 3)),
                        reads=[b_dg, b_pre], pwrites=[b_bk2])
                P.op("act", lambda e, bk2=bk2, m=m: e.activation(out=qkvT[:, m, :], in_=bk2[:, 0:BLK], func=AF.Silu),
                     reads=[b_bk2], writes=[b_qkv[m]])
                if m < 8:
                    P.op("pool", lambda e, m=m: e.tensor_tensor(out=sq[:, m, :], in0=qkvT[:, m, :], in1=qkvT[:, m, :], op=ALU.mult),
                         reads=[b_qkv[m]], writes=[b_sq[m]])

            prev = None
            for m in chunks:
                bk, b_bk = big.next()
                for kc in range(KC):
                    P.op("pe", lambda e, bk=bk, m=m, kc=kc, xb=xb: e.matmul(
                        bk[:, 0:BLK + HALO], lhsT=Wqkv[:, kc, m * 128:(m + 1) * 128], rhs=xb[:, kc, :],
                        start=(kc == 0), stop=(kc == KC - 1)), reads=[wqkv_b(kc, m * 128), b_xb], pwrites=[b_bk])
                pre, b_pre = pre_r.next()
                P.op("dve", lambda e, pre=pre, bk=bk: e.tensor_copy(out=pre[:], in_=bk[:, 0:BLK + HALO]), reads=[b_bk], writes=[b_pre])
                if prev is not None:
                    conv_stage(*prev)
                prev = (m, pre, b_pre)
            conv_stage(*prev)
            pending = None
            for c in range(6):
                tsc, b_tsc = tsc_r.next()
                nchunk[0] += 1
                ctx = dict(bi=bi, main=main, xb=xb, b_xb=b_xb, c=c, tsc=tsc, b_tsc=b_tsc, par=nchunk[0] % 2,
                           szi=((c // 2) % 2) * 2)
                gens = [prep(ctx)] + ([pending] if pending is not None else [])
                run_interleaved(gens)
                pending = scan(ctx)
            run_interleaved([pending])

    def load_xb(xb, b_xb, src_i, t0, key):
        v = hT_s[src_i].rearrange("(kc p) t -> p kc t", p=128)
        P.dma("sp", lambda e: e.dma_start(out=xb[:], in_=v[:, :, HALO + t0 - 2:HALO + t0 + BLK]), key,
              reads=[b_hTs[src_i]], writes=[b_xb])

    def phase_ffn(li, src_i, dst_i, final):
        Wup = P.sb([128, KC, 2 * DFF], BF16, "Wup")
        Wd = P.sb([128, FC, D], BF16, "Wd")
        dgF = P.sb([128, 3, FC, 128], BF16, "dgF")
        b_Wup, b_Wd, b_dgF = P.buf(), P.buf(), P.buf()
        wup_b = load_w_cast(Wup, b_Wup, f_w_up[li].rearrange("(kc p) n -> p kc n", p=128), 2 * DFF, "Wup")
        wd_b = load_w_cast(Wd, b_Wd, f_w_down[li].rearrange("(m p) n -> p m n", p=128), D, "Wd")
        make_diag(dgF, b_dgF, f_convT[li], FC, 3, "ctF")
        load_ln(1 + 2 * li)
        wide_r = Ring([(wide0, wide1, b_wide), (banks[6], banks[7], b_wideB)])
        hres2_r = Ring([(P.sb([128, D], F32, f"hresX{i}"), P.buf(), f"hresX{i}") for i in range(2)])
        xb_r = Ring([(P.sb([128, KC, BLK + 2], BF16, f"fxb{i}"), P.buf()) for i in range(1)])
        pre_r = Ring([(P.sb([128, BLK + 2], BF16, f"fpre{i}"), P.buf()) for i in range(2)])
        su_r = Ring([(P.sb([128, BLK], BF16, f"fsu{i}"), P.buf()) for i in range(2)])
        aT = P.sb([128, FC, BLK], BF16, "aT")
        b_aT = P.buf()
        for bi in range(NMAIN):
            t0 = bi * BLK
            xb, b_xb = xb_r.next()
            load_xb(xb, b_xb, src_i, t0, "fxb0")
            for m in range(FC):
                bk, b_bk = big6.next()
                for kc in range(KC):
                    P.op("pe", lambda e, bk=bk, m=m, kc=kc, xb=xb: e.matmul(
                        bk[:, 0:BLK + 2], lhsT=Wup[:, kc, m * 128:(m + 1) * 128], rhs=xb[:, kc, :],
                        start=(kc == 0), stop=(kc == KC - 1)), reads=[wup_b(kc, m * 128), b_xb], pwrites=[b_bk])
                pre, b_pre = pre_r.next()
                P.op("dve", lambda e, pre=pre, bk=bk: e.tensor_copy(out=pre[:], in_=bk[:, 0:BLK + 2]), reads=[b_bk], writes=[b_pre])
                bk3, b_bk3 = big6.next()
                for kc in range(KC):
                    P.op("pe", lambda e, bk3=bk3, m=m, kc=kc, xb=xb: e.matmul(
                        bk3[:, 0:BLK], lhsT=Wup[:, kc, DFF + m * 128:DFF + (m + 1) * 128], rhs=xb[:, kc, 2:BLK + 2],
                        start=(kc == 0), stop=(kc == KC - 1)), reads=[wup_b(kc, DFF + m * 128), b_xb], pwrites=[b_bk3])
                bk2, b_bk2 = big6.next()
                for j in range(3):
                    P.op("pe", lambda e, bk2=bk2, m=m, j=j, pre=pre: e.matmul(
                        bk2[:, 0:BLK], lhsT=dgF[:, j, m, :], rhs=pre[:, j:j + BLK], start=(j == 0), stop=(j == 2)),
                        reads=[b_dgF, b_pre], pwrites=[b_bk2])
                su, b_su = su_r.next()
                P.op("act", lambda e, bk2=bk2, su=su: e.activation(out=su[:], in_=bk2[:, 0:BLK], func=AF.Silu), reads=[b_bk2], writes=[b_su])
                P.op("dve", lambda e, bk3=bk3, su=su, m=m: e.tensor_tensor(out=aT[:, m, :], in0=bk3[:, 0:BLK], in1=su[:], op=ALU.mult),
                     reads=[b_bk3, b_su], pwrites=[b_aT])
            for tl in range(3):
                wsel = wide_r.next()
                for m in range(FC):
                    for hf, bkw in ((0, wsel[0]), (1, wsel[1])):
                        P.op("pe", lambda e, m=m, hf=hf, bkw=bkw, tl=tl: e.matmul(
                            bkw[:, :], lhsT=aT[:, m, tl * TILE:(tl + 1) * TILE], rhs=Wd[:, m, hf * 512:(hf + 1) * 512],
                            start=(m == 0), stop=(m == FC - 1)), reads=[b_aT, wd_b(m, 0)], pwrites=[wsel[2]])
                ln_epilogue(bi * 3 + tl, htok_s[src_i], b_hts[src_i], dst_i, final, wide=wsel, hres_ring=hres2_r)

    def phase_sconv(src_i, dst_i, final):
        Win = P.sb([128, KC, 3 * D], BF16, "Win")
        Wo = P.sb([128, KC, D], BF16, "Wo")
        dgB = P.sb([128, 3, 8, 128], BF16, "dgB")
        b_Win, b_Wo, b_dgB = P.buf(), P.buf(), P.buf()
        win_b = load_w_cast(Win, b_Win, b_w_in.rearrange("(kc p) n -> p kc n", p=128), 3 * D, "Win")
        b_Wo = load_w_cast(Wo, b_Wo, b_w_out.rearrange("(kc p) n -> p kc n", p=128), D, "Wo")(0, 0)
        make_diag(dgB, b_dgB, b_convT[:, :], 8, 3, "ctB")
        load_ln(2)
        wide_r = Ring([(wide0, wide1, b_wide), (banks[6], banks[7], b_wideB)])
        hres2_r = Ring([(P.sb([128, D], F32, f"hresX{i}"), P.buf(), f"hresX{i}") for i in range(2)])
        xb_r = Ring([(P.sb([128, KC, BLK + 2], BF16, f"sxb{i}"), P.buf()) for i in range(2)])
        c_r = Ring([(P.sb([128, BLK + 2], BF16, f"scs{i}"), P.buf()) for i in range(2)])
        cx_r = Ring([(P.sb([128, BLK + 2], BF16, f"scx{i}"), P.buf()) for i in range(2)])
        bs_r = Ring([(P.sb([128, BLK], BF16, f"sbs{i}"), P.buf()) for i in range(2)])
        vT = P.sb([128, 8, BLK], BF16, "vTs")
        b_vT = P.buf()
        for bi in range(NMAIN):
            t0 = bi * BLK
            xb, b_xb = xb_r.next()
            load_xb(xb, b_xb, src_i, t0, f"sxb{xb_r.i % 2}")
            for m in range(8):
                bk, b_bk = big6.next()
                for kc in range(KC):
                    P.op("pe", lambda e, bk=bk, m=m, kc=kc, xb=xb: e.matmul(
                        bk[:, 0:BLK + 2], lhsT=Win[:, kc, D + m * 128:D + (m + 1) * 128], rhs=xb[:, kc, :],
                        start=(kc == 0), stop=(kc == KC - 1)), reads=[win_b(kc, D + m * 128), b_xb], pwrites=[b_bk])
                cs_, b_cs = c_r.next()
                P.op("act", lambda e, cs_=cs_, bk=bk: e.activation(out=cs_[:], in_=bk[:, 0:BLK + 2], func=AF.Copy), reads=[b_bk], writes=[b_cs])
                bk2, b_bk2 = big6.next()
                for kc in range(KC):
                    P.op("pe", lambda e, bk2=bk2, m=m, kc=kc, xb=xb: e.matmul(
                        bk2[:, 0:BLK + 2], lhsT=Win[:, kc, 2 * D + m * 128:2 * D + (m + 1) * 128], rhs=xb[:, kc, :],
                        start=(kc == 0), stop=(kc == KC - 1)), reads=[win_b(kc, 2 * D + m * 128), b_xb], pwrites=[b_bk2])
                cx, b_cx = cx_r.next()
                P.op("dve", lambda e, cx=cx, bk2=bk2, cs_=cs_: e.tensor_tensor(out=cx[:], in0=bk2[:, 0:BLK + 2], in1=cs_[:], op=ALU.mult),
                     reads=[b_bk2, b_cs], writes=[b_cx])
                bk4, b_bk4 = big6.next()
                for kc in range(KC):
                    P.op("pe", lambda e, bk4=bk4, m=m, kc=kc, xb=xb: e.matmul(
                        bk4[:, 0:BLK], lhsT=Win[:, kc, m * 128:(m + 1) * 128], rhs=xb[:, kc, 2:BLK + 2],
                        start=(kc == 0), stop=(kc == KC - 1)), reads=[win_b(kc, m * 128), b_xb], pwrites=[b_bk4])
                bk3, b_bk3 = big6.next()
                for j in range(3):
                    P.op("pe", lambda e, bk3=bk3, m=m, j=j, cx=cx: e.matmul(
                        bk3[:, 0:BLK], lhsT=dgB[:, j, m, :], rhs=cx[:, j:j + BLK], start=(j == 0), stop=(j == 2)),
                        reads=[b_dgB, b_cx], pwrites=[b_bk3])
                bs, b_bs = bs_r.next()
                P.op("act", lambda e, bs=bs, bk4=bk4: e.activation(out=bs[:], in_=bk4[:, 0:BLK], func=AF.Copy), reads=[b_bk4], writes=[b_bs])
                P.op("dve", lambda e, bk3=bk3, bs=bs, m=m: e.tensor_tensor(out=vT[:, m, :], in0=bk3[:, 0:BLK], in1=bs[:], op=ALU.mult),
                     reads=[b_bk3, b_bs], pwrites=[b_vT])
            for tl in range(3):
                wsel = wide_r.next()
                for m in range(8):
                    for hf, bkw in ((0, wsel[0]), (1, wsel[1])):
                        P.op("pe", lambda e, m=m, hf=hf, bkw=bkw, tl=tl: e.matmul(
                            bkw[:, :], lhsT=vT[:, m, tl * TILE:(tl + 1) * TILE], rhs=Wo[:, m, hf * 512:(hf + 1) * 512],
                            start=(m == 0), stop=(m == 7)), reads=[b_vT, b_Wo], pwrites=[wsel[2]])
                ln_epilogue(bi * 3 + tl, htok_s[src_i], b_hts[src_i], dst_i, final, wide=wsel, hres_ring=hres2_r)

    with P.phase():
        phase1()
    if upto >= 2:
        with P.phase():
            phase_ffn(0, 0, 1, final=(upto == 2))
    if upto >= 3:
        with P.phase():
            phase_sconv(1, 2, final=(upto == 3))
    if upto >= 4:
        with P.phase():
            phase_ffn(1, 2, 0, final=True)
    P.stack.close()
    return nc


def make_consts():
    c = np.zeros((128, 448), np.float32)
    c[:, 0:128] = np.eye(128, dtype=np.float32)
    r = np.arange(64)[:, None]
    q = np.arange(64)[None, :]
    c[0:64, 128:192] = (r <= q)
    c[0:64, 192:256] = (r > q)
    c[0:64, 256:320] = (r < q)
    c[:, 320:448] = 1.0
    return c


def rep128(v):
    v = np.asarray(v, np.float32).reshape(1, -1)
    return np.ascontiguousarray(np.repeat(v, 128, axis=0))


def weight_inputs(a_w_in, a_conv, a_log, a_dt_bias, a_norm, a_w_out, b_w_in, b_conv, b_w_out,
                  ln_mix_g, ln_mix_b, ffn_w_up, ffn_conv, ffn_w_down, ln_ffn_g, ln_ffn_b):
    f = np.float32
    d = {}
    d["consts"] = make_consts()
    d["a_w_in"] = np.ascontiguousarray(a_w_in[0], f)
    d["a_convT"] = np.ascontiguousarray(a_conv[0].T.reshape(24, 128, 4).transpose(1, 0, 2).reshape(128, 96), f)
    d["a_vec"] = np.concatenate([rep128(a_log[0]), rep128(a_dt_bias[0])], axis=1)
    d["a_normT"] = np.ascontiguousarray(a_norm[0].reshape(128, 1), f)
    d["a_w_out"] = np.ascontiguousarray(a_w_out[0], f)
    d["b_w_in"] = np.ascontiguousarray(b_w_in[0], f)
    d["b_convT"] = np.ascontiguousarray(b_conv[0].T.reshape(8, 128, 3).transpose(1, 0, 2).reshape(128, 24), f)
    d["b_w_out"] = np.ascontiguousarray(b_w_out[0], f)
    d["lng"] = np.stack([rep128(ln_mix_g[0]), rep128(ln_ffn_g[0]), rep128(ln_mix_g[1]), rep128(ln_ffn_g[1])])
    d["lnb"] = np.stack([rep128(ln_mix_b[0]), rep128(ln_ffn_b[0]), rep128(ln_mix_b[1]), rep128(ln_ffn_b[1])])
    d["f_w_up"] = np.ascontiguousarray(ffn_w_up, f)
    d["f_convT"] = np.stack([np.ascontiguousarray(ffn_conv[i].T.reshape(FC, 128, 3).transpose(1, 0, 2).reshape(128, FC * 3)) for i in range(2)]).astype(f)
    d["f_w_down"] = np.ascontiguousarray(ffn_w_down, f)
    return d


def core_stream_inputs(stream, valid, NPRE, NMAIN):
    ntok = (NPRE + NMAIN) * BLK
    assert stream.shape[0] == ntok
    xT = np.zeros((D, HALO + ntok), np.float32)
    xT[:, HALO:] = stream.T
    m0 = NPRE * BLK
    return {"xT": xT, "xtok": np.ascontiguousarray(stream[m0:]), "mask": valid[m0:m0 + 128].astype(np.float32).reshape(128, 1)}


_CACHE = {}


def kernel(x, meta, a_w_in, a_conv, a_log, a_dt_bias, a_norm, a_w_out, b_w_in, b_conv, b_w_out,
           ln_mix_g, ln_mix_b, ffn_w_up, ffn_conv, ffn_w_down, ln_ffn_g, ln_ffn_b):
    x = np.asarray(x, np.float32)
    meta = np.asarray(meta, np.float32)
    B, SEQ, _ = x.shape
    NPRE, NMAIN = NPRE_FULL, NMAIN_FULL
    ntok = (NPRE + NMAIN) * BLK
    half_tok = NMAIN * BLK - TILE
    w = weight_inputs(*[np.asarray(a, np.float32) for a in (a_w_in, a_conv, a_log, a_dt_bias, a_norm, a_w_out, b_w_in, b_conv, b_w_out,
                                                            ln_mix_g, ln_mix_b, ffn_w_up, ffn_conv, ffn_w_down, ln_ffn_g, ln_ffn_b)])
    in_maps = []
    for core in range(8):
        b, half = core // 2, core % 2
        stream = np.zeros((ntok, D), np.float32)
        valid = np.zeros((ntok,), bool)
        n_x = half_tok * (half + 1)
        seq = np.concatenate([meta, x[b, :n_x]], axis=0)
        stream[ntok - seq.shape[0]:] = seq
        valid[ntok - seq.shape[0]:] = True
        m = dict(w)
        m.update(core_stream_inputs(stream, valid, NPRE, NMAIN))
        in_maps.append(m)
    if "nc" not in _CACHE:
        _CACHE["nc"] = build_program(NPRE, NMAIN)
    res = run_bass_kernel_spmd(_CACHE["nc"], in_maps, core_ids=list(range(8)))
    outp = np.zeros((B, SEQ, D), np.float32)
    for core in range(8):
        b, half = core // 2, core % 2
        outp[b, half * half_tok:(half + 1) * half_tok] = res.results[core]["out"]
    return outp
```

```python
import contextlib
import numpy as np
import concourse.bass as bass
import concourse.mybir as mybir
from concourse.bass_utils import run_bass_kernel_spmd

F32 = mybir.dt.float32
BF16 = mybir.dt.bfloat16
AF = mybir.ActivationFunctionType
ALU = mybir.AluOpType
AX = mybir.AxisListType

NSEM_PER_ENG = 6
SAME_ENG_SYNC = True
NEU_DT = BF16

D = 1024
KC = 8
TILE = 128
BLK = 384
CH = 64
HALO = 3
DFF = 2816
FC = DFF // 128
ALPHA = float((2.0 * 2) ** 0.25)
NPRE_FULL = 11
NMAIN_FULL = 11


class Buf:
    __slots__ = ("name", "writers", "readers", "open", "pre")

    def __init__(self, name):
        self.name = name
        self.writers = []
        self.readers = []
        self.open = False
        self.pre = []


class Op:
    __slots__ = ("eng", "emit", "deps", "idx", "dma_key", "dma_cnt", "waits")

    def __init__(self, eng, emit):
        self.eng = eng
        self.emit = emit
        self.deps = []
        self.idx = -1
        self.dma_key = None
        self.dma_cnt = 0
        self.waits = []


class Prog:
    ENGS = ("pe", "act", "dve", "pool", "sp")

    def __init__(self, nc):
        self.nc = nc
        self.ops = {e: [] for e in self.ENGS}
        self.ncomp = {e: 0 for e in self.ENGS}
        self.dma_counts = {}
        self.stack = contextlib.ExitStack()
        self.nbuf = 0
        self.nname = 0

    def sb(self, shape, dt, name=None):
        self.nname += 1
        return self.stack.enter_context(self.nc.sbuf_tensor(f"{name or 'sb'}_{self.nname}", list(shape), dt))

    def ps(self, shape, dt=F32, name=None):
        self.nname += 1
        return self.stack.enter_context(self.nc.psum_tensor(name or f"ps{self.nname}", list(shape), dt))

    def buf(self, name=None):
        self.nbuf += 1
        return Buf(name or f"b{self.nbuf}")

    def _track(self, op, reads, writes, pwrites):
        for b in reads:
            op.deps.extend(b.writers)
            b.readers.append(op)
            b.open = False
        for b in writes:
            op.deps.extend(b.readers)
            op.deps.extend(b.writers)
            b.pre = list(b.readers) + list(b.writers)
            b.writers = [op]
            b.readers = []
            b.open = True
        for b in pwrites:
            if b.open:
                op.deps.extend(b.pre)
                b.writers.append(op)
            else:
                op.deps.extend(b.readers)
                op.deps.extend(b.writers)
                b.pre = list(b.readers) + list(b.writers)
                b.writers = [op]
                b.readers = []
                b.open = True

    def op(self, eng, emit, reads=(), writes=(), pwrites=()):
        o = Op(eng, emit)
        self._track(o, reads, writes, pwrites)
        o.idx = self.ncomp[eng]
        self.ncomp[eng] += 1
        self.ops[eng].append(o)
        return o

    def dma(self, eng, emit, key, reads=(), writes=(), pwrites=()):
        o = Op(eng, emit)
        self._track(o, reads, writes, pwrites)
        o.idx = -1
        o.dma_key = key
        self.dma_counts[key] = self.dma_counts.get(key, 0) + 1
        o.dma_cnt = self.dma_counts[key]
        self.ops[eng].append(o)
        return o

    def setup_sems(self):
        nc = self.nc
        st = self.stack
        self.comp_sems = {}
        for e in ("pe", "act", "dve", "pool"):
            self.comp_sems[e] = [st.enter_context(nc.semaphore(f"s_{e}_{i}")) for i in range(NSEM_PER_ENG)]
        self.dma_sems = {}
        self.emitted = {e: 0 for e in self.ENGS}
        self.waited_idx = {e: {x: -1 for x in self.ENGS} for e in self.ENGS}
        self.waited_dma = {e: {} for e in self.ENGS}

    def emit(self):
        nc = self.nc
        comp_sems, dma_sems = self.comp_sems, self.dma_sems
        for k in self.dma_counts:
            if k not in dma_sems:
                dma_sems[k] = self.stack.enter_context(nc.semaphore(f"d_{len(dma_sems)}"))
        new_ops = {e: self.ops[e][self.emitted[e]:] for e in self.ENGS}
        for e in self.ENGS:
            waited_idx = self.waited_idx[e]
            waited_dma = self.waited_dma[e]
            for o in new_ops[e]:
                need_idx = {}
                need_dma = {}
                for d in o.deps:
                    if d is o:
                        continue
                    if d.dma_key is not None:
                        if d.dma_cnt > need_dma.get(d.dma_key, 0):
                            need_dma[d.dma_key] = d.dma_cnt
                    else:
                        if d.eng == e and (e == "pe" or not SAME_ENG_SYNC):
                            continue
                        if d.idx > need_idx.get(d.eng, -1):
                            need_idx[d.eng] = d.idx
                for src, k in need_idx.items():
                    if k > waited_idx[src]:
                        waited_idx[src] = k
                        o.waits.append((comp_sems[src][k % NSEM_PER_ENG], k // NSEM_PER_ENG + 1))
                for key, c in need_dma.items():
                    if c > waited_dma.get(key, 0):
                        waited_dma[key] = c
                        o.waits.append((dma_sems[key], 16 * c))
            self.emitted[e] = len(self.ops[e])
        final = [(dma_sems[k], 16 * c) for k, c in self.dma_counts.items()]

        def replay(ename, eng):
            for o in new_ops[ename]:
                for s, v in o.waits:
                    eng.wait_ge(s, v)
                ins = o.emit(eng)
                if o.dma_key is not None:
                    ins.then_inc(dma_sems[o.dma_key], 16)
                else:
                    ins.then_inc(comp_sems[ename][o.idx % NSEM_PER_ENG], 1)
            if ename == "sp":
                for s, v in final:
                    eng.wait_ge(s, v)

        with nc.Block() as block:
            @block.tensor
            def _(eng):
                replay("pe", eng)

            @block.scalar
            def _(eng):
                replay("act", eng)

            @block.vector
            def _(eng):
                replay("dve", eng)

            @block.gpsimd
            def _(eng):
                replay("pool", eng)

            @block.sync
            def _(eng):
                replay("sp", eng)

    @contextlib.contextmanager
    def phase(self):
        outer = self.stack
        self.stack = contextlib.ExitStack()
        ph = self.stack
        try:
            yield
            self.stack = outer
            self.emit()
        finally:
            self.stack = outer
            ph.close()


class Ring:
    def __init__(self, slots):
        self.slots = slots
        self.i = 0

    def next(self):
        s = self.slots[self.i % len(self.slots)]
        self.i += 1
        return s


def build_program(NPRE, NMAIN, upto=4):
    nc = bass.Bass("TRN2", target_bir_lowering=False)
    NB = NPRE + NMAIN
    NTOK = NB * BLK
    NMT = NMAIN * BLK
    MAIN0 = NPRE * BLK

    def din(name, shape):
        return nc.dram_tensor(name, list(shape), F32, kind="ExternalInput").ap()

    xT = din("xT", [D, HALO + NTOK])
    xtok = din("xtok", [NMT, D])
    maskd = din("mask", [128, 1])
    consts = din("consts", [128, 448])
    a_w_in = din("a_w_in", [D, 4112])
    a_convT = din("a_convT", [128, 24 * 4])
    a_vec = din("a_vec", [128, 16])
    a_normT = din("a_normT", [128, 1])
    a_w_out = din("a_w_out", [D, D])
    b_w_in = din("b_w_in", [D, 3 * D])
    b_convT = din("b_convT", [128, 8 * 3])
    b_w_out = din("b_w_out", [D, D])
    lng = din("lng", [4, 128, D])
    lnb = din("lnb", [4, 128, D])
    f_w_up = din("f_w_up", [2, D, 2 * DFF])
    f_convT = din("f_convT", [2, 128, FC * 3])
    f_w_down = din("f_w_down", [2, DFF, D])
    out = nc.dram_tensor("out", [NMT - TILE, D], F32, kind="ExternalOutput").ap()

    htok_s = [nc.dram_tensor(f"htok_s{i}", [NMT, D], F32, kind="Internal").ap() for i in range(3)]
    hT_s = [nc.dram_tensor(f"hT_s{i}", [D, HALO + NMT], BF16, kind="Internal").ap() for i in range(3)]

    P = Prog(nc)
    P.setup_sems()
    store_ops = []

    cst = P.sb([128, 448], F32, "cst")
    b_cst = P.buf("cst")
    P.dma("sp", lambda e: e.dma_start(out=cst[:], in_=consts[:, :]), "cst", writes=[b_cst])
    ident = cst[:, 0:128]
    Uincl = cst[0:64, 128:192]
    Lstrict = cst[0:64, 192:256]
    Ustrict = cst[0:64, 256:320]
    ones64 = cst[0:64, 320:448]
    ones1 = cst[:, 320:321]
    identb = P.sb([128, 128], BF16, "identb")
    b_identb = P.buf()
    P.op("dve", lambda e: e.tensor_copy(out=identb[:], in_=cst[:, 0:128]), reads=[b_cst], writes=[b_identb])
    onesb = P.sb([128, 2], BF16, "onesb")
    b_onesb = P.buf()
    P.op("dve", lambda e: e.tensor_copy(out=onesb[:], in_=cst[:, 320:322]), reads=[b_cst], writes=[b_onesb])
    maskt = P.sb([128, 1], F32, "maskt")
    b_mask = P.buf()
    P.dma("sp", lambda e: e.dma_start(out=maskt[:], in_=maskd[:, :]), "maskt", writes=[b_mask])
    zeros = P.sb([128, KC * HALO], BF16, "zeros")
    b_zeros = P.buf()
    P.op("pool", lambda e: e.memset(zeros[:], 0.0), writes=[b_zeros])
    b_hTs = [P.buf(f"hTs{i}") for i in range(3)]
    b_hts = [P.buf(f"htoks{i}") for i in range(3)]
    for i in range(3):
        v = hT_s[i].rearrange("(kc p) t -> p kc t", p=128)
        P.dma("sp", lambda e, v=v: e.dma_start(out=v[:, :, 0:HALO], in_=zeros[:].rearrange("p (k t) -> p k t", k=KC)),
              f"zeros{i}", reads=[b_zeros], pwrites=[b_hTs[i]])

    banks = [P.ps([128, 512], F32, f"bank{i}") for i in range(8)]
    bbank = [P.buf(f"bank{i}") for i in range(8)]
    big = Ring([(banks[i], bbank[i]) for i in (0, 1)])
    big6 = Ring([(banks[i], bbank[i]) for i in (0, 1, 4, 5)])
    b_wideB = P.buf("wideB")
    wide0 = banks[2]
    wide1 = banks[3]
    b_wide = P.buf("wide")
    small = Ring([((banks[i], 0), bbank[i]) for i in (4, 5, 6, 7)])
    sm64 = md128 = su_ring = sc_ring = small

    gam = P.sb([128, D], F32, "gam")
    bet = P.sb([128, D], F32, "bet")
    b_gam = P.buf()
    b_bet = P.buf()
    hres_r = Ring([(P.sb([128, D], F32, f"hres{i}"), P.buf()) for i in range(1)])
    hTt_r = Ring([(P.sb([128, KC, TILE], BF16, f"hTt{i}"), P.buf()) for i in range(1)])
    lnst = P.sb([128, 16], F32, "lnst")
    b_lnst = P.buf()

    def load_ln(i):
        P.dma("sp", lambda e: e.dma_start(out=gam[:], in_=lng[i]), "gam", writes=[b_gam])
        P.dma("sp", lambda e: e.dma_start(out=bet[:], in_=lnb[i]), "bet", writes=[b_bet])

    def ln_epilogue(ti, src_tok, b_src, dst_i, final, wide=None, hres_ring=None):
        r0, r1 = ti * TILE, (ti + 1) * TILE
        w0_, w1_, bw_ = wide if wide is not None else (wide0, wide1, b_wide)
        if hres_ring is None:
            hres, b_hres = hres_r.next()
            hkey = "hres0"
        else:
            hres, b_hres, hkey = hres_ring.next()
        hn, b_hn = hres, b_hres
        P.dma("sp", lambda e: e.dma_start(out=hres[:], in_=src_tok[r0:r1, :]), hkey,
              reads=[b_src] if b_src is not None else [], writes=[b_hres])
        for hf, bk in ((0, w0_), (1, w1_)):
            P.op("dve", lambda e, hf=hf, bk=bk: e.scalar_tensor_tensor(
                out=hres[:, hf * 512:(hf + 1) * 512], in0=hres[:, hf * 512:(hf + 1) * 512], scalar=ALPHA,
                in1=bk[:, :], op0=ALU.mult, op1=ALU.add), reads=[b_hres, bw_], writes=[b_hres])
        for hf in range(2):
            P.op("dve", lambda e, hf=hf: e.bn_stats(out=lnst[:, hf * 6:(hf + 1) * 6], in_=hres[:, hf * 512:(hf + 1) * 512]),
                 reads=[b_hres], pwrites=[b_lnst])
        P.op("dve", lambda e: e.bn_aggr(out=lnst[:, 12:14], in_=lnst[:, 0:12]), reads=[b_lnst], writes=[b_lnst])
        P.op("dve", lambda e: e.tensor_scalar(out=lnst[:, 14:15], in0=lnst[:, 13:14], scalar1=1e-5, scalar2=None, op0=ALU.add),
             reads=[b_lnst], writes=[b_lnst])
        P.op("act", lambda e: e.activation(out=lnst[:, 14:15], in_=lnst[:, 14:15], func=AF.Ln), reads=[b_lnst], writes=[b_lnst])
        P.op("act", lambda e: e.activation(out=lnst[:, 15:16], in_=lnst[:, 14:15], func=AF.Exp, scale=-0.5),
             reads=[b_lnst], writes=[b_lnst])
        P.op("dve", lambda e: e.tensor_scalar(out=hres[:], in0=hres[:], scalar1=lnst[:, 12:13], scalar2=lnst[:, 15:16],
                                              op0=ALU.subtract, op1=ALU.mult), reads=[b_hres, b_lnst], writes=[b_hres])
        P.op("pool", lambda e: e.tensor_tensor(out=hn[:], in0=hn[:], in1=gam[:], op=ALU.mult), reads=[b_hn, b_gam], writes=[b_hn])
        P.op("pool", lambda e: e.tensor_tensor(out=hn[:], in0=hn[:], in1=bet[:], op=ALU.add), reads=[b_hn, b_bet], writes=[b_hn])
        if ti == 0:
            P.op("dve", lambda e: e.tensor_scalar(out=hn[:], in0=hn[:], scalar1=maskt[:, 0:1], scalar2=None, op0=ALU.mult),
                 reads=[b_hn, b_mask], writes=[b_hn])
        if final:
            if ti > 0:
                o = P.dma("sp", lambda e: e.dma_start(out=out[r0 - TILE:r1 - TILE, :], in_=hn[:]), hkey, reads=[b_hn])
                store_ops.append(o)
            return
        P.dma("sp", lambda e: e.dma_start(out=htok_s[dst_i][r0:r1, :], in_=hn[:]), hkey,
              reads=[b_hn], pwrites=[b_hts[dst_i]])
        hTt, b_hTt = hTt_r.next()
        for half in range(2):
            bk, b_bk = big.next()
            for q in range(4):
                kc = half * 4 + q
                P.op("pe", lambda e, bk=bk, q=q, kc=kc: e.transpose(out=bk[:, q * 128:(q + 1) * 128], in_=hn[:, kc * 128:(kc + 1) * 128],
                                                                   identity=ident), reads=[b_hn, b_cst], pwrites=[b_bk])
            P.op("act", lambda e, bk=bk, half=half: e.activation(
                out=hTt[:, half * 4:(half + 1) * 4, :], in_=bk[:, :].rearrange("p (k t) -> p k t", k=4), func=AF.Copy),
                reads=[b_bk], pwrites=[b_hTt])
        v = hT_s[dst_i].rearrange("(kc p) t -> p kc t", p=128)
        P.dma("sp", lambda e: e.dma_start(out=v[:, :, HALO + r0:HALO + r1], in_=hTt[:]), "hTt0",
              reads=[b_hTt], pwrites=[b_hTs[dst_i]])

    def load_w_cast(dst, b_dst, src_view, ncols, key):
        nmid = src_view.shape[1]
        table = {}
        for m0 in range(0, nmid, 8):
            m1 = min(nmid, m0 + 8)
            c = 0
            while c < ncols:
                w = min(2048, ncols - c)
                pb = P.buf(f"{key}_{m0}_{c}")
                table[(m0 // 8, c // 2048)] = pb
                P.dma("pool", lambda e, c=c, w=w, m0=m0, m1=m1: e.dma_start(out=dst[:, m0:m1, c:c + w], in_=src_view[:, m0:m1, c:c + w]),
                      f"{key}_{m0}_{c}", writes=[pb])
                c += w
        return lambda mid, col: table[(mid // 8, col // 2048)]

    def make_diag(dg, b_dg, convT_dram, nchunk, ntap, tmp_name):
        ct = P.sb([128, nchunk * ntap], F32, tmp_name)
        b_ct = P.buf()
        P.dma("sp", lambda e: e.dma_start(out=ct[:], in_=convT_dram), tmp_name, writes=[b_ct])
        for m in range(nchunk):
            for j in range(ntap):
                P.op("dve", lambda e, m=m, j=j: e.tensor_scalar(out=dg[:, j, m, :], in0=ident, scalar1=ct[:, m * ntap + j:m * ntap + j + 1],
                                                                scalar2=None, op0=ALU.mult), reads=[b_cst, b_ct], pwrites=[b_dg])

    def phase1():
        Wqkv = P.sb([128, KC, 3072], BF16, "Wqkv")
        Wz = P.sb([128, KC, 1024], BF16, "Wz")
        Wba = P.sb([128, KC, 16], BF16, "Wba")
        Wout = P.sb([128, KC, D], BF16, "Wout")
        dg = P.sb([128, 4, 24, 128], BF16, "dgA")
        b_Wqkv, b_Wz, b_Wba, b_Wout, b_dg = P.buf(), P.buf(), P.buf(), P.buf(), P.buf()
        win_v = a_w_in.rearrange("(kc p) n -> p kc n", p=128)
        wqkv_b = load_w_cast(Wqkv, b_Wqkv, win_v[:, :, 0:3072], 3072, "Wqkv")
        b_Wz = load_w_cast(Wz, b_Wz, win_v[:, :, 3072:4096], 1024, "Wz")(0, 0)
        b_Wba = load_w_cast(Wba, b_Wba, win_v[:, :, 4096:4112], 16, "Wba")(0, 0)
        b_Wout = load_w_cast(Wout, b_Wout, a_w_out.rearrange("(kc p) n -> p kc n", p=128), D, "Wout")(0, 0)
        make_diag(dg, b_dg, a_convT[:, :], 24, 4, "ctA")
        load_ln(0)
        avec = P.sb([128, 16], F32, "avec")
        b_avec = P.buf()
        P.dma("sp", lambda e: e.dma_start(out=avec[:], in_=a_vec[:, :]), "avec", writes=[b_avec])
        negA = P.sb([128, 8], F32, "negA")
        b_negA = P.buf()
        P.op("act", lambda e: e.activation(out=negA[:], in_=avec[:, 0:8], func=AF.Exp), reads=[b_avec], writes=[b_negA])
        P.op("dve", lambda e: e.tensor_scalar(out=negA[:], in0=negA[:], scalar1=-1.0, scalar2=None, op0=ALU.mult),
             reads=[b_negA], writes=[b_negA])
        normT = P.sb([128, 1], F32, "normT")
        b_normT = P.buf()
        P.dma("sp", lambda e: e.dma_start(out=normT[:], in_=a_normT[:, :]), "normT", writes=[b_normT])

        GD = NEU_DT
        S = P.sb([128, 8, 128], F32, "S")
        Sbf = P.sb([128, 8, 128], BF16, "Sbf")
        b_S, b_Sbf = P.buf("S"), P.buf("Sbf")
        P.op("pool", lambda e: e.memset(S[:], 0.0), writes=[b_S])
        P.op("pool", lambda e: e.memset(Sbf[:], 0.0), writes=[b_Sbf])

        xb_r = Ring([(P.sb([128, KC, BLK + HALO], BF16, f"xb{i}"), P.buf()) for i in range(1)])
        pre_r = Ring([(P.sb([128, BLK + HALO], BF16, f"pre{i}"), P.buf()) for i in range(2)])
        qkvT = P.sb([128, 24, BLK], BF16, "qkvT")
        b_qkv = [P.buf(f"qkv{m}") for m in range(24)]
        sq = P.sb([128, 8, BLK], BF16, "sq")
        b_sq = [P.buf(f"sq{m}") for m in range(8)]
        sz = P.sb([64, 4, D], BF16, "sz")
        b_sz = [P.buf(f"sz{c}") for c in range(4)]
        ogT = P.sb([128, 8, TILE], BF16, "ogT")
        b_ogT = P.buf()
        NS = 20
        tsc_r = Ring([(P.sb([128, NS, 8], F32, f"tsc{i}"), P.buf()) for i in range(2)])

        def one(shape, dt, name):
            return P.sb(shape, dt, name), P.buf(name)

        kd_2 = [one([64, 8, 128], BF16, f"kd_all{i}") for i in range(2)]
        vp_2 = [one([64, 8, 128], BF16, f"vp_all{i}") for i in range(2)]
        gdU_all, b_gdU = one([64, 8, 64], F32, "gdU_all")
        E_all, b_E = one([64, 8, 64], F32, "E_all")
        dS_all, b_dS = one([64, 8, 64], F32, "dS_all")
        dI_all, b_dI = E_all, b_E
        QKT_2 = [one([64, 8, 64], BF16, f"QKT_all{i}") for i in range(2)]
        Y_all, b_Y = one([64, 8, 64], GD, "Y_all")
        Z_all, b_Z = one([64, 8, 64], GD, "Z_all")
        P_2 = [one([64, 8, 64], GD, f"P_all{i}") for i in range(2)]
        KK_sb, b_KK = one([64, 8, 64], F32, "KK_sb")
        tR, b_tR = one([64, 8, 128], F32, "tR")
        D1_all, b_D1 = (tR, b_tR) if GD == F32 else one([64, 8, 128], GD, "D1_all")
        vn_all, b_vn = one([64, 8, 128], BF16, "vn_all")
        o_all, b_oall = one([64, 8, 128], F32, "o_all")
        og, b_og = one([64, 8, 128], F32, "og")
        identG = ident if GD == F32 else identb
        b_identG = b_cst if GD == F32 else b_identb

        pairs = Ring([((banks[4], banks[5]), (bbank[4], bbank[5])), ((banks[6], banks[7]), (bbank[6], bbank[7]))])

        def bc_in(ap2, k, n, rows=64):
            return ap2.unsqueeze(2).to_broadcast([rows, k, n])

        def bc_mid(ap2, k, n):
            return ap2.unsqueeze(1).to_broadcast([64, k, n])

        def single():
            (bk, _c0), b = small.next()
            return bk, b

        def mm8(dst_fn, lhs_fn, rhs_fn, reads, bbufs):
            for h in range(8):
                d_, l_, r_ = dst_fn(h), lhs_fn(h), rhs_fn(h)
                P.op("pe", lambda e, d_=d_, l_=l_, r_=r_: e.matmul(d_, lhsT=l_, rhs=r_, start=True, stop=True),
                     reads=reads(h), pwrites=bbufs(h))

        def unpack(ctx):
            return (ctx["bi"], ctx["main"], ctx["xb"], ctx["b_xb"], ctx["c"], ctx["tsc"], ctx["b_tsc"], ctx["par"], ctx["szi"])

        def prep(ctx):
            bi, main, xb, b_xb, c, tsc, b_tsc, par, szi = unpack(ctx)
            kd_all, b_kd = kd_2[par]
            vp_all, b_vp = vp_2[par]
            QKT_all, b_QKT = QKT_2[par]
            P_all, b_P = P_2[par]
            cs = slice(c * CH, (c + 1) * CH)
            if main and c % 2 == 0:
                for cc in (c, c + 1):
                    for kc in range(KC):
                        for hf, bkw in ((0, wide0), (1, wide1)):
                            P.op("pe", lambda e, cc=cc, kc=kc, hf=hf, bkw=bkw, xb=xb: e.matmul(
                                bkw[0:64, :], lhsT=xb[:, kc, HALO + cc * CH:HALO + (cc + 1) * CH],
                                rhs=Wz[:, kc, hf * 512:(hf + 1) * 512], start=(kc == 0), stop=(kc == KC - 1)),
                                reads=[b_xb, b_Wz], pwrites=[b_wide])
                    for hf, bkw in ((0, wide0), (1, wide1)):
                        P.op("act", lambda e, cc=cc, hf=hf, bkw=bkw: e.activation(out=sz[:, szi + cc % 2, hf * 512:(hf + 1) * 512], in_=bkw[0:64, :],
                                                                               func=AF.Silu), reads=[b_wide], pwrites=[b_sz[szi + cc % 2]])
            def T(i, rows=64, tsc=tsc):
                return tsc[0:rows, i, :]

            def tiny(eng, fn):
                P.op(eng, fn, reads=[b_tsc], writes=[b_tsc])
            yield
            kTh = lambda h: qkvT[:, 8 + h, cs]
            vTh = lambda h: qkvT[:, 16 + h, cs]
            qTh = lambda h: qkvT[:, h, cs]
            kk_bk, b_kk = single()
            mm8(lambda h: kk_bk[0:64, h * 64:(h + 1) * 64], kTh, kTh, lambda h: [b_qkv[8 + h]], lambda h: [b_kk])
            bk_sc, b_sc = single()
            ba_ps = bk_sc[0:64, 0:16]
            for kc in range(KC):
                P.op("pe", lambda e, kc=kc, ba_ps=ba_ps, xb=xb, c=c: e.matmul(
                    ba_ps, lhsT=xb[:, kc, HALO + c * CH:HALO + (c + 1) * CH], rhs=Wba[:, kc, :],
                    start=(kc == 0), stop=(kc == KC - 1)), reads=[b_xb, b_Wba], pwrites=[b_sc])
            if main:
                ssq_ps = bk_sc[0:64, 16:24]
                for h in range(8):
                    P.op("pe", lambda e, h=h, ssq_ps=ssq_ps: e.matmul(ssq_ps[:, h:h + 1], lhsT=sq[:, h, cs], rhs=onesb[:, 0:1],
                                                                    start=True, stop=True), reads=[b_sq[h], b_onesb], pwrites=[b_sc])
            P.op("act", lambda e, T=T: e.activation(out=T(0), in_=ba_ps[:, 0:8], func=AF.Exp, scale=-1.0), reads=[b_sc], writes=[b_tsc])
            tiny("dve", lambda e, T=T: e.tensor_scalar(out=T(0), in0=T(0), scalar1=1.0, scalar2=None, op0=ALU.add))
            tiny("dve", lambda e, T=T: e.reciprocal(out=T(0), in_=T(0)))
            P.op("dve", lambda e, T=T: e.tensor_tensor(out=T(1), in0=ba_ps[:, 8:16], in1=avec[0:64, 8:16], op=ALU.add),
                 reads=[b_sc, b_avec, b_tsc], writes=[b_tsc])
            tiny("act", lambda e, T=T: e.activation(out=T(1), in_=T(1), func=AF.Exp))
            tiny("dve", lambda e, T=T: e.tensor_scalar(out=T(1), in0=T(1), scalar1=1.0, scalar2=None, op0=ALU.add))
            tiny("act", lambda e, T=T: e.activation(out=T(1), in_=T(1), func=AF.Ln))
            P.op("dve", lambda e, T=T: e.tensor_tensor(out=T(1), in0=T(1), in1=negA[0:64, :], op=ALU.mult),
                 reads=[b_tsc, b_negA], writes=[b_tsc])
            if main:
                P.op("dve", lambda e, T=T, ssq_ps=ssq_ps: e.tensor_scalar(out=T(14), in0=ssq_ps, scalar1=1e-6, scalar2=None, op0=ALU.add),
                     reads=[b_sc, b_tsc], writes=[b_tsc])
            yield
            P.op("act", lambda e: e.activation(out=KK_sb[:], in_=kk_bk[0:64, :].rearrange("p (h c) -> p h c", h=8), func=AF.Copy),
                 reads=[b_kk], writes=[b_KK])
            P.op("pool", lambda e: e.tensor_tensor(out=dS_all[:], in0=KK_sb[:], in1=bc_mid(cst[0:64, 0:64], 8, 64), op=ALU.mult),
                 reads=[b_KK, b_cst], writes=[b_dS])
            P.op("dve", lambda e, T=T: e.tensor_reduce(out=T(12), in_=dS_all[:], axis=AX.X, op=ALU.add), reads=[b_dS, b_tsc], writes=[b_tsc])
            tiny("dve", lambda e, T=T: e.tensor_scalar(out=T(12), in0=T(12), scalar1=1e-6, scalar2=None, op0=ALU.add))
            tiny("act", lambda e, T=T: e.activation(out=T(12), in_=T(12), func=AF.Ln))
            tiny("act", lambda e, T=T: e.activation(out=T(2), in_=T(12), func=AF.Exp, scale=-0.5))
            tiny("act", lambda e, T=T: e.activation(out=T(3), in_=T(12), func=AF.Exp, scale=0.5))
            tiny("dve", lambda e, T=T: e.tensor_tensor(out=T(9), in0=T(0), in1=T(2), op=ALU.mult))
            tiny("dve", lambda e, T=T: e.scalar_tensor_tensor(out=T(4), in0=T(9), scalar=-1.0, in1=T(2), op0=ALU.mult, op1=ALU.mult))
            yield
            bk_g, b_g = single()
            Gc_ps, Gr_ps, Gl_ps = bk_g[0:64, 0:8], bk_g[0:64, 8:16], bk_g[:, 16:24]
            P.op("pe", lambda e, T=T: e.matmul(Gc_ps, lhsT=Uincl, rhs=T(1), start=True, stop=True), reads=[b_cst, b_tsc], pwrites=[b_g])
            P.op("pe", lambda e, T=T: e.matmul(Gr_ps, lhsT=Lstrict, rhs=T(1), start=True, stop=True), reads=[b_cst, b_tsc], pwrites=[b_g])
            P.op("pe", lambda e, T=T: e.matmul(Gl_ps, lhsT=ones64, rhs=T(1), start=True, stop=True), reads=[b_cst, b_tsc], pwrites=[b_g])
            P.op("act", lambda e, T=T: e.activation(out=T(5), in_=Gc_ps, func=AF.Exp), reads=[b_g], writes=[b_tsc])
            P.op("act", lambda e, T=T: e.activation(out=T(6), in_=Gr_ps, func=AF.Exp), reads=[b_g], writes=[b_tsc])
            P.op("act", lambda e, tsc=tsc: e.activation(out=tsc[:, 7, :], in_=Gl_ps, func=AF.Exp), reads=[b_g], writes=[b_tsc])
            tiny("dve", lambda e, T=T: e.tensor_tensor(out=T(6), in0=T(6), in1=T(2), op=ALU.mult))
            tiny("dve", lambda e, T=T: e.tensor_scalar(out=T(8), in0=T(5), scalar1=-1.0, scalar2=None, op0=ALU.mult))
            if main:
                tiny("act", lambda e, T=T: e.activation(out=T(14), in_=T(14), func=AF.Ln))
                tiny("act", lambda e, T=T: e.activation(out=T(10), in_=T(14), func=AF.Exp, scale=-0.5))
                tiny("dve", lambda e, T=T: e.tensor_scalar(out=T(10), in0=T(10), scalar1=float(128 ** -0.5), scalar2=None, op0=ALU.mult))
            yield
            (pa, pb), (b_pa, b_pb) = pairs.next()
            pk = (pa, pb)
            bpk = (b_pa, b_pb)
            mm8(lambda h: pk[h // 4][0:64, (h % 4) * 128:(h % 4 + 1) * 128], kTh, lambda h: identb[:],
                lambda h: [b_qkv[8 + h], b_identb], lambda h: [bpk[h // 4]])
            for i in range(2):
                P.op("dve", lambda e, i=i, T=T: e.tensor_tensor(
                    out=kd_all[:, 4 * i:4 * i + 4, :], in0=pk[i][0:64, :].rearrange("p (h d) -> p h d", h=4),
                    in1=bc_in(T(6)[:, 4 * i:4 * i + 4], 4, 128), op=ALU.mult), reads=[bpk[i], b_tsc], pwrites=[b_kd])
            yield
            (pa2, pb2), (b_pa2, b_pb2) = pairs.next()
            pv = (pa2, pb2)
            bpv = (b_pa2, b_pb2)
            mm8(lambda h: pv[h // 4][0:64, (h % 4) * 128:(h % 4 + 1) * 128], vTh, lambda h: identb[:],
                lambda h: [b_qkv[16 + h], b_identb], lambda h: [bpv[h // 4]])
            for i in range(2):
                P.op("dve", lambda e, i=i, T=T: e.tensor_tensor(
                    out=vp_all[:, 4 * i:4 * i + 4, :], in0=pv[i][0:64, :].rearrange("p (h d) -> p h d", h=4),
                    in1=bc_in(T(3)[:, 4 * i:4 * i + 4], 4, 128), op=ALU.mult), reads=[bpv[i], b_tsc], pwrites=[b_vp])
            yield
            P.op("pool", lambda e, T=T: e.tensor_tensor(out=gdU_all[:], in0=bc_mid(Uincl, 8, 64), in1=bc_in(T(1), 8, 64), op=ALU.mult),
                 reads=[b_cst, b_tsc], writes=[b_gdU])
            yield
            gd_bk, b_gd = single()
            P.op("pe", lambda e: e.matmul(gd_bk[0:64, :], lhsT=Lstrict, rhs=gdU_all[:].rearrange("p h c -> p (h c)"), start=True, stop=True),
                 reads=[b_cst, b_gdU], pwrites=[b_gd])
            P.op("act", lambda e: e.activation(out=E_all[:], in_=gd_bk[0:64, :].rearrange("p (h c) -> p h c", h=8), func=AF.Exp),
                 reads=[b_gd], writes=[b_E])
            yield
            P.op("pool", lambda e: e.tensor_tensor(out=dS_all[:], in0=E_all[:], in1=bc_mid(Ustrict, 8, 64), op=ALU.mult),
                 reads=[b_E, b_cst], writes=[b_dS])
            P.op("pool", lambda e, T=T: e.tensor_tensor(out=dS_all[:], in0=dS_all[:], in1=bc_in(T(4), 8, 64), op=ALU.mult),
                 reads=[b_dS, b_tsc], writes=[b_dS])
            P.op("pool", lambda e: e.tensor_tensor(out=Y_all[:], in0=KK_sb[:], in1=dS_all[:], op=ALU.mult),
                 reads=[b_KK, b_dS], writes=[b_Y])
            if main:
                P.op("pool", lambda e: e.tensor_tensor(out=dI_all[:], in0=E_all[:], in1=bc_mid(Uincl, 8, 64), op=ALU.mult),
                     reads=[b_E, b_cst], writes=[b_dI])
                P.op("pool", lambda e, T=T: e.tensor_tensor(out=dI_all[:], in0=dI_all[:], in1=bc_in(T(2), 8, 64), op=ALU.mult),
                     reads=[b_dI, b_tsc], writes=[b_dI])
                kq_bk, b_kq = single()
                mm8(lambda h: kq_bk[0:64, h * 64:(h + 1) * 64], kTh, qTh, lambda h: [b_qkv[8 + h], b_qkv[h]], lambda h: [b_kq])
                P.op("dve", lambda e: e.tensor_tensor(out=QKT_all[:], in0=kq_bk[0:64, :].rearrange("p (h c) -> p h c", h=8), in1=dI_all[:], op=ALU.mult),
                     reads=[b_kq, b_dI], writes=[b_QKT])
            yield
            z_bk, b_zb = single()
            mm8(lambda h: z_bk[0:64, h * 64:(h + 1) * 64], lambda h: Y_all[:, h, :], lambda h: identG[0:64, 0:64],
                lambda h: [b_Y, b_identG], lambda h: [b_zb])
            P.op("act", lambda e: e.activation(out=Z_all[:], in_=z_bk[0:64, :].rearrange("p (h c) -> p h c", h=8), func=AF.Copy),
                 reads=[b_zb], writes=[b_Z])
            P.op("pool", lambda e: e.tensor_tensor(out=P_all[:], in0=Y_all[:], in1=bc_mid(cst[0:64, 0:64], 8, 64), op=ALU.add),
                 reads=[b_Y, b_cst], writes=[b_P])
            def squares(lvl):
                zn_bk, b_zn = single()
                mm8(lambda h: zn_bk[0:64, h * 64:(h + 1) * 64], lambda h: Y_all[:, h, :], lambda h: Z_all[:, h, :],
                    lambda h: [b_Y, b_Z], lambda h: [b_zn])
                yn = None
                if lvl < 5:
                    yn_bk, b_yn = single()
                    mm8(lambda h: yn_bk[0:64, h * 64:(h + 1) * 64], lambda h: Z_all[:, h, :], lambda h: Y_all[:, h, :],
                        lambda h: [b_Y, b_Z], lambda h: [b_yn])
                    yn = (yn_bk, b_yn)
                return (zn_bk, b_zn), yn

            def evacs(zn, yn):
                zn_bk, b_zn = zn
                P.op("act", lambda e: e.activation(out=Z_all[:], in_=zn_bk[0:64, :].rearrange("p (h c) -> p h c", h=8), func=AF.Copy),
                     reads=[b_zn], writes=[b_Z])
                if yn is not None:
                    yn_bk, b_yn = yn
                    P.op("act", lambda e: e.activation(out=Y_all[:], in_=yn_bk[0:64, :].rearrange("p (h c) -> p h c", h=8), func=AF.Copy),
                         reads=[b_yn], writes=[b_Y])

            def pupdate():
                pu_bk, b_pu = single()
                mm8(lambda h: pu_bk[0:64, h * 64:(h + 1) * 64], lambda h: Z_all[:, h, :], lambda h: P_all[:, h, :],
                    lambda h: [b_Z, b_P], lambda h: [b_pu])
                P.op("dve", lambda e: e.tensor_tensor(out=P_all[:], in0=pu_bk[0:64, :].rearrange("p (h c) -> p h c", h=8), in1=P_all[:], op=ALU.add),
                     reads=[b_pu, b_P], writes=[b_P])

            zn, yn = squares(1)
            yield
            evacs(zn, yn)
            yield
            for lvl in range(2, 6):
                zn, yn = squares(lvl)
                yield
                pupdate()
                yield
                evacs(zn, yn)
                yield
            pupdate()
            yield

        def scan(ctx):
            bi, main, xb, b_xb, c, tsc, b_tsc, par, szi = unpack(ctx)
            kd_all, b_kd = kd_2[par]
            vp_all, b_vp = vp_2[par]
            QKT_all, b_QKT = QKT_2[par]
            P_all, b_P = P_2[par]
            cs = slice(c * CH, (c + 1) * CH)
            kTh = lambda h: qkvT[:, 8 + h, cs]
            qTh = lambda h: qkvT[:, h, cs]

            def T(i, rows=64, tsc=tsc):
                return tsc[0:rows, i, :]

            def tiny(eng, fn):
                P.op(eng, fn, reads=[b_tsc], writes=[b_tsc])
            (ra, rb), (b_ra, b_rb) = pairs.next()
            pr = (ra, rb)
            bpr = (b_ra, b_rb)
            mm8(lambda h: pr[h // 4][0:64, (h % 4) * 128:(h % 4 + 1) * 128], kTh, lambda h: Sbf[:, h, :],
                lambda h: [b_qkv[8 + h], b_Sbf], lambda h: [bpr[h // 4]])
            for i in range(2):
                P.op("dve", lambda e, i=i, T=T: e.tensor_tensor(
                    out=tR[:, 4 * i:4 * i + 4, :], in0=pr[i][0:64, :].rearrange("p (h d) -> p h d", h=4),
                    in1=bc_in(T(8)[:, 4 * i:4 * i + 4], 4, 128), op=ALU.mult), reads=[bpr[i], b_tsc], pwrites=[b_tR])
            yield
            P.op("pool", lambda e: e.tensor_tensor(out=D1_all[:], in0=tR[:], in1=vp_all[:], op=ALU.add), reads=[b_tR, b_vp], writes=[b_D1])
            yield
            (va, vb), (b_va, b_vb) = pairs.next()
            pvn = (va, vb)
            bpvn = (b_va, b_vb)
            mm8(lambda h: pvn[h // 4][0:64, (h % 4) * 128:(h % 4 + 1) * 128], lambda h: P_all[:, h, :], lambda h: D1_all[:, h, :],
                lambda h: [b_P, b_D1], lambda h: [bpvn[h // 4]])
            for i in range(2):
                P.op("dve", lambda e, i=i, T=T: e.tensor_tensor(
                    out=vn_all[:, 4 * i:4 * i + 4, :], in0=pvn[i][0:64, :].rearrange("p (h d) -> p h d", h=4),
                    in1=bc_in(T(9)[:, 4 * i:4 * i + 4], 4, 128), op=ALU.mult), reads=[bpvn[i], b_tsc], pwrites=[b_vn])
            yield
            if main:
                (qa, qb), (b_qa, b_qb) = pairs.next()
                p1 = (qa, qb)
                bp1 = (b_qa, b_qb)
                mm8(lambda h: p1[h // 4][0:64, (h % 4) * 128:(h % 4 + 1) * 128], qTh, lambda h: Sbf[:, h, :],
                    lambda h: [b_qkv[h], b_Sbf], lambda h: [bp1[h // 4]])
                for i in range(2):
                    P.op("dve", lambda e, i=i, T=T: e.tensor_tensor(
                        out=o_all[:, 4 * i:4 * i + 4, :], in0=p1[i][0:64, :].rearrange("p (h d) -> p h d", h=4),
                        in1=bc_in(T(5)[:, 4 * i:4 * i + 4], 4, 128), op=ALU.mult), reads=[bp1[i], b_tsc], pwrites=[b_oall])
                yield
                (wa, wb), (b_wa, b_wb) = pairs.next()
                p2 = (wa, wb)
                bp2 = (b_wa, b_wb)
                mm8(lambda h: p2[h // 4][0:64, (h % 4) * 128:(h % 4 + 1) * 128], lambda h: QKT_all[:, h, :], lambda h: vn_all[:, h, :],
                    lambda h: [b_QKT, b_vn], lambda h: [bp2[h // 4]])
                for i in range(2):
                    P.op("dve", lambda e, i=i: e.tensor_tensor(
                        out=o_all[:, 4 * i:4 * i + 4, :], in0=p2[i][0:64, :].rearrange("p (h d) -> p h d", h=4),
                        in1=o_all[:, 4 * i:4 * i + 4, :], op=ALU.add), reads=[bp2[i], b_oall], writes=[b_oall])
            yield
            (sa, sb_), (b_sa, b_sb) = pairs.next()
            psu = (sa, sb_)
            bpsu = (b_sa, b_sb)
            mm8(lambda h: psu[h // 4][:, (h % 4) * 128:(h % 4 + 1) * 128], lambda h: kd_all[:, h, :], lambda h: vn_all[:, h, :],
                lambda h: [b_kd, b_vn], lambda h: [bpsu[h // 4]])
            yield
            P.op("pool", lambda e, tsc=tsc: e.tensor_tensor(out=S[:], in0=S[:], in1=bc_in(tsc[:, 7, :], 8, 128, rows=128), op=ALU.mult),
                 reads=[b_S, b_tsc], writes=[b_S])
            for i in range(2):
                P.op("dve", lambda e, i=i: e.tensor_tensor(
                    out=S[:, 4 * i:4 * i + 4, :], in0=psu[i][:, :].rearrange("p (h d) -> p h d", h=4),
                    in1=S[:, 4 * i:4 * i + 4, :], op=ALU.add), reads=[bpsu[i], b_S], writes=[b_S])
            P.op("act", lambda e: e.activation(out=Sbf[:], in_=S[:], func=AF.Copy), reads=[b_S], writes=[b_Sbf])
            if not main:
                return
            yield
            P.op("pool", lambda e: e.tensor_tensor(out=og[:], in0=o_all[:], in1=o_all[:], op=ALU.mult), reads=[b_oall], writes=[b_og])
            P.op("dve", lambda e, T=T: e.tensor_reduce(out=T(13), in_=og[:], axis=AX.X, op=ALU.add), reads=[b_og, b_tsc], writes=[b_tsc])
            tiny("dve", lambda e, T=T: e.tensor_tensor(out=T(13), in0=T(13), in1=T(10), op=ALU.mult))
            tiny("dve", lambda e, T=T: e.tensor_tensor(out=T(13), in0=T(13), in1=T(10), op=ALU.mult))
            tiny("dve", lambda e, T=T: e.tensor_scalar(out=T(13), in0=T(13), scalar1=float(1.0 / 128), scalar2=1e-6, op0=ALU.mult, op1=ALU.add))
            tiny("act", lambda e, T=T: e.activation(out=T(13), in_=T(13), func=AF.Ln))
            tiny("act", lambda e, T=T: e.activation(out=T(13), in_=T(13), func=AF.Exp, scale=-0.5))
            tiny("dve", lambda e, T=T: e.tensor_tensor(out=T(11), in0=T(13), in1=T(10), op=ALU.mult))
            P.op("pool", lambda e, T=T: e.tensor_tensor(out=og[:], in0=o_all[:], in1=bc_in(T(11), 8, 128), op=ALU.mult),
                 reads=[b_oall, b_tsc, b_og], writes=[b_og])
            P.op("pool", lambda e, c=c: e.tensor_tensor(out=og[:], in0=og[:], in1=sz[:, szi + c % 2, :].rearrange("p (h d) -> p h d", h=8), op=ALU.mult),
                 reads=[b_og, b_sz[szi + c % 2]], writes=[b_og])
            yield
            bk, b_bk = big.next()
            for h in range(8):
                P.op("pe", lambda e, bk=bk, h=h: e.transpose(out=bk[:, h * 64:(h + 1) * 64], in_=og[:, h, :], identity=cst[0:64, 0:64]),
                     reads=[b_og, b_cst], pwrites=[b_bk])
            half = c % 2
            P.op("act", lambda e, bk=bk, half=half: e.activation(
                out=ogT[:, :, half * 64:(half + 1) * 64], in_=bk[:, :].rearrange("p (h t) -> p h t", h=8), func=AF.Identity, scale=normT[:, 0:1]),
                reads=[b_bk, b_normT], pwrites=[b_ogT])
            yield
            if half == 1:
                ti = (bi - NPRE) * 3 + c // 2
                for h in range(8):
                    for hf, bkw in ((0, wide0), (1, wide1)):
                        P.op("pe", lambda e, h=h, hf=hf, bkw=bkw: e.matmul(
                            bkw[:, :], lhsT=ogT[:, h, :], rhs=Wout[:, h, hf * 512:(hf + 1) * 512], start=(h == 0), stop=(h == 7)),
                            reads=[b_ogT, b_Wout], pwrites=[b_wide])
                ln_epilogue(ti, xtok, None, 0, final=(upto == 1))

        nchunk = [0]

        def run_interleaved(gens):
            gens = list(gens)
            while gens:
                for g in list(gens):
                    try:
                        next(g)
                    except StopIteration:
                        gens.remove(g)

        for bi in range(NB):
            main = bi >= NPRE
            t0 = bi * BLK
            xb, b_xb = xb_r.next()
            xv = xT.rearrange("(kc p) t -> p kc t", p=128)
            P.dma("pool", lambda e, xb=xb, t0=t0: e.dma_start(out=xb[:], in_=xv[:, :, t0:t0 + BLK + HALO]), "xb0",
                  writes=[b_xb])
            chunks = list(range(24)) if main else list(range(8, 24))

            def conv_stage(m, pre, b_pre):
                bk2, b_bk2 = big.next()
                for j in range(4):
                    P.op("pe", lambda e, bk2=bk2, m=m, j=j, pre=pre: e.matmul(
                        bk2[:, 0:BLK], lhsT=dg[:, j, m, :], rhs=pre[:, j:j + BLK], start=(j == 0), stop=(j == 3)),
                        reads=[b_dg, b_pre], pwrites=[b_bk2])
                P.op("act", lambda e, bk2=bk2, m=m: e.activation(out=qkvT[:, m, :], in_=bk2[:, 0:BLK], func=AF.Silu),
                     reads=[b_bk2], writes=[b_qkv[m]])
                if m < 8:
                    P.op("pool", lambda e, m=m: e.tensor_tensor(out=sq[:, m, :], in0=qkvT[:, m, :], in1=qkvT[:, m, :], op=ALU.mult),
                         reads=[b_qkv[m]], writes=[b_sq[m]])

            prev = None
            for m in chunks:
                bk, b_bk = big.next()
                for kc in range(KC):
                    P.op("pe", lambda e, bk=bk, m=m, kc=kc, xb=xb: e.matmul(
                        bk[:, 0:BLK + HALO], lhsT=Wqkv[:, kc, m * 128:(m + 1) * 128], rhs=xb[:, kc, :],
                        start=(kc == 0), stop=(kc == KC - 1)), reads=[wqkv_b(kc, m * 128), b_xb], pwrites=[b_bk])
                pre, b_pre = pre_r.next()
                P.op("dve", lambda e, pre=pre, bk=bk: e.tensor_copy(out=pre[:], in_=bk[:, 0:BLK + HALO]), reads=[b_bk], writes=[b_pre])
                if prev is not None:
                    conv_stage(*prev)
                prev = (m, pre, b_pre)
            conv_stage(*prev)
            pending = None
            for c in range(6):
                tsc, b_tsc = tsc_r.next()
                nchunk[0] += 1
                ctx = dict(bi=bi, main=main, xb=xb, b_xb=b_xb, c=c, tsc=tsc, b_tsc=b_tsc, par=nchunk[0] % 2,
                           szi=((c // 2) % 2) * 2)
                gens = [prep(ctx)] + ([pending] if pending is not None else [])
                run_interleaved(gens)
                pending = scan(ctx)
            run_interleaved([pending])

    def load_xb(xb, b_xb, src_i, t0, key):
        v = hT_s[src_i].rearrange("(kc p) t -> p kc t", p=128)
        P.dma("sp", lambda e: e.dma_start(out=xb[:], in_=v[:, :, HALO + t0 - 2:HALO + t0 + BLK]), key,
              reads=[b_hTs[src_i]], writes=[b_xb])

    def phase_ffn(li, src_i, dst_i, final):
        Wup = P.sb([128, KC, 2 * DFF], BF16, "Wup")
        Wd = P.sb([128, FC, D], BF16, "Wd")
        dgF = P.sb([128, 3, FC, 128], BF16, "dgF")
        b_Wup, b_Wd, b_dgF = P.buf(), P.buf(), P.buf()
        wup_b = load_w_cast(Wup, b_Wup, f_w_up[li].rearrange("(kc p) n -> p kc n", p=128), 2 * DFF, "Wup")
        wd_b = load_w_cast(Wd, b_Wd, f_w_down[li].rearrange("(m p) n -> p m n", p=128), D, "Wd")
        make_diag(dgF, b_dgF, f_convT[li], FC, 3, "ctF")
        load_ln(1 + 2 * li)
        wide_r = Ring([(wide0, wide1, b_wide), (banks[6], banks[7], b_wideB)])
        hres2_r = Ring([(P.sb([128, D], F32, f"hresX{i}"), P.buf(), f"hresX{i}") for i in range(2)])
        xb_r = Ring([(P.sb([128, KC, BLK + 2], BF16, f"fxb{i}"), P.buf()) for i in range(1)])
        pre_r = Ring([(P.sb([128, BLK + 2], BF16, f"fpre{i}"), P.buf()) for i in range(2)])
        su_r = Ring([(P.sb([128, BLK], BF16, f"fsu{i}"), P.buf()) for i in range(2)])
        aT = P.sb([128, FC, BLK], BF16, "aT")
        b_aT = P.buf()
        for bi in range(NMAIN):
            t0 = bi * BLK
            xb, b_xb = xb_r.next()
            load_xb(xb, b_xb, src_i, t0, "fxb0")
            for m in range(FC):
                bk, b_bk = big6.next()
                for kc in range(KC):
                    P.op("pe", lambda e, bk=bk, m=m, kc=kc, xb=xb: e.matmul(
                        bk[:, 0:BLK + 2], lhsT=Wup[:, kc, m * 128:(m + 1) * 128], rhs=xb[:, kc, :],
                        start=(kc == 0), stop=(kc == KC - 1)), reads=[wup_b(kc, m * 128), b_xb], pwrites=[b_bk])
                pre, b_pre = pre_r.next()
                if m % 2 == 0:
                    P.op("dve", lambda e, pre=pre, bk=bk: e.tensor_copy(out=pre[:], in_=bk[:, 0:BLK + 2]), reads=[b_bk], writes=[b_pre])
                else:
                    P.op("act", lambda e, pre=pre, bk=bk: e.activation(out=pre[:], in_=bk[:, 0:BLK + 2], func=AF.Copy), reads=[b_bk], writes=[b_pre])
                bk3, b_bk3 = big6.next()
                for kc in range(KC):
                    P.op("pe", lambda e, bk3=bk3, m=m, kc=kc, xb=xb: e.matmul(
                        bk3[:, 0:BLK], lhsT=Wup[:, kc, DFF + m * 128:DFF + (m + 1) * 128], rhs=xb[:, kc, 2:BLK + 2],
                        start=(kc == 0), stop=(kc == KC - 1)), reads=[wup_b(kc, DFF + m * 128), b_xb], pwrites=[b_bk3])
                bk2, b_bk2 = big6.next()
                for j in range(3):
                    P.op("pe", lambda e, bk2=bk2, m=m, j=j, pre=pre: e.matmul(
                        bk2[:, 0:BLK], lhsT=dgF[:, j, m, :], rhs=pre[:, j:j + BLK], start=(j == 0), stop=(j == 2)),
                        reads=[b_dgF, b_pre], pwrites=[b_bk2])
                su, b_su = su_r.next()
                P.op("act", lambda e, bk2=bk2, su=su: e.activation(out=su[:], in_=bk2[:, 0:BLK], func=AF.Silu), reads=[b_bk2], writes=[b_su])
                P.op("dve", lambda e, bk3=bk3, su=su, m=m: e.tensor_tensor(out=aT[:, m, :], in0=bk3[:, 0:BLK], in1=su[:], op=ALU.mult),
                     reads=[b_bk3, b_su], pwrites=[b_aT])
            for tl in range(3):
                wsel = wide_r.next()
                for m in range(FC):
                    for hf, bkw in ((0, wsel[0]), (1, wsel[1])):
                        P.op("pe", lambda e, m=m, hf=hf, bkw=bkw, tl=tl: e.matmul(
                            bkw[:, :], lhsT=aT[:, m, tl * TILE:(tl + 1) * TILE], rhs=Wd[:, m, hf * 512:(hf + 1) * 512],
                            start=(m == 0), stop=(m == FC - 1)), reads=[b_aT, wd_b(m, 0)], pwrites=[wsel[2]])
                ln_epilogue(bi * 3 + tl, htok_s[src_i], b_hts[src_i], dst_i, final, wide=wsel, hres_ring=hres2_r)

    def phase_sconv(src_i, dst_i, final):
        Win = P.sb([128, KC, 3 * D], BF16, "Win")
        Wo = P.sb([128, KC, D], BF16, "Wo")
        dgB = P.sb([128, 3, 8, 128], BF16, "dgB")
        b_Win, b_Wo, b_dgB = P.buf(), P.buf(), P.buf()
        win_b = load_w_cast(Win, b_Win, b_w_in.rearrange("(kc p) n -> p kc n", p=128), 3 * D, "Win")
        b_Wo = load_w_cast(Wo, b_Wo, b_w_out.rearrange("(kc p) n -> p kc n", p=128), D, "Wo")(0, 0)
        make_diag(dgB, b_dgB, b_convT[:, :], 8, 3, "ctB")
        load_ln(2)
        wide_r = Ring([(wide0, wide1, b_wide), (banks[6], banks[7], b_wideB)])
        hres2_r = Ring([(P.sb([128, D], F32, f"hresX{i}"), P.buf(), f"hresX{i}") for i in range(2)])
        xb_r = Ring([(P.sb([128, KC, BLK + 2], BF16, f"sxb{i}"), P.buf()) for i in range(2)])
        c_r = Ring([(P.sb([128, BLK + 2], BF16, f"scs{i}"), P.buf()) for i in range(2)])
        cx_r = Ring([(P.sb([128, BLK + 2], BF16, f"scx{i}"), P.buf()) for i in range(2)])
        bs_r = Ring([(P.sb([128, BLK], BF16, f"sbs{i}"), P.buf()) for i in range(2)])
        vT = P.sb([128, 8, BLK], BF16, "vTs")
        b_vT = P.buf()
        for bi in range(NMAIN):
            t0 = bi * BLK
            xb, b_xb = xb_r.next()
            load_xb(xb, b_xb, src_i, t0, f"sxb{xb_r.i % 2}")
            for m in range(8):
                bk, b_bk = big6.next()
                for kc in range(KC):
                    P.op("pe", lambda e, bk=bk, m=m, kc=kc, xb=xb: e.matmul(
                        bk[:, 0:BLK + 2], lhsT=Win[:, kc, D + m * 128:D + (m + 1) * 128], rhs=xb[:, kc, :],
                        start=(kc == 0), stop=(kc == KC - 1)), reads=[win_b(kc, D + m * 128), b_xb], pwrites=[b_bk])
                cs_, b_cs = c_r.next()
                P.op("act", lambda e, cs_=cs_, bk=bk: e.activation(out=cs_[:], in_=bk[:, 0:BLK + 2], func=AF.Copy), reads=[b_bk], writes=[b_cs])
                bk2, b_bk2 = big6.next()
                for kc in range(KC):
                    P.op("pe", lambda e, bk2=bk2, m=m, kc=kc, xb=xb: e.matmul(
                        bk2[:, 0:BLK + 2], lhsT=Win[:, kc, 2 * D + m * 128:2 * D + (m + 1) * 128], rhs=xb[:, kc, :],
                        start=(kc == 0), stop=(kc == KC - 1)), reads=[win_b(kc, 2 * D + m * 128), b_xb], pwrites=[b_bk2])
                cx, b_cx = cx_r.next()
                P.op("dve", lambda e, cx=cx, bk2=bk2, cs_=cs_: e.tensor_tensor(out=cx[:], in0=bk2[:, 0:BLK + 2], in1=cs_[:], op=ALU.mult),
                     reads=[b_bk2, b_cs], writes=[b_cx])
                bk4, b_bk4 = big6.next()
                for kc in range(KC):
                    P.op("pe", lambda e, bk4=bk4, m=m, kc=kc, xb=xb: e.matmul(
                        bk4[:, 0:BLK], lhsT=Win[:, kc, m * 128:(m + 1) * 128], rhs=xb[:, kc, 2:BLK + 2],
                        start=(kc == 0), stop=(kc == KC - 1)), reads=[win_b(kc, m * 128), b_xb], pwrites=[b_bk4])
                bk3, b_bk3 = big6.next()
                for j in range(3):
                    P.op("pe", lambda e, bk3=bk3, m=m, j=j, cx=cx: e.matmul(
                        bk3[:, 0:BLK], lhsT=dgB[:, j, m, :], rhs=cx[:, j:j + BLK], start=(j == 0), stop=(j == 2)),
                        reads=[b_dgB, b_cx], pwrites=[b_bk3])
                bs, b_bs = bs_r.next()
                P.op("act", lambda e, bs=bs, bk4=bk4: e.activation(out=bs[:], in_=bk4[:, 0:BLK], func=AF.Copy), reads=[b_bk4], writes=[b_bs])
                P.op("dve", lambda e, bk3=bk3, bs=bs, m=m: e.tensor_tensor(out=vT[:, m, :], in0=bk3[:, 0:BLK], in1=bs[:], op=ALU.mult),
                     reads=[b_bk3, b_bs], pwrites=[b_vT])
            for tl in range(3):
                wsel = wide_r.next()
                for m in range(8):
                    for hf, bkw in ((0, wsel[0]), (1, wsel[1])):
                        P.op("pe", lambda e, m=m, hf=hf, bkw=bkw, tl=tl: e.matmul(
                            bkw[:, :], lhsT=vT[:, m, tl * TILE:(tl + 1) * TILE], rhs=Wo[:, m, hf * 512:(hf + 1) * 512],
                            start=(m == 0), stop=(m == 7)), reads=[b_vT, b_Wo], pwrites=[wsel[2]])
                ln_epilogue(bi * 3 + tl, htok_s[src_i], b_hts[src_i], dst_i, final, wide=wsel, hres_ring=hres2_r)

    with P.phase():
        phase1()
    if upto >= 2:
        with P.phase():
            phase_ffn(0, 0, 1, final=(upto == 2))
    if upto >= 3:
        with P.phase():
            phase_sconv(1, 2, final=(upto == 3))
    if upto >= 4:
        with P.phase():
            phase_ffn(1, 2, 0, final=True)
    P.stack.close()
    return nc


def make_consts():
    c = np.zeros((128, 448), np.float32)
    c[:, 0:128] = np.eye(128, dtype=np.float32)
    r = np.arange(64)[:, None]
    q = np.arange(64)[None, :]
    c[0:64, 128:192] = (r <= q)
    c[0:64, 192:256] = (r > q)
    c[0:64, 256:320] = (r < q)
    c[:, 320:448] = 1.0
    return c


def rep128(v):
    v = np.asarray(v, np.float32).reshape(1, -1)
    return np.ascontiguousarray(np.repeat(v, 128, axis=0))


def weight_inputs(a_w_in, a_conv, a_log, a_dt_bias, a_norm, a_w_out, b_w_in, b_conv, b_w_out,
                  ln_mix_g, ln_mix_b, ffn_w_up, ffn_conv, ffn_w_down, ln_ffn_g, ln_ffn_b):
    f = np.float32
    d = {}
    d["consts"] = make_consts()
    d["a_w_in"] = np.ascontiguousarray(a_w_in[0], f)
    d["a_convT"] = np.ascontiguousarray(a_conv[0].T.reshape(24, 128, 4).transpose(1, 0, 2).reshape(128, 96), f)
    d["a_vec"] = np.concatenate([rep128(a_log[0]), rep128(a_dt_bias[0])], axis=1)
    d["a_normT"] = np.ascontiguousarray(a_norm[0].reshape(128, 1), f)
    d["a_w_out"] = np.ascontiguousarray(a_w_out[0], f)
    d["b_w_in"] = np.ascontiguousarray(b_w_in[0], f)
    d["b_convT"] = np.ascontiguousarray(b_conv[0].T.reshape(8, 128, 3).transpose(1, 0, 2).reshape(128, 24), f)
    d["b_w_out"] = np.ascontiguousarray(b_w_out[0], f)
    d["lng"] = np.stack([rep128(ln_mix_g[0]), rep128(ln_ffn_g[0]), rep128(ln_mix_g[1]), rep128(ln_ffn_g[1])])
    d["lnb"] = np.stack([rep128(ln_mix_b[0]), rep128(ln_ffn_b[0]), rep128(ln_mix_b[1]), rep128(ln_ffn_b[1])])
    d["f_w_up"] = np.ascontiguousarray(ffn_w_up, f)
    d["f_convT"] = np.stack([np.ascontiguousarray(ffn_conv[i].T.reshape(FC, 128, 3).transpose(1, 0, 2).reshape(128, FC * 3)) for i in range(2)]).astype(f)
    d["f_w_down"] = np.ascontiguousarray(ffn_w_down, f)
    return d


def core_stream_inputs(stream, valid, NPRE, NMAIN):
    ntok = (NPRE + NMAIN) * BLK
    assert stream.shape[0] == ntok
    xT = np.zeros((D, HALO + ntok), np.float32)
    xT[:, HALO:] = stream.T
    m0 = NPRE * BLK
    return {"xT": xT, "xtok": np.ascontiguousarray(stream[m0:]), "mask": valid[m0:m0 + 128].astype(np.float32).reshape(128, 1)}


_CACHE = {}


def kernel(x, meta, a_w_in, a_conv, a_log, a_dt_bias, a_norm, a_w_out, b_w_in, b_conv, b_w_out,
           ln_mix_g, ln_mix_b, ffn_w_up, ffn_conv, ffn_w_down, ln_ffn_g, ln_ffn_b):
    x = np.asarray(x, np.float32)
    meta = np.asarray(meta, np.float32)
    B, SEQ, _ = x.shape
    NPRE, NMAIN = NPRE_FULL, NMAIN_FULL
    ntok = (NPRE + NMAIN) * BLK
    half_tok = NMAIN * BLK - TILE
    w = weight_inputs(*[np.asarray(a, np.float32) for a in (a_w_in, a_conv, a_log, a_dt_bias, a_norm, a_w_out, b_w_in, b_conv, b_w_out,
                                                            ln_mix_g, ln_mix_b, ffn_w_up, ffn_conv, ffn_w_down, ln_ffn_g, ln_ffn_b)])
    in_maps = []
    for core in range(8):
        b, half = core // 2, core % 2
        stream = np.zeros((ntok, D), np.float32)
        valid = np.zeros((ntok,), bool)
        n_x = half_tok * (half + 1)
        seq = np.concatenate([meta, x[b, :n_x]], axis=0)
        stream[ntok - seq.shape[0]:] = seq
        valid[ntok - seq.shape[0]:] = True
        m = dict(w)
        m.update(core_stream_inputs(stream, valid, NPRE, NMAIN))
        in_maps.append(m)
    if "nc" not in _CACHE:
        _CACHE["nc"] = build_program(NPRE, NMAIN)
    res = run_bass_kernel_spmd(_CACHE["nc"], in_maps, core_ids=list(range(8)))
    outp = np.zeros((B, SEQ, D), np.float32)
    for core in range(8):
        b, half = core // 2, core % 2
        outp[b, half * half_tok:(half + 1) * half_tok] = res.results[core]["out"]
    return outp
```
